# Optimizing a Trainium2 kernel written in Bass

```python
import jax, jax.numpy as jnp
from jax import lax
import numpy as np

D_MODEL = 2048
BATCH = 4
SEQ = 8192
DEPTH = 1

CHUNK = 64
EPS = 1e-6
SSM_D_INNER = 2048
SSM_HEAD_DIM = 64
SSM_HEADS = SSM_D_INNER // SSM_HEAD_DIM
SSM_GROUPS = 4
SSM_HEADS_PER_GROUP = SSM_HEADS // SSM_GROUPS
SSM_STATE = 128
SSM_CONV = 4
SSD_CHUNK = 64
SSM_CONV_DIM = SSM_D_INNER + 2 * SSM_GROUPS * SSM_STATE
ATTN_HEADS = 16
ATTN_HEAD_DIM = 128
ATTN_D = ATTN_HEADS * ATTN_HEAD_DIM
IDX_HEADS = 16
IDX_HEAD_DIM = 64
TOPK_MAX = 256
ATTN_SCALE = ATTN_HEAD_DIM ** -0.5
IDX_SCALE = (IDX_HEAD_DIM ** -0.5) * (IDX_HEADS ** -0.5)
N_BRANCHES = 2

IN_SPLITS = (
    SSM_D_INNER,
    SSM_CONV_DIM,
    SSM_HEADS,
    ATTN_D,
    ATTN_D,
    ATTN_D,
    ATTN_D,
    IDX_HEADS * IDX_HEAD_DIM,
    IDX_HEAD_DIM,
    IDX_HEADS,
    N_BRANCHES * D_MODEL,
)
D_IN = sum(IN_SPLITS)
BRANCH_IN = SSM_D_INNER + ATTN_D

kernel_name = "hybrid_ssd_dsa_gated_block"


def rmsnorm(x, g):
    xf = x.astype(jnp.float32)
    y = xf * lax.rsqrt(jnp.mean(xf * xf, axis=-1, keepdims=True) + EPS)
    return (y * g.astype(jnp.float32)).astype(x.dtype)


def causal_dwconv(u, w, b):
    L = u.shape[1]
    up = jnp.pad(u, ((0, 0), (SSM_CONV - 1, 0), (0, 0)))
    out = up[:, 0:L] * w[0]
    for k in range(1, SSM_CONV):
        out = out + up[:, k:k + L] * w[k]
    return out + b


def ssd_scan(xh, dt, a_neg, bm, cm):
    bsz, L = xh.shape[:2]
    nc = L // SSD_CHUNK

    def to_chunks(t):
        return jnp.moveaxis(t.reshape(bsz, nc, SSD_CHUNK, *t.shape[2:]), 1, 0)

    tri = jnp.tril(jnp.ones((SSD_CHUNK, SSD_CHUNK), dtype=bool))[None, :, :, None, None]

    def step(h, inp):
        xc, dtc, bc, cc = inp
        acum = jnp.cumsum(dtc * a_neg, axis=1)
        diff = acum[:, :, None] - acum[:, None, :]
        decay = jnp.exp(jnp.where(tri, diff, -jnp.inf))
        cb = jnp.einsum('btgn,bsgn->btsg', cc, bc)
        y_diag = jnp.einsum('btsg,btsgi,bsgi,bsgip->btgip', cb, decay, dtc, xc)
        y_off = jnp.einsum('btgn,bgipn,btgi->btgip', cc, h, jnp.exp(acum))
        a_last = acum[:, -1]
        w_s = jnp.exp(a_last[:, None] - acum) * dtc
        h_new = jnp.exp(a_last)[..., None, None] * h + jnp.einsum('bsgn,bsgi,bsgip->bgipn', bc, w_s, xc)
        return h_new, y_diag + y_off

    h0 = jnp.zeros((bsz, SSM_GROUPS, SSM_HEADS_PER_GROUP, SSM_HEAD_DIM, SSM_STATE), jnp.float32)
    _, ys = lax.scan(step, h0, (to_chunks(xh), to_chunks(dt), to_chunks(bm), to_chunks(cm)))
    return jnp.moveaxis(ys, 0, 1).reshape(bsz, L, SSM_GROUPS, SSM_HEADS_PER_GROUP, SSM_HEAD_DIM)


def ssd_branch(z, xbc_raw, dt_raw, conv_w, conv_b, dt_bias, a_log, d_skip, gn_m):
    bsz, L = z.shape[:2]
    xbc = jax.nn.silu(causal_dwconv(xbc_raw, conv_w, conv_b)).astype(jnp.float32)
    gn = SSM_GROUPS * SSM_STATE
    xs = xbc[..., :SSM_D_INNER]
    bm = xbc[..., SSM_D_INNER:SSM_D_INNER + gn].reshape(bsz, L, SSM_GROUPS, SSM_STATE)
    cm = xbc[..., SSM_D_INNER + gn:].reshape(bsz, L, SSM_GROUPS, SSM_STATE)
    xh = xs.reshape(bsz, L, SSM_GROUPS, SSM_HEADS_PER_GROUP, SSM_HEAD_DIM)
    dt = jax.nn.softplus(dt_raw.astype(jnp.float32) + dt_bias.astype(jnp.float32))
    dt = dt.reshape(bsz, L, SSM_GROUPS, SSM_HEADS_PER_GROUP)
    a_neg = -jnp.exp(a_log.astype(jnp.float32)).reshape(SSM_GROUPS, SSM_HEADS_PER_GROUP)
    y = ssd_scan(xh, dt, a_neg, bm, cm)
    y = y + d_skip.astype(jnp.float32).reshape(SSM_GROUPS, SSM_HEADS_PER_GROUP)[:, :, None] * xh
    y = y.reshape(bsz, L, SSM_D_INNER) * jax.nn.silu(z.astype(jnp.float32))
    yg = y.reshape(bsz, L, SSM_GROUPS, -1)
    yg = yg * lax.rsqrt(jnp.mean(yg * yg, axis=-1, keepdims=True) + EPS)
    y = yg.reshape(bsz, L, SSM_D_INNER) * gn_m.astype(jnp.float32)
    return y.astype(z.dtype)


def dsa_branch(q, k, v, gate, q_idx, k_idx, w_idx):
    bsz, L = q.shape[:2]
    nq = L // CHUNK
    topk = min(TOPK_MAX, L // 4)
    key_pos = jnp.arange(L)

    def to_chunks(t):
        return jnp.moveaxis(t.reshape(bsz, nq, CHUNK, *t.shape[2:]), 1, 0)

    qh = q.reshape(bsz, L, ATTN_HEADS, ATTN_HEAD_DIM)
    qi = q_idx.reshape(bsz, L, IDX_HEADS, IDX_HEAD_DIM)

    def block(inp):
        c, qb, qib, wb = inp
        s = jnp.einsum('bthd,bsd->bths', qib, k_idx, preferred_element_type=jnp.float32)
        score = jnp.einsum('bths,bth->bts', jax.nn.relu(s), wb.astype(jnp.float32)) * IDX_SCALE
        visible = key_pos < (c + 1) * CHUNK
        score = jnp.where(visible[None, None, :], score, -jnp.inf)
        top_val, top_idx = lax.top_k(score, topk)
        valid = jnp.isfinite(top_val)
        kg = jax.vmap(lambda kb, ib: kb[ib])(k, top_idx)
        vg = jax.vmap(lambda vb, ib: vb[ib])(v, top_idx)
        kg = kg.reshape(bsz, CHUNK, topk, ATTN_HEADS, ATTN_HEAD_DIM)
        vg = vg.reshape(bsz, CHUNK, topk, ATTN_HEADS, ATTN_HEAD_DIM)
        logits = jnp.einsum('bthd,btkhd->bthk', qb, kg, preferred_element_type=jnp.float32) * ATTN_SCALE
        logits = jnp.where(valid[:, :, None, :], logits, -jnp.inf)
        p = jax.nn.softmax(logits, axis=-1)
        return jnp.einsum('bthk,btkhd->bthd', p.astype(vg.dtype), vg)

    outs = lax.map(block, (jnp.arange(nq), to_chunks(qh), to_chunks(qi), to_chunks(w_idx)))
    o = jnp.moveaxis(outs, 0, 1).reshape(bsz, L, ATTN_D)
    return o * jax.nn.silu(gate)


def setup_inputs(seed: int = 0) -> dict:
    key = jax.random.key(seed)
    ks = jax.random.split(key, 14)
    f32 = jnp.float32
    x = jax.random.normal(ks[0], (BATCH, SEQ, D_MODEL), f32)
    w_in = jax.random.normal(ks[1], (DEPTH, D_MODEL, D_IN), f32) * D_MODEL ** -0.5
    conv_w = jax.random.normal(ks[2], (DEPTH, SSM_CONV, SSM_CONV_DIM), f32) * SSM_CONV ** -0.5
    conv_b = jax.random.normal(ks[3], (DEPTH, SSM_CONV_DIM), f32) * 0.02
    dt0 = jnp.exp(jax.random.uniform(ks[4], (DEPTH, SSM_HEADS), f32, np.log(1e-3), np.log(1e-1)))
    dt_bias = dt0 + jnp.log(-jnp.expm1(-dt0))
    a_log = jnp.log(jax.random.uniform(ks[5], (DEPTH, SSM_HEADS), f32, 1.0, 16.0))
    d_skip = 1.0 + 0.1 * jax.random.normal(ks[6], (DEPTH, SSM_HEADS), f32)
    gn_m = 1.0 + 0.02 * jax.random.normal(ks[7], (DEPTH, SSM_D_INNER), f32)
    gate_bias = 0.02 * jax.random.normal(ks[8], (DEPTH, N_BRANCHES * D_MODEL), f32)
    w_branch = jax.random.normal(ks[9], (DEPTH, BRANCH_IN, D_MODEL), f32) * (BRANCH_IN // N_BRANCHES) ** -0.5
    w_out = jax.random.normal(ks[10], (DEPTH, D_MODEL, D_MODEL), f32) * D_MODEL ** -0.5
    norm_in = 1.0 + 0.02 * jax.random.normal(ks[11], (DEPTH, D_MODEL), f32)
    norm_final = 1.0 + 0.02 * jax.random.normal(ks[12], (D_MODEL,), f32)
    return {"x": x, "w_in": w_in, "conv_w": conv_w, "conv_b": conv_b, "dt_bias": dt_bias,
            "a_log": a_log, "d_skip": d_skip, "gn_m": gn_m, "gate_bias": gate_bias,
            "w_branch": w_branch, "w_out": w_out, "norm_in": norm_in, "norm_final": norm_final}


def reference(x, w_in, conv_w, conv_b, dt_bias, a_log, d_skip, gn_m, gate_bias,
              w_branch, w_out, norm_in, norm_final):
    offsets = list(np.cumsum(IN_SPLITS)[:-1])
    for i in range(DEPTH):
        h = rmsnorm(x, norm_in[i])
        proj = jnp.einsum('bld,de->ble', h, w_in[i])
        (z, xbc_raw, dt_raw, q, k, v, gate_a, q_idx, k_idx, w_idx, gates) = jnp.split(proj, offsets, axis=-1)
        y_m = ssd_branch(z, xbc_raw, dt_raw, conv_w[i], conv_b[i], dt_bias[i], a_log[i], d_skip[i], gn_m[i])
        y_a = dsa_branch(q, k, v, gate_a, q_idx, k_idx, w_idx)
        o_m = jnp.einsum('ble,ed->bld', y_m, w_branch[i, :SSM_D_INNER])
        o_a = jnp.einsum('ble,ed->bld', y_a, w_branch[i, SSM_D_INNER:])
        g = jax.nn.sigmoid(gates + gate_bias[i])
        merged = g[..., :D_MODEL] * o_m + g[..., D_MODEL:] * o_a
        x = x + jnp.einsum('bld,de->ble', merged, w_out[i])
    return rmsnorm(x, norm_final)
```

```python
import numpy as np
import ml_dtypes
from contextlib import ExitStack
import concourse.bass as bass
import concourse.mybir as mybir
from concourse.bass_utils import run_bass_kernel_spmd

F32 = mybir.dt.float32
BF16 = mybir.dt.bfloat16
AF = mybir.ActivationFunctionType
ALU = mybir.AluOpType
AX = mybir.AxisListType

P = 128
D = 2048
TB = 512
NT = 4
EPS = 1e-6
OFF_Z, OFF_XBC, OFF_DT, OFF_Q, OFF_K, OFF_V, OFF_GA, OFF_QI, OFF_KI, OFF_WI, OFF_G = (
    0, 2048, 5120, 5152, 7200, 9248, 11296, 13344, 14368, 14432, 14448)
D_IN = 18544
ATTN_SCALE = 128 ** -0.5
IDX_SCALE = (64 ** -0.5) * (16 ** -0.5)
TOPK = 256
NITER = 22
CLAMP = 16.0
NEG = -30000.0

C_ID, C_L, C_U, C_CW, C_CB, C_GB, C_G, C_GN, C_DTB, C_ALOG, C_DSK, C_PV, C_CVEC, C_BP, C_PVROW, NCST = (
    0, 128, 256, 384, 480, 504, 536, 552, 568, 600, 632, 664, 672, 704, 1216, 1344)


class Buf:
    def __init__(self, t, name=""):
        self.t = t
        self.name = name
        self.w = None
        self.r = []
        self.alias = []
        self.pending = 0

    def __getitem__(self, k):
        return self.t[k]


class Trk:
    LIM = 30000

    def __init__(self, nc, es):
        self.nc = nc
        self.es = es
        self.engs = {"pe": nc.tensor, "act": nc.scalar, "dve": nc.vector, "pool": nc.gpsimd, "sp": nc.sync}
        self.ctr = {}
        self.prog = {k: [] for k in self.engs}
        self.known = {k: {} for k in self.engs}
        self.nsem = 0
        for k in self.engs:
            self._newctr(k, 1)

    def _newsem(self, name):
        self.nsem += 1
        return self.es.enter_context(self.nc.semaphore("%s_%d" % (name, self.nsem)))

    def _newctr(self, name, step):
        self.ctr[name] = {"step": step, "ep": 0, "cnt": 0, "sems": {0: self._newsem(name)}}

    def _next_ms(self, name):
        c = self.ctr[name]
        if (c["cnt"] + 1) * c["step"] > self.LIM:
            return (name, c["ep"] + 1, 1)
        return (name, c["ep"], c["cnt"] + 1)

    def _inc(self, name):
        c = self.ctr[name]
        ms = self._next_ms(name)
        if ms[1] != c["ep"]:
            c["ep"] = ms[1]
            c["sems"][ms[1]] = self._newsem(name)
        c["cnt"] = ms[2]
        return ms, c["sems"][c["ep"]], c["step"]

    def _cur_ms(self, name):
        c = self.ctr[name]
        return (name, c["ep"], c["cnt"])

    def _need(self, e, deps):
        need = {}
        for d in deps:
            if d is None:
                continue
            name, ep, idx = d
            if idx == 0:
                continue
            k = (name, ep)
            if idx > need.get(k, 0):
                need[k] = idx
        for (name, ep), idx in need.items():
            c = self.ctr[name]
            if name == e and (ep, idx) > (c["ep"], c["cnt"]):
                continue
            if self.known[e].get((name, ep), 0) >= idx:
                continue
            later = [kk for kk in self.known[e] if kk[0] == name and kk[1] > ep]
            if later:
                continue
            self.prog[e].append((lambda en, v_=idx * c["step"], nm=(name, ep): en.wait_ge(self._sem_of(nm), v_)))
            self.known[e][(name, ep)] = idx

    def _sem_of(self, nm):
        return self.ctr[nm[0]]["sems"][nm[1]]

    def _deps(self, reads, writes):
        deps = []
        for b in reads:
            deps.append(b.w)
            for a in b.alias:
                deps.append(a.w)
        for b in writes:
            deps.append(b.w)
            deps.extend(b.r)
            for a in b.alias:
                deps.append(a.w)
                deps.extend(a.r)
        return deps

    def _mark(self, me, reads, writes):
        for b in reads:
            b.r.append(me)
        for b in writes:
            b.w = me
            b.r = []

    def op(self, e, fn, reads=(), writes=(), inc=True):
        self._need(e, self._deps(reads, writes))
        if inc:
            me, sem, step = self._inc(e)
            self.prog[e].append((lambda en, f_=fn, s_=sem: f_(en).then_inc(s_, 1)))
        else:
            me = self._next_ms(e)
            self.prog[e].append(fn)
        self._mark(me, reads, writes)

    def dma(self, q, out, in_, reads=(), writes=(), slot=None, **kw):
        self._need(q, self._deps(reads, writes))
        name = "d:" + slot
        if name not in self.ctr:
            self._newctr(name, 16)
        me, sem, step = self._inc(name)
        self.prog[q].append((lambda en, o_=out, i_=in_, k_=kw, s_=sem: en.dma_start(out=o_, in_=i_, **k_).then_inc(s_, 16)))
        self._mark(me, reads, writes)

    def barrier(self):
        allms = [self._cur_ms(n) for n in self.ctr]
        for e in self.engs:
            self._need(e, [m for m in allms if m[0] != e])

    def emit(self):
        with self.nc.Block() as block:
            @block.tensor
            def _(en):
                for f in self.prog["pe"]:
                    f(en)

            @block.scalar
            def _(en):
                for f in self.prog["act"]:
                    f(en)

            @block.vector
            def _(en):
                for f in self.prog["dve"]:
                    f(en)

            @block.gpsimd
            def _(en):
                for f in self.prog["pool"]:
                    f(en)

            @block.sync
            def _(en):
                for f in self.prog["sp"]:
                    f(en)


class Rot:
    def __init__(self, items):
        self.items = items
        self.i = 0

    def next(self, pend=None):
        b = self.items[self.i % len(self.items)]
        while b.pending > 0 and pend:
            pend.pop(0)()
        assert b.pending == 0, "rotation reuses %s while deferred users pending" % b.name
        self.i += 1
        return b


def defer(pend, fn, bufs):
    for b in bufs:
        b.pending += 1

    def w():
        fn()
        for b in bufs:
            b.pending -= 1
    pend.append(w)


def build(NPB, NOB, dbg=()):
    NK = (NPB + NOB) * TB
    nc = bass.Bass("TRN2", target_bir_lowering=False)
    dt_in = lambda n, s, d=F32: nc.dram_tensor(n, s, d, kind="ExternalInput").ap()
    xp = dt_in("xp", [max(NPB, 1) * TB, D])
    xo = dt_in("xo", [NOB * TB, D])
    w_in = dt_in("w_in", [D, D_IN])
    w_br = dt_in("w_br", [2 * D, D])
    w_out = dt_in("w_out", [D, D])
    cst = dt_in("cst", [P, NCST])
    normf = dt_in("normf", [1, D])
    out = nc.dram_tensor("out", [NOB * TB, D], F32, kind="ExternalOutput").ap()
    dti = lambda n, s, d=BF16: nc.dram_tensor(n, s, d, kind="Internal").ap()
    Wb = dti("Wb", [D, D_IN])
    Wbr = dti("Wbr", [2 * D, D])
    Wo = dti("Wo", [D, D])
    KT = dti("KT", [16, P, NK])
    NCH = (NK // P + 15) // 16
    VS = dti("VS", [16, NCH, P, 16, 129])
    KI = dti("KI", [2, P, NK])
    YM = dti("YM", [P, 16 * TB])
    dbg_out = {}

    with ExitStack() as es:
        T = Trk(nc, es)

        uniq = [0]

        def sbg(name, shape, dt, stack=es):
            uniq[0] += 1
            return Buf(stack.enter_context(nc.sbuf_tensor("%s_u%d" % (name, uniq[0]), shape, dt)), name)

        CST = sbg("CST", [P, NCST], F32)
        ident_b = sbg("ident_b", [P, P], BF16)
        bpat_b = sbg("bpat_b", [P, 2, 256], BF16)
        pvm = sbg("pvm", [P, P], BF16)
        ones_b = sbg("ones_b", [P, 512], BF16)
        ones_f = sbg("ones_f", [P, P], F32)
        aneg = sbg("aneg", [P, 32], F32)
        hT = sbg("hT", [P, 16, TB], BF16)
        WT = sbg("WT", [P, 2 * 16 * 512], BF16)
        XM = sbg("XM", [P, 8192], BF16)
        xs = sbg("xs", [P, D], BF16)
        hs = sbg("hs", [P, 4, 512], F32)
        halo = sbg("halo", [P, 24, 3], F32)
        st = sbg("st", [P, 16], F32)
        PS = [Buf(es.enter_context(nc.psum_tensor("ps%d" % i, [P, 512], F32)), "ps%d" % i) for i in range(8)]
        psA = Rot(PS)
        psG = Rot(PS[0:4])
        psO = Rot(PS[4:8])

        Wt = [Buf(WT.t[:, i * 8192:(i + 1) * 8192].rearrange("p (k c) -> p k c", c=512), "Wt%d" % i) for i in range(2)]
        score = Buf(WT.t[:].bitcast(F32), "score")
        score.alias = [Wt[0], Wt[1]]
        for w in Wt:
            w.alias = [score]
        xtb = [Buf(XM.t[:, i * 4096:(i + 1) * 4096].bitcast(F32), "xt%d" % i) for i in range(2)]
        mjD = Buf(XM.t[:, 0:4096], "mjD")
        mjA = Buf(XM.t[:, 4096:8192], "mjA")
        mjD.alias = [xtb[0]]
        mjA.alias = [xtb[1]]
        xtb[0].alias = [mjD]
        xtb[1].alias = [mjA]
        XM8 = XM.t[:, 4096:8192].bitcast(mybir.dt.int8)
        wt_rot = Rot(Wt)
        xt_rot = Rot(xtb)

        ident_f = CST.t[:, C_ID:C_ID + 128]
        Lmask = CST.t[:, C_L:C_L + 128]
        Ustr = CST.t[:, C_U:C_U + 128]

        def cs(c0, n):
            return CST.t[:, c0:c0 + n]

        def psb(b):
            return b.t[:].bitcast(BF16).rearrange("p (a b) -> p a b", b=128)

        def dump(name, ap, shape, dtype, rd):
            if name not in dbg:
                return
            o = nc.dram_tensor("dbg_" + name, list(shape), dtype, kind="ExternalOutput").ap()
            dbg_out[name] = o
            T.dma("sp", o, ap, reads=rd, slot="dbg")

        T.dma("sp", CST[:], cst, writes=[CST], slot="cst")
        T.op("dve", lambda e: e.tensor_copy(out=ident_b[:], in_=ident_f), reads=[CST], writes=[ident_b])
        T.op("dve", lambda e: e.tensor_copy(out=bpat_b[:], in_=cs(C_BP, 512).rearrange("p (a b) -> p a b", b=256)), reads=[CST], writes=[bpat_b])
        T.op("dve", lambda e: e.tensor_copy(out=pvm[:], in_=cs(C_PVROW, 128)), reads=[CST], writes=[pvm])
        T.op("dve", lambda e: e.memset(ones_b[:], 1.0), writes=[ones_b])
        T.op("pool", lambda e: e.memset(ones_f[:], 1.0), writes=[ones_f])
        T.op("pool", lambda e: e.memset(hs[:], 0.0), writes=[hs])
        T.op("pool", lambda e: e.memset(halo[:], 0.0), writes=[halo])
        T.op("act", lambda e: e.activation(out=aneg[:], in_=cs(C_ALOG, 32), func=AF.Exp), reads=[CST], writes=[aneg])
        T.op("dve", lambda e: e.tensor_scalar(out=aneg[:], in0=aneg[:], scalar1=-1.0, scalar2=None, op0=ALU.mult), reads=[aneg], writes=[aneg])

        with ExitStack() as s0:
            fb = Rot([sbg("pcf%d" % i, [P, 2048], F32, s0) for i in range(3)])
            bb = Rot([sbg("pcb%d" % i, [P, 2048], BF16, s0) for i in range(3)])
            cnt = 0

            def cast_piece(src_ap, dst_ap, n, scal):
                nonlocal cnt
                f = fb.next()
                b = bb.next()
                T.dma("sp", f[:, 0:n], src_ap, writes=[f], slot=f.name)
                eng = "dve" if cnt % 2 == 0 else "pool"
                if scal is None:
                    T.op(eng, lambda e: e.tensor_copy(out=b[:, 0:n], in_=f[:, 0:n]), reads=[f], writes=[b])
                else:
                    T.op(eng, lambda e: e.tensor_scalar(out=b[:, 0:n], in0=f[:, 0:n], scalar1=scal, scalar2=1.0, op0=ALU.mult, op1=ALU.mult), reads=[f, CST], writes=[b])
                T.dma("act", dst_ap, b[:, 0:n], reads=[b], slot=b.name)
                cnt += 1

            for kc in range(16):
                for c0 in range(0, D_IN, 2048):
                    n = min(2048, D_IN - c0)
                    cast_piece(w_in[kc * P:(kc + 1) * P, c0:c0 + n], Wb[kc * P:(kc + 1) * P, c0:c0 + n], n, CST.t[:, C_G + kc:C_G + kc + 1])
            for kc in range(32):
                cast_piece(w_br[kc * P:(kc + 1) * P, :], Wbr[kc * P:(kc + 1) * P, :], 2048, None)
            for kc in range(16):
                cast_piece(w_out[kc * P:(kc + 1) * P, :], Wo[kc * P:(kc + 1) * P, :], 2048, None)
            zt = sbg("zt", [P, 2048], BF16, s0)
            T.op("pool", lambda e: e.memset(zt[:], 0.0), writes=[zt])
            for c0 in range(0, NK, 2048):
                n = min(2048, NK - c0)
                T.dma("act", KI[0, 64:128, c0:c0 + n], zt[64:128, 0:n], reads=[zt], slot="zfill")
                T.dma("act", KI[1, 0:64, c0:c0 + n], zt[0:64, 0:n], reads=[zt], slot="zfill")
            T.barrier()

        def load_w(src, c0, n, wt, dcol=0, rows0=0):
            T.dma("sp", wt.t[:, :, dcol:dcol + n], src[rows0:rows0 + D, c0:c0 + n].rearrange("(kc p) c -> p kc c", p=P), writes=[wt], slot=wt.name)

        def mm_group(ps_ap, pairs, reads, ps_buf, extra_first_reads=()):
            n = len(pairs)
            for k, (l, r) in enumerate(pairs):
                T.op("pe", lambda e, l=l, r=r, k=k: e.matmul(ps_ap, lhsT=l, rhs=r, start=(k == 0), stop=(k == n - 1)),
                     reads=reads, writes=[ps_buf], inc=(k == n - 1))

        def proj_F(wt, ncols, consumer, rhs_cols=slice(0, TB), nfree=TB):
            for ci in range(ncols // P):
                ps = psA.next()
                mm_group(ps.t[:, 0:nfree], [(wt.t[:, kc, ci * P:(ci + 1) * P], hT.t[:, kc, rhs_cols]) for kc in range(16)], [wt, hT], ps)
                consumer(ci, ps)

        def proj_T(wt, ncols, consumer, tiles=range(NT)):
            for i in tiles:
                ps = psA.next()
                mm_group(ps.t[:, 0:ncols], [(hT.t[:, kc, i * P:(i + 1) * P], wt.t[:, kc, 0:ncols]) for kc in range(16)], [wt, hT], ps)
                consumer(i, ps)

        def run_pipe(pend, nkeep):
            while len(pend) > nkeep:
                pend.pop(0)()

        evac_flip = [0]

        def evac(out_ap, in_ap, reads, writes, scale=None):
            evac_flip[0] ^= 1
            if evac_flip[0]:
                if scale is None:
                    T.op("act", lambda e: e.copy(out=out_ap, in_=in_ap), reads=reads, writes=writes)
                else:
                    T.op("act", lambda e: e.activation(out=out_ap, in_=in_ap, func=AF.Copy, scale=scale), reads=reads, writes=writes)
            else:
                if scale is None:
                    T.op("dve", lambda e: e.tensor_copy(out=out_ap, in_=in_ap), reads=reads, writes=writes)
                else:
                    T.op("dve", lambda e: e.tensor_scalar(out=out_ap, in0=in_ap, scalar1=scale, scalar2=None, op0=ALU.mult), reads=reads, writes=writes)

        def rstd_from_ssq(ssq_ap, out_ap, n, buf):
            T.op("dve", lambda e: e.tensor_scalar(out=out_ap, in0=ssq_ap, scalar1=1.0 / n, scalar2=EPS, op0=ALU.mult, op1=ALU.add), reads=[buf], writes=[buf])
            T.op("act", lambda e: e.activation(out=out_ap, in_=out_ap, func=AF.Sqrt), reads=[buf], writes=[buf])
            T.op("dve", lambda e: e.reciprocal(out=out_ap, in_=out_ap), reads=[buf], writes=[buf])

        def build_hT(xsrc, row0):
            for i in range(NT):
                xt = xt_rot.next()
                T.dma("sp", xt[:, :], xsrc[row0 + i * P:row0 + (i + 1) * P, :], writes=[xt], slot=xt.name)
                T.op("act", lambda e, xt=xt: e.activation(out=xs[:], in_=xt[:, :], func=AF.Square, accum_out=st[:, 0:1]), reads=[xt], writes=[xs, st])
                rstd_from_ssq(st[:, 0:1], st[:, 1:2], D, st)
                T.op("dve", lambda e, xt=xt: e.tensor_scalar(out=xs[:], in0=xt[:, :], scalar1=st[:, 1:2], scalar2=None, op0=ALU.mult), reads=[xt, st], writes=[xs])
                for hb in range(2):
                    ps = psA.next()
                    for k in range(8):
                        kc = hb * 8 + k
                        T.op("pe", lambda e, ps=ps, k=k, kc=kc: e.transpose(out=psb(ps)[:, k, :], in_=xs[:, kc * P:(kc + 1) * P], identity=ident_b[:]),
                             reads=[xs, ident_b], writes=[ps], inc=(k == 7))
                    evac(hT.t[:, hb * 8:(hb + 1) * 8, i * P:(i + 1) * P], psb(ps), [ps], [hT])

        def ssd_block(S, is_own, bi):
            x_tok = sbg("x_tok", [P, NT, D], F32, S)
            BT = sbg("BT", [P, 4, TB], BF16, S)
            CTt = sbg("CTt", [P, 4, TB], BF16, S)
            Btok = sbg("Btok", [P, NT, 4, P], BF16, S)
            dt_all = sbg("dt_all", [P, NT, 32], F32, S)
            sm = sbg("sm", [P, NT, 160], F32, S)
            y_mT = sbg("y_mT", [P, 16, TB], BF16, S) if is_own else None
            SX = ExitStack()
            ut = Rot([sbg("ut%d" % i, [P, 515], F32, SX) for i in range(3)])
            acc_r = Rot([sbg("cacc%d" % i, [P, TB], F32, SX) for i in range(2)])
            xsT = Rot([sbg("xsT%d" % i, [P, TB], F32, SX) for i in range(4)])

            cpend = []
            for grp in range(6):
                wt = wt_rot.next()
                load_w(Wb, OFF_XBC + grp * 512, 512, wt)

                def cons(ci, ps, grp=grp):
                    cg = grp * 4 + ci
                    u = ut.next()
                    T.op("act", lambda e: e.copy(out=u[:, 3:515], in_=ps.t[:, 0:512]), reads=[ps], writes=[u])
                    T.op("pool", lambda e: e.tensor_copy(out=u[:, 0:3], in_=halo[:, cg, :]), reads=[halo], writes=[u])
                    T.op("pool", lambda e: e.tensor_copy(out=halo[:, cg, :], in_=u[:, 512:515]), reads=[u], writes=[halo])
                    acc = acc_r.next()
                    cw = lambda k: CST.t[:, C_CW + cg * 4 + k:C_CW + cg * 4 + k + 1]
                    T.op("dve", lambda e: e.tensor_scalar(out=acc[:, :], in0=u[:, 3:515], scalar1=cw(3), scalar2=None, op0=ALU.mult), reads=[u, CST], writes=[acc])
                    for k in (2, 1, 0):
                        T.op("dve", lambda e, k=k: e.scalar_tensor_tensor(out=acc[:, :], in0=u[:, k:k + 512], scalar=cw(k), in1=acc[:, :], op0=ALU.mult, op1=ALU.add), reads=[u, CST, acc], writes=[acc])
                    cb = CST.t[:, C_CB + cg:C_CB + cg + 1]
                    if cg < 16:
                        xT = xsT.next(cpend)
                        T.op("act", lambda e: e.activation(out=xT[:, :], in_=acc[:, :], func=AF.Silu, bias=cb), reads=[acc, CST], writes=[xT])

                        def later(xT=xT, cg=cg):
                            ps2 = psA.next()
                            for i in range(NT):
                                T.op("pe", lambda e, i=i: e.transpose(out=ps2.t[:, i * P:(i + 1) * P], in_=xT[:, i * P:(i + 1) * P], identity=ident_f),
                                     reads=[xT, CST], writes=[ps2], inc=(i == NT - 1))
                            evac(x_tok.t[:, :, cg * P:(cg + 1) * P], ps2.t[:, :].rearrange("p (a b) -> p a b", b=P), [ps2], [x_tok])
                        defer(cpend, later, [xT])
                    elif cg < 20:
                        g = cg - 16
                        T.op("act", lambda e: e.activation(out=BT.t[:, g, :], in_=acc[:, :], func=AF.Silu, bias=cb), reads=[acc, CST], writes=[BT])

                        def later(g=g):
                            ps2 = psA.next()
                            for i in range(NT):
                                T.op("pe", lambda e, i=i: e.transpose(out=psb(ps2)[:, i, :], in_=BT.t[:, g, i * P:(i + 1) * P], identity=ident_b[:]),
                                     reads=[BT, ident_b], writes=[ps2], inc=(i == NT - 1))
                            evac(Btok.t[:, :, g, :], psb(ps2)[:, 0:NT, :], [ps2], [Btok])
                        defer(cpend, later, [])
                    else:
                        g = cg - 20
                        T.op("act", lambda e: e.activation(out=CTt.t[:, g, :], in_=acc[:, :], func=AF.Silu, bias=cb), reads=[acc, CST], writes=[CTt])
                    run_pipe(cpend, 2)

                proj_F(wt, 512, cons)
            run_pipe(cpend, 0)
            T.barrier()
            SX.close()

            wt = wt_rot.next()
            load_w(Wb, OFF_DT, 32, wt)

            def cons_dt(i, ps):
                T.op("dve", lambda e: e.tensor_tensor(out=dt_all.t[:, i, :], in0=ps.t[:, 0:32], in1=cs(C_DTB, 32), op=ALU.add), reads=[ps, CST], writes=[dt_all])
                T.op("act", lambda e: e.activation(out=dt_all.t[:, i, :], in_=dt_all.t[:, i, :], func=AF.Exp), reads=[dt_all], writes=[dt_all])
                T.op("act", lambda e: e.activation(out=dt_all.t[:, i, :], in_=dt_all.t[:, i, :], func=AF.Ln, bias=1.0), reads=[dt_all], writes=[dt_all])

            proj_T(wt, 32, cons_dt)

            for i in range(NT):
                a_i = sm.t[:, i, 0:32]
                T.op("dve", lambda e, i=i, a_i=a_i: e.tensor_tensor(out=a_i, in0=dt_all.t[:, i, :], in1=aneg[:], op=ALU.mult), reads=[dt_all, aneg], writes=[sm])
                ps = psA.next()
                T.op("pe", lambda e, a_i=a_i, ps=ps: e.matmul(ps.t[:, 0:32], lhsT=Lmask, rhs=a_i, start=True, stop=True), reads=[CST, sm], writes=[ps])
                T.op("pe", lambda e, a_i=a_i, ps=ps: e.matmul(ps.t[:, 32:64], lhsT=ones_f[:], rhs=a_i, start=True, stop=True), reads=[ones_f, sm], writes=[ps])
                T.op("act", lambda e, i=i, ps=ps: e.activation(out=sm.t[:, i, 32:96], in_=ps.t[:, 0:64], func=AF.Exp), reads=[ps], writes=[sm])
                T.op("dve", lambda e, i=i, ps=ps: e.tensor_copy(out=sm.t[:, i, 128:160], in_=ps.t[:, 0:32]), reads=[ps], writes=[sm])
                T.op("dve", lambda e, i=i, ps=ps: e.tensor_tensor(out=sm.t[:, i, 96:128], in0=ps.t[:, 32:64], in1=sm.t[:, i, 128:160], op=ALU.subtract), reads=[ps, sm], writes=[sm])
                T.op("act", lambda e, i=i: e.activation(out=sm.t[:, i, 96:128], in_=sm.t[:, i, 96:128], func=AF.Exp), reads=[sm], writes=[sm])
                T.op("dve", lambda e, i=i: e.tensor_tensor(out=sm.t[:, i, 96:128], in0=sm.t[:, i, 96:128], in1=dt_all.t[:, i, :], op=ALU.mult), reads=[sm, dt_all], writes=[sm])

            xw = Rot([sbg("xw%d" % i, [P, TB], BF16, S) for i in range(2)])
            tmpS = Rot([sbg("tmpS%d" % i, [P, TB], F32, S) for i in range(2)])
            if is_own:
                xdt = Rot([sbg("xdt%d" % i, [P, TB], BF16, S) for i in range(3)])
                hsb = Rot([sbg("hsb%d" % i, [P, TB], BF16, S) for i in range(3)])
                cbm = Rot([sbg("cbm%d" % i, [P, P], F32, S) for i in range(3)])
                Rr = Rot([sbg("Rr%d" % i, [P, 8, P], F32, S) for i in range(2)])
                dec = Rot([sbg("dec%d" % i, [P, 8, P], F32, S) for i in range(2)])
                Mm = Rot([sbg("Mm%d" % i, [P, 8, P], BF16, S) for i in range(3)])
                t1 = Rot([sbg("t1_%d" % i, [P, TB], F32, S) for i in range(3)])
                t2 = Rot([sbg("t2_%d" % i, [P, TB], F32, S) for i in range(1)])
                zs = Rot([sbg("zs%d" % i, [P, TB], F32, S) for i in range(3)])
                yn = Rot([sbg("yn%d" % i, [P, TB], BF16, S) for i in range(3)])
                st2r = Rot([sbg("st2_%d" % i, [P, 8], F32, S) for i in range(2)])
            v3 = lambda b: b.t[:, :].rearrange("p (h q) -> p h q", q=64)
            stageC = []
            stageD = []
            for g in range(4):
                if is_own:
                    wz = wt_rot.next()
                    load_w(Wb, OFF_Z + g * 512, 512, wz)
                for i in range(NT):
                    tsl = slice(i * P, (i + 1) * P)
                    xg3 = x_tok.t[:, i, g * 512:(g + 1) * 512].rearrange("p (h q) -> p h q", q=64)
                    bc = lambda c0, i=i, g=g: sm.t[:, i, c0 + g * 8:c0 + g * 8 + 8].unsqueeze(2).to_broadcast([P, 8, 64])
                    xwb = xw.next()
                    T.op("dve" if not is_own else "pool", lambda e, xwb=xwb, xg3=xg3, bc=bc: e.tensor_tensor(out=v3(xwb), in0=xg3, in1=bc(96), op=ALU.mult), reads=[x_tok, sm], writes=[xwb])
                    if is_own:
                        xdb = xdt.next(stageC)
                        dtb3 = dt_all.t[:, i, g * 8:(g + 1) * 8].unsqueeze(2).to_broadcast([P, 8, 64])
                        T.op("pool", lambda e, xdb=xdb, xg3=xg3, dtb3=dtb3: e.tensor_tensor(out=v3(xdb), in0=xg3, in1=dtb3, op=ALU.mult), reads=[x_tok, dt_all], writes=[xdb])
                        hb_ = hsb.next(stageC)
                        T.op("act", lambda e, hb_=hb_, g=g: e.copy(out=hb_[:, :], in_=hs.t[:, g, :]), reads=[hs], writes=[hb_])
                        Rb = Rr.next()
                        a3 = sm.t[:, i, g * 8:(g + 1) * 8].unsqueeze(2).to_broadcast([P, 8, P])
                        L3 = Lmask.unsqueeze(1).to_broadcast([P, 8, P])
                        T.op("dve", lambda e, Rb=Rb, a3=a3, L3=L3: e.tensor_tensor(out=Rb[:, :, :], in0=a3, in1=L3, op=ALU.mult), reads=[sm, CST], writes=[Rb])
                        psc = psA.next()
                        T.op("pe", lambda e, psc=psc, g=g, tsl=tsl: e.matmul(psc.t[:, 0:P], lhsT=BT.t[:, g, tsl], rhs=CTt.t[:, g, tsl], start=True, stop=True), reads=[BT, CTt], writes=[psc])
                        psd = [psA.next(), psA.next()]
                        for hh in range(2):
                            T.op("pe", lambda e, pd=psd[hh], Rb=Rb, hh=hh: e.matmul(pd.t[:, :], lhsT=Ustr, rhs=Rb.t[:, hh * 4:(hh + 1) * 4, :].rearrange("p a b -> p (a b)"), start=True, stop=True), reads=[CST, Rb], writes=[psd[hh]])
                    pss = psA.next()
                    T.op("pe", lambda e, pss=pss, xwb=xwb, i=i, g=g: e.matmul(pss.t[:, :], lhsT=Btok.t[:, i, g, :], rhs=xwb[:, :], start=True, stop=True), reads=[Btok, xwb], writes=[pss])
                    if is_own:
                        psz = psA.next()
                        mm_group(psz.t[:, :], [(hT.t[:, kc, tsl], wz.t[:, kc, :]) for kc in range(16)], [hT, wz], psz)
                        cbb = cbm.next()
                        T.op("dve", lambda e, psc=psc, cbb=cbb: e.tensor_tensor(out=cbb[:, :], in0=psc.t[:, 0:P], in1=Lmask, op=ALU.mult), reads=[psc, CST], writes=[cbb])
                        db = dec.next()
                        for hh in range(2):
                            T.op("act", lambda e, pd=psd[hh], db=db, hh=hh: e.activation(out=db.t[:, hh * 4:(hh + 1) * 4, :].rearrange("p a b -> p (a b)"), in_=pd.t[:, :], func=AF.Exp), reads=[psd[hh]], writes=[db])
                    tS = tmpS.next()
                    hs3 = hs.t[:, g, :].rearrange("p (h q) -> p h q", q=64)
                    T.op("pool", lambda e, tS=tS, hs3=hs3, bc=bc: e.tensor_tensor(out=v3(tS), in0=hs3, in1=bc(64), op=ALU.mult), reads=[hs, sm], writes=[tS])
                    T.op("dve", lambda e, tS=tS, pss=pss, g=g: e.tensor_tensor(out=hs.t[:, g, :], in0=pss.t[:, :], in1=tS[:, :], op=ALU.add), reads=[pss, tS], writes=[hs])
                    if not is_own:
                        continue
                    zb = zs.next(stageC)
                    T.op("act", lambda e, zb=zb, psz=psz: e.activation(out=zb[:, :], in_=psz.t[:, :], func=AF.Silu), reads=[psz], writes=[zb])
                    Mb = Mm.next(stageC)
                    c3 = cbb.t[:, :].unsqueeze(1).to_broadcast([P, 8, P])
                    T.op("dve", lambda e, Mb=Mb, db=db, c3=c3: e.tensor_tensor(out=Mb[:, :, :], in0=db[:, :, :], in1=c3, op=ALU.mult), reads=[db, cbb], writes=[Mb])

                    def stC(Mb=Mb, xdb=xdb, hb_=hb_, g=g, i=i, tsl=tsl, xg3=xg3, bc=bc, zb=zb):
                        psy = psA.next()
                        for h in range(8):
                            T.op("pe", lambda e, h=h: e.matmul(psy.t[:, h * 64:(h + 1) * 64], lhsT=Mb.t[:, h, :], rhs=xdb.t[:, h * 64:(h + 1) * 64], start=True, stop=True),
                                 reads=[Mb, xdb], writes=[psy], inc=(h == 7))
                        pso = psA.next()
                        T.op("pe", lambda e: e.matmul(pso.t[:, :], lhsT=CTt.t[:, g, tsl], rhs=hb_[:, :], start=True, stop=True), reads=[CTt, hb_], writes=[pso])
                        t1b = t1.next()
                        t2b = t2.next()
                        st2 = st2r.next()
                        dsk3 = CST.t[:, C_DSK + g * 8:C_DSK + g * 8 + 8].unsqueeze(2).to_broadcast([P, 8, 64])
                        T.op("dve", lambda e: e.tensor_tensor(out=v3(t1b), in0=pso.t[:, :].rearrange("p (h q) -> p h q", q=64), in1=bc(32), op=ALU.mult), reads=[pso, sm], writes=[t1b])
                        T.op("pool", lambda e: e.tensor_tensor(out=v3(t2b), in0=xg3, in1=dsk3, op=ALU.mult), reads=[x_tok, CST], writes=[t2b])
                        T.op("dve", lambda e: e.tensor_tensor(out=t1b[:, :], in0=psy.t[:, :], in1=t1b[:, :], op=ALU.add), reads=[psy, t1b], writes=[t1b])
                        T.op("pool", lambda e: e.tensor_tensor(out=t1b[:, :], in0=t1b[:, :], in1=t2b[:, :], op=ALU.add), reads=[t1b, t2b], writes=[t1b])
                        T.op("pool", lambda e: e.tensor_tensor(out=t1b[:, :], in0=t1b[:, :], in1=zb[:, :], op=ALU.mult), reads=[t1b, zb], writes=[t1b])
                        T.op("act", lambda e: e.activation(out=t2b[:, :], in_=t1b[:, :], func=AF.Square, accum_out=st2[:, 0:1]), reads=[t1b], writes=[t2b, st2])
                        rstd_from_ssq(st2[:, 0:1], st2[:, 1:2], 512, st2)
                        ynb = yn.next(stageD)
                        T.op("dve", lambda e: e.tensor_scalar(out=ynb[:, :], in0=t1b[:, :], scalar1=st2[:, 1:2], scalar2=None, op0=ALU.mult), reads=[t1b, st2], writes=[ynb])

                        def stD(ynb=ynb):
                            pst = psA.next()
                            for j in range(4):
                                T.op("pe", lambda e, j=j: e.transpose(out=psb(pst)[:, j, :], in_=ynb.t[:, j * P:(j + 1) * P], identity=ident_b[:]), reads=[ynb, ident_b], writes=[pst], inc=(j == 3))
                            for j in range(4):
                                c = g * 4 + j
                                T.op("act", lambda e, j=j, c=c: e.activation(out=y_mT.t[:, c, tsl], in_=psb(pst)[:, j, :], func=AF.Copy, scale=CST.t[:, C_GN + c:C_GN + c + 1]), reads=[pst, CST], writes=[y_mT])
                        defer(stageD, stD, [ynb])
                    defer(stageC, stC, [Mb, xdb, hb_, zb])
                    run_pipe(stageD, 1)
                    run_pipe(stageC, 1)
            run_pipe(stageC, 0)
            run_pipe(stageD, 0)
            if is_own:
                dump("x_tok%d" % bi, x_tok.t[:, :, :].rearrange("p a b -> p (a b)"), [P, NT * D], F32, [x_tok])
                dump("dt%d" % bi, dt_all.t[:, :, :].rearrange("p a b -> p (a b)"), [P, NT * 32], F32, [dt_all])
                dump("ymT%d" % bi, y_mT.t[:, :, :].rearrange("p a b -> p (a b)"), [P, 16 * TB], BF16, [y_mT])
            return y_mT

        def kv_proj(S, key0, is_own, qT, qiT, ws):
            ktmp = Rot([sbg("ktmp%d" % i, [P, TB], BF16, S) for i in range(2)])
            vtmp = Rot([sbg("vtmp%d" % i, [P, 4, 129], BF16, S) for i in range(2)])
            for v_ in vtmp.items:
                T.op("pool", lambda e, v_=v_: e.memset(v_[:, :, :], 1.0), writes=[v_])
            for grp in range(4):
                wt = wt_rot.next()
                load_w(Wb, OFF_K + grp * 512, 512, wt)

                def cons(ci, ps, grp=grp):
                    hh = grp * 4 + ci
                    kb_ = ktmp.next()
                    evac(kb_[:, :], ps.t[:, :], [ps], [kb_])
                    T.dma("pool", KT[hh, :, key0:key0 + TB], kb_[:, :], reads=[kb_], slot=kb_.name)
                proj_F(wt, 512, cons)
            for grp in range(4):
                wt = wt_rot.next()
                load_w(Wb, OFF_V + grp * 512, 512, wt)

                def cons(i, ps, grp=grp):
                    vb_ = vtmp.next()
                    evac(vb_.t[:, :, 0:P], ps.t[:, :].rearrange("p (h d) -> p h d", d=P), [ps], [vb_])
                    kbg = key0 // P + i
                    T.dma("pool", VS[grp * 4:(grp + 1) * 4, kbg // 16, :, kbg % 16, :].rearrange("h p d -> p h d"), vb_[:, :, :], reads=[vb_], slot=vb_.name)
                proj_T(wt, 512, cons)
            wt = wt_rot.next()
            load_w(Wb, OFF_KI, 64, wt, dcol=0)
            load_w(Wb, OFF_KI, 64, wt, dcol=64)

            def cons(ci, ps):
                kb_ = ktmp.next()
                evac(kb_[:, :], ps.t[:, :], [ps], [kb_])
                T.dma("pool", KI[0, 0:64, key0:key0 + TB], kb_[0:64, :], reads=[kb_], slot=kb_.name)
                T.dma("pool", KI[1, 64:128, key0:key0 + TB], kb_[64:128, :], reads=[kb_], slot=kb_.name)
            proj_F(wt, 128, cons)
            if not is_own:
                return
            for grp in range(4):
                wt = wt_rot.next()
                load_w(Wb, OFF_Q + grp * 512, 512, wt)

                def cons(ci, ps, grp=grp):
                    evac(qT.t[:, grp * 4 + ci, :], ps.t[:, :], [ps], [qT], scale=ATTN_SCALE)
                proj_F(wt, 512, cons)
            for grp in range(2):
                wt = wt_rot.next()
                load_w(Wb, OFF_QI + grp * 512, 512, wt)

                def cons(ci, ps, grp=grp):
                    evac(qiT.t[:, grp * 4 + ci, :], ps.t[:, :], [ps], [qiT])
                proj_F(wt, 512, cons)
            wt = wt_rot.next()
            load_w(Wb, OFF_WI, 16, wt)

            def cons(i, ps):
                T.op("dve", lambda e: e.tensor_scalar(out=ws.t[:, i, :], in0=ps.t[:, 0:16], scalar1=IDX_SCALE, scalar2=None, op0=ALU.mult), reads=[ps], writes=[ws])
            proj_T(wt, 16, cons)

        def dsa_half(S, bi, key0, hb, qT, qiT, ws, y_aT):
            nkb = (key0 + 256 * (hb + 1)) // P
            W = nkb * P
            nb5 = (W + 511) // 512
            LA = 4
            LA2 = 6
            maskT = sbg("maskT", [P, 64, 256], BF16, S)
            SI = ExitStack()
            Dgr = Rot([sbg("Dg%d" % i, [P, 16, P], BF16, SI) for i in range(2)])
            Rl = Rot([sbg("Rl%d" % i, [P, 512], BF16, SI) for i in range(6)])
            kib = Rot([sbg("kib%d" % i, [P, 2, 512], BF16, SI) for i in range(3)])

            def indexer(j2):
                i = 2 * hb + j2
                tsl = slice(i * P, (i + 1) * P)
                Dg = Dgr.next()
                pend = []
                for h in range(16):
                    T.op("pool", lambda e, h=h, i=i, Dg=Dg: e.tensor_scalar(out=Dg.t[:, h, :], in0=ident_f, scalar1=ws.t[:, i, h:h + 1], scalar2=1.0, op0=ALU.mult, op1=ALU.mult), reads=[CST, ws], writes=[Dg])
                for b5 in range(nb5):
                    c0 = b5 * 512
                    wd = min(512, W - c0)
                    kb_ = kib.next()
                    T.dma("sp", kb_[:, 0, 0:wd], KI[0, :, c0:c0 + wd], writes=[kb_], slot=kb_.name)
                    T.dma("sp", kb_[:, 1, 0:wd], KI[1, :, c0:c0 + wd], writes=[kb_], slot=kb_.name)
                    acc = psO.next(pend)
                    extra = []
                    if c0 < NPB * TB:
                        extra.append("pv")
                    if b5 == nb5 - 1:
                        extra.append("bp")
                    nacc = 16 + len(extra)
                    for h in range(16):
                        j, sub = h // 2, h % 2
                        ps = psG.next()
                        T.op("pe", lambda e, ps=ps, j=j, sub=sub, kb_=kb_, wd=wd, tsl=tsl: e.matmul(ps.t[:, 0:wd], lhsT=qiT.t[:, j, tsl], rhs=kb_.t[:, sub, 0:wd], start=True, stop=True), reads=[qiT, kb_], writes=[ps])
                        rb = Rl.next(pend)
                        if h % 2 == 0:
                            T.op("act", lambda e, rb=rb, ps=ps, wd=wd: e.activation(out=rb[:, 0:wd], in_=ps.t[:, 0:wd], func=AF.Relu), reads=[ps], writes=[rb])
                        else:
                            T.op("dve", lambda e, rb=rb, ps=ps, wd=wd: e.tensor_scalar(out=rb[:, 0:wd], in0=ps.t[:, 0:wd], scalar1=0.0, scalar2=None, op0=ALU.max), reads=[ps], writes=[rb])

                        def accmm(acc=acc, h=h, rb=rb, wd=wd, nacc=nacc, Dg=Dg):
                            T.op("pe", lambda e: e.matmul(acc.t[:, 0:wd], lhsT=Dg.t[:, h, :], rhs=rb.t[:, 0:wd], start=(h == 0), stop=(h == nacc - 1)), reads=[Dg, rb], writes=[acc])
                        defer(pend, accmm, [rb, acc])
                        run_pipe(pend, LA)

                    def fin(acc=acc, wd=wd, nacc=nacc, extra=tuple(extra), c0=c0):
                        k = 16
                        for ex in extra:
                            if ex == "pv":
                                T.op("pe", lambda e, k=k: e.matmul(acc.t[:, 0:wd], lhsT=pvm[:], rhs=ones_b.t[:, 0:wd], start=False, stop=(k == nacc - 1)), reads=[pvm, ones_b], writes=[acc])
                            else:
                                T.op("pe", lambda e, k=k: e.matmul(acc.t[:, wd - 256:wd], lhsT=ident_b[:], rhs=bpat_b.t[:, j2, :], start=False, stop=(k == nacc - 1)), reads=[ident_b, bpat_b], writes=[acc])
                            k += 1
                        T.op("act", lambda e: e.copy(out=score.t[:, c0:c0 + wd], in_=acc.t[:, 0:wd]), reads=[acc], writes=[score])
                    defer(pend, fin, [acc])
                run_pipe(pend, 0)

            def bisect(j2):
                i = 2 * hb + j2
                hi_t, mid_t, cnt_t, t_t, g_t, lo_t, sa_t = [sbg("bq%d_%d" % (j2, c), [P, 2], F32, SI) for c in range(7)]
                WD = min(4096, max(P, ((W * 45 // 100) // P) * P))
                nA = W - WD
                assert nA >= P and nA <= 8192
                hi, mid, cntc, tt, gg, lo, sa = [b_.t[:, 0:1] for b_ in (hi_t, mid_t, cnt_t, t_t, g_t, lo_t, sa_t)]
                T.op("dve", lambda e: e.reduce_max(out=hi, in_=score.t[:, 0:W], axis=AX.X), reads=[score], writes=[hi_t])
                T.op("dve", lambda e: e.tensor_scalar(out=mid, in0=hi, scalar1=-CLAMP * 0.5, scalar2=None, op0=ALU.add), reads=[hi_t], writes=[mid_t])
                for it in range(NITER):
                    hk = CLAMP * (0.5 ** (it + 1))
                    T.op("dve", lambda e: e.tensor_scalar(out=XM.t[:, 0:WD], in0=score.t[:, 0:WD], scalar1=mid, scalar2=None, op0=ALU.is_ge, op1=ALU.add, accum_out=cntc), reads=[score, mid_t], writes=[mjD, cnt_t])
                    T.op("act", lambda e: e.activation(out=XM8[:, 0:nA], in_=score.t[:, WD:W], func=AF.Sign, bias=mid, scale=-1.0, accum_out=sa), reads=[score, mid_t], writes=[mjA, sa_t])
                    T.op("dve", lambda e: e.scalar_tensor_tensor(out=tt, in0=cntc, scalar=2.0, in1=sa, op0=ALU.mult, op1=ALU.subtract), reads=[cnt_t, sa_t], writes=[t_t])
                    T.op("dve", lambda e: e.tensor_scalar(out=gg, in0=tt, scalar1=float(2 * TOPK - 1 - nA), scalar2=-0.5, op0=ALU.is_ge, op1=ALU.add), reads=[t_t], writes=[g_t])
                    T.op("dve", lambda e, hk=hk: e.scalar_tensor_tensor(out=mid, in0=gg, scalar=hk, in1=mid, op0=ALU.mult, op1=ALU.add), reads=[g_t, mid_t], writes=[mid_t])
                hN = CLAMP * (0.5 ** (NITER + 1))
                T.op("dve", lambda e: e.tensor_scalar(out=lo, in0=mid, scalar1=-hN, scalar2=None, op0=ALU.add), reads=[mid_t], writes=[lo_t])
                T.op("dve", lambda e: e.tensor_scalar(out=XM.t[:, 0:W], in0=score.t[:, 0:W], scalar1=lo, scalar2=None, op0=ALU.is_ge), reads=[score, lo_t], writes=[mjD, mjA])
                dump("thr%d_%d" % (bi, i), lo, [P, 1], F32, [lo_t])
                dump("score%d_%d" % (bi, i), score.t[:, 0:W], [P, W], F32, [score])

            def mask_transposes(j2):
                for kb0 in range(0, nkb, 8):
                    n = min(8, nkb - kb0)
                    ps = psG.next()
                    for t in range(n):
                        T.op("pe", lambda e, ps=ps, t=t, kb0=kb0: e.transpose(out=psb(ps)[:, t, :], in_=XM.t[:, (kb0 + t) * P:(kb0 + t + 1) * P], identity=ident_b[:]), reads=[mjD, mjA, ident_b], writes=[ps], inc=(t == n - 1))
                    T.op("act", lambda e, ps=ps, n=n, kb0=kb0, j2=j2: e.copy(out=maskT.t[:, kb0:kb0 + n, j2 * P:(j2 + 1) * P], in_=psb(ps)[:, 0:n, :]), reads=[ps], writes=[maskT])

            indexer(0)
            bisect(0)
            indexer(1)
            mask_transposes(0)
            bisect(1)
            mask_transposes(1)
            T.barrier()
            SI.close()

            wga = Rot([sbg("wga%d" % i, [P, 16, P], BF16, S) for i in range(1)])
            gaT = Rot([sbg("gaT%d" % i, [P, 256], F32, S) for i in range(2)])
            ktb = Rot([sbg("ktb%d" % i, [P, 2048], BF16, S) for i in range(2)])
            vtb = Rot([sbg("vtb%d" % i, [P, 16, 129], BF16, S) for i in range(2)])
            eb = Rot([sbg("eb%d" % i, [P, 512], BF16, S) for i in range(4)])
            pmb = Rot([sbg("pmb%d" % i, [P, 512], BF16, S) for i in range(8)])
            onb = Rot([sbg("onb%d" % i, [P, P], BF16, S) for i in range(2)])
            rdr = Rot([sbg("rd%d" % i, [P, 4], F32, S) for i in range(2)])
            psQ = Rot(PS[0:3])
            psX = Rot(PS[3:4])
            qsl = slice(hb * 256, (hb + 1) * 256)
            pend = []
            for hh in range(16):
                ga = gaT.next(pend)
                O = [psO.next(pend), psO.next(pend)]
                wg = wga.next()
                T.dma("sp", wg[:, :, :], Wb[:, OFF_GA + hh * P:OFF_GA + (hh + 1) * P].rearrange("(kc p) c -> p kc c", p=P), writes=[wg], slot=wg.name)
                psg = psX.next()
                mm_group(psg.t[:, 0:256], [(wg.t[:, kc, :], hT.t[:, kc, qsl]) for kc in range(16)], [wg, hT], psg)
                T.op("act", lambda e, ga=ga, psg=psg: e.activation(out=ga[:, :], in_=psg.t[:, 0:256], func=AF.Silu), reads=[psg], writes=[ga])
                for c2 in range((nkb + 15) // 16):
                    kb_lo = c2 * 16
                    n = min(16, nkb - kb_lo)
                    kt = ktb.next(pend)
                    vt = vtb.next(pend)
                    T.dma("sp", kt[:, 0:n * P], KT[hh, :, kb_lo * P:(kb_lo + n) * P], writes=[kt], slot=kt.name)
                    T.dma("sp", vt[:, 0:n, :], VS[hh, c2, :, 0:n, :], writes=[vt], slot=vt.name)
                    for t in range(0, n, 2):
                        kb = kb_lo + t
                        ps = psQ.next()
                        for u in range(2):
                            T.op("pe", lambda e, ps=ps, kt=kt, t=t, u=u, hh=hh: e.matmul(ps.t[:, u * 256:(u + 1) * 256], lhsT=kt.t[:, (t + u) * P:(t + u + 1) * P], rhs=qT.t[:, hh, qsl], start=True, stop=True), reads=[kt, qT], writes=[ps], inc=(u == 1))
                        eb_ = eb.next()
                        T.op("act", lambda e, eb_=eb_, ps=ps: e.activation(out=eb_[:, :], in_=ps.t[:, :], func=AF.Exp), reads=[ps], writes=[eb_])
                        pm = pmb.next(pend)
                        T.op("dve", lambda e, pm=pm, eb_=eb_, kb=kb: e.tensor_tensor(out=pm[:, :], in0=eb_[:, :], in1=maskT.t[:, kb:kb + 2, :].rearrange("p a b -> p (a b)"), op=ALU.mult), reads=[eb_, maskT], writes=[pm])

                        def pv(pm=pm, vt=vt, t=t, kb=kb, O=O):
                            for u in range(2):
                                for j2 in range(2):
                                    T.op("pe", lambda e, u=u, j2=j2: e.matmul(O[j2].t[:, 0:129], lhsT=pm.t[:, u * 256 + j2 * P:u * 256 + (j2 + 1) * P], rhs=vt.t[:, t + u, :], start=(kb + u == 0), stop=(kb + u == nkb - 1)),
                                         reads=[pm, vt], writes=[O[j2]], inc=(u == 1 and j2 == 1))
                        defer(pend, pv, [pm, vt, O[0], O[1]])
                        run_pipe(pend, LA2)

                def finalize(O=O, ga=ga, hh=hh):
                    pst = psX.next()
                    rd = rdr.next()
                    for j2 in range(2):
                        T.op("dve", lambda e, j2=j2: e.reciprocal(out=rd.t[:, j2:j2 + 1], in_=O[j2].t[:, 128:129]), reads=[O[j2]], writes=[rd])
                        ob = onb.next()
                        T.op("dve", lambda e, j2=j2, ob=ob: e.tensor_scalar(out=ob[:, :], in0=O[j2].t[:, 0:P], scalar1=rd.t[:, j2:j2 + 1], scalar2=None, op0=ALU.mult), reads=[O[j2], rd], writes=[ob])
                        T.op("pe", lambda e, j2=j2, ob=ob: e.transpose(out=psb(pst)[:, j2, :], in_=ob[:, :], identity=ident_b[:]), reads=[ob, ident_b], writes=[pst])
                    T.op("dve", lambda e: e.tensor_tensor(out=y_aT.t[:, hh, qsl], in0=psb(pst)[:, 0:2, :].rearrange("p a b -> p (a b)"), in1=ga[:, :], op=ALU.mult), reads=[pst, ga], writes=[y_aT])
                defer(pend, finalize, [ga, O[0], O[1]])
            run_pipe(pend, 0)

        def out_block2(S, bi, orow0, y_aT):
            y_mT = sbg("y_mT2", [P, 16, TB], BF16, S)
            T.dma("sp", y_mT.t[:, :, :].rearrange("p a b -> p (a b)"), YM, writes=[y_mT], slot="ymld")
            mT = sbg("mT", [P, 16, TB], BF16, S)
            r = sbg("r", [P, NT, D], F32, S)
            nfb = sbg("nfb", [P, D], F32, S)
            stashb = sbg("stash", [P, 4, TB], F32, S)
            gm = Rot([sbg("gm%d" % i, [P, TB], F32, S) for i in range(2)])
            m2 = Rot([sbg("m2_%d" % i, [P, TB], F32, S) for i in range(2)])
            st3 = sbg("st3", [P, 8], F32, S)
            T.dma("sp", nfb[:, :], normf.partition_broadcast(P), writes=[nfb], slot="nfb")
            for i in range(NT):
                T.dma("sp", r.t[:, i, :], xo[orow0 + i * P:orow0 + (i + 1) * P, :], writes=[r], slot="rld")
            for cg in range(4):
                plan = ((Wbr, cg * 512, 0), (Wb, OFF_G + cg * 512, 0), (Wbr, cg * 512, D), (Wb, OFF_G + D + cg * 512, 0))
                for half in range(2):
                    wA = wt_rot.next()
                    load_w(plan[2 * half][0], plan[2 * half][1], 512, wA, rows0=plan[2 * half][2])
                    ysrc = y_mT if half == 0 else y_aT
                    psas = []
                    for ci in range(4):
                        psa = psA.next()
                        mm_group(psa.t[:, :], [(wA.t[:, kc, ci * P:(ci + 1) * P], ysrc.t[:, kc, :]) for kc in range(16)], [wA, ysrc], psa)
                        psas.append(psa)
                    wG = wt_rot.next()
                    load_w(plan[2 * half + 1][0], plan[2 * half + 1][1], 512, wG, rows0=plan[2 * half + 1][2])
                    for ci in range(4):
                        c = cg * 4 + ci
                        psa = psas[ci]
                        psg = psA.next()
                        mm_group(psg.t[:, :], [(wG.t[:, kc, ci * P:(ci + 1) * P], hT.t[:, kc, :]) for kc in range(16)], [wG, hT], psg)
                        gb = gm.next()
                        gcol = C_GB + half * 16 + c
                        T.op("act", lambda e, gb=gb, psg=psg, gcol=gcol: e.activation(out=gb[:, :], in_=psg.t[:, :], func=AF.Sigmoid, bias=CST.t[:, gcol:gcol + 1]), reads=[psg, CST], writes=[gb])
                        if half == 0:
                            T.op("dve", lambda e, psa=psa, gb=gb, ci=ci: e.tensor_tensor(out=stashb.t[:, ci, :], in0=psa.t[:, :], in1=gb[:, :], op=ALU.mult), reads=[psa, gb], writes=[stashb])
                        else:
                            mb2 = m2.next()
                            T.op("dve", lambda e, mb2=mb2, psa=psa, gb=gb: e.tensor_tensor(out=mb2[:, :], in0=psa.t[:, :], in1=gb[:, :], op=ALU.mult), reads=[psa, gb], writes=[mb2])
                            T.op("pool", lambda e, mb2=mb2, c=c, ci=ci: e.tensor_tensor(out=mT.t[:, c, :], in0=mb2[:, :], in1=stashb.t[:, ci, :], op=ALU.add), reads=[mb2, stashb], writes=[mT])
            dump("mT%d" % bi, mT.t[:, :, :].rearrange("p a b -> p (a b)"), [P, 16 * TB], BF16, [mT])
            for n in range(4):
                wo = wt_rot.next()
                load_w(Wo, n * 512, 512, wo)
                for i in range(NT):
                    ps = psA.next()
                    mm_group(ps.t[:, :], [(mT.t[:, kc, i * P:(i + 1) * P], wo.t[:, kc, :]) for kc in range(16)], [mT, wo], ps)
                    T.op("dve", lambda e, ps=ps, i=i, n=n: e.tensor_tensor(out=r.t[:, i, n * 512:(n + 1) * 512], in0=ps.t[:, :], in1=r.t[:, i, n * 512:(n + 1) * 512], op=ALU.add), reads=[ps, r], writes=[r])
            for i in range(NT):
                T.op("act", lambda e, i=i: e.activation(out=xs[:], in_=r.t[:, i, :], func=AF.Square, accum_out=st3[:, 0:1]), reads=[r], writes=[xs, st3])
                rstd_from_ssq(st3[:, 0:1], st3[:, 1:2], D, st3)
                T.op("dve", lambda e, i=i: e.scalar_tensor_tensor(out=r.t[:, i, :], in0=r.t[:, i, :], scalar=st3[:, 1:2], in1=nfb[:, :], op0=ALU.mult, op1=ALU.mult), reads=[r, st3, nfb], writes=[r])
                T.dma("pool", out[orow0 + i * P:orow0 + (i + 1) * P, :], r.t[:, i, :], reads=[r], slot="ost")

        for bi in range(NPB + NOB):
            is_own = bi >= NPB
            key0 = bi * TB
            xsrc, row0 = (xo, (bi - NPB) * TB) if is_own else (xp, bi * TB)
            build_hT(xsrc, row0)
            if bi == NPB:
                dump("hT", hT.t[:, :, :].rearrange("p a b -> p (a b)"), [P, 16 * TB], BF16, [hT])
            with ExitStack() as S1:
                y_mT = ssd_block(S1, is_own, bi)
                if is_own:
                    T.dma("pool", YM, y_mT.t[:, :, :].rearrange("p a b -> p (a b)"), reads=[y_mT], slot="ymst")
                if bi == NPB - 1:
                    T.op("dve", lambda e: e.tensor_scalar(out=hs.t[:, :, :], in0=hs.t[:, :, :], scalar1=CST.t[:, C_PV:C_PV + 1], scalar2=None, op0=ALU.mult), reads=[hs, CST], writes=[hs])
                T.barrier()
            if not is_own:
                with ExitStack() as S2:
                    kv_proj(S2, key0, False, None, None, None)
                    T.barrier()
                continue
            with ExitStack() as SO:
                y_aT = sbg("y_aT", [P, 16, TB], BF16, SO)
                with ExitStack() as S2:
                    qT = sbg("qT", [P, 16, TB], BF16, S2)
                    qiT = sbg("qiT", [P, 8, TB], BF16, S2)
                    ws = sbg("ws", [P, NT, 16], F32, S2)
                    with ExitStack() as S2a:
                        kv_proj(S2a, key0, True, qT, qiT, ws)
                        T.barrier()
                    for hb in range(2):
                        with ExitStack() as S2b:
                            dsa_half(S2b, bi, key0, hb, qT, qiT, ws, y_aT)
                            T.barrier()
                dump("yaT%d" % bi, y_aT.t[:, :, :].rearrange("p a b -> p (a b)"), [P, 16 * TB], BF16, [y_aT])
                with ExitStack() as S3:
                    out_block2(S3, bi, (bi - NPB) * TB, y_aT)
                    T.barrier()
        T.barrier()
        T.emit()
    return nc, dbg_out


def make_cst(inputs, half):
    c = np.zeros((P, NCST), np.float32)
    c[:, C_ID:C_ID + 128] = np.eye(128, dtype=np.float32)
    u = np.arange(128)
    c[:, C_L:C_L + 128] = (u[:, None] <= u[None, :]).astype(np.float32)
    c[:, C_U:C_U + 128] = (u[:, None] > u[None, :]).astype(np.float32)
    cw = np.asarray(inputs["conv_w"][0], np.float32)
    c[:, C_CW:C_CW + 96] = cw.T.reshape(24, 128, 4).transpose(1, 0, 2).reshape(128, 96)
    c[:, C_CB:C_CB + 24] = np.asarray(inputs["conv_b"][0], np.float32).reshape(24, 128).T
    c[:, C_GB:C_GB + 32] = np.asarray(inputs["gate_bias"][0], np.float32).reshape(32, 128).T
    c[:, C_G:C_G + 16] = np.asarray(inputs["norm_in"][0], np.float32).reshape(16, 128).T
    c[:, C_GN:C_GN + 16] = np.asarray(inputs["gn_m"][0], np.float32).reshape(16, 128).T
    c[:, C_DTB:C_DTB + 32] = np.asarray(inputs["dt_bias"][0], np.float32)[None, :]
    c[:, C_ALOG:C_ALOG + 32] = np.asarray(inputs["a_log"][0], np.float32)[None, :]
    c[:, C_DSK:C_DSK + 32] = np.asarray(inputs["d_skip"][0], np.float32)[None, :]
    c[:, C_PV] = 1.0 if half == 1 else 0.0
    c[:, C_CVEC:C_CVEC + NITER] = (0.5 ** np.arange(1, NITER + 1, dtype=np.float64)).astype(np.float32)[None, :]
    q = np.arange(128)[:, None]
    k = np.arange(128)[None, :]
    diag = np.where((q < 64) & (k >= 64), NEG, 0.0).astype(np.float32)
    full = np.full((128, 128), NEG, np.float32)
    zero = np.zeros((128, 128), np.float32)
    c[:, C_BP:C_BP + 256] = np.concatenate([diag, full], 1)
    c[:, C_BP + 256:C_BP + 512] = np.concatenate([zero, diag], 1)
    c[0, C_PVROW:C_PVROW + 128] = 0.0 if half == 1 else NEG
    return c


_NC_CACHE = {}


def run(inputs, L, dbg=()):
    B = inputs["x"].shape[0]
    assert B * 2 == 8
    H = L // 2
    NPB = NOB = H // TB
    key = (NPB, NOB, tuple(dbg))
    if key not in _NC_CACHE:
        _NC_CACHE[key] = build(NPB, NOB, dbg)
    nc, dbg_out = _NC_CACHE[key]
    x = np.asarray(inputs["x"], np.float32)
    w_in = np.ascontiguousarray(np.asarray(inputs["w_in"][0], np.float32))
    w_br = np.ascontiguousarray(np.asarray(inputs["w_branch"][0], np.float32))
    w_out = np.ascontiguousarray(np.asarray(inputs["w_out"][0], np.float32))
    normf = np.asarray(inputs["norm_final"], np.float32).reshape(1, D)
    csts = [make_cst(inputs, 0), make_cst(inputs, 1)]
    zeros = np.zeros((H, D), np.float32)
    in_maps = []
    for c in range(8):
        b, half = c // 2, c % 2
        in_maps.append({
            "xp": np.ascontiguousarray(x[b, 0:H]) if half == 1 else zeros,
            "xo": np.ascontiguousarray(x[b, half * H:(half + 1) * H]),
            "w_in": w_in, "w_br": w_br, "w_out": w_out,
            "cst": csts[half], "normf": normf,
        })
    res = run_bass_kernel_spmd(nc, in_maps, core_ids=list(range(8)))
    outp = np.empty((B, L, D), np.float32)
    for c in range(8):
        b, half = c // 2, c % 2
        outp[b, half * H:(half + 1) * H] = res.results[c]["out"]
    return outp, res


def kernel(**inputs):
    L = inputs["x"].shape[1]
    outp, _ = run(inputs, L)
    return outp
```

```python
import numpy as np
import ml_dtypes
from contextlib import ExitStack
import concourse.bass as bass
import concourse.mybir as mybir
from concourse.bass_utils import run_bass_kernel_spmd

F32 = mybir.dt.float32
BF16 = mybir.dt.bfloat16
AF = mybir.ActivationFunctionType
ALU = mybir.AluOpType
AX = mybir.AxisListType

P = 128
D = 2048
TB = 512
NT = 4
EPS = 1e-6
OFF_Z, OFF_XBC, OFF_DT, OFF_Q, OFF_K, OFF_V, OFF_GA, OFF_QI, OFF_KI, OFF_WI, OFF_G = (
    0, 2048, 5120, 5152, 7200, 9248, 11296, 13344, 14368, 14432, 14448)
D_IN = 18544
ATTN_SCALE = 128 ** -0.5
IDX_SCALE = (64 ** -0.5) * (16 ** -0.5)
TOPK = 256
NITER = 21
CLAMP = 8.0
NEG = -30000.0

C_ID, C_L, C_U, C_CW, C_CB, C_GB, C_G, C_GN, C_DTB, C_ALOG, C_DSK, C_PV, C_CVEC, C_BP, C_PVROW, NCST = (
    0, 128, 256, 384, 480, 504, 536, 552, 568, 600, 632, 664, 672, 704, 1216, 1344)


class Buf:
    def __init__(self, t, name=""):
        self.t = t
        self.name = name
        self.w = None
        self.r = []
        self.alias = []
        self.pending = 0

    def __getitem__(self, k):
        return self.t[k]


class Trk:
    LIM = 30000

    def __init__(self, nc, es):
        self.nc = nc
        self.es = es
        self.engs = {"pe": nc.tensor, "act": nc.scalar, "dve": nc.vector, "pool": nc.gpsimd, "sp": nc.sync}
        self.ctr = {}
        self.prog = {k: [] for k in self.engs}
        self.known = {k: {} for k in self.engs}
        self.nsem = 0
        for k in self.engs:
            self._newctr(k, 1)

    def _newsem(self, name):
        self.nsem += 1
        return self.es.enter_context(self.nc.semaphore("%s_%d" % (name, self.nsem)))

    def _newctr(self, name, step):
        self.ctr[name] = {"step": step, "ep": 0, "cnt": 0, "sems": {0: self._newsem(name)}}

    def _next_ms(self, name):
        c = self.ctr[name]
        if (c["cnt"] + 1) * c["step"] > self.LIM:
            return (name, c["ep"] + 1, 1)
        return (name, c["ep"], c["cnt"] + 1)

    def _inc(self, name):
        c = self.ctr[name]
        ms = self._next_ms(name)
        if ms[1] != c["ep"]:
            c["ep"] = ms[1]
            c["sems"][ms[1]] = self._newsem(name)
        c["cnt"] = ms[2]
        return ms, c["sems"][c["ep"]], c["step"]

    def _cur_ms(self, name):
        c = self.ctr[name]
        return (name, c["ep"], c["cnt"])

    def _need(self, e, deps):
        need = {}
        for d in deps:
            if d is None:
                continue
            name, ep, idx = d
            if idx == 0:
                continue
            k = (name, ep)
            if idx > need.get(k, 0):
                need[k] = idx
        for (name, ep), idx in need.items():
            c = self.ctr[name]
            if name == e and (ep, idx) > (c["ep"], c["cnt"]):
                continue
            if self.known[e].get((name, ep), 0) >= idx:
                continue
            later = [kk for kk in self.known[e] if kk[0] == name and kk[1] > ep]
            if later:
                continue
            self.prog[e].append((lambda en, v_=idx * c["step"], nm=(name, ep): en.wait_ge(self._sem_of(nm), v_)))
            self.known[e][(name, ep)] = idx

    def _sem_of(self, nm):
        return self.ctr[nm[0]]["sems"][nm[1]]

    def _deps(self, reads, writes):
        deps = []
        for b in reads:
            deps.append(b.w)
            for a in b.alias:
                deps.append(a.w)
        for b in writes:
            deps.append(b.w)
            deps.extend(b.r)
            for a in b.alias:
                deps.append(a.w)
                deps.extend(a.r)
        return deps

    def _mark(self, me, reads, writes):
        for b in reads:
            b.r.append(me)
        for b in writes:
            b.w = me
            b.r = []

    def op(self, e, fn, reads=(), writes=(), inc=True):
        self._need(e, self._deps(reads, writes))
        if inc:
            me, sem, step = self._inc(e)
            self.prog[e].append((lambda en, f_=fn, s_=sem: f_(en).then_inc(s_, 1)))
        else:
            me = self._next_ms(e)
            self.prog[e].append(fn)
        self._mark(me, reads, writes)

    def dma(self, q, out, in_, reads=(), writes=(), slot=None, **kw):
        self._need(q, self._deps(reads, writes))
        name = "d:" + slot
        if name not in self.ctr:
            self._newctr(name, 16)
        me, sem, step = self._inc(name)
        self.prog[q].append((lambda en, o_=out, i_=in_, k_=kw, s_=sem: en.dma_start(out=o_, in_=i_, **k_).then_inc(s_, 16)))
        self._mark(me, reads, writes)

    def barrier(self):
        allms = [self._cur_ms(n) for n in self.ctr]
        for e in self.engs:
            self._need(e, [m for m in allms if m[0] != e])

    def emit(self):
        with self.nc.Block() as block:
            @block.tensor
            def _(en):
                for f in self.prog["pe"]:
                    f(en)

            @block.scalar
            def _(en):
                for f in self.prog["act"]:
                    f(en)

            @block.vector
            def _(en):
                for f in self.prog["dve"]:
                    f(en)

            @block.gpsimd
            def _(en):
                for f in self.prog["pool"]:
                    f(en)

            @block.sync
            def _(en):
                for f in self.prog["sp"]:
                    f(en)


class Rot:
    def __init__(self, items):
        self.items = items
        self.i = 0

    def next(self, pend=None):
        b = self.items[self.i % len(self.items)]
        while b.pending > 0 and pend:
            pend.pop(0)()
        assert b.pending == 0, "rotation reuses %s while deferred users pending" % b.name
        self.i += 1
        return b


def defer(pend, fn, bufs):
    for b in bufs:
        b.pending += 1

    def w():
        fn()
        for b in bufs:
            b.pending -= 1
    pend.append(w)


def build(NPB, NOB, dbg=()):
    NK = (NPB + NOB) * TB
    nc = bass.Bass("TRN2", target_bir_lowering=False)
    dt_in = lambda n, s, d=F32: nc.dram_tensor(n, s, d, kind="ExternalInput").ap()
    xp = dt_in("xp", [max(NPB, 1) * TB, D])
    xo = dt_in("xo", [NOB * TB, D])
    w_in = dt_in("w_in", [D, D_IN])
    w_br = dt_in("w_br", [2 * D, D])
    w_out = dt_in("w_out", [D, D])
    cst = dt_in("cst", [P, NCST])
    normf = dt_in("normf", [1, D])
    out = nc.dram_tensor("out", [NOB * TB, D], F32, kind="ExternalOutput").ap()
    dti = lambda n, s, d=BF16: nc.dram_tensor(n, s, d, kind="Internal").ap()
    Wb = dti("Wb", [D, D_IN])
    Wbr = dti("Wbr", [2 * D, D])
    Wo = dti("Wo", [D, D])
    KT = dti("KT", [16, P, NK])
    NCH = (NK // P + 15) // 16
    VS = dti("VS", [16, NCH, P, 16, 129])
    KI = dti("KI", [2, P, NK])
    YM = dti("YM", [P, 16 * TB])
    dbg_out = {}

    with ExitStack() as es:
        T = Trk(nc, es)

        uniq = [0]

        def sbg(name, shape, dt, stack=es):
            uniq[0] += 1
            return Buf(stack.enter_context(nc.sbuf_tensor("%s_u%d" % (name, uniq[0]), shape, dt)), name)

        CST = sbg("CST", [P, NCST], F32)
        ident_b = sbg("ident_b", [P, P], BF16)
        bpat_b = sbg("bpat_b", [P, 2, 256], BF16)
        pvm = sbg("pvm", [P, P], BF16)
        ones_b = sbg("ones_b", [P, 512], BF16)
        ones_f = sbg("ones_f", [P, P], F32)
        aneg = sbg("aneg", [P, 32], F32)
        eps_t = sbg("eps_t", [P, 2], F32)
        hT = sbg("hT", [P, 16, TB], BF16)
        WT = sbg("WT", [P, 2 * 16 * 512], BF16)
        XM = sbg("XM", [P, 8192], BF16)
        xs = sbg("xs", [P, D], BF16)
        hs = sbg("hs", [P, 4, 512], F32)
        halo = sbg("halo", [P, 24, 3], F32)
        st = sbg("st", [P, 16], F32)
        PS = [Buf(es.enter_context(nc.psum_tensor("ps%d" % i, [P, 512], F32)), "ps%d" % i) for i in range(8)]
        psA = Rot(PS)
        psG = Rot(PS[0:4])
        psO = Rot(PS[4:8])

        Wt = [Buf(WT.t[:, i * 8192:(i + 1) * 8192].rearrange("p (k c) -> p k c", c=512), "Wt%d" % i) for i in range(2)]
        score = Buf(WT.t[:].bitcast(F32), "score")
        score.alias = [Wt[0], Wt[1]]
        for w in Wt:
            w.alias = [score]
        xtb = [Buf(XM.t[:, i * 4096:(i + 1) * 4096].bitcast(F32), "xt%d" % i) for i in range(2)]
        mjD = Buf(XM.t[:, 0:4096], "mjD")
        mjA = Buf(XM.t[:, 4096:8192], "mjA")
        mjD.alias = [xtb[0]]
        mjA.alias = [xtb[1]]
        xtb[0].alias = [mjD]
        xtb[1].alias = [mjA]
        XM8 = XM.t[:, 4096:8192].bitcast(mybir.dt.int8)
        wt_rot = Rot(Wt)
        xt_rot = Rot(xtb)

        ident_f = CST.t[:, C_ID:C_ID + 128]
        Lmask = CST.t[:, C_L:C_L + 128]
        Ustr = CST.t[:, C_U:C_U + 128]

        def cs(c0, n):
            return CST.t[:, c0:c0 + n]

        def psb(b):
            return b.t[:].bitcast(BF16).rearrange("p (a b) -> p a b", b=128)

        def dump(name, ap, shape, dtype, rd):
            if name not in dbg:
                return
            o = nc.dram_tensor("dbg_" + name, list(shape), dtype, kind="ExternalOutput").ap()
            dbg_out[name] = o
            T.dma("sp", o, ap, reads=rd, slot="dbg")

        T.dma("sp", CST[:], cst, writes=[CST], slot="cst")
        T.op("dve", lambda e: e.tensor_copy(out=ident_b[:], in_=ident_f), reads=[CST], writes=[ident_b])
        T.op("dve", lambda e: e.tensor_copy(out=bpat_b[:], in_=cs(C_BP, 512).rearrange("p (a b) -> p a b", b=256)), reads=[CST], writes=[bpat_b])
        T.op("dve", lambda e: e.tensor_copy(out=pvm[:], in_=cs(C_PVROW, 128)), reads=[CST], writes=[pvm])
        T.op("dve", lambda e: e.memset(ones_b[:], 1.0), writes=[ones_b])
        T.op("pool", lambda e: e.memset(ones_f[:], 1.0), writes=[ones_f])
        T.op("pool", lambda e: e.memset(eps_t[:], EPS), writes=[eps_t])
        T.op("pool", lambda e: e.memset(hs[:], 0.0), writes=[hs])
        T.op("pool", lambda e: e.memset(halo[:], 0.0), writes=[halo])
        T.op("act", lambda e: e.activation(out=aneg[:], in_=cs(C_ALOG, 32), func=AF.Exp), reads=[CST], writes=[aneg])
        T.op("dve", lambda e: e.tensor_scalar(out=aneg[:], in0=aneg[:], scalar1=-1.0, scalar2=None, op0=ALU.mult), reads=[aneg], writes=[aneg])

        with ExitStack() as s0:
            fb = Rot([sbg("pcf%d" % i, [P, 2048], F32, s0) for i in range(3)])
            bb = Rot([sbg("pcb%d" % i, [P, 2048], BF16, s0) for i in range(3)])
            cnt = 0

            def cast_piece(src_ap, dst_ap, n, scal):
                nonlocal cnt
                f = fb.next()
                b = bb.next()
                T.dma("sp", f[:, 0:n], src_ap, writes=[f], slot=f.name)
                eng = "dve" if cnt % 2 == 0 else "pool"
                if scal is None:
                    T.op(eng, lambda e: e.tensor_copy(out=b[:, 0:n], in_=f[:, 0:n]), reads=[f], writes=[b])
                else:
                    T.op(eng, lambda e: e.tensor_scalar(out=b[:, 0:n], in0=f[:, 0:n], scalar1=scal, scalar2=1.0, op0=ALU.mult, op1=ALU.mult), reads=[f, CST], writes=[b])
                T.dma("act", dst_ap, b[:, 0:n], reads=[b], slot=b.name)
                cnt += 1

            for kc in range(16):
                for c0 in range(0, D_IN, 2048):
                    n = min(2048, D_IN - c0)
                    cast_piece(w_in[kc * P:(kc + 1) * P, c0:c0 + n], Wb[kc * P:(kc + 1) * P, c0:c0 + n], n, CST.t[:, C_G + kc:C_G + kc + 1])
            for kc in range(32):
                cast_piece(w_br[kc * P:(kc + 1) * P, :], Wbr[kc * P:(kc + 1) * P, :], 2048, None)
            for kc in range(16):
                cast_piece(w_out[kc * P:(kc + 1) * P, :], Wo[kc * P:(kc + 1) * P, :], 2048, None)
            zt = sbg("zt", [P, 2048], BF16, s0)
            T.op("pool", lambda e: e.memset(zt[:], 0.0), writes=[zt])
            for c0 in range(0, NK, 2048):
                n = min(2048, NK - c0)
                T.dma("act", KI[0, 64:128, c0:c0 + n], zt[64:128, 0:n], reads=[zt], slot="zfill")
                T.dma("act", KI[1, 0:64, c0:c0 + n], zt[0:64, 0:n], reads=[zt], slot="zfill")
            T.barrier()

        def load_w(src, c0, n, wt, dcol=0, rows0=0):
            T.dma("sp", wt.t[:, :, dcol:dcol + n], src[rows0:rows0 + D, c0:c0 + n].rearrange("(kc p) c -> p kc c", p=P), writes=[wt], slot=wt.name)

        def mm_group(ps_ap, pairs, reads, ps_buf, extra_first_reads=()):
            n = len(pairs)
            for k, (l, r) in enumerate(pairs):
                T.op("pe", lambda e, l=l, r=r, k=k: e.matmul(ps_ap, lhsT=l, rhs=r, start=(k == 0), stop=(k == n - 1)),
                     reads=reads, writes=[ps_buf], inc=(k == n - 1))

        def proj_F(wt, ncols, consumer, rhs_cols=slice(0, TB), nfree=TB):
            for ci in range(ncols // P):
                ps = psA.next()
                mm_group(ps.t[:, 0:nfree], [(wt.t[:, kc, ci * P:(ci + 1) * P], hT.t[:, kc, rhs_cols]) for kc in range(16)], [wt, hT], ps)
                consumer(ci, ps)

        def proj_T(wt, ncols, consumer, tiles=range(NT)):
            for i in tiles:
                ps = psA.next()
                mm_group(ps.t[:, 0:ncols], [(hT.t[:, kc, i * P:(i + 1) * P], wt.t[:, kc, 0:ncols]) for kc in range(16)], [wt, hT], ps)
                consumer(i, ps)

        def run_pipe(pend, nkeep):
            while len(pend) > nkeep:
                pend.pop(0)()

        evac_flip = [0]

        def evac(out_ap, in_ap, reads, writes, scale=None):
            evac_flip[0] ^= 1
            if evac_flip[0]:
                if scale is None:
                    T.op("act", lambda e: e.copy(out=out_ap, in_=in_ap), reads=reads, writes=writes)
                else:
                    T.op("act", lambda e: e.activation(out=out_ap, in_=in_ap, func=AF.Copy, scale=scale), reads=reads, writes=writes)
            else:
                if scale is None:
                    T.op("dve", lambda e: e.tensor_copy(out=out_ap, in_=in_ap), reads=reads, writes=writes)
                else:
                    T.op("dve", lambda e: e.tensor_scalar(out=out_ap, in0=in_ap, scalar1=scale, scalar2=None, op0=ALU.mult), reads=reads, writes=writes)

        def rstd_from_ssq(ssq_ap, out_ap, n, buf):
            T.op("act", lambda e: e.activation(out=out_ap, in_=ssq_ap, func=AF.Ln, scale=1.0 / n, bias=eps_t[:, 0:1]), reads=[buf, eps_t], writes=[buf])
            T.op("act", lambda e: e.activation(out=out_ap, in_=out_ap, func=AF.Exp, scale=-0.5), reads=[buf], writes=[buf])

        def build_hT(xsrc, row0):
            for i in range(NT):
                xt = xt_rot.next()
                T.dma("sp", xt[:, :], xsrc[row0 + i * P:row0 + (i + 1) * P, :], writes=[xt], slot=xt.name)
                T.op("act", lambda e, xt=xt: e.activation(out=xs[:], in_=xt[:, :], func=AF.Square, accum_out=st[:, 0:1]), reads=[xt], writes=[xs, st])
                rstd_from_ssq(st[:, 0:1], st[:, 1:2], D, st)
                T.op("dve", lambda e, xt=xt: e.tensor_scalar(out=xs[:], in0=xt[:, :], scalar1=st[:, 1:2], scalar2=None, op0=ALU.mult), reads=[xt, st], writes=[xs])
                for hb in range(2):
                    ps = psA.next()
                    for k in range(8):
                        kc = hb * 8 + k
                        T.op("pe", lambda e, ps=ps, k=k, kc=kc: e.transpose(out=psb(ps)[:, k, :], in_=xs[:, kc * P:(kc + 1) * P], identity=ident_b[:]),
                             reads=[xs, ident_b], writes=[ps], inc=(k == 7))
                    evac(hT.t[:, hb * 8:(hb + 1) * 8, i * P:(i + 1) * P], psb(ps), [ps], [hT])

        def ssd_block(S, is_own, bi):
            x_tok = sbg("x_tok", [P, NT, D], F32, S)
            BT = sbg("BT", [P, 4, TB], BF16, S)
            CTt = sbg("CTt", [P, 4, TB], BF16, S)
            Btok = sbg("Btok", [P, NT, 4, P], BF16, S)
            dt_all = sbg("dt_all", [P, NT, 32], F32, S)
            sm = sbg("sm", [P, NT, 160], F32, S)
            y_mT = sbg("y_mT", [P, 16, TB], BF16, S) if is_own else None
            SX = ExitStack()
            ut = Rot([sbg("ut%d" % i, [P, 515], F32, SX) for i in range(3)])
            acc_r = Rot([sbg("cacc%d" % i, [P, TB], F32, SX) for i in range(2)])
            xsT = Rot([sbg("xsT%d" % i, [P, TB], F32, SX) for i in range(4)])

            cpend = []
            for grp in range(6):
                wt = wt_rot.next()
                load_w(Wb, OFF_XBC + grp * 512, 512, wt)

                def cons(ci, ps, grp=grp):
                    cg = grp * 4 + ci
                    u = ut.next()
                    T.op("act", lambda e: e.copy(out=u[:, 3:515], in_=ps.t[:, 0:512]), reads=[ps], writes=[u])
                    T.op("pool", lambda e: e.tensor_copy(out=u[:, 0:3], in_=halo[:, cg, :]), reads=[halo], writes=[u])
                    T.op("pool", lambda e: e.tensor_copy(out=halo[:, cg, :], in_=u[:, 512:515]), reads=[u], writes=[halo])
                    acc = acc_r.next()
                    cw = lambda k: CST.t[:, C_CW + cg * 4 + k:C_CW + cg * 4 + k + 1]
                    T.op("dve", lambda e: e.tensor_scalar(out=acc[:, :], in0=u[:, 3:515], scalar1=cw(3), scalar2=None, op0=ALU.mult), reads=[u, CST], writes=[acc])
                    for k in (2, 1, 0):
                        T.op("dve", lambda e, k=k: e.scalar_tensor_tensor(out=acc[:, :], in0=u[:, k:k + 512], scalar=cw(k), in1=acc[:, :], op0=ALU.mult, op1=ALU.add), reads=[u, CST, acc], writes=[acc])
                    cb = CST.t[:, C_CB + cg:C_CB + cg + 1]
                    if cg < 16:
                        xT = xsT.next(cpend)
                        T.op("act", lambda e: e.activation(out=xT[:, :], in_=acc[:, :], func=AF.Silu, bias=cb), reads=[acc, CST], writes=[xT])

                        def later(xT=xT, cg=cg):
                            ps2 = psA.next()
                            for i in range(NT):
                                T.op("pe", lambda e, i=i: e.transpose(out=ps2.t[:, i * P:(i + 1) * P], in_=xT[:, i * P:(i + 1) * P], identity=ident_f),
                                     reads=[xT, CST], writes=[ps2], inc=(i == NT - 1))
                            evac(x_tok.t[:, :, cg * P:(cg + 1) * P], ps2.t[:, :].rearrange("p (a b) -> p a b", b=P), [ps2], [x_tok])
                        defer(cpend, later, [xT])
                    elif cg < 20:
                        g = cg - 16
                        T.op("act", lambda e: e.activation(out=BT.t[:, g, :], in_=acc[:, :], func=AF.Silu, bias=cb), reads=[acc, CST], writes=[BT])

                        def later(g=g):
                            ps2 = psA.next()
                            for i in range(NT):
                                T.op("pe", lambda e, i=i: e.transpose(out=psb(ps2)[:, i, :], in_=BT.t[:, g, i * P:(i + 1) * P], identity=ident_b[:]),
                                     reads=[BT, ident_b], writes=[ps2], inc=(i == NT - 1))
                            evac(Btok.t[:, :, g, :], psb(ps2)[:, 0:NT, :], [ps2], [Btok])
                        defer(cpend, later, [])
                    else:
                        g = cg - 20
                        T.op("act", lambda e: e.activation(out=CTt.t[:, g, :], in_=acc[:, :], func=AF.Silu, bias=cb), reads=[acc, CST], writes=[CTt])
                    run_pipe(cpend, 2)

                proj_F(wt, 512, cons)
            run_pipe(cpend, 0)
            T.barrier()
            SX.close()

            wt = wt_rot.next()
            load_w(Wb, OFF_DT, 32, wt)

            def cons_dt(i, ps):
                T.op("dve", lambda e: e.tensor_tensor(out=dt_all.t[:, i, :], in0=ps.t[:, 0:32], in1=cs(C_DTB, 32), op=ALU.add), reads=[ps, CST], writes=[dt_all])
                T.op("act", lambda e: e.activation(out=dt_all.t[:, i, :], in_=dt_all.t[:, i, :], func=AF.Exp), reads=[dt_all], writes=[dt_all])
                T.op("act", lambda e: e.activation(out=dt_all.t[:, i, :], in_=dt_all.t[:, i, :], func=AF.Ln, bias=1.0), reads=[dt_all], writes=[dt_all])

            proj_T(wt, 32, cons_dt)

            for i in range(NT):
                a_i = sm.t[:, i, 0:32]
                T.op("dve", lambda e, i=i, a_i=a_i: e.tensor_tensor(out=a_i, in0=dt_all.t[:, i, :], in1=aneg[:], op=ALU.mult), reads=[dt_all, aneg], writes=[sm])
                ps = psA.next()
                T.op("pe", lambda e, a_i=a_i, ps=ps: e.matmul(ps.t[:, 0:32], lhsT=Lmask, rhs=a_i, start=True, stop=True), reads=[CST, sm], writes=[ps])
                T.op("pe", lambda e, a_i=a_i, ps=ps: e.matmul(ps.t[:, 32:64], lhsT=ones_f[:], rhs=a_i, start=True, stop=True), reads=[ones_f, sm], writes=[ps])
                T.op("act", lambda e, i=i, ps=ps: e.activation(out=sm.t[:, i, 32:96], in_=ps.t[:, 0:64], func=AF.Exp), reads=[ps], writes=[sm])
                T.op("dve", lambda e, i=i, ps=ps: e.tensor_copy(out=sm.t[:, i, 128:160], in_=ps.t[:, 0:32]), reads=[ps], writes=[sm])
                T.op("dve", lambda e, i=i, ps=ps: e.tensor_tensor(out=sm.t[:, i, 96:128], in0=ps.t[:, 32:64], in1=sm.t[:, i, 128:160], op=ALU.subtract), reads=[ps, sm], writes=[sm])
                T.op("act", lambda e, i=i: e.activation(out=sm.t[:, i, 96:128], in_=sm.t[:, i, 96:128], func=AF.Exp), reads=[sm], writes=[sm])
                T.op("dve", lambda e, i=i: e.tensor_tensor(out=sm.t[:, i, 96:128], in0=sm.t[:, i, 96:128], in1=dt_all.t[:, i, :], op=ALU.mult), reads=[sm, dt_all], writes=[sm])

            xw = Rot([sbg("xw%d" % i, [P, TB], BF16, S) for i in range(2)])
            tmpS = Rot([sbg("tmpS%d" % i, [P, TB], F32, S) for i in range(2)])
            if is_own:
                xdt = Rot([sbg("xdt%d" % i, [P, TB], BF16, S) for i in range(3)])
                hsb = Rot([sbg("hsb%d" % i, [P, TB], BF16, S) for i in range(3)])
                cbm = Rot([sbg("cbm%d" % i, [P, P], F32, S) for i in range(3)])
                Rr = Rot([sbg("Rr%d" % i, [P, 8, P], F32, S) for i in range(2)])
                dec = Rot([sbg("dec%d" % i, [P, 8, P], F32, S) for i in range(2)])
                Mm = Rot([sbg("Mm%d" % i, [P, 8, P], BF16, S) for i in range(3)])
                t1 = Rot([sbg("t1_%d" % i, [P, TB], F32, S) for i in range(3)])
                t2 = Rot([sbg("t2_%d" % i, [P, TB], F32, S) for i in range(2)])
                zs = Rot([sbg("zs%d" % i, [P, TB], F32, S) for i in range(3)])
                yn = Rot([sbg("yn%d" % i, [P, TB], BF16, S) for i in range(3)])
                st2r = Rot([sbg("st2_%d" % i, [P, 8], F32, S) for i in range(2)])
            v3 = lambda b: b.t[:, :].rearrange("p (h q) -> p h q", q=64)
            stageC = []
            stageD = []

            def make_R(g, i):
                Rb = Rr.next()
                a3 = sm.t[:, i, g * 8:(g + 1) * 8].unsqueeze(2).to_broadcast([P, 8, P])
                L3 = Lmask.unsqueeze(1).to_broadcast([P, 8, P])
                T.op("dve", lambda e: e.tensor_tensor(out=Rb[:, :, :], in0=a3, in1=L3, op=ALU.mult), reads=[sm, CST], writes=[Rb])
                return Rb
            Rnext = make_R(0, 0) if is_own else None
            for g in range(4):
                if is_own:
                    wz = wt_rot.next()
                    load_w(Wb, OFF_Z + g * 512, 512, wz)
                for i in range(NT):
                    tsl = slice(i * P, (i + 1) * P)
                    xg3 = x_tok.t[:, i, g * 512:(g + 1) * 512].rearrange("p (h q) -> p h q", q=64)
                    bc = lambda c0, i=i, g=g: sm.t[:, i, c0 + g * 8:c0 + g * 8 + 8].unsqueeze(2).to_broadcast([P, 8, 64])
                    xwb = xw.next()
                    T.op("dve" if not is_own else "pool", lambda e, xwb=xwb, xg3=xg3, bc=bc: e.tensor_tensor(out=v3(xwb), in0=xg3, in1=bc(96), op=ALU.mult), reads=[x_tok, sm], writes=[xwb])
                    if is_own:
                        xdb = xdt.next(stageC)
                        dtb3 = dt_all.t[:, i, g * 8:(g + 1) * 8].unsqueeze(2).to_broadcast([P, 8, 64])
                        T.op("pool", lambda e, xdb=xdb, xg3=xg3, dtb3=dtb3: e.tensor_tensor(out=v3(xdb), in0=xg3, in1=dtb3, op=ALU.mult), reads=[x_tok, dt_all], writes=[xdb])
                        hb_ = hsb.next(stageC)
                        T.op("act", lambda e, hb_=hb_, g=g: e.copy(out=hb_[:, :], in_=hs.t[:, g, :]), reads=[hs], writes=[hb_])
                        Rb = Rnext
                        psc = psA.next()
                        T.op("pe", lambda e, psc=psc, g=g, tsl=tsl: e.matmul(psc.t[:, 0:P], lhsT=BT.t[:, g, tsl], rhs=CTt.t[:, g, tsl], start=True, stop=True), reads=[BT, CTt], writes=[psc])
                        psd = [psA.next(), psA.next()]
                        for hh in range(2):
                            T.op("pe", lambda e, pd=psd[hh], Rb=Rb, hh=hh: e.matmul(pd.t[:, :], lhsT=Ustr, rhs=Rb.t[:, hh * 4:(hh + 1) * 4, :].rearrange("p a b -> p (a b)"), start=True, stop=True), reads=[CST, Rb], writes=[psd[hh]])
                    pss = psA.next()
                    T.op("pe", lambda e, pss=pss, xwb=xwb, i=i, g=g: e.matmul(pss.t[:, :], lhsT=Btok.t[:, i, g, :], rhs=xwb[:, :], start=True, stop=True), reads=[Btok, xwb], writes=[pss])
                    if is_own:
                        psz = psA.next()
                        mm_group(psz.t[:, :], [(hT.t[:, kc, tsl], wz.t[:, kc, :]) for kc in range(16)], [hT, wz], psz)
                        cbb = cbm.next()
                        T.op("dve", lambda e, psc=psc, cbb=cbb: e.tensor_tensor(out=cbb[:, :], in0=psc.t[:, 0:P], in1=Lmask, op=ALU.mult), reads=[psc, CST], writes=[cbb])
                        db = dec.next()
                        for hh in range(2):
                            T.op("act", lambda e, pd=psd[hh], db=db, hh=hh: e.activation(out=db.t[:, hh * 4:(hh + 1) * 4, :].rearrange("p a b -> p (a b)"), in_=pd.t[:, :], func=AF.Exp), reads=[psd[hh]], writes=[db])
                    tS = tmpS.next()
                    hs3 = hs.t[:, g, :].rearrange("p (h q) -> p h q", q=64)
                    T.op("pool", lambda e, tS=tS, hs3=hs3, bc=bc: e.tensor_tensor(out=v3(tS), in0=hs3, in1=bc(64), op=ALU.mult), reads=[hs, sm], writes=[tS])
                    T.op("dve", lambda e, tS=tS, pss=pss, g=g: e.tensor_tensor(out=hs.t[:, g, :], in0=pss.t[:, :], in1=tS[:, :], op=ALU.add), reads=[pss, tS], writes=[hs])
                    if not is_own:
                        continue
                    zb = zs.next(stageC)
                    T.op("act", lambda e, zb=zb, psz=psz: e.activation(out=zb[:, :], in_=psz.t[:, :], func=AF.Silu), reads=[psz], writes=[zb])
                    Mb = Mm.next(stageC)
                    c3 = cbb.t[:, :].unsqueeze(1).to_broadcast([P, 8, P])
                    T.op("dve", lambda e, Mb=Mb, db=db, c3=c3: e.tensor_tensor(out=Mb[:, :, :], in0=db[:, :, :], in1=c3, op=ALU.mult), reads=[db, cbb], writes=[Mb])
                    nxt = (g, i + 1) if i + 1 < NT else (g + 1, 0)
                    if nxt[0] < 4:
                        Rnext = make_R(*nxt)

                    def stC(Mb=Mb, xdb=xdb, hb_=hb_, g=g, i=i, tsl=tsl, xg3=xg3, bc=bc, zb=zb):
                        psy = psA.next()
                        for h in range(8):
                            T.op("pe", lambda e, h=h: e.matmul(psy.t[:, h * 64:(h + 1) * 64], lhsT=Mb.t[:, h, :], rhs=xdb.t[:, h * 64:(h + 1) * 64], start=True, stop=True),
                                 reads=[Mb, xdb], writes=[psy], inc=(h == 7))
                        pso = psA.next()
                        T.op("pe", lambda e: e.matmul(pso.t[:, :], lhsT=CTt.t[:, g, tsl], rhs=hb_[:, :], start=True, stop=True), reads=[CTt, hb_], writes=[pso])
                        t1b = t1.next()
                        t2b = t2.next()
                        st2 = st2r.next()
                        dsk3 = CST.t[:, C_DSK + g * 8:C_DSK + g * 8 + 8].unsqueeze(2).to_broadcast([P, 8, 64])
                        T.op("dve", lambda e: e.tensor_tensor(out=v3(t1b), in0=pso.t[:, :].rearrange("p (h q) -> p h q", q=64), in1=bc(32), op=ALU.mult), reads=[pso, sm], writes=[t1b])
                        T.op("pool", lambda e: e.tensor_tensor(out=v3(t2b), in0=xg3, in1=dsk3, op=ALU.mult), reads=[x_tok, CST], writes=[t2b])
                        T.op("dve", lambda e: e.tensor_tensor(out=t1b[:, :], in0=psy.t[:, :], in1=t1b[:, :], op=ALU.add), reads=[psy, t1b], writes=[t1b])
                        T.op("dve", lambda e: e.tensor_tensor(out=t1b[:, :], in0=t1b[:, :], in1=t2b[:, :], op=ALU.add), reads=[t1b, t2b], writes=[t1b])
                        T.op("dve", lambda e: e.tensor_tensor(out=t1b[:, :], in0=t1b[:, :], in1=zb[:, :], op=ALU.mult), reads=[t1b, zb], writes=[t1b])
                        T.op("act", lambda e: e.activation(out=t2b[:, :], in_=t1b[:, :], func=AF.Square, accum_out=st2[:, 0:1]), reads=[t1b], writes=[t2b, st2])
                        rstd_from_ssq(st2[:, 0:1], st2[:, 1:2], 512, st2)
                        ynb = yn.next(stageD)
                        T.op("dve", lambda e: e.tensor_scalar(out=ynb[:, :], in0=t1b[:, :], scalar1=st2[:, 1:2], scalar2=None, op0=ALU.mult), reads=[t1b, st2], writes=[ynb])

                        def stD(ynb=ynb):
                            pst = psA.next()
                            for j in range(4):
                                T.op("pe", lambda e, j=j: e.transpose(out=psb(pst)[:, j, :], in_=ynb.t[:, j * P:(j + 1) * P], identity=ident_b[:]), reads=[ynb, ident_b], writes=[pst], inc=(j == 3))
                            for j in range(4):
                                c = g * 4 + j
                                T.op("act", lambda e, j=j, c=c: e.activation(out=y_mT.t[:, c, tsl], in_=psb(pst)[:, j, :], func=AF.Copy, scale=CST.t[:, C_GN + c:C_GN + c + 1]), reads=[pst, CST], writes=[y_mT])
                        defer(stageD, stD, [ynb])
                    defer(stageC, stC, [Mb, xdb, hb_, zb])
                    run_pipe(stageD, 1)
                    run_pipe(stageC, 1)
            run_pipe(stageC, 0)
            run_pipe(stageD, 0)
            if is_own:
                dump("x_tok%d" % bi, x_tok.t[:, :, :].rearrange("p a b -> p (a b)"), [P, NT * D], F32, [x_tok])
                dump("dt%d" % bi, dt_all.t[:, :, :].rearrange("p a b -> p (a b)"), [P, NT * 32], F32, [dt_all])
                dump("ymT%d" % bi, y_mT.t[:, :, :].rearrange("p a b -> p (a b)"), [P, 16 * TB], BF16, [y_mT])
            return y_mT

        def kv_proj(S, key0, is_own, qT, qiT, ws):
            ktmp = Rot([sbg("ktmp%d" % i, [P, TB], BF16, S) for i in range(2)])
            vtmp = Rot([sbg("vtmp%d" % i, [P, 4, 129], BF16, S) for i in range(2)])
            for v_ in vtmp.items:
                T.op("pool", lambda e, v_=v_: e.memset(v_[:, :, :], 1.0), writes=[v_])
            for grp in range(4):
                wt = wt_rot.next()
                load_w(Wb, OFF_K + grp * 512, 512, wt)

                def cons(ci, ps, grp=grp):
                    hh = grp * 4 + ci
                    kb_ = ktmp.next()
                    evac(kb_[:, :], ps.t[:, :], [ps], [kb_])
                    T.dma("pool", KT[hh, :, key0:key0 + TB], kb_[:, :], reads=[kb_], slot=kb_.name)
                proj_F(wt, 512, cons)
            for grp in range(4):
                wt = wt_rot.next()
                load_w(Wb, OFF_V + grp * 512, 512, wt)

                def cons(i, ps, grp=grp):
                    vb_ = vtmp.next()
                    evac(vb_.t[:, :, 0:P], ps.t[:, :].rearrange("p (h d) -> p h d", d=P), [ps], [vb_])
                    kbg = key0 // P + i
                    T.dma("pool", VS[grp * 4:(grp + 1) * 4, kbg // 16, :, kbg % 16, :].rearrange("h p d -> p h d"), vb_[:, :, :], reads=[vb_], slot=vb_.name)
                proj_T(wt, 512, cons)
            wt = wt_rot.next()
            load_w(Wb, OFF_KI, 64, wt, dcol=0)
            load_w(Wb, OFF_KI, 64, wt, dcol=64)

            def cons(ci, ps):
                kb_ = ktmp.next()
                evac(kb_[:, :], ps.t[:, :], [ps], [kb_])
                T.dma("pool", KI[0, 0:64, key0:key0 + TB], kb_[0:64, :], reads=[kb_], slot=kb_.name)
                T.dma("pool", KI[1, 64:128, key0:key0 + TB], kb_[64:128, :], reads=[kb_], slot=kb_.name)
            proj_F(wt, 128, cons)
            if not is_own:
                return
            for grp in range(4):
                wt = wt_rot.next()
                load_w(Wb, OFF_Q + grp * 512, 512, wt)

                def cons(ci, ps, grp=grp):
                    evac(qT.t[:, grp * 4 + ci, :], ps.t[:, :], [ps], [qT], scale=ATTN_SCALE)
                proj_F(wt, 512, cons)
            for grp in range(2):
                wt = wt_rot.next()
                load_w(Wb, OFF_QI + grp * 512, 512, wt)

                def cons(ci, ps, grp=grp):
                    evac(qiT.t[:, grp * 4 + ci, :], ps.t[:, :], [ps], [qiT])
                proj_F(wt, 512, cons)
            wt = wt_rot.next()
            load_w(Wb, OFF_WI, 16, wt)

            def cons(i, ps):
                T.op("dve", lambda e: e.tensor_scalar(out=ws.t[:, i, :], in0=ps.t[:, 0:16], scalar1=IDX_SCALE, scalar2=None, op0=ALU.mult), reads=[ps], writes=[ws])
            proj_T(wt, 16, cons)

        def dsa_half(S, bi, key0, hb, qT, qiT, ws, y_aT):
            nkb = (key0 + 256 * (hb + 1)) // P
            W = nkb * P
            nb5 = (W + 511) // 512
            LA = 4
            LA2 = 6
            maskT = sbg("maskT", [P, 64, 256], BF16, S)
            SI = ExitStack()
            Dgr = Rot([sbg("Dg%d" % i, [P, 16, P], BF16, SI) for i in range(2)])
            Rl = Rot([sbg("Rl%d" % i, [P, 512], BF16, SI) for i in range(6)])
            kib = Rot([sbg("kib%d" % i, [P, 2, 512], BF16, SI) for i in range(3)])

            def indexer(j2):
                i = 2 * hb + j2
                tsl = slice(i * P, (i + 1) * P)
                Dg = Dgr.next()
                pend = []
                for h in range(16):
                    T.op("pool", lambda e, h=h, i=i, Dg=Dg: e.tensor_scalar(out=Dg.t[:, h, :], in0=ident_f, scalar1=ws.t[:, i, h:h + 1], scalar2=1.0, op0=ALU.mult, op1=ALU.mult), reads=[CST, ws], writes=[Dg])
                for b5 in range(nb5):
                    c0 = b5 * 512
                    wd = min(512, W - c0)
                    kb_ = kib.next()
                    T.dma("sp", kb_[:, 0, 0:wd], KI[0, :, c0:c0 + wd], writes=[kb_], slot=kb_.name)
                    T.dma("sp", kb_[:, 1, 0:wd], KI[1, :, c0:c0 + wd], writes=[kb_], slot=kb_.name)
                    acc = psO.next(pend)
                    extra = []
                    if c0 < NPB * TB:
                        extra.append("pv")
                    if b5 == nb5 - 1:
                        extra.append("bp")
                    nacc = 16 + len(extra)
                    for h in range(16):
                        j, sub = h // 2, h % 2
                        ps = psG.next()
                        T.op("pe", lambda e, ps=ps, j=j, sub=sub, kb_=kb_, wd=wd, tsl=tsl: e.matmul(ps.t[:, 0:wd], lhsT=qiT.t[:, j, tsl], rhs=kb_.t[:, sub, 0:wd], start=True, stop=True), reads=[qiT, kb_], writes=[ps])
                        rb = Rl.next(pend)
                        if h % 2 == 0:
                            T.op("act", lambda e, rb=rb, ps=ps, wd=wd: e.activation(out=rb[:, 0:wd], in_=ps.t[:, 0:wd], func=AF.Relu), reads=[ps], writes=[rb])
                        else:
                            T.op("dve", lambda e, rb=rb, ps=ps, wd=wd: e.tensor_scalar(out=rb[:, 0:wd], in0=ps.t[:, 0:wd], scalar1=0.0, scalar2=None, op0=ALU.max), reads=[ps], writes=[rb])

                        def accmm(acc=acc, h=h, rb=rb, wd=wd, nacc=nacc, Dg=Dg):
                            T.op("pe", lambda e: e.matmul(acc.t[:, 0:wd], lhsT=Dg.t[:, h, :], rhs=rb.t[:, 0:wd], start=(h == 0), stop=(h == nacc - 1)), reads=[Dg, rb], writes=[acc])
                        defer(pend, accmm, [rb, acc])
                        run_pipe(pend, LA)

                    def fin(acc=acc, wd=wd, nacc=nacc, extra=tuple(extra), c0=c0):
                        k = 16
                        for ex in extra:
                            if ex == "pv":
                                T.op("pe", lambda e, k=k: e.matmul(acc.t[:, 0:wd], lhsT=pvm[:], rhs=ones_b.t[:, 0:wd], start=False, stop=(k == nacc - 1)), reads=[pvm, ones_b], writes=[acc])
                            else:
                                T.op("pe", lambda e, k=k: e.matmul(acc.t[:, wd - 256:wd], lhsT=ident_b[:], rhs=bpat_b.t[:, j2, :], start=False, stop=(k == nacc - 1)), reads=[ident_b, bpat_b], writes=[acc])
                            k += 1
                        T.op("act", lambda e: e.copy(out=score.t[:, c0:c0 + wd], in_=acc.t[:, 0:wd]), reads=[acc], writes=[score])
                    defer(pend, fin, [acc])
                run_pipe(pend, 0)

            def bisect(j2):
                i = 2 * hb + j2
                hi_t, mid_t, cnt_t, t_t, g_t, lo_t, sa_t = [sbg("bq%d_%d" % (j2, c), [P, 2], F32, SI) for c in range(7)]
                WD = min(4096, max(P, ((W * 45 // 100) // P) * P))
                nA = W - WD
                assert nA >= P and nA <= 8192
                hi, mid, cntc, tt, gg, lo, sa = [b_.t[:, 0:1] for b_ in (hi_t, mid_t, cnt_t, t_t, g_t, lo_t, sa_t)]
                T.op("dve", lambda e: e.reduce_max(out=hi, in_=score.t[:, 0:W], axis=AX.X), reads=[score], writes=[hi_t])
                T.op("dve", lambda e: e.tensor_scalar(out=mid, in0=hi, scalar1=-CLAMP * 0.5, scalar2=None, op0=ALU.add), reads=[hi_t], writes=[mid_t])
                for it in range(NITER):
                    hk = CLAMP * (0.5 ** (it + 1))
                    T.op("dve", lambda e: e.tensor_scalar(out=XM.t[:, 0:WD], in0=score.t[:, 0:WD], scalar1=mid, scalar2=None, op0=ALU.is_ge, op1=ALU.add, accum_out=cntc), reads=[score, mid_t], writes=[mjD, cnt_t])
                    T.op("act", lambda e: e.activation(out=XM8[:, 0:nA], in_=score.t[:, WD:W], func=AF.Sign, bias=mid, scale=-1.0, accum_out=sa), reads=[score, mid_t], writes=[mjA, sa_t])
                    T.op("dve", lambda e: e.scalar_tensor_tensor(out=tt, in0=cntc, scalar=2.0, in1=sa, op0=ALU.mult, op1=ALU.subtract), reads=[cnt_t, sa_t], writes=[t_t])
                    T.op("dve", lambda e: e.tensor_scalar(out=gg, in0=tt, scalar1=float(2 * TOPK - 1 - nA), scalar2=-0.5, op0=ALU.is_ge, op1=ALU.add), reads=[t_t], writes=[g_t])
                    T.op("dve", lambda e, hk=hk: e.scalar_tensor_tensor(out=mid, in0=gg, scalar=hk, in1=mid, op0=ALU.mult, op1=ALU.add), reads=[g_t, mid_t], writes=[mid_t])
                hN = CLAMP * (0.5 ** (NITER + 1))
                T.op("dve", lambda e: e.tensor_scalar(out=lo, in0=mid, scalar1=-hN, scalar2=None, op0=ALU.add), reads=[mid_t], writes=[lo_t])
                T.op("dve", lambda e: e.tensor_scalar(out=XM.t[:, 0:W], in0=score.t[:, 0:W], scalar1=lo, scalar2=None, op0=ALU.is_ge), reads=[score, lo_t], writes=[mjD, mjA])
                dump("thr%d_%d" % (bi, i), lo, [P, 1], F32, [lo_t])
                dump("score%d_%d" % (bi, i), score.t[:, 0:W], [P, W], F32, [score])

            def mask_transposes(j2):
                for kb0 in range(0, nkb, 8):
                    n = min(8, nkb - kb0)
                    ps = psG.next()
                    for t in range(n):
                        T.op("pe", lambda e, ps=ps, t=t, kb0=kb0: e.transpose(out=psb(ps)[:, t, :], in_=XM.t[:, (kb0 + t) * P:(kb0 + t + 1) * P], identity=ident_b[:]), reads=[mjD, mjA, ident_b], writes=[ps], inc=(t == n - 1))
                    T.op("act", lambda e, ps=ps, n=n, kb0=kb0, j2=j2: e.copy(out=maskT.t[:, kb0:kb0 + n, j2 * P:(j2 + 1) * P], in_=psb(ps)[:, 0:n, :]), reads=[ps], writes=[maskT])

            indexer(0)
            bisect(0)
            indexer(1)
            mask_transposes(0)
            bisect(1)
            mask_transposes(1)
            T.barrier()
            SI.close()

            wga = Rot([sbg("wga%d" % i, [P, 16, P], BF16, S) for i in range(1)])
            gaT = Rot([sbg("gaT%d" % i, [P, 256], F32, S) for i in range(2)])
            ktb = Rot([sbg("ktb%d" % i, [P, 2048], BF16, S) for i in range(2)])
            vtb = Rot([sbg("vtb%d" % i, [P, 16, 129], BF16, S) for i in range(2)])
            eb = Rot([sbg("eb%d" % i, [P, 512], BF16, S) for i in range(4)])
            pmb = Rot([sbg("pmb%d" % i, [P, 512], BF16, S) for i in range(8)])
            onb = Rot([sbg("onb%d" % i, [P, P], BF16, S) for i in range(2)])
            rdr = Rot([sbg("rd%d" % i, [P, 4], F32, S) for i in range(2)])
            psQ = Rot(PS[0:3])
            psX = Rot(PS[3:4])
            qsl = slice(hb * 256, (hb + 1) * 256)
            pend = []
            for hh in range(16):
                ga = gaT.next(pend)
                O = [psO.next(pend), psO.next(pend)]
                wg = wga.next()
                T.dma("sp", wg[:, :, :], Wb[:, OFF_GA + hh * P:OFF_GA + (hh + 1) * P].rearrange("(kc p) c -> p kc c", p=P), writes=[wg], slot=wg.name)
                psg = psX.next()
                mm_group(psg.t[:, 0:256], [(wg.t[:, kc, :], hT.t[:, kc, qsl]) for kc in range(16)], [wg, hT], psg)
                T.op("act", lambda e, ga=ga, psg=psg: e.activation(out=ga[:, :], in_=psg.t[:, 0:256], func=AF.Silu), reads=[psg], writes=[ga])
                for c2 in range((nkb + 15) // 16):
                    kb_lo = c2 * 16
                    n = min(16, nkb - kb_lo)
                    kt = ktb.next(pend)
                    vt = vtb.next(pend)
                    T.dma("sp", kt[:, 0:n * P], KT[hh, :, kb_lo * P:(kb_lo + n) * P], writes=[kt], slot=kt.name)
                    T.dma("sp", vt[:, 0:n, :], VS[hh, c2, :, 0:n, :], writes=[vt], slot=vt.name)
                    for t in range(0, n, 2):
                        kb = kb_lo + t
                        ps = psQ.next()
                        for u in range(2):
                            T.op("pe", lambda e, ps=ps, kt=kt, t=t, u=u, hh=hh: e.matmul(ps.t[:, u * 256:(u + 1) * 256], lhsT=kt.t[:, (t + u) * P:(t + u + 1) * P], rhs=qT.t[:, hh, qsl], start=True, stop=True), reads=[kt, qT], writes=[ps], inc=(u == 1))
                        eb_ = eb.next()
                        T.op("act", lambda e, eb_=eb_, ps=ps: e.activation(out=eb_[:, :], in_=ps.t[:, :], func=AF.Exp), reads=[ps], writes=[eb_])
                        pm = pmb.next(pend)
                        T.op("dve", lambda e, pm=pm, eb_=eb_, kb=kb: e.tensor_tensor(out=pm[:, :], in0=eb_[:, :], in1=maskT.t[:, kb:kb + 2, :].rearrange("p a b -> p (a b)"), op=ALU.mult), reads=[eb_, maskT], writes=[pm])

                        def pv(pm=pm, vt=vt, t=t, kb=kb, O=O):
                            for u in range(2):
                                for j2 in range(2):
                                    T.op("pe", lambda e, u=u, j2=j2: e.matmul(O[j2].t[:, 0:129], lhsT=pm.t[:, u * 256 + j2 * P:u * 256 + (j2 + 1) * P], rhs=vt.t[:, t + u, :], start=(kb + u == 0), stop=(kb + u == nkb - 1)),
                                         reads=[pm, vt], writes=[O[j2]], inc=(u == 1 and j2 == 1))
                        defer(pend, pv, [pm, vt, O[0], O[1]])
                        run_pipe(pend, LA2)

                def finalize(O=O, ga=ga, hh=hh):
                    pst = psX.next()
                    rd = rdr.next()
                    for j2 in range(2):
                        T.op("dve", lambda e, j2=j2: e.reciprocal(out=rd.t[:, j2:j2 + 1], in_=O[j2].t[:, 128:129]), reads=[O[j2]], writes=[rd])
                        ob = onb.next()
                        T.op("dve", lambda e, j2=j2, ob=ob: e.tensor_scalar(out=ob[:, :], in0=O[j2].t[:, 0:P], scalar1=rd.t[:, j2:j2 + 1], scalar2=None, op0=ALU.mult), reads=[O[j2], rd], writes=[ob])
                        T.op("pe", lambda e, j2=j2, ob=ob: e.transpose(out=psb(pst)[:, j2, :], in_=ob[:, :], identity=ident_b[:]), reads=[ob, ident_b], writes=[pst])
                    T.op("dve", lambda e: e.tensor_tensor(out=y_aT.t[:, hh, qsl], in0=psb(pst)[:, 0:2, :].rearrange("p a b -> p (a b)"), in1=ga[:, :], op=ALU.mult), reads=[pst, ga], writes=[y_aT])
                defer(pend, finalize, [ga, O[0], O[1]])
            run_pipe(pend, 0)

        def out_block2(S, bi, orow0, y_aT):
            y_mT = sbg("y_mT2", [P, 16, TB], BF16, S)
            T.dma("sp", y_mT.t[:, :, :].rearrange("p a b -> p (a b)"), YM, writes=[y_mT], slot="ymld")
            mT = sbg("mT", [P, 16, TB], BF16, S)
            r = sbg("r", [P, NT, D], F32, S)
            nfb = sbg("nfb", [P, D], F32, S)
            stashb = sbg("stash", [P, 4, TB], F32, S)
            gm = Rot([sbg("gm%d" % i, [P, TB], F32, S) for i in range(2)])
            m2 = Rot([sbg("m2_%d" % i, [P, TB], F32, S) for i in range(2)])
            st3 = sbg("st3", [P, 8], F32, S)
            T.dma("sp", nfb[:, :], normf.partition_broadcast(P), writes=[nfb], slot="nfb")
            for i in range(NT):
                T.dma("sp", r.t[:, i, :], xo[orow0 + i * P:orow0 + (i + 1) * P, :], writes=[r], slot="rld")
            for cg in range(4):
                plan = ((Wbr, cg * 512, 0), (Wb, OFF_G + cg * 512, 0), (Wbr, cg * 512, D), (Wb, OFF_G + D + cg * 512, 0))
                for half in range(2):
                    wA = wt_rot.next()
                    load_w(plan[2 * half][0], plan[2 * half][1], 512, wA, rows0=plan[2 * half][2])
                    ysrc = y_mT if half == 0 else y_aT
                    psas = []
                    for ci in range(4):
                        psa = psA.next()
                        mm_group(psa.t[:, :], [(wA.t[:, kc, ci * P:(ci + 1) * P], ysrc.t[:, kc, :]) for kc in range(16)], [wA, ysrc], psa)
                        psas.append(psa)
                    wG = wt_rot.next()
                    load_w(plan[2 * half + 1][0], plan[2 * half + 1][1], 512, wG, rows0=plan[2 * half + 1][2])
                    for ci in range(4):
                        c = cg * 4 + ci
                        psa = psas[ci]
                        psg = psA.next()
                        mm_group(psg.t[:, :], [(wG.t[:, kc, ci * P:(ci + 1) * P], hT.t[:, kc, :]) for kc in range(16)], [wG, hT], psg)
                        gb = gm.next()
                        gcol = C_GB + half * 16 + c
                        T.op("act", lambda e, gb=gb, psg=psg, gcol=gcol: e.activation(out=gb[:, :], in_=psg.t[:, :], func=AF.Sigmoid, bias=CST.t[:, gcol:gcol + 1]), reads=[psg, CST], writes=[gb])
                        if half == 0:
                            T.op("dve", lambda e, psa=psa, gb=gb, ci=ci: e.tensor_tensor(out=stashb.t[:, ci, :], in0=psa.t[:, :], in1=gb[:, :], op=ALU.mult), reads=[psa, gb], writes=[stashb])
                        else:
                            mb2 = m2.next()
                            T.op("dve", lambda e, mb2=mb2, psa=psa, gb=gb: e.tensor_tensor(out=mb2[:, :], in0=psa.t[:, :], in1=gb[:, :], op=ALU.mult), reads=[psa, gb], writes=[mb2])
                            T.op("pool", lambda e, mb2=mb2, c=c, ci=ci: e.tensor_tensor(out=mT.t[:, c, :], in0=mb2[:, :], in1=stashb.t[:, ci, :], op=ALU.add), reads=[mb2, stashb], writes=[mT])
            dump("mT%d" % bi, mT.t[:, :, :].rearrange("p a b -> p (a b)"), [P, 16 * TB], BF16, [mT])
            for n in range(4):
                wo = wt_rot.next()
                load_w(Wo, n * 512, 512, wo)
                for i in range(NT):
                    ps = psA.next()
                    mm_group(ps.t[:, :], [(mT.t[:, kc, i * P:(i + 1) * P], wo.t[:, kc, :]) for kc in range(16)], [mT, wo], ps)
                    T.op("dve", lambda e, ps=ps, i=i, n=n: e.tensor_tensor(out=r.t[:, i, n * 512:(n + 1) * 512], in0=ps.t[:, :], in1=r.t[:, i, n * 512:(n + 1) * 512], op=ALU.add), reads=[ps, r], writes=[r])
            for i in range(NT):
                T.op("act", lambda e, i=i: e.activation(out=xs[:], in_=r.t[:, i, :], func=AF.Square, accum_out=st3[:, 0:1]), reads=[r], writes=[xs, st3])
                rstd_from_ssq(st3[:, 0:1], st3[:, 1:2], D, st3)
                T.op("dve", lambda e, i=i: e.scalar_tensor_tensor(out=r.t[:, i, :], in0=r.t[:, i, :], scalar=st3[:, 1:2], in1=nfb[:, :], op0=ALU.mult, op1=ALU.mult), reads=[r, st3, nfb], writes=[r])
                T.dma("pool", out[orow0 + i * P:orow0 + (i + 1) * P, :], r.t[:, i, :], reads=[r], slot="ost")

        for bi in range(NPB + NOB):
            is_own = bi >= NPB
            key0 = bi * TB
            xsrc, row0 = (xo, (bi - NPB) * TB) if is_own else (xp, bi * TB)
            build_hT(xsrc, row0)
            if bi == NPB:
                dump("hT", hT.t[:, :, :].rearrange("p a b -> p (a b)"), [P, 16 * TB], BF16, [hT])
            with ExitStack() as S1:
                y_mT = ssd_block(S1, is_own, bi)
                if is_own:
                    T.dma("pool", YM, y_mT.t[:, :, :].rearrange("p a b -> p (a b)"), reads=[y_mT], slot="ymst")
                if bi == NPB - 1:
                    T.op("dve", lambda e: e.tensor_scalar(out=hs.t[:, :, :], in0=hs.t[:, :, :], scalar1=CST.t[:, C_PV:C_PV + 1], scalar2=None, op0=ALU.mult), reads=[hs, CST], writes=[hs])
                T.barrier()
            if not is_own:
                with ExitStack() as S2:
                    kv_proj(S2, key0, False, None, None, None)
                    T.barrier()
                continue
            with ExitStack() as SO:
                y_aT = sbg("y_aT", [P, 16, TB], BF16, SO)
                with ExitStack() as S2:
                    qT = sbg("qT", [P, 16, TB], BF16, S2)
                    qiT = sbg("qiT", [P, 8, TB], BF16, S2)
                    ws = sbg("ws", [P, NT, 16], F32, S2)
                    with ExitStack() as S2a:
                        kv_proj(S2a, key0, True, qT, qiT, ws)
                        T.barrier()
                    for hb in range(2):
                        with ExitStack() as S2b:
                            dsa_half(S2b, bi, key0, hb, qT, qiT, ws, y_aT)
                            T.barrier()
                dump("yaT%d" % bi, y_aT.t[:, :, :].rearrange("p a b -> p (a b)"), [P, 16 * TB], BF16, [y_aT])
                with ExitStack() as S3:
                    out_block2(S3, bi, (bi - NPB) * TB, y_aT)
                    T.barrier()
        T.barrier()
        T.emit()
    return nc, dbg_out


def make_cst(inputs, half):
    c = np.zeros((P, NCST), np.float32)
    c[:, C_ID:C_ID + 128] = np.eye(128, dtype=np.float32)
    u = np.arange(128)
    c[:, C_L:C_L + 128] = (u[:, None] <= u[None, :]).astype(np.float32)
    c[:, C_U:C_U + 128] = (u[:, None] > u[None, :]).astype(np.float32)
    cw = np.asarray(inputs["conv_w"][0], np.float32)
    c[:, C_CW:C_CW + 96] = cw.T.reshape(24, 128, 4).transpose(1, 0, 2).reshape(128, 96)
    c[:, C_CB:C_CB + 24] = np.asarray(inputs["conv_b"][0], np.float32).reshape(24, 128).T
    c[:, C_GB:C_GB + 32] = np.asarray(inputs["gate_bias"][0], np.float32).reshape(32, 128).T
    c[:, C_G:C_G + 16] = np.asarray(inputs["norm_in"][0], np.float32).reshape(16, 128).T
    c[:, C_GN:C_GN + 16] = np.asarray(inputs["gn_m"][0], np.float32).reshape(16, 128).T
    c[:, C_DTB:C_DTB + 32] = np.asarray(inputs["dt_bias"][0], np.float32)[None, :]
    c[:, C_ALOG:C_ALOG + 32] = np.asarray(inputs["a_log"][0], np.float32)[None, :]
    c[:, C_DSK:C_DSK + 32] = np.asarray(inputs["d_skip"][0], np.float32)[None, :]
    c[:, C_PV] = 1.0 if half == 1 else 0.0
    c[:, C_CVEC:C_CVEC + NITER] = (0.5 ** np.arange(1, NITER + 1, dtype=np.float64)).astype(np.float32)[None, :]
    q = np.arange(128)[:, None]
    k = np.arange(128)[None, :]
    diag = np.where((q < 64) & (k >= 64), NEG, 0.0).astype(np.float32)
    full = np.full((128, 128), NEG, np.float32)
    zero = np.zeros((128, 128), np.float32)
    c[:, C_BP:C_BP + 256] = np.concatenate([diag, full], 1)
    c[:, C_BP + 256:C_BP + 512] = np.concatenate([zero, diag], 1)
    c[0, C_PVROW:C_PVROW + 128] = 0.0 if half == 1 else NEG
    return c


_NC_CACHE = {}


def run(inputs, L, dbg=()):
    B = inputs["x"].shape[0]
    assert B * 2 == 8
    H = L // 2
    NPB = NOB = H // TB
    key = (NPB, NOB, tuple(dbg))
    if key not in _NC_CACHE:
        _NC_CACHE[key] = build(NPB, NOB, dbg)
    nc, dbg_out = _NC_CACHE[key]
    x = np.asarray(inputs["x"], np.float32)
    w_in = np.ascontiguousarray(np.asarray(inputs["w_in"][0], np.float32))
    w_br = np.ascontiguousarray(np.asarray(inputs["w_branch"][0], np.float32))
    w_out = np.ascontiguousarray(np.asarray(inputs["w_out"][0], np.float32))
    normf = np.asarray(inputs["norm_final"], np.float32).reshape(1, D)
    csts = [make_cst(inputs, 0), make_cst(inputs, 1)]
    zeros = np.zeros((H, D), np.float32)
    in_maps = []
    for c in range(8):
        b, half = c // 2, c % 2
        in_maps.append({
            "xp": np.ascontiguousarray(x[b, 0:H]) if half == 1 else zeros,
            "xo": np.ascontiguousarray(x[b, half * H:(half + 1) * H]),
            "w_in": w_in, "w_br": w_br, "w_out": w_out,
            "cst": csts[half], "normf": normf,
        })
    res = run_bass_kernel_spmd(nc, in_maps, core_ids=list(range(8)))
    outp = np.empty((B, L, D), np.float32)
    for c in range(8):
        b, half = c // 2, c % 2
        outp[b, half * H:(half + 1) * H] = res.results[c]["out"]
    return outp, res


def kernel(**inputs):
    L = inputs["x"].shape[1]
    outp, _ = run(inputs, L)
    return outp
```

```python
import numpy as np
import ml_dtypes
from contextlib import ExitStack
import concourse.bass as bass
import concourse.mybir as mybir
from concourse.bass_utils import run_bass_kernel_spmd

F32 = mybir.dt.float32
BF16 = mybir.dt.bfloat16
AF = mybir.ActivationFunctionType
ALU = mybir.AluOpType
AX = mybir.AxisListType

P = 128
D = 2048
TB = 512
NT = 4
EPS = 1e-6
OFF_Z, OFF_XBC, OFF_DT, OFF_Q, OFF_K, OFF_V, OFF_GA, OFF_QI, OFF_KI, OFF_WI, OFF_G = (
    0, 2048, 5120, 5152, 7200, 9248, 11296, 13344, 14368, 14432, 14448)
D_IN = 18544
ATTN_SCALE = 128 ** -0.5
IDX_SCALE = (64 ** -0.5) * (16 ** -0.5)
TOPK = 256
NITER = 21
CLAMP = 8.0
NEG = -30000.0

C_ID, C_L, C_U, C_CW, C_CB, C_GB, C_G, C_GN, C_DTB, C_ALOG, C_DSK, C_PV, C_CVEC, C_BP, C_PVROW, NCST = (
    0, 128, 256, 384, 480, 504, 536, 552, 568, 600, 632, 664, 672, 704, 1216, 1344)


class Buf:
    def __init__(self, t, name=""):
        self.t = t
        self.name = name
        self.w = None
        self.r = []
        self.alias = []
        self.pending = 0

    def __getitem__(self, k):
        return self.t[k]


class Trk:
    LIM = 30000

    def __init__(self, nc, es):
        self.nc = nc
        self.es = es
        self.engs = {"pe": nc.tensor, "act": nc.scalar, "dve": nc.vector, "pool": nc.gpsimd, "sp": nc.sync}
        self.ctr = {}
        self.prog = {k: [] for k in self.engs}
        self.known = {k: {} for k in self.engs}
        self.nsem = 0
        for k in self.engs:
            self._newctr(k, 1)

    def _newsem(self, name):
        self.nsem += 1
        return self.es.enter_context(self.nc.semaphore("%s_%d" % (name, self.nsem)))

    def _newctr(self, name, step):
        self.ctr[name] = {"step": step, "ep": 0, "cnt": 0, "sems": {0: self._newsem(name)}}

    def _next_ms(self, name):
        c = self.ctr[name]
        if (c["cnt"] + 1) * c["step"] > self.LIM:
            return (name, c["ep"] + 1, 1)
        return (name, c["ep"], c["cnt"] + 1)

    def _inc(self, name):
        c = self.ctr[name]
        ms = self._next_ms(name)
        if ms[1] != c["ep"]:
            c["ep"] = ms[1]
            c["sems"][ms[1]] = self._newsem(name)
        c["cnt"] = ms[2]
        return ms, c["sems"][c["ep"]], c["step"]

    def _cur_ms(self, name):
        c = self.ctr[name]
        return (name, c["ep"], c["cnt"])

    def _need(self, e, deps):
        need = {}
        for d in deps:
            if d is None:
                continue
            name, ep, idx = d
            if idx == 0:
                continue
            k = (name, ep)
            if idx > need.get(k, 0):
                need[k] = idx
        for (name, ep), idx in need.items():
            c = self.ctr[name]
            if name == e and (ep, idx) > (c["ep"], c["cnt"]):
                continue
            if self.known[e].get((name, ep), 0) >= idx:
                continue
            later = [kk for kk in self.known[e] if kk[0] == name and kk[1] > ep]
            if later:
                continue
            self.prog[e].append((lambda en, v_=idx * c["step"], nm=(name, ep): en.wait_ge(self._sem_of(nm), v_)))
            self.known[e][(name, ep)] = idx

    def _sem_of(self, nm):
        return self.ctr[nm[0]]["sems"][nm[1]]

    def _deps(self, reads, writes):
        deps = []
        for b in reads:
            deps.append(b.w)
            for a in b.alias:
                deps.append(a.w)
        for b in writes:
            deps.append(b.w)
            deps.extend(b.r)
            for a in b.alias:
                deps.append(a.w)
                deps.extend(a.r)
        return deps

    def _mark(self, me, reads, writes):
        for b in reads:
            b.r.append(me)
        for b in writes:
            b.w = me
            b.r = []

    def op(self, e, fn, reads=(), writes=(), inc=True):
        self._need(e, self._deps(reads, writes))
        if inc:
            me, sem, step = self._inc(e)
            self.prog[e].append((lambda en, f_=fn, s_=sem: f_(en).then_inc(s_, 1)))
        else:
            me = self._next_ms(e)
            self.prog[e].append(fn)
        self._mark(me, reads, writes)

    def dma(self, q, out, in_, reads=(), writes=(), slot=None, **kw):
        self._need(q, self._deps(reads, writes))
        name = "d:" + slot
        if name not in self.ctr:
            self._newctr(name, 16)
        me, sem, step = self._inc(name)
        self.prog[q].append((lambda en, o_=out, i_=in_, k_=kw, s_=sem: en.dma_start(out=o_, in_=i_, **k_).then_inc(s_, 16)))
        self._mark(me, reads, writes)

    def barrier(self):
        allms = [self._cur_ms(n) for n in self.ctr]
        for e in self.engs:
            self._need(e, [m for m in allms if m[0] != e])

    def emit(self):
        with self.nc.Block() as block:
            @block.tensor
            def _(en):
                for f in self.prog["pe"]:
                    f(en)

            @block.scalar
            def _(en):
                for f in self.prog["act"]:
                    f(en)

            @block.vector
            def _(en):
                for f in self.prog["dve"]:
                    f(en)

            @block.gpsimd
            def _(en):
                for f in self.prog["pool"]:
                    f(en)

            @block.sync
            def _(en):
                for f in self.prog["sp"]:
                    f(en)


class Rot:
    def __init__(self, items):
        self.items = items
        self.i = 0

    def next(self, pend=None):
        b = self.items[self.i % len(self.items)]
        while b.pending > 0 and pend:
            pend.pop(0)()
        assert b.pending == 0, "rotation reuses %s while deferred users pending" % b.name
        self.i += 1
        return b


def defer(pend, fn, bufs):
    for b in bufs:
        b.pending += 1

    def w():
        fn()
        for b in bufs:
            b.pending -= 1
    pend.append(w)


def build(NPB, NOB, dbg=()):
    NK = (NPB + NOB) * TB
    nc = bass.Bass("TRN2", target_bir_lowering=False)
    dt_in = lambda n, s, d=F32: nc.dram_tensor(n, s, d, kind="ExternalInput").ap()
    xp = dt_in("xp", [max(NPB, 1) * TB, D])
    xo = dt_in("xo", [NOB * TB, D])
    w_in = dt_in("w_in", [D, D_IN])
    w_br = dt_in("w_br", [2 * D, D])
    w_out = dt_in("w_out", [D, D])
    cst = dt_in("cst", [P, NCST])
    normf = dt_in("normf", [1, D])
    out = nc.dram_tensor("out", [NOB * TB, D], F32, kind="ExternalOutput").ap()
    dti = lambda n, s, d=BF16: nc.dram_tensor(n, s, d, kind="Internal").ap()
    Wb = dti("Wb", [D, D_IN])
    Wbr = dti("Wbr", [2 * D, D])
    Wo = dti("Wo", [D, D])
    KT = dti("KT", [16, P, NK])
    NCH = (NK // P + 15) // 16
    VS = dti("VS", [16, NCH, P, 16, 129])
    KI = dti("KI", [2, P, NK])
    YM = dti("YM", [P, 16 * TB])
    dbg_out = {}

    with ExitStack() as es:
        T = Trk(nc, es)

        uniq = [0]

        def sbg(name, shape, dt, stack=es):
            uniq[0] += 1
            return Buf(stack.enter_context(nc.sbuf_tensor("%s_u%d" % (name, uniq[0]), shape, dt)), name)

        CST = sbg("CST", [P, NCST], F32)
        ident_b = sbg("ident_b", [P, P], BF16)
        bpat_b = sbg("bpat_b", [P, 2, 256], BF16)
        pvm = sbg("pvm", [P, P], BF16)
        ones_b = sbg("ones_b", [P, 512], BF16)
        ones_f = sbg("ones_f", [P, P], F32)
        aneg = sbg("aneg", [P, 32], F32)
        eps_t = sbg("eps_t", [P, 2], F32)
        hT = sbg("hT", [P, 16, TB], BF16)
        WT = sbg("WT", [P, 2 * 16 * 512], BF16)
        XM = sbg("XM", [P, 8192], BF16)
        xs = sbg("xs", [P, D], BF16)
        hs = sbg("hs", [P, 4, 512], F32)
        halo = sbg("halo", [P, 24, 3], F32)
        st = sbg("st", [P, 16], F32)
        PS = [Buf(es.enter_context(nc.psum_tensor("ps%d" % i, [P, 512], F32)), "ps%d" % i) for i in range(8)]
        psA = Rot(PS)
        psG = Rot(PS[0:4])
        psO = Rot(PS[4:8])

        Wt = [Buf(WT.t[:, i * 8192:(i + 1) * 8192].rearrange("p (k c) -> p k c", c=512), "Wt%d" % i) for i in range(2)]
        score = Buf(WT.t[:].bitcast(F32), "score")
        score.alias = [Wt[0], Wt[1]]
        for w in Wt:
            w.alias = [score]
        xtb = [Buf(XM.t[:, i * 4096:(i + 1) * 4096].bitcast(F32), "xt%d" % i) for i in range(2)]
        mjD = Buf(XM.t[:, 0:4096], "mjD")
        mjA = Buf(XM.t[:, 4096:8192], "mjA")
        mjD.alias = [xtb[0]]
        mjA.alias = [xtb[1]]
        xtb[0].alias = [mjD]
        xtb[1].alias = [mjA]
        XM8 = XM.t[:, 4096:8192].bitcast(mybir.dt.int8)
        wt_rot = Rot(Wt)
        xt_rot = Rot(xtb)

        ident_f = CST.t[:, C_ID:C_ID + 128]
        Lmask = CST.t[:, C_L:C_L + 128]
        Ustr = CST.t[:, C_U:C_U + 128]

        def cs(c0, n):
            return CST.t[:, c0:c0 + n]

        def psb(b):
            return b.t[:].bitcast(BF16).rearrange("p (a b) -> p a b", b=128)

        def dump(name, ap, shape, dtype, rd):
            if name not in dbg:
                return
            o = nc.dram_tensor("dbg_" + name, list(shape), dtype, kind="ExternalOutput").ap()
            dbg_out[name] = o
            T.dma("sp", o, ap, reads=rd, slot="dbg")

        T.dma("sp", CST[:], cst, writes=[CST], slot="cst")
        T.op("dve", lambda e: e.tensor_copy(out=ident_b[:], in_=ident_f), reads=[CST], writes=[ident_b])
        T.op("dve", lambda e: e.tensor_copy(out=bpat_b[:], in_=cs(C_BP, 512).rearrange("p (a b) -> p a b", b=256)), reads=[CST], writes=[bpat_b])
        T.op("dve", lambda e: e.tensor_copy(out=pvm[:], in_=cs(C_PVROW, 128)), reads=[CST], writes=[pvm])
        T.op("dve", lambda e: e.memset(ones_b[:], 1.0), writes=[ones_b])
        T.op("pool", lambda e: e.memset(ones_f[:], 1.0), writes=[ones_f])
        T.op("pool", lambda e: e.memset(eps_t[:], EPS), writes=[eps_t])
        T.op("pool", lambda e: e.memset(hs[:], 0.0), writes=[hs])
        T.op("pool", lambda e: e.memset(halo[:], 0.0), writes=[halo])
        T.op("act", lambda e: e.activation(out=aneg[:], in_=cs(C_ALOG, 32), func=AF.Exp), reads=[CST], writes=[aneg])
        T.op("dve", lambda e: e.tensor_scalar(out=aneg[:], in0=aneg[:], scalar1=-1.0, scalar2=None, op0=ALU.mult), reads=[aneg], writes=[aneg])

        with ExitStack() as s0:
            fb = Rot([sbg("pcf%d" % i, [P, 2048], F32, s0) for i in range(3)])
            bb = Rot([sbg("pcb%d" % i, [P, 2048], BF16, s0) for i in range(3)])
            cnt = 0

            def cast_piece(src_ap, dst_ap, n, scal):
                nonlocal cnt
                f = fb.next()
                b = bb.next()
                T.dma("sp", f[:, 0:n], src_ap, writes=[f], slot=f.name)
                eng = "dve" if cnt % 2 == 0 else "pool"
                if scal is None:
                    T.op(eng, lambda e: e.tensor_copy(out=b[:, 0:n], in_=f[:, 0:n]), reads=[f], writes=[b])
                else:
                    T.op(eng, lambda e: e.tensor_scalar(out=b[:, 0:n], in0=f[:, 0:n], scalar1=scal, scalar2=1.0, op0=ALU.mult, op1=ALU.mult), reads=[f, CST], writes=[b])
                T.dma("act", dst_ap, b[:, 0:n], reads=[b], slot=b.name)
                cnt += 1

            for kc in range(16):
                for c0 in range(0, D_IN, 2048):
                    n = min(2048, D_IN - c0)
                    cast_piece(w_in[kc * P:(kc + 1) * P, c0:c0 + n], Wb[kc * P:(kc + 1) * P, c0:c0 + n], n, CST.t[:, C_G + kc:C_G + kc + 1])
            for kc in range(32):
                cast_piece(w_br[kc * P:(kc + 1) * P, :], Wbr[kc * P:(kc + 1) * P, :], 2048, None)
            for kc in range(16):
                cast_piece(w_out[kc * P:(kc + 1) * P, :], Wo[kc * P:(kc + 1) * P, :], 2048, None)
            zt = sbg("zt", [P, 2048], BF16, s0)
            T.op("pool", lambda e: e.memset(zt[:], 0.0), writes=[zt])
            for c0 in range(0, NK, 2048):
                n = min(2048, NK - c0)
                T.dma("act", KI[0, 64:128, c0:c0 + n], zt[64:128, 0:n], reads=[zt], slot="zfill")
                T.dma("act", KI[1, 0:64, c0:c0 + n], zt[0:64, 0:n], reads=[zt], slot="zfill")
            T.barrier()

        def load_w(src, c0, n, wt, dcol=0, rows0=0):
            T.dma("sp", wt.t[:, :, dcol:dcol + n], src[rows0:rows0 + D, c0:c0 + n].rearrange("(kc p) c -> p kc c", p=P), writes=[wt], slot=wt.name)

        def mm_group(ps_ap, pairs, reads, ps_buf, extra_first_reads=()):
            n = len(pairs)
            for k, (l, r) in enumerate(pairs):
                T.op("pe", lambda e, l=l, r=r, k=k: e.matmul(ps_ap, lhsT=l, rhs=r, start=(k == 0), stop=(k == n - 1)),
                     reads=reads, writes=[ps_buf], inc=(k == n - 1))

        def proj_F(wt, ncols, consumer, rhs_cols=slice(0, TB), nfree=TB):
            for ci in range(ncols // P):
                ps = psA.next()
                mm_group(ps.t[:, 0:nfree], [(wt.t[:, kc, ci * P:(ci + 1) * P], hT.t[:, kc, rhs_cols]) for kc in range(16)], [wt, hT], ps)
                consumer(ci, ps)

        def proj_T(wt, ncols, consumer, tiles=range(NT)):
            for i in tiles:
                ps = psA.next()
                mm_group(ps.t[:, 0:ncols], [(hT.t[:, kc, i * P:(i + 1) * P], wt.t[:, kc, 0:ncols]) for kc in range(16)], [wt, hT], ps)
                consumer(i, ps)

        def run_pipe(pend, nkeep):
            while len(pend) > nkeep:
                pend.pop(0)()

        evac_flip = [0]

        def evac(out_ap, in_ap, reads, writes, scale=None):
            evac_flip[0] ^= 1
            if evac_flip[0]:
                if scale is None:
                    T.op("act", lambda e: e.copy(out=out_ap, in_=in_ap), reads=reads, writes=writes)
                else:
                    T.op("act", lambda e: e.activation(out=out_ap, in_=in_ap, func=AF.Copy, scale=scale), reads=reads, writes=writes)
            else:
                if scale is None:
                    T.op("dve", lambda e: e.tensor_copy(out=out_ap, in_=in_ap), reads=reads, writes=writes)
                else:
                    T.op("dve", lambda e: e.tensor_scalar(out=out_ap, in0=in_ap, scalar1=scale, scalar2=None, op0=ALU.mult), reads=reads, writes=writes)

        def rstd_from_ssq(ssq_ap, out_ap, n, buf):
            T.op("act", lambda e: e.activation(out=out_ap, in_=ssq_ap, func=AF.Ln, scale=1.0 / n, bias=eps_t[:, 0:1]), reads=[buf, eps_t], writes=[buf])
            T.op("act", lambda e: e.activation(out=out_ap, in_=out_ap, func=AF.Exp, scale=-0.5), reads=[buf], writes=[buf])

        def build_hT(xsrc, row0):
            for i in range(NT):
                xt = xt_rot.next()
                T.dma("sp", xt[:, :], xsrc[row0 + i * P:row0 + (i + 1) * P, :], writes=[xt], slot=xt.name)
                T.op("act", lambda e, xt=xt: e.activation(out=xs[:], in_=xt[:, :], func=AF.Square, accum_out=st[:, 0:1]), reads=[xt], writes=[xs, st])
                rstd_from_ssq(st[:, 0:1], st[:, 1:2], D, st)
                T.op("dve", lambda e, xt=xt: e.tensor_scalar(out=xs[:], in0=xt[:, :], scalar1=st[:, 1:2], scalar2=None, op0=ALU.mult), reads=[xt, st], writes=[xs])
                for hb in range(2):
                    ps = psA.next()
                    for k in range(8):
                        kc = hb * 8 + k
                        T.op("pe", lambda e, ps=ps, k=k, kc=kc: e.transpose(out=psb(ps)[:, k, :], in_=xs[:, kc * P:(kc + 1) * P], identity=ident_b[:]),
                             reads=[xs, ident_b], writes=[ps], inc=(k == 7))
                    evac(hT.t[:, hb * 8:(hb + 1) * 8, i * P:(i + 1) * P], psb(ps), [ps], [hT])

        def ssd_block(S, is_own, bi):
            x_tok = sbg("x_tok", [P, NT, D], F32, S)
            BT = sbg("BT", [P, 4, TB], BF16, S)
            CTt = sbg("CTt", [P, 4, TB], BF16, S)
            Btok = sbg("Btok", [P, NT, 4, P], BF16, S)
            dt_all = sbg("dt_all", [P, NT, 32], F32, S)
            sm = sbg("sm", [P, NT, 160], F32, S)
            y_mT = sbg("y_mT", [P, 16, TB], BF16, S) if is_own else None
            SX = ExitStack()
            ut = Rot([sbg("ut%d" % i, [P, 515], F32, SX) for i in range(3)])
            acc_r = Rot([sbg("cacc%d" % i, [P, TB], F32, SX) for i in range(2)])
            xsT = Rot([sbg("xsT%d" % i, [P, TB], F32, SX) for i in range(4)])

            cpend = []
            for grp in range(6):
                wt = wt_rot.next()
                load_w(Wb, OFF_XBC + grp * 512, 512, wt)

                def cons(ci, ps, grp=grp):
                    cg = grp * 4 + ci
                    u = ut.next()
                    T.op("act", lambda e: e.copy(out=u[:, 3:515], in_=ps.t[:, 0:512]), reads=[ps], writes=[u])
                    T.op("pool", lambda e: e.tensor_copy(out=u[:, 0:3], in_=halo[:, cg, :]), reads=[halo], writes=[u])
                    T.op("pool", lambda e: e.tensor_copy(out=halo[:, cg, :], in_=u[:, 512:515]), reads=[u], writes=[halo])
                    acc = acc_r.next()
                    cw = lambda k: CST.t[:, C_CW + cg * 4 + k:C_CW + cg * 4 + k + 1]
                    T.op("dve", lambda e: e.tensor_scalar(out=acc[:, :], in0=u[:, 3:515], scalar1=cw(3), scalar2=None, op0=ALU.mult), reads=[u, CST], writes=[acc])
                    for k in (2, 1, 0):
                        T.op("dve", lambda e, k=k: e.scalar_tensor_tensor(out=acc[:, :], in0=u[:, k:k + 512], scalar=cw(k), in1=acc[:, :], op0=ALU.mult, op1=ALU.add), reads=[u, CST, acc], writes=[acc])
                    cb = CST.t[:, C_CB + cg:C_CB + cg + 1]
                    if cg < 16:
                        xT = xsT.next(cpend)
                        T.op("act", lambda e: e.activation(out=xT[:, :], in_=acc[:, :], func=AF.Silu, bias=cb), reads=[acc, CST], writes=[xT])

                        def later(xT=xT, cg=cg):
                            ps2 = psA.next()
                            for i in range(NT):
                                T.op("pe", lambda e, i=i: e.transpose(out=ps2.t[:, i * P:(i + 1) * P], in_=xT[:, i * P:(i + 1) * P], identity=ident_f),
                                     reads=[xT, CST], writes=[ps2], inc=(i == NT - 1))
                            evac(x_tok.t[:, :, cg * P:(cg + 1) * P], ps2.t[:, :].rearrange("p (a b) -> p a b", b=P), [ps2], [x_tok])
                        defer(cpend, later, [xT])
                    elif cg < 20:
                        g = cg - 16
                        T.op("act", lambda e: e.activation(out=BT.t[:, g, :], in_=acc[:, :], func=AF.Silu, bias=cb), reads=[acc, CST], writes=[BT])

                        def later(g=g):
                            ps2 = psA.next()
                            for i in range(NT):
                                T.op("pe", lambda e, i=i: e.transpose(out=psb(ps2)[:, i, :], in_=BT.t[:, g, i * P:(i + 1) * P], identity=ident_b[:]),
                                     reads=[BT, ident_b], writes=[ps2], inc=(i == NT - 1))
                            evac(Btok.t[:, :, g, :], psb(ps2)[:, 0:NT, :], [ps2], [Btok])
                        defer(cpend, later, [])
                    else:
                        g = cg - 20
                        T.op("act", lambda e: e.activation(out=CTt.t[:, g, :], in_=acc[:, :], func=AF.Silu, bias=cb), reads=[acc, CST], writes=[CTt])
                    run_pipe(cpend, 2)

                proj_F(wt, 512, cons)
            run_pipe(cpend, 0)
            T.barrier()
            SX.close()

            wt = wt_rot.next()
            load_w(Wb, OFF_DT, 32, wt)

            def cons_dt(i, ps):
                T.op("dve", lambda e: e.tensor_tensor(out=dt_all.t[:, i, :], in0=ps.t[:, 0:32], in1=cs(C_DTB, 32), op=ALU.add), reads=[ps, CST], writes=[dt_all])
                T.op("act", lambda e: e.activation(out=dt_all.t[:, i, :], in_=dt_all.t[:, i, :], func=AF.Exp), reads=[dt_all], writes=[dt_all])
                T.op("act", lambda e: e.activation(out=dt_all.t[:, i, :], in_=dt_all.t[:, i, :], func=AF.Ln, bias=1.0), reads=[dt_all], writes=[dt_all])

            proj_T(wt, 32, cons_dt)

            for i in range(NT):
                a_i = sm.t[:, i, 0:32]
                T.op("dve", lambda e, i=i, a_i=a_i: e.tensor_tensor(out=a_i, in0=dt_all.t[:, i, :], in1=aneg[:], op=ALU.mult), reads=[dt_all, aneg], writes=[sm])
                ps = psA.next()
                T.op("pe", lambda e, a_i=a_i, ps=ps: e.matmul(ps.t[:, 0:32], lhsT=Lmask, rhs=a_i, start=True, stop=True), reads=[CST, sm], writes=[ps])
                T.op("pe", lambda e, a_i=a_i, ps=ps: e.matmul(ps.t[:, 32:64], lhsT=ones_f[:], rhs=a_i, start=True, stop=True), reads=[ones_f, sm], writes=[ps])
                T.op("act", lambda e, i=i, ps=ps: e.activation(out=sm.t[:, i, 32:96], in_=ps.t[:, 0:64], func=AF.Exp), reads=[ps], writes=[sm])
                T.op("dve", lambda e, i=i, ps=ps: e.tensor_copy(out=sm.t[:, i, 128:160], in_=ps.t[:, 0:32]), reads=[ps], writes=[sm])
                T.op("dve", lambda e, i=i, ps=ps: e.tensor_tensor(out=sm.t[:, i, 96:128], in0=ps.t[:, 32:64], in1=sm.t[:, i, 128:160], op=ALU.subtract), reads=[ps, sm], writes=[sm])
                T.op("act", lambda e, i=i: e.activation(out=sm.t[:, i, 96:128], in_=sm.t[:, i, 96:128], func=AF.Exp), reads=[sm], writes=[sm])
                T.op("dve", lambda e, i=i: e.tensor_tensor(out=sm.t[:, i, 96:128], in0=sm.t[:, i, 96:128], in1=dt_all.t[:, i, :], op=ALU.mult), reads=[sm, dt_all], writes=[sm])

            xw = Rot([sbg("xw%d" % i, [P, TB], BF16, S) for i in range(2)])
            tmpS = Rot([sbg("tmpS%d" % i, [P, TB], F32, S) for i in range(2)])
            if is_own:
                xdt = Rot([sbg("xdt%d" % i, [P, TB], BF16, S) for i in range(3)])
                hsb = Rot([sbg("hsb%d" % i, [P, TB], BF16, S) for i in range(3)])
                cbm = Rot([sbg("cbm%d" % i, [P, P], F32, S) for i in range(3)])
                Rr = Rot([sbg("Rr%d" % i, [P, 8, P], F32, S) for i in range(2)])
                dec = Rot([sbg("dec%d" % i, [P, 8, P], F32, S) for i in range(2)])
                Mm = Rot([sbg("Mm%d" % i, [P, 8, P], BF16, S) for i in range(3)])
                t1 = Rot([sbg("t1_%d" % i, [P, TB], F32, S) for i in range(3)])
                t2 = Rot([sbg("t2_%d" % i, [P, TB], F32, S) for i in range(2)])
                zs = Rot([sbg("zs%d" % i, [P, TB], F32, S) for i in range(3)])
                yn = Rot([sbg("yn%d" % i, [P, TB], BF16, S) for i in range(3)])
                st2r = Rot([sbg("st2_%d" % i, [P, 8], F32, S) for i in range(2)])
            v3 = lambda b: b.t[:, :].rearrange("p (h q) -> p h q", q=64)
            stageC = []
            stageD = []

            def make_R(g, i):
                Rb = Rr.next()
                a3 = sm.t[:, i, g * 8:(g + 1) * 8].unsqueeze(2).to_broadcast([P, 8, P])
                L3 = Lmask.unsqueeze(1).to_broadcast([P, 8, P])
                T.op("dve", lambda e: e.tensor_tensor(out=Rb[:, :, :], in0=a3, in1=L3, op=ALU.mult), reads=[sm, CST], writes=[Rb])
                return Rb
            Rnext = make_R(0, 0) if is_own else None
            for g in range(4):
                if is_own:
                    wz = wt_rot.next()
                    load_w(Wb, OFF_Z + g * 512, 512, wz)
                for i in range(NT):
                    tsl = slice(i * P, (i + 1) * P)
                    xg3 = x_tok.t[:, i, g * 512:(g + 1) * 512].rearrange("p (h q) -> p h q", q=64)
                    bc = lambda c0, i=i, g=g: sm.t[:, i, c0 + g * 8:c0 + g * 8 + 8].unsqueeze(2).to_broadcast([P, 8, 64])
                    xwb = xw.next()
                    T.op("dve" if not is_own else "pool", lambda e, xwb=xwb, xg3=xg3, bc=bc: e.tensor_tensor(out=v3(xwb), in0=xg3, in1=bc(96), op=ALU.mult), reads=[x_tok, sm], writes=[xwb])
                    if is_own:
                        xdb = xdt.next(stageC)
                        dtb3 = dt_all.t[:, i, g * 8:(g + 1) * 8].unsqueeze(2).to_broadcast([P, 8, 64])
                        T.op("pool", lambda e, xdb=xdb, xg3=xg3, dtb3=dtb3: e.tensor_tensor(out=v3(xdb), in0=xg3, in1=dtb3, op=ALU.mult), reads=[x_tok, dt_all], writes=[xdb])
                        hb_ = hsb.next(stageC)
                        T.op("act", lambda e, hb_=hb_, g=g: e.copy(out=hb_[:, :], in_=hs.t[:, g, :]), reads=[hs], writes=[hb_])
                        Rb = Rnext
                        psc = psA.next()
                        T.op("pe", lambda e, psc=psc, g=g, tsl=tsl: e.matmul(psc.t[:, 0:P], lhsT=BT.t[:, g, tsl], rhs=CTt.t[:, g, tsl], start=True, stop=True), reads=[BT, CTt], writes=[psc])
                        psd = [psA.next(), psA.next()]
                        for hh in range(2):
                            T.op("pe", lambda e, pd=psd[hh], Rb=Rb, hh=hh: e.matmul(pd.t[:, :], lhsT=Ustr, rhs=Rb.t[:, hh * 4:(hh + 1) * 4, :].rearrange("p a b -> p (a b)"), start=True, stop=True), reads=[CST, Rb], writes=[psd[hh]])
                    pss = psA.next()
                    T.op("pe", lambda e, pss=pss, xwb=xwb, i=i, g=g: e.matmul(pss.t[:, :], lhsT=Btok.t[:, i, g, :], rhs=xwb[:, :], start=True, stop=True), reads=[Btok, xwb], writes=[pss])
                    if is_own:
                        psz = psA.next()
                        mm_group(psz.t[:, :], [(hT.t[:, kc, tsl], wz.t[:, kc, :]) for kc in range(16)], [hT, wz], psz)
                        cbb = cbm.next()
                        T.op("dve", lambda e, psc=psc, cbb=cbb: e.tensor_tensor(out=cbb[:, :], in0=psc.t[:, 0:P], in1=Lmask, op=ALU.mult), reads=[psc, CST], writes=[cbb])
                        db = dec.next()
                        for hh in range(2):
                            T.op("act", lambda e, pd=psd[hh], db=db, hh=hh: e.activation(out=db.t[:, hh * 4:(hh + 1) * 4, :].rearrange("p a b -> p (a b)"), in_=pd.t[:, :], func=AF.Exp), reads=[psd[hh]], writes=[db])
                    tS = tmpS.next()
                    hs3 = hs.t[:, g, :].rearrange("p (h q) -> p h q", q=64)
                    T.op("pool" if is_own else "dve", lambda e, tS=tS, hs3=hs3, bc=bc: e.tensor_tensor(out=v3(tS), in0=hs3, in1=bc(64), op=ALU.mult), reads=[hs, sm], writes=[tS])
                    T.op("dve", lambda e, tS=tS, pss=pss, g=g: e.tensor_tensor(out=hs.t[:, g, :], in0=pss.t[:, :], in1=tS[:, :], op=ALU.add), reads=[pss, tS], writes=[hs])
                    if not is_own:
                        continue
                    zb = zs.next(stageC)
                    T.op("act", lambda e, zb=zb, psz=psz: e.activation(out=zb[:, :], in_=psz.t[:, :], func=AF.Silu), reads=[psz], writes=[zb])
                    Mb = Mm.next(stageC)
                    c3 = cbb.t[:, :].unsqueeze(1).to_broadcast([P, 8, P])
                    T.op("dve", lambda e, Mb=Mb, db=db, c3=c3: e.tensor_tensor(out=Mb[:, :, :], in0=db[:, :, :], in1=c3, op=ALU.mult), reads=[db, cbb], writes=[Mb])
                    nxt = (g, i + 1) if i + 1 < NT else (g + 1, 0)
                    if nxt[0] < 4:
                        Rnext = make_R(*nxt)

                    def stC(Mb=Mb, xdb=xdb, hb_=hb_, g=g, i=i, tsl=tsl, xg3=xg3, bc=bc, zb=zb):
                        psy = psA.next()
                        for h in range(8):
                            T.op("pe", lambda e, h=h: e.matmul(psy.t[:, h * 64:(h + 1) * 64], lhsT=Mb.t[:, h, :], rhs=xdb.t[:, h * 64:(h + 1) * 64], start=True, stop=True),
                                 reads=[Mb, xdb], writes=[psy], inc=(h == 7))
                        pso = psA.next()
                        T.op("pe", lambda e: e.matmul(pso.t[:, :], lhsT=CTt.t[:, g, tsl], rhs=hb_[:, :], start=True, stop=True), reads=[CTt, hb_], writes=[pso])
                        t1b = t1.next()
                        t2b = t2.next()
                        st2 = st2r.next()
                        dsk3 = CST.t[:, C_DSK + g * 8:C_DSK + g * 8 + 8].unsqueeze(2).to_broadcast([P, 8, 64])
                        T.op("dve", lambda e: e.tensor_tensor(out=v3(t1b), in0=pso.t[:, :].rearrange("p (h q) -> p h q", q=64), in1=bc(32), op=ALU.mult), reads=[pso, sm], writes=[t1b])
                        T.op("pool", lambda e: e.tensor_tensor(out=v3(t2b), in0=xg3, in1=dsk3, op=ALU.mult), reads=[x_tok, CST], writes=[t2b])
                        T.op("dve", lambda e: e.tensor_tensor(out=t1b[:, :], in0=psy.t[:, :], in1=t1b[:, :], op=ALU.add), reads=[psy, t1b], writes=[t1b])
                        T.op("dve", lambda e: e.tensor_tensor(out=t1b[:, :], in0=t1b[:, :], in1=t2b[:, :], op=ALU.add), reads=[t1b, t2b], writes=[t1b])
                        T.op("dve", lambda e: e.tensor_tensor(out=t1b[:, :], in0=t1b[:, :], in1=zb[:, :], op=ALU.mult), reads=[t1b, zb], writes=[t1b])
                        T.op("act", lambda e: e.activation(out=t2b[:, :], in_=t1b[:, :], func=AF.Square, accum_out=st2[:, 0:1]), reads=[t1b], writes=[t2b, st2])
                        rstd_from_ssq(st2[:, 0:1], st2[:, 1:2], 512, st2)
                        ynb = yn.next(stageD)
                        T.op("dve", lambda e: e.tensor_scalar(out=ynb[:, :], in0=t1b[:, :], scalar1=st2[:, 1:2], scalar2=None, op0=ALU.mult), reads=[t1b, st2], writes=[ynb])

                        def stD(ynb=ynb):
                            pst = psA.next()
                            for j in range(4):
                                T.op("pe", lambda e, j=j: e.transpose(out=psb(pst)[:, j, :], in_=ynb.t[:, j * P:(j + 1) * P], identity=ident_b[:]), reads=[ynb, ident_b], writes=[pst], inc=(j == 3))
                            for j in range(4):
                                c = g * 4 + j
                                T.op("act", lambda e, j=j, c=c: e.activation(out=y_mT.t[:, c, tsl], in_=psb(pst)[:, j, :], func=AF.Copy, scale=CST.t[:, C_GN + c:C_GN + c + 1]), reads=[pst, CST], writes=[y_mT])
                        defer(stageD, stD, [ynb])
                    defer(stageC, stC, [Mb, xdb, hb_, zb])
                    run_pipe(stageD, 1)
                    run_pipe(stageC, 1)
            run_pipe(stageC, 0)
            run_pipe(stageD, 0)
            if is_own:
                dump("x_tok%d" % bi, x_tok.t[:, :, :].rearrange("p a b -> p (a b)"), [P, NT * D], F32, [x_tok])
                dump("dt%d" % bi, dt_all.t[:, :, :].rearrange("p a b -> p (a b)"), [P, NT * 32], F32, [dt_all])
                dump("ymT%d" % bi, y_mT.t[:, :, :].rearrange("p a b -> p (a b)"), [P, 16 * TB], BF16, [y_mT])
            return y_mT

        def kv_proj(S, key0, is_own, qT, qiT, ws):
            ktmp = Rot([sbg("ktmp%d" % i, [P, TB], BF16, S) for i in range(2)])
            vtmp = Rot([sbg("vtmp%d" % i, [P, 4, 129], BF16, S) for i in range(2)])
            for v_ in vtmp.items:
                T.op("pool", lambda e, v_=v_: e.memset(v_[:, :, :], 1.0), writes=[v_])
            for grp in range(4):
                wt = wt_rot.next()
                load_w(Wb, OFF_K + grp * 512, 512, wt)

                def cons(ci, ps, grp=grp):
                    hh = grp * 4 + ci
                    kb_ = ktmp.next()
                    evac(kb_[:, :], ps.t[:, :], [ps], [kb_])
                    T.dma("pool", KT[hh, :, key0:key0 + TB], kb_[:, :], reads=[kb_], slot=kb_.name)
                proj_F(wt, 512, cons)
            for grp in range(4):
                wt = wt_rot.next()
                load_w(Wb, OFF_V + grp * 512, 512, wt)

                def cons(i, ps, grp=grp):
                    vb_ = vtmp.next()
                    evac(vb_.t[:, :, 0:P], ps.t[:, :].rearrange("p (h d) -> p h d", d=P), [ps], [vb_])
                    kbg = key0 // P + i
                    T.dma("pool", VS[grp * 4:(grp + 1) * 4, kbg // 16, :, kbg % 16, :].rearrange("h p d -> p h d"), vb_[:, :, :], reads=[vb_], slot=vb_.name)
                proj_T(wt, 512, cons)
            wt = wt_rot.next()
            load_w(Wb, OFF_KI, 64, wt, dcol=0)
            load_w(Wb, OFF_KI, 64, wt, dcol=64)

            def cons(ci, ps):
                kb_ = ktmp.next()
                evac(kb_[:, :], ps.t[:, :], [ps], [kb_])
                T.dma("pool", KI[0, 0:64, key0:key0 + TB], kb_[0:64, :], reads=[kb_], slot=kb_.name)
                T.dma("pool", KI[1, 64:128, key0:key0 + TB], kb_[64:128, :], reads=[kb_], slot=kb_.name)
            proj_F(wt, 128, cons)
            if not is_own:
                return
            for grp in range(4):
                wt = wt_rot.next()
                load_w(Wb, OFF_Q + grp * 512, 512, wt)

                def cons(ci, ps, grp=grp):
                    evac(qT.t[:, grp * 4 + ci, :], ps.t[:, :], [ps], [qT], scale=ATTN_SCALE)
                proj_F(wt, 512, cons)
            for grp in range(2):
                wt = wt_rot.next()
                load_w(Wb, OFF_QI + grp * 512, 512, wt)

                def cons(ci, ps, grp=grp):
                    evac(qiT.t[:, grp * 4 + ci, :], ps.t[:, :], [ps], [qiT])
                proj_F(wt, 512, cons)
            wt = wt_rot.next()
            load_w(Wb, OFF_WI, 16, wt)

            def cons(i, ps):
                T.op("dve", lambda e: e.tensor_scalar(out=ws.t[:, i, :], in0=ps.t[:, 0:16], scalar1=IDX_SCALE, scalar2=None, op0=ALU.mult), reads=[ps], writes=[ws])
            proj_T(wt, 16, cons)

        def dsa_half(S, bi, key0, hb, qT, qiT, ws, y_aT):
            nkb = (key0 + 256 * (hb + 1)) // P
            W = nkb * P
            nb5 = (W + 511) // 512
            LA = 4
            LA2 = 6
            maskT = sbg("maskT", [P, 64, 256], BF16, S)
            SI = ExitStack()
            Dgr = Rot([sbg("Dg%d" % i, [P, 16, P], BF16, SI) for i in range(2)])
            Rl = Rot([sbg("Rl%d" % i, [P, 512], BF16, SI) for i in range(6)])
            kib = Rot([sbg("kib%d" % i, [P, 2, 512], BF16, SI) for i in range(3)])

            def indexer(j2):
                i = 2 * hb + j2
                tsl = slice(i * P, (i + 1) * P)
                Dg = Dgr.next()
                pend = []
                for h in range(16):
                    T.op("pool", lambda e, h=h, i=i, Dg=Dg: e.tensor_scalar(out=Dg.t[:, h, :], in0=ident_f, scalar1=ws.t[:, i, h:h + 1], scalar2=1.0, op0=ALU.mult, op1=ALU.mult), reads=[CST, ws], writes=[Dg])
                for b5 in range(nb5):
                    c0 = b5 * 512
                    wd = min(512, W - c0)
                    kb_ = kib.next()
                    T.dma("sp", kb_[:, 0, 0:wd], KI[0, :, c0:c0 + wd], writes=[kb_], slot=kb_.name)
                    T.dma("sp", kb_[:, 1, 0:wd], KI[1, :, c0:c0 + wd], writes=[kb_], slot=kb_.name)
                    acc = psO.next(pend)
                    extra = []
                    if c0 < NPB * TB:
                        extra.append("pv")
                    if b5 == nb5 - 1:
                        extra.append("bp")
                    nacc = 16 + len(extra)
                    for h in range(16):
                        j, sub = h // 2, h % 2
                        ps = psG.next()
                        T.op("pe", lambda e, ps=ps, j=j, sub=sub, kb_=kb_, wd=wd, tsl=tsl: e.matmul(ps.t[:, 0:wd], lhsT=qiT.t[:, j, tsl], rhs=kb_.t[:, sub, 0:wd], start=True, stop=True), reads=[qiT, kb_], writes=[ps])
                        rb = Rl.next(pend)
                        if h % 2 == 0:
                            T.op("act", lambda e, rb=rb, ps=ps, wd=wd: e.activation(out=rb[:, 0:wd], in_=ps.t[:, 0:wd], func=AF.Relu), reads=[ps], writes=[rb])
                        else:
                            T.op("dve", lambda e, rb=rb, ps=ps, wd=wd: e.tensor_scalar(out=rb[:, 0:wd], in0=ps.t[:, 0:wd], scalar1=0.0, scalar2=None, op0=ALU.max), reads=[ps], writes=[rb])

                        def accmm(acc=acc, h=h, rb=rb, wd=wd, nacc=nacc, Dg=Dg):
                            T.op("pe", lambda e: e.matmul(acc.t[:, 0:wd], lhsT=Dg.t[:, h, :], rhs=rb.t[:, 0:wd], start=(h == 0), stop=(h == nacc - 1)), reads=[Dg, rb], writes=[acc])
                        defer(pend, accmm, [rb, acc])
                        run_pipe(pend, LA)

                    def fin(acc=acc, wd=wd, nacc=nacc, extra=tuple(extra), c0=c0):
                        k = 16
                        for ex in extra:
                            if ex == "pv":
                                T.op("pe", lambda e, k=k: e.matmul(acc.t[:, 0:wd], lhsT=pvm[:], rhs=ones_b.t[:, 0:wd], start=False, stop=(k == nacc - 1)), reads=[pvm, ones_b], writes=[acc])
                            else:
                                T.op("pe", lambda e, k=k: e.matmul(acc.t[:, wd - 256:wd], lhsT=ident_b[:], rhs=bpat_b.t[:, j2, :], start=False, stop=(k == nacc - 1)), reads=[ident_b, bpat_b], writes=[acc])
                            k += 1
                        T.op("act", lambda e: e.copy(out=score.t[:, c0:c0 + wd], in_=acc.t[:, 0:wd]), reads=[acc], writes=[score])
                    defer(pend, fin, [acc])
                run_pipe(pend, 0)

            def bisect(j2):
                i = 2 * hb + j2
                hi_t, mid_t, cnt_t, t_t, g_t, lo_t, sa_t = [sbg("bq%d_%d" % (j2, c), [P, 2], F32, SI) for c in range(7)]
                WD = min(4096, max(P, ((W * 45 // 100) // P) * P))
                nA = W - WD
                assert nA >= P and nA <= 8192
                hi, mid, cntc, tt, gg, lo, sa = [b_.t[:, 0:1] for b_ in (hi_t, mid_t, cnt_t, t_t, g_t, lo_t, sa_t)]
                T.op("dve", lambda e: e.reduce_max(out=hi, in_=score.t[:, 0:W], axis=AX.X), reads=[score], writes=[hi_t])
                T.op("dve", lambda e: e.tensor_scalar(out=mid, in0=hi, scalar1=-CLAMP * 0.5, scalar2=None, op0=ALU.add), reads=[hi_t], writes=[mid_t])
                for it in range(NITER):
                    hk = CLAMP * (0.5 ** (it + 1))
                    T.op("dve", lambda e: e.tensor_scalar(out=XM.t[:, 0:WD], in0=score.t[:, 0:WD], scalar1=mid, scalar2=None, op0=ALU.is_ge, op1=ALU.add, accum_out=cntc), reads=[score, mid_t], writes=[mjD, cnt_t])
                    T.op("act", lambda e: e.activation(out=XM8[:, 0:nA], in_=score.t[:, WD:W], func=AF.Sign, bias=mid, scale=-1.0, accum_out=sa), reads=[score, mid_t], writes=[mjA, sa_t])
                    T.op("dve", lambda e: e.scalar_tensor_tensor(out=tt, in0=cntc, scalar=2.0, in1=sa, op0=ALU.mult, op1=ALU.subtract), reads=[cnt_t, sa_t], writes=[t_t])
                    T.op("dve", lambda e: e.tensor_scalar(out=gg, in0=tt, scalar1=float(2 * TOPK - 1 - nA), scalar2=-0.5, op0=ALU.is_ge, op1=ALU.add), reads=[t_t], writes=[g_t])
                    T.op("dve", lambda e, hk=hk: e.scalar_tensor_tensor(out=mid, in0=gg, scalar=hk, in1=mid, op0=ALU.mult, op1=ALU.add), reads=[g_t, mid_t], writes=[mid_t])
                hN = CLAMP * (0.5 ** (NITER + 1))
                T.op("dve", lambda e: e.tensor_scalar(out=lo, in0=mid, scalar1=-hN, scalar2=None, op0=ALU.add), reads=[mid_t], writes=[lo_t])
                T.op("dve", lambda e: e.tensor_scalar(out=XM.t[:, 0:W], in0=score.t[:, 0:W], scalar1=lo, scalar2=None, op0=ALU.is_ge), reads=[score, lo_t], writes=[mjD, mjA])
                dump("thr%d_%d" % (bi, i), lo, [P, 1], F32, [lo_t])
                dump("score%d_%d" % (bi, i), score.t[:, 0:W], [P, W], F32, [score])

            def mask_transposes(j2):
                for kb0 in range(0, nkb, 8):
                    n = min(8, nkb - kb0)
                    ps = psG.next()
                    for t in range(n):
                        T.op("pe", lambda e, ps=ps, t=t, kb0=kb0: e.transpose(out=psb(ps)[:, t, :], in_=XM.t[:, (kb0 + t) * P:(kb0 + t + 1) * P], identity=ident_b[:]), reads=[mjD, mjA, ident_b], writes=[ps], inc=(t == n - 1))
                    T.op("act", lambda e, ps=ps, n=n, kb0=kb0, j2=j2: e.copy(out=maskT.t[:, kb0:kb0 + n, j2 * P:(j2 + 1) * P], in_=psb(ps)[:, 0:n, :]), reads=[ps], writes=[maskT])

            indexer(0)
            bisect(0)
            indexer(1)
            mask_transposes(0)
            bisect(1)
            mask_transposes(1)
            T.barrier()
            SI.close()

            wga = Rot([sbg("wga%d" % i, [P, 16, P], BF16, S) for i in range(1)])
            gaT = Rot([sbg("gaT%d" % i, [P, 256], F32, S) for i in range(2)])
            gex = Rot([sbg("gex%d" % i, [P, 256], F32, S) for i in range(2)])
            ktb = Rot([sbg("ktb%d" % i, [P, 2048], BF16, S) for i in range(2)])
            vtb = Rot([sbg("vtb%d" % i, [P, 16, 129], BF16, S) for i in range(2)])
            eb = Rot([sbg("eb%d" % i, [P, 512], BF16, S) for i in range(4)])
            pmb = Rot([sbg("pmb%d" % i, [P, 512], BF16, S) for i in range(8)])
            onb = Rot([sbg("onb%d" % i, [P, P], BF16, S) for i in range(2)])
            rdr = Rot([sbg("rd%d" % i, [P, 4], F32, S) for i in range(2)])
            psQ = Rot(PS[0:3])
            psX = Rot(PS[3:4])
            qsl = slice(hb * 256, (hb + 1) * 256)
            pend = []
            for hh in range(16):
                ga = gaT.next(pend)
                O = [psO.next(pend), psO.next(pend)]
                wg = wga.next()
                T.dma("sp", wg[:, :, :], Wb[:, OFF_GA + hh * P:OFF_GA + (hh + 1) * P].rearrange("(kc p) c -> p kc c", p=P), writes=[wg], slot=wg.name)
                psg = psX.next()
                mm_group(psg.t[:, 0:256], [(wg.t[:, kc, :], hT.t[:, kc, qsl]) for kc in range(16)], [wg, hT], psg)
                gx = gex.next()
                T.op("act", lambda e, gx=gx, psg=psg: e.activation(out=gx[:, :], in_=psg.t[:, 0:256], func=AF.Exp, scale=-1.0), reads=[psg], writes=[gx])
                T.op("dve", lambda e, gx=gx: e.tensor_scalar(out=gx[:, :], in0=gx[:, :], scalar1=1.0, scalar2=None, op0=ALU.add), reads=[gx], writes=[gx])
                T.op("dve", lambda e, gx=gx: e.reciprocal(out=gx[:, :], in_=gx[:, :]), reads=[gx], writes=[gx])
                T.op("dve", lambda e, ga=ga, gx=gx, psg=psg: e.tensor_tensor(out=ga[:, :], in0=psg.t[:, 0:256], in1=gx[:, :], op=ALU.mult), reads=[psg, gx], writes=[ga])
                for c2 in range((nkb + 15) // 16):
                    kb_lo = c2 * 16
                    n = min(16, nkb - kb_lo)
                    kt = ktb.next(pend)
                    vt = vtb.next(pend)
                    T.dma("sp", kt[:, 0:n * P], KT[hh, :, kb_lo * P:(kb_lo + n) * P], writes=[kt], slot=kt.name)
                    T.dma("sp", vt[:, 0:n, :], VS[hh, c2, :, 0:n, :], writes=[vt], slot=vt.name)
                    for t in range(0, n, 2):
                        kb = kb_lo + t
                        ps = psQ.next()
                        for u in range(2):
                            T.op("pe", lambda e, ps=ps, kt=kt, t=t, u=u, hh=hh: e.matmul(ps.t[:, u * 256:(u + 1) * 256], lhsT=kt.t[:, (t + u) * P:(t + u + 1) * P], rhs=qT.t[:, hh, qsl], start=True, stop=True), reads=[kt, qT], writes=[ps], inc=(u == 1))
                        eb_ = eb.next()
                        T.op("act", lambda e, eb_=eb_, ps=ps: e.activation(out=eb_[:, :], in_=ps.t[:, :], func=AF.Exp), reads=[ps], writes=[eb_])
                        pm = pmb.next(pend)
                        T.op("dve", lambda e, pm=pm, eb_=eb_, kb=kb: e.tensor_tensor(out=pm[:, :], in0=eb_[:, :], in1=maskT.t[:, kb:kb + 2, :].rearrange("p a b -> p (a b)"), op=ALU.mult), reads=[eb_, maskT], writes=[pm])

                        def pv(pm=pm, vt=vt, t=t, kb=kb, O=O):
                            for u in range(2):
                                for j2 in range(2):
                                    T.op("pe", lambda e, u=u, j2=j2: e.matmul(O[j2].t[:, 0:129], lhsT=pm.t[:, u * 256 + j2 * P:u * 256 + (j2 + 1) * P], rhs=vt.t[:, t + u, :], start=(kb + u == 0), stop=(kb + u == nkb - 1)),
                                         reads=[pm, vt], writes=[O[j2]], inc=(u == 1 and j2 == 1))
                        defer(pend, pv, [pm, vt, O[0], O[1]])
                        run_pipe(pend, LA2)

                def finalize(O=O, ga=ga, hh=hh):
                    pst = psX.next()
                    rd = rdr.next()
                    for j2 in range(2):
                        T.op("dve", lambda e, j2=j2: e.reciprocal(out=rd.t[:, j2:j2 + 1], in_=O[j2].t[:, 128:129]), reads=[O[j2]], writes=[rd])
                        ob = onb.next()
                        T.op("dve", lambda e, j2=j2, ob=ob: e.tensor_scalar(out=ob[:, :], in0=O[j2].t[:, 0:P], scalar1=rd.t[:, j2:j2 + 1], scalar2=None, op0=ALU.mult), reads=[O[j2], rd], writes=[ob])
                        T.op("pe", lambda e, j2=j2, ob=ob: e.transpose(out=psb(pst)[:, j2, :], in_=ob[:, :], identity=ident_b[:]), reads=[ob, ident_b], writes=[pst])
                    T.op("dve", lambda e: e.tensor_tensor(out=y_aT.t[:, hh, qsl], in0=psb(pst)[:, 0:2, :].rearrange("p a b -> p (a b)"), in1=ga[:, :], op=ALU.mult), reads=[pst, ga], writes=[y_aT])
                defer(pend, finalize, [ga, O[0], O[1]])
            run_pipe(pend, 0)

        def out_block2(S, bi, orow0, y_aT):
            y_mT = sbg("y_mT2", [P, 16, TB], BF16, S)
            T.dma("sp", y_mT.t[:, :, :].rearrange("p a b -> p (a b)"), YM, writes=[y_mT], slot="ymld")
            mT = sbg("mT", [P, 16, TB], BF16, S)
            r = sbg("r", [P, NT, D], F32, S)
            nfb = sbg("nfb", [P, D], F32, S)
            stashb = sbg("stash", [P, 4, TB], F32, S)
            gm = Rot([sbg("gm%d" % i, [P, TB], F32, S) for i in range(2)])
            m2 = Rot([sbg("m2_%d" % i, [P, TB], F32, S) for i in range(2)])
            st3 = sbg("st3", [P, 8], F32, S)
            T.dma("sp", nfb[:, :], normf.partition_broadcast(P), writes=[nfb], slot="nfb")
            for i in range(NT):
                T.dma("sp", r.t[:, i, :], xo[orow0 + i * P:orow0 + (i + 1) * P, :], writes=[r], slot="rld")
            for cg in range(4):
                plan = ((Wbr, cg * 512, 0), (Wb, OFF_G + cg * 512, 0), (Wbr, cg * 512, D), (Wb, OFF_G + D + cg * 512, 0))
                for half in range(2):
                    wA = wt_rot.next()
                    load_w(plan[2 * half][0], plan[2 * half][1], 512, wA, rows0=plan[2 * half][2])
                    ysrc = y_mT if half == 0 else y_aT
                    psas = []
                    for ci in range(4):
                        psa = psA.next()
                        mm_group(psa.t[:, :], [(wA.t[:, kc, ci * P:(ci + 1) * P], ysrc.t[:, kc, :]) for kc in range(16)], [wA, ysrc], psa)
                        psas.append(psa)
                    wG = wt_rot.next()
                    load_w(plan[2 * half + 1][0], plan[2 * half + 1][1], 512, wG, rows0=plan[2 * half + 1][2])
                    for ci in range(4):
                        c = cg * 4 + ci
                        psa = psas[ci]
                        psg = psA.next()
                        mm_group(psg.t[:, :], [(wG.t[:, kc, ci * P:(ci + 1) * P], hT.t[:, kc, :]) for kc in range(16)], [wG, hT], psg)
                        gb = gm.next()
                        gcol = C_GB + half * 16 + c
                        T.op("act", lambda e, gb=gb, psg=psg, gcol=gcol: e.activation(out=gb[:, :], in_=psg.t[:, :], func=AF.Sigmoid, bias=CST.t[:, gcol:gcol + 1]), reads=[psg, CST], writes=[gb])
                        if half == 0:
                            T.op("dve", lambda e, psa=psa, gb=gb, ci=ci: e.tensor_tensor(out=stashb.t[:, ci, :], in0=psa.t[:, :], in1=gb[:, :], op=ALU.mult), reads=[psa, gb], writes=[stashb])
                        else:
                            mb2 = m2.next()
                            T.op("dve", lambda e, mb2=mb2, psa=psa, gb=gb: e.tensor_tensor(out=mb2[:, :], in0=psa.t[:, :], in1=gb[:, :], op=ALU.mult), reads=[psa, gb], writes=[mb2])
                            T.op("pool", lambda e, mb2=mb2, c=c, ci=ci: e.tensor_tensor(out=mT.t[:, c, :], in0=mb2[:, :], in1=stashb.t[:, ci, :], op=ALU.add), reads=[mb2, stashb], writes=[mT])
            dump("mT%d" % bi, mT.t[:, :, :].rearrange("p a b -> p (a b)"), [P, 16 * TB], BF16, [mT])
            for n in range(4):
                wo = wt_rot.next()
                load_w(Wo, n * 512, 512, wo)
                for i in range(NT):
                    ps = psA.next()
                    mm_group(ps.t[:, :], [(mT.t[:, kc, i * P:(i + 1) * P], wo.t[:, kc, :]) for kc in range(16)], [mT, wo], ps)
                    T.op("dve", lambda e, ps=ps, i=i, n=n: e.tensor_tensor(out=r.t[:, i, n * 512:(n + 1) * 512], in0=ps.t[:, :], in1=r.t[:, i, n * 512:(n + 1) * 512], op=ALU.add), reads=[ps, r], writes=[r])
            for i in range(NT):
                T.op("act", lambda e, i=i: e.activation(out=xs[:], in_=r.t[:, i, :], func=AF.Square, accum_out=st3[:, 0:1]), reads=[r], writes=[xs, st3])
                rstd_from_ssq(st3[:, 0:1], st3[:, 1:2], D, st3)
                T.op("dve", lambda e, i=i: e.scalar_tensor_tensor(out=r.t[:, i, :], in0=r.t[:, i, :], scalar=st3[:, 1:2], in1=nfb[:, :], op0=ALU.mult, op1=ALU.mult), reads=[r, st3, nfb], writes=[r])
                T.dma("pool", out[orow0 + i * P:orow0 + (i + 1) * P, :], r.t[:, i, :], reads=[r], slot="ost")

        for bi in range(NPB + NOB):
            is_own = bi >= NPB
            key0 = bi * TB
            xsrc, row0 = (xo, (bi - NPB) * TB) if is_own else (xp, bi * TB)
            build_hT(xsrc, row0)
            if bi == NPB:
                dump("hT", hT.t[:, :, :].rearrange("p a b -> p (a b)"), [P, 16 * TB], BF16, [hT])
            with ExitStack() as S1:
                y_mT = ssd_block(S1, is_own, bi)
                if is_own:
                    T.dma("pool", YM, y_mT.t[:, :, :].rearrange("p a b -> p (a b)"), reads=[y_mT], slot="ymst")
                if bi == NPB - 1:
                    T.op("dve", lambda e: e.tensor_scalar(out=hs.t[:, :, :], in0=hs.t[:, :, :], scalar1=CST.t[:, C_PV:C_PV + 1], scalar2=None, op0=ALU.mult), reads=[hs, CST], writes=[hs])
                T.barrier()
            if not is_own:
                with ExitStack() as S2:
                    kv_proj(S2, key0, False, None, None, None)
                    T.barrier()
                continue
            with ExitStack() as SO:
                y_aT = sbg("y_aT", [P, 16, TB], BF16, SO)
                with ExitStack() as S2:
                    qT = sbg("qT", [P, 16, TB], BF16, S2)
                    qiT = sbg("qiT", [P, 8, TB], BF16, S2)
                    ws = sbg("ws", [P, NT, 16], F32, S2)
                    with ExitStack() as S2a:
                        kv_proj(S2a, key0, True, qT, qiT, ws)
                        T.barrier()
                    for hb in range(2):
                        with ExitStack() as S2b:
                            dsa_half(S2b, bi, key0, hb, qT, qiT, ws, y_aT)
                            T.barrier()
                dump("yaT%d" % bi, y_aT.t[:, :, :].rearrange("p a b -> p (a b)"), [P, 16 * TB], BF16, [y_aT])
                with ExitStack() as S3:
                    out_block2(S3, bi, (bi - NPB) * TB, y_aT)
                    T.barrier()
        T.barrier()
        T.emit()
    return nc, dbg_out


def make_cst(inputs, half):
    c = np.zeros((P, NCST), np.float32)
    c[:, C_ID:C_ID + 128] = np.eye(128, dtype=np.float32)
    u = np.arange(128)
    c[:, C_L:C_L + 128] = (u[:, None] <= u[None, :]).astype(np.float32)
    c[:, C_U:C_U + 128] = (u[:, None] > u[None, :]).astype(np.float32)
    cw = np.asarray(inputs["conv_w"][0], np.float32)
    c[:, C_CW:C_CW + 96] = cw.T.reshape(24, 128, 4).transpose(1, 0, 2).reshape(128, 96)
    c[:, C_CB:C_CB + 24] = np.asarray(inputs["conv_b"][0], np.float32).reshape(24, 128).T
    c[:, C_GB:C_GB + 32] = np.asarray(inputs["gate_bias"][0], np.float32).reshape(32, 128).T
    c[:, C_G:C_G + 16] = np.asarray(inputs["norm_in"][0], np.float32).reshape(16, 128).T
    c[:, C_GN:C_GN + 16] = np.asarray(inputs["gn_m"][0], np.float32).reshape(16, 128).T
    c[:, C_DTB:C_DTB + 32] = np.asarray(inputs["dt_bias"][0], np.float32)[None, :]
    c[:, C_ALOG:C_ALOG + 32] = np.asarray(inputs["a_log"][0], np.float32)[None, :]
    c[:, C_DSK:C_DSK + 32] = np.asarray(inputs["d_skip"][0], np.float32)[None, :]
    c[:, C_PV] = 1.0 if half == 1 else 0.0
    c[:, C_CVEC:C_CVEC + NITER] = (0.5 ** np.arange(1, NITER + 1, dtype=np.float64)).astype(np.float32)[None, :]
    q = np.arange(128)[:, None]
    k = np.arange(128)[None, :]
    diag = np.where((q < 64) & (k >= 64), NEG, 0.0).astype(np.float32)
    full = np.full((128, 128), NEG, np.float32)
    zero = np.zeros((128, 128), np.float32)
    c[:, C_BP:C_BP + 256] = np.concatenate([diag, full], 1)
    c[:, C_BP + 256:C_BP + 512] = np.concatenate([zero, diag], 1)
    c[0, C_PVROW:C_PVROW + 128] = 0.0 if half == 1 else NEG
    return c


_NC_CACHE = {}


def run(inputs, L, dbg=()):
    B = inputs["x"].shape[0]
    assert B * 2 == 8
    H = L // 2
    NPB = NOB = H // TB
    key = (NPB, NOB, tuple(dbg))
    if key not in _NC_CACHE:
        _NC_CACHE[key] = build(NPB, NOB, dbg)
    nc, dbg_out = _NC_CACHE[key]
    x = np.asarray(inputs["x"], np.float32)
    w_in = np.ascontiguousarray(np.asarray(inputs["w_in"][0], np.float32))
    w_br = np.ascontiguousarray(np.asarray(inputs["w_branch"][0], np.float32))
    w_out = np.ascontiguousarray(np.asarray(inputs["w_out"][0], np.float32))
    normf = np.asarray(inputs["norm_final"], np.float32).reshape(1, D)
    csts = [make_cst(inputs, 0), make_cst(inputs, 1)]
    zeros = np.zeros((H, D), np.float32)
    in_maps = []
    for c in range(8):
        b, half = c // 2, c % 2
        in_maps.append({
            "xp": np.ascontiguousarray(x[b, 0:H]) if half == 1 else zeros,
            "xo": np.ascontiguousarray(x[b, half * H:(half + 1) * H]),
            "w_in": w_in, "w_br": w_br, "w_out": w_out,
            "cst": csts[half], "normf": normf,
        })
    res = run_bass_kernel_spmd(nc, in_maps, core_ids=list(range(8)))
    outp = np.empty((B, L, D), np.float32)
    for c in range(8):
        b, half = c // 2, c % 2
        outp[b, half * H:(half + 1) * H] = res.results[c]["out"]
    return outp, res


def kernel(**inputs):
    L = inputs["x"].shape[1]
    outp, _ = run(inputs, L)
    return outp
```

```python
import numpy as np
import ml_dtypes
from contextlib import ExitStack
import concourse.bass as bass
import concourse.mybir as mybir
from concourse.bass_utils import run_bass_kernel_spmd

F32 = mybir.dt.float32
BF16 = mybir.dt.bfloat16
AF = mybir.ActivationFunctionType
ALU = mybir.AluOpType
AX = mybir.AxisListType

P = 128
D = 2048
TB = 512
NT = 4
EPS = 1e-6
OFF_Z, OFF_XBC, OFF_DT, OFF_Q, OFF_K, OFF_V, OFF_GA, OFF_QI, OFF_KI, OFF_WI, OFF_G = (
    0, 2048, 5120, 5152, 7200, 9248, 11296, 13344, 14368, 14432, 14448)
D_IN = 18544
ATTN_SCALE = 128 ** -0.5
IDX_SCALE = (64 ** -0.5) * (16 ** -0.5)
TOPK = 256
NITER = 21
CLAMP = 8.0
NEG = -30000.0

C_ID, C_L, C_U, C_CW, C_CB, C_GB, C_G, C_GN, C_DTB, C_ALOG, C_DSK, C_PV, C_CVEC, C_BP, C_PVROW, NCST = (
    0, 128, 256, 384, 480, 504, 536, 552, 568, 600, 632, 664, 672, 704, 1216, 1344)


class Buf:
    def __init__(self, t, name=""):
        self.t = t
        self.name = name
        self.w = None
        self.r = []
        self.alias = []
        self.pending = 0

    def __getitem__(self, k):
        return self.t[k]


class Trk:
    LIM = 30000

    def __init__(self, nc, es):
        self.nc = nc
        self.es = es
        self.engs = {"pe": nc.tensor, "act": nc.scalar, "dve": nc.vector, "pool": nc.gpsimd, "sp": nc.sync}
        self.ctr = {}
        self.prog = {k: [] for k in self.engs}
        self.known = {k: {} for k in self.engs}
        self.nsem = 0
        for k in self.engs:
            self._newctr(k, 1)

    def _newsem(self, name):
        self.nsem += 1
        return self.es.enter_context(self.nc.semaphore("%s_%d" % (name, self.nsem)))

    def _newctr(self, name, step):
        self.ctr[name] = {"step": step, "ep": 0, "cnt": 0, "sems": {0: self._newsem(name)}}

    def _next_ms(self, name):
        c = self.ctr[name]
        if (c["cnt"] + 1) * c["step"] > self.LIM:
            return (name, c["ep"] + 1, 1)
        return (name, c["ep"], c["cnt"] + 1)

    def _inc(self, name):
        c = self.ctr[name]
        ms = self._next_ms(name)
        if ms[1] != c["ep"]:
            c["ep"] = ms[1]
            c["sems"][ms[1]] = self._newsem(name)
        c["cnt"] = ms[2]
        return ms, c["sems"][c["ep"]], c["step"]

    def _cur_ms(self, name):
        c = self.ctr[name]
        return (name, c["ep"], c["cnt"])

    def _need(self, e, deps):
        need = {}
        for d in deps:
            if d is None:
                continue
            name, ep, idx = d
            if idx == 0:
                continue
            k = (name, ep)
            if idx > need.get(k, 0):
                need[k] = idx
        for (name, ep), idx in need.items():
            c = self.ctr[name]
            if name == e and (ep, idx) > (c["ep"], c["cnt"]):
                continue
            if self.known[e].get((name, ep), 0) >= idx:
                continue
            later = [kk for kk in self.known[e] if kk[0] == name and kk[1] > ep]
            if later:
                continue
            self.prog[e].append((lambda en, v_=idx * c["step"], nm=(name, ep): en.wait_ge(self._sem_of(nm), v_)))
            self.known[e][(name, ep)] = idx

    def _sem_of(self, nm):
        return self.ctr[nm[0]]["sems"][nm[1]]

    def _deps(self, reads, writes):
        deps = []
        for b in reads:
            deps.append(b.w)
            for a in b.alias:
                deps.append(a.w)
        for b in writes:
            deps.append(b.w)
            deps.extend(b.r)
            for a in b.alias:
                deps.append(a.w)
                deps.extend(a.r)
        return deps

    def _mark(self, me, reads, writes):
        for b in reads:
            b.r.append(me)
        for b in writes:
            b.w = me
            b.r = []

    def op(self, e, fn, reads=(), writes=(), inc=True):
        self._need(e, self._deps(reads, writes))
        if inc:
            me, sem, step = self._inc(e)
            self.prog[e].append((lambda en, f_=fn, s_=sem: f_(en).then_inc(s_, 1)))
        else:
            me = self._next_ms(e)
            self.prog[e].append(fn)
        self._mark(me, reads, writes)

    def dma(self, q, out, in_, reads=(), writes=(), slot=None, **kw):
        self._need(q, self._deps(reads, writes))
        name = "d:" + slot
        if name not in self.ctr:
            self._newctr(name, 16)
        me, sem, step = self._inc(name)
        self.prog[q].append((lambda en, o_=out, i_=in_, k_=kw, s_=sem: en.dma_start(out=o_, in_=i_, **k_).then_inc(s_, 16)))
        self._mark(me, reads, writes)

    def barrier(self):
        allms = [self._cur_ms(n) for n in self.ctr]
        for e in self.engs:
            self._need(e, [m for m in allms if m[0] != e])

    def emit(self):
        with self.nc.Block() as block:
            @block.tensor
            def _(en):
                for f in self.prog["pe"]:
                    f(en)

            @block.scalar
            def _(en):
                for f in self.prog["act"]:
                    f(en)

            @block.vector
            def _(en):
                for f in self.prog["dve"]:
                    f(en)

            @block.gpsimd
            def _(en):
                for f in self.prog["pool"]:
                    f(en)

            @block.sync
            def _(en):
                for f in self.prog["sp"]:
                    f(en)


class Rot:
    def __init__(self, items):
        self.items = items
        self.i = 0

    def next(self, pend=None):
        b = self.items[self.i % len(self.items)]
        while b.pending > 0 and pend:
            pend.pop(0)()
        assert b.pending == 0, "rotation reuses %s while deferred users pending" % b.name
        self.i += 1
        return b


def defer(pend, fn, bufs):
    for b in bufs:
        b.pending += 1

    def w():
        fn()
        for b in bufs:
            b.pending -= 1
    pend.append(w)


def build(NPB, NOB, dbg=()):
    NK = (NPB + NOB) * TB
    nc = bass.Bass("TRN2", target_bir_lowering=False)
    dt_in = lambda n, s, d=F32: nc.dram_tensor(n, s, d, kind="ExternalInput").ap()
    xp = dt_in("xp", [max(NPB, 1) * TB, D])
    xo = dt_in("xo", [NOB * TB, D])
    w_in = dt_in("w_in", [D, D_IN])
    w_br = dt_in("w_br", [2 * D, D])
    w_out = dt_in("w_out", [D, D])
    cst = dt_in("cst", [P, NCST])
    normf = dt_in("normf", [1, D])
    out = nc.dram_tensor("out", [NOB * TB, D], F32, kind="ExternalOutput").ap()
    dti = lambda n, s, d=BF16: nc.dram_tensor(n, s, d, kind="Internal").ap()
    Wb = dti("Wb", [D, D_IN])
    Wbr = dti("Wbr", [2 * D, D])
    Wo = dti("Wo", [D, D])
    KT = dti("KT", [16, P, NK])
    NCH = (NK // P + 15) // 16
    VS = dti("VS", [16, NCH, P, 16, 129])
    KI = dti("KI", [2, P, NK])
    YM = dti("YM", [P, 16 * TB])
    dbg_out = {}

    with ExitStack() as es:
        T = Trk(nc, es)

        uniq = [0]

        def sbg(name, shape, dt, stack=es):
            uniq[0] += 1
            return Buf(stack.enter_context(nc.sbuf_tensor("%s_u%d" % (name, uniq[0]), shape, dt)), name)

        CST = sbg("CST", [P, NCST], F32)
        ident_b = sbg("ident_b", [P, P], BF16)
        bpat_b = sbg("bpat_b", [P, 2, 256], BF16)
        pvm = sbg("pvm", [P, P], BF16)
        ones_b = sbg("ones_b", [P, 512], BF16)
        ones_f = sbg("ones_f", [P, P], F32)
        aneg = sbg("aneg", [P, 32], F32)
        eps_t = sbg("eps_t", [P, 2], F32)
        hT = sbg("hT", [P, 16, TB], BF16)
        WT = sbg("WT", [P, 2 * 16 * 512], BF16)
        XM = sbg("XM", [P, 8192], BF16)
        xs = sbg("xs", [P, D], BF16)
        hs = sbg("hs", [P, 4, 512], F32)
        halo = sbg("halo", [P, 24, 3], F32)
        st = sbg("st", [P, 16], F32)
        PS = [Buf(es.enter_context(nc.psum_tensor("ps%d" % i, [P, 512], F32)), "ps%d" % i) for i in range(8)]
        psA = Rot(PS)
        psG = Rot(PS[0:4])
        psO = Rot(PS[4:8])

        Wt = [Buf(WT.t[:, i * 8192:(i + 1) * 8192].rearrange("p (k c) -> p k c", c=512), "Wt%d" % i) for i in range(2)]
        score = Buf(WT.t[:].bitcast(F32), "score")
        score.alias = [Wt[0], Wt[1]]
        for w in Wt:
            w.alias = [score]
        xtb = [Buf(XM.t[:, i * 4096:(i + 1) * 4096].bitcast(F32), "xt%d" % i) for i in range(2)]
        mjD = Buf(XM.t[:, 0:4096], "mjD")
        mjA = Buf(XM.t[:, 4096:8192], "mjA")
        mjD.alias = [xtb[0]]
        mjA.alias = [xtb[1]]
        xtb[0].alias = [mjD]
        xtb[1].alias = [mjA]
        XM8 = XM.t[:, 4096:8192].bitcast(mybir.dt.int8)
        wt_rot = Rot(Wt)
        xt_rot = Rot(xtb)

        ident_f = CST.t[:, C_ID:C_ID + 128]
        Lmask = CST.t[:, C_L:C_L + 128]
        Ustr = CST.t[:, C_U:C_U + 128]

        def cs(c0, n):
            return CST.t[:, c0:c0 + n]

        def psb(b):
            return b.t[:].bitcast(BF16).rearrange("p (a b) -> p a b", b=128)

        def dump(name, ap, shape, dtype, rd):
            if name not in dbg:
                return
            o = nc.dram_tensor("dbg_" + name, list(shape), dtype, kind="ExternalOutput").ap()
            dbg_out[name] = o
            T.dma("sp", o, ap, reads=rd, slot="dbg")

        T.dma("sp", CST[:], cst, writes=[CST], slot="cst")
        T.op("dve", lambda e: e.tensor_copy(out=ident_b[:], in_=ident_f), reads=[CST], writes=[ident_b])
        T.op("dve", lambda e: e.tensor_copy(out=bpat_b[:], in_=cs(C_BP, 512).rearrange("p (a b) -> p a b", b=256)), reads=[CST], writes=[bpat_b])
        T.op("dve", lambda e: e.tensor_copy(out=pvm[:], in_=cs(C_PVROW, 128)), reads=[CST], writes=[pvm])
        T.op("dve", lambda e: e.memset(ones_b[:], 1.0), writes=[ones_b])
        T.op("pool", lambda e: e.memset(ones_f[:], 1.0), writes=[ones_f])
        T.op("pool", lambda e: e.memset(eps_t[:], EPS), writes=[eps_t])
        T.op("pool", lambda e: e.memset(hs[:], 0.0), writes=[hs])
        T.op("pool", lambda e: e.memset(halo[:], 0.0), writes=[halo])
        T.op("act", lambda e: e.activation(out=aneg[:], in_=cs(C_ALOG, 32), func=AF.Exp), reads=[CST], writes=[aneg])
        T.op("dve", lambda e: e.tensor_scalar(out=aneg[:], in0=aneg[:], scalar1=-1.0, scalar2=None, op0=ALU.mult), reads=[aneg], writes=[aneg])

        with ExitStack() as s0:
            fb = Rot([sbg("pcf%d" % i, [P, 2048], F32, s0) for i in range(3)])
            bb = Rot([sbg("pcb%d" % i, [P, 2048], BF16, s0) for i in range(3)])
            cnt = 0

            def cast_piece(src_ap, dst_ap, n, scal):
                nonlocal cnt
                f = fb.next()
                b = bb.next()
                T.dma("sp", f[:, 0:n], src_ap, writes=[f], slot=f.name)
                eng = "dve" if cnt % 2 == 0 else "pool"
                if scal is None:
                    T.op(eng, lambda e: e.tensor_copy(out=b[:, 0:n], in_=f[:, 0:n]), reads=[f], writes=[b])
                else:
                    T.op(eng, lambda e: e.tensor_scalar(out=b[:, 0:n], in0=f[:, 0:n], scalar1=scal, scalar2=1.0, op0=ALU.mult, op1=ALU.mult), reads=[f, CST], writes=[b])
                T.dma("act", dst_ap, b[:, 0:n], reads=[b], slot=b.name)
                cnt += 1

            for kc in range(16):
                for c0 in range(0, D_IN, 2048):
                    n = min(2048, D_IN - c0)
                    cast_piece(w_in[kc * P:(kc + 1) * P, c0:c0 + n], Wb[kc * P:(kc + 1) * P, c0:c0 + n], n, CST.t[:, C_G + kc:C_G + kc + 1])
            for kc in range(32):
                cast_piece(w_br[kc * P:(kc + 1) * P, :], Wbr[kc * P:(kc + 1) * P, :], 2048, None)
            for kc in range(16):
                cast_piece(w_out[kc * P:(kc + 1) * P, :], Wo[kc * P:(kc + 1) * P, :], 2048, None)
            zt = sbg("zt", [P, 2048], BF16, s0)
            T.op("pool", lambda e: e.memset(zt[:], 0.0), writes=[zt])
            for c0 in range(0, NK, 2048):
                n = min(2048, NK - c0)
                T.dma("act", KI[0, 64:128, c0:c0 + n], zt[64:128, 0:n], reads=[zt], slot="zfill")
                T.dma("act", KI[1, 0:64, c0:c0 + n], zt[0:64, 0:n], reads=[zt], slot="zfill")
            T.barrier()

        def load_w(src, c0, n, wt, dcol=0, rows0=0):
            T.dma("sp", wt.t[:, :, dcol:dcol + n], src[rows0:rows0 + D, c0:c0 + n].rearrange("(kc p) c -> p kc c", p=P), writes=[wt], slot=wt.name)

        def mm_group(ps_ap, pairs, reads, ps_buf, extra_first_reads=()):
            n = len(pairs)
            for k, (l, r) in enumerate(pairs):
                T.op("pe", lambda e, l=l, r=r, k=k: e.matmul(ps_ap, lhsT=l, rhs=r, start=(k == 0), stop=(k == n - 1)),
                     reads=reads, writes=[ps_buf], inc=(k == n - 1))

        def proj_F(wt, ncols, consumer, rhs_cols=slice(0, TB), nfree=TB):
            for ci in range(ncols // P):
                ps = psA.next()
                mm_group(ps.t[:, 0:nfree], [(wt.t[:, kc, ci * P:(ci + 1) * P], hT.t[:, kc, rhs_cols]) for kc in range(16)], [wt, hT], ps)
                consumer(ci, ps)

        def proj_T(wt, ncols, consumer, tiles=range(NT)):
            for i in tiles:
                ps = psA.next()
                mm_group(ps.t[:, 0:ncols], [(hT.t[:, kc, i * P:(i + 1) * P], wt.t[:, kc, 0:ncols]) for kc in range(16)], [wt, hT], ps)
                consumer(i, ps)

        def run_pipe(pend, nkeep):
            while len(pend) > nkeep:
                pend.pop(0)()

        evac_flip = [0]

        def evac(out_ap, in_ap, reads, writes, scale=None):
            evac_flip[0] ^= 1
            if evac_flip[0]:
                if scale is None:
                    T.op("act", lambda e: e.copy(out=out_ap, in_=in_ap), reads=reads, writes=writes)
                else:
                    T.op("act", lambda e: e.activation(out=out_ap, in_=in_ap, func=AF.Copy, scale=scale), reads=reads, writes=writes)
            else:
                if scale is None:
                    T.op("dve", lambda e: e.tensor_copy(out=out_ap, in_=in_ap), reads=reads, writes=writes)
                else:
                    T.op("dve", lambda e: e.tensor_scalar(out=out_ap, in0=in_ap, scalar1=scale, scalar2=None, op0=ALU.mult), reads=reads, writes=writes)

        def rstd_from_ssq(ssq_ap, out_ap, n, buf):
            T.op("act", lambda e: e.activation(out=out_ap, in_=ssq_ap, func=AF.Ln, scale=1.0 / n, bias=eps_t[:, 0:1]), reads=[buf, eps_t], writes=[buf])
            T.op("act", lambda e: e.activation(out=out_ap, in_=out_ap, func=AF.Exp, scale=-0.5), reads=[buf], writes=[buf])

        def build_hT(xsrc, row0):
            for i in range(NT):
                xt = xt_rot.next()
                T.dma("sp", xt[:, :], xsrc[row0 + i * P:row0 + (i + 1) * P, :], writes=[xt], slot=xt.name)
                T.op("act", lambda e, xt=xt: e.activation(out=xs[:], in_=xt[:, :], func=AF.Square, accum_out=st[:, 0:1]), reads=[xt], writes=[xs, st])
                rstd_from_ssq(st[:, 0:1], st[:, 1:2], D, st)
                T.op("dve", lambda e, xt=xt: e.tensor_scalar(out=xs[:], in0=xt[:, :], scalar1=st[:, 1:2], scalar2=None, op0=ALU.mult), reads=[xt, st], writes=[xs])
                for hb in range(2):
                    ps = psA.next()
                    for k in range(8):
                        kc = hb * 8 + k
                        T.op("pe", lambda e, ps=ps, k=k, kc=kc: e.transpose(out=psb(ps)[:, k, :], in_=xs[:, kc * P:(kc + 1) * P], identity=ident_b[:]),
                             reads=[xs, ident_b], writes=[ps], inc=(k == 7))
                    evac(hT.t[:, hb * 8:(hb + 1) * 8, i * P:(i + 1) * P], psb(ps), [ps], [hT])

        def ssd_block(S, is_own, bi):
            x_tok = sbg("x_tok", [P, NT, D], F32, S)
            BT = sbg("BT", [P, 4, TB], BF16, S)
            CTt = sbg("CTt", [P, 4, TB], BF16, S)
            Btok = sbg("Btok", [P, NT, 4, P], BF16, S)
            dt_all = sbg("dt_all", [P, NT, 32], F32, S)
            sm = sbg("sm", [P, NT, 160], F32, S)
            y_mT = sbg("y_mT", [P, 16, TB], BF16, S) if is_own else None
            SX = ExitStack()
            ut = Rot([sbg("ut%d" % i, [P, 515], F32, SX) for i in range(3)])
            acc_r = Rot([sbg("cacc%d" % i, [P, TB], F32, SX) for i in range(2)])
            xsT = Rot([sbg("xsT%d" % i, [P, TB], F32, SX) for i in range(4)])

            cpend = []
            for grp in range(6):
                wt = wt_rot.next()
                load_w(Wb, OFF_XBC + grp * 512, 512, wt)

                def cons(ci, ps, grp=grp):
                    cg = grp * 4 + ci
                    u = ut.next()
                    T.op("act", lambda e: e.copy(out=u[:, 3:515], in_=ps.t[:, 0:512]), reads=[ps], writes=[u])
                    T.op("pool", lambda e: e.tensor_copy(out=u[:, 0:3], in_=halo[:, cg, :]), reads=[halo], writes=[u])
                    T.op("pool", lambda e: e.tensor_copy(out=halo[:, cg, :], in_=u[:, 512:515]), reads=[u], writes=[halo])
                    acc = acc_r.next()
                    cw = lambda k: CST.t[:, C_CW + cg * 4 + k:C_CW + cg * 4 + k + 1]
                    T.op("dve", lambda e: e.tensor_scalar(out=acc[:, :], in0=u[:, 3:515], scalar1=cw(3), scalar2=None, op0=ALU.mult), reads=[u, CST], writes=[acc])
                    for k in (2, 1, 0):
                        T.op("dve", lambda e, k=k: e.scalar_tensor_tensor(out=acc[:, :], in0=u[:, k:k + 512], scalar=cw(k), in1=acc[:, :], op0=ALU.mult, op1=ALU.add), reads=[u, CST, acc], writes=[acc])
                    cb = CST.t[:, C_CB + cg:C_CB + cg + 1]
                    if cg < 16:
                        xT = xsT.next(cpend)
                        T.op("act", lambda e: e.activation(out=xT[:, :], in_=acc[:, :], func=AF.Silu, bias=cb), reads=[acc, CST], writes=[xT])

                        def later(xT=xT, cg=cg):
                            ps2 = psA.next()
                            for i in range(NT):
                                T.op("pe", lambda e, i=i: e.transpose(out=ps2.t[:, i * P:(i + 1) * P], in_=xT[:, i * P:(i + 1) * P], identity=ident_f),
                                     reads=[xT, CST], writes=[ps2], inc=(i == NT - 1))
                            evac(x_tok.t[:, :, cg * P:(cg + 1) * P], ps2.t[:, :].rearrange("p (a b) -> p a b", b=P), [ps2], [x_tok])
                        defer(cpend, later, [xT])
                    elif cg < 20:
                        g = cg - 16
                        T.op("act", lambda e: e.activation(out=BT.t[:, g, :], in_=acc[:, :], func=AF.Silu, bias=cb), reads=[acc, CST], writes=[BT])

                        def later(g=g):
                            ps2 = psA.next()
                            for i in range(NT):
                                T.op("pe", lambda e, i=i: e.transpose(out=psb(ps2)[:, i, :], in_=BT.t[:, g, i * P:(i + 1) * P], identity=ident_b[:]),
                                     reads=[BT, ident_b], writes=[ps2], inc=(i == NT - 1))
                            evac(Btok.t[:, :, g, :], psb(ps2)[:, 0:NT, :], [ps2], [Btok])
                        defer(cpend, later, [])
                    else:
                        g = cg - 20
                        T.op("act", lambda e: e.activation(out=CTt.t[:, g, :], in_=acc[:, :], func=AF.Silu, bias=cb), reads=[acc, CST], writes=[CTt])
                    run_pipe(cpend, 2)

                proj_F(wt, 512, cons)
            run_pipe(cpend, 0)
            T.barrier()
            SX.close()

            wt = wt_rot.next()
            load_w(Wb, OFF_DT, 32, wt)

            def cons_dt(i, ps):
                T.op("dve", lambda e: e.tensor_tensor(out=dt_all.t[:, i, :], in0=ps.t[:, 0:32], in1=cs(C_DTB, 32), op=ALU.add), reads=[ps, CST], writes=[dt_all])
                T.op("act", lambda e: e.activation(out=dt_all.t[:, i, :], in_=dt_all.t[:, i, :], func=AF.Exp), reads=[dt_all], writes=[dt_all])
                T.op("act", lambda e: e.activation(out=dt_all.t[:, i, :], in_=dt_all.t[:, i, :], func=AF.Ln, bias=1.0), reads=[dt_all], writes=[dt_all])

            proj_T(wt, 32, cons_dt)

            for i in range(NT):
                a_i = sm.t[:, i, 0:32]
                T.op("dve", lambda e, i=i, a_i=a_i: e.tensor_tensor(out=a_i, in0=dt_all.t[:, i, :], in1=aneg[:], op=ALU.mult), reads=[dt_all, aneg], writes=[sm])
                ps = psA.next()
                T.op("pe", lambda e, a_i=a_i, ps=ps: e.matmul(ps.t[:, 0:32], lhsT=Lmask, rhs=a_i, start=True, stop=True), reads=[CST, sm], writes=[ps])
                T.op("pe", lambda e, a_i=a_i, ps=ps: e.matmul(ps.t[:, 32:64], lhsT=ones_f[:], rhs=a_i, start=True, stop=True), reads=[ones_f, sm], writes=[ps])
                T.op("act", lambda e, i=i, ps=ps: e.activation(out=sm.t[:, i, 32:96], in_=ps.t[:, 0:64], func=AF.Exp), reads=[ps], writes=[sm])
                T.op("dve", lambda e, i=i, ps=ps: e.tensor_copy(out=sm.t[:, i, 128:160], in_=ps.t[:, 0:32]), reads=[ps], writes=[sm])
                T.op("dve", lambda e, i=i, ps=ps: e.tensor_tensor(out=sm.t[:, i, 96:128], in0=ps.t[:, 32:64], in1=sm.t[:, i, 128:160], op=ALU.subtract), reads=[ps, sm], writes=[sm])
                T.op("act", lambda e, i=i: e.activation(out=sm.t[:, i, 96:128], in_=sm.t[:, i, 96:128], func=AF.Exp), reads=[sm], writes=[sm])
                T.op("dve", lambda e, i=i: e.tensor_tensor(out=sm.t[:, i, 96:128], in0=sm.t[:, i, 96:128], in1=dt_all.t[:, i, :], op=ALU.mult), reads=[sm, dt_all], writes=[sm])

            xw = Rot([sbg("xw%d" % i, [P, TB], BF16, S) for i in range(2)])
            tmpS = Rot([sbg("tmpS%d" % i, [P, TB], F32, S) for i in range(2)])
            if is_own:
                xdt = Rot([sbg("xdt%d" % i, [P, TB], BF16, S) for i in range(3)])
                hsb = Rot([sbg("hsb%d" % i, [P, TB], BF16, S) for i in range(3)])
                cbm = Rot([sbg("cbm%d" % i, [P, P], F32, S) for i in range(3)])
                Rr = Rot([sbg("Rr%d" % i, [P, 8, P], F32, S) for i in range(2)])
                dec = Rot([sbg("dec%d" % i, [P, 8, P], F32, S) for i in range(2)])
                Mm = Rot([sbg("Mm%d" % i, [P, 8, P], BF16, S) for i in range(3)])
                t1 = Rot([sbg("t1_%d" % i, [P, TB], F32, S) for i in range(3)])
                t2 = Rot([sbg("t2_%d" % i, [P, TB], F32, S) for i in range(2)])
                zs = Rot([sbg("zs%d" % i, [P, TB], F32, S) for i in range(3)])
                yn = Rot([sbg("yn%d" % i, [P, TB], BF16, S) for i in range(3)])
                st2r = Rot([sbg("st2_%d" % i, [P, 8], F32, S) for i in range(2)])
            v3 = lambda b: b.t[:, :].rearrange("p (h q) -> p h q", q=64)
            stageC = []
            stageD = []

            def make_R(g, i):
                Rb = Rr.next()
                a3 = sm.t[:, i, g * 8:(g + 1) * 8].unsqueeze(2).to_broadcast([P, 8, P])
                L3 = Lmask.unsqueeze(1).to_broadcast([P, 8, P])
                T.op("dve", lambda e: e.tensor_tensor(out=Rb[:, :, :], in0=a3, in1=L3, op=ALU.mult), reads=[sm, CST], writes=[Rb])
                return Rb
            Rnext = make_R(0, 0) if is_own else None
            for g in range(4):
                if is_own:
                    wz = wt_rot.next()
                    load_w(Wb, OFF_Z + g * 512, 512, wz)
                for i in range(NT):
                    tsl = slice(i * P, (i + 1) * P)
                    xg3 = x_tok.t[:, i, g * 512:(g + 1) * 512].rearrange("p (h q) -> p h q", q=64)
                    bc = lambda c0, i=i, g=g: sm.t[:, i, c0 + g * 8:c0 + g * 8 + 8].unsqueeze(2).to_broadcast([P, 8, 64])
                    xwb = xw.next()
                    T.op("dve" if not is_own else "pool", lambda e, xwb=xwb, xg3=xg3, bc=bc: e.tensor_tensor(out=v3(xwb), in0=xg3, in1=bc(96), op=ALU.mult), reads=[x_tok, sm], writes=[xwb])
                    if is_own:
                        xdb = xdt.next(stageC)
                        dtb3 = dt_all.t[:, i, g * 8:(g + 1) * 8].unsqueeze(2).to_broadcast([P, 8, 64])
                        T.op("pool", lambda e, xdb=xdb, xg3=xg3, dtb3=dtb3: e.tensor_tensor(out=v3(xdb), in0=xg3, in1=dtb3, op=ALU.mult), reads=[x_tok, dt_all], writes=[xdb])
                        hb_ = hsb.next(stageC)
                        T.op("act", lambda e, hb_=hb_, g=g: e.copy(out=hb_[:, :], in_=hs.t[:, g, :]), reads=[hs], writes=[hb_])
                        Rb = Rnext
                        psc = psA.next()
                        T.op("pe", lambda e, psc=psc, g=g, tsl=tsl: e.matmul(psc.t[:, 0:P], lhsT=BT.t[:, g, tsl], rhs=CTt.t[:, g, tsl], start=True, stop=True), reads=[BT, CTt], writes=[psc])
                        psd = [psA.next(), psA.next()]
                        for hh in range(2):
                            T.op("pe", lambda e, pd=psd[hh], Rb=Rb, hh=hh: e.matmul(pd.t[:, :], lhsT=Ustr, rhs=Rb.t[:, hh * 4:(hh + 1) * 4, :].rearrange("p a b -> p (a b)"), start=True, stop=True), reads=[CST, Rb], writes=[psd[hh]])
                    pss = psA.next()
                    T.op("pe", lambda e, pss=pss, xwb=xwb, i=i, g=g: e.matmul(pss.t[:, :], lhsT=Btok.t[:, i, g, :], rhs=xwb[:, :], start=True, stop=True), reads=[Btok, xwb], writes=[pss])
                    if is_own:
                        psz = psA.next()
                        mm_group(psz.t[:, :], [(hT.t[:, kc, tsl], wz.t[:, kc, :]) for kc in range(16)], [hT, wz], psz)
                        cbb = cbm.next()
                        T.op("dve", lambda e, psc=psc, cbb=cbb: e.tensor_tensor(out=cbb[:, :], in0=psc.t[:, 0:P], in1=Lmask, op=ALU.mult), reads=[psc, CST], writes=[cbb])
                        db = dec.next()
                        for hh in range(2):
                            T.op("act", lambda e, pd=psd[hh], db=db, hh=hh: e.activation(out=db.t[:, hh * 4:(hh + 1) * 4, :].rearrange("p a b -> p (a b)"), in_=pd.t[:, :], func=AF.Exp), reads=[psd[hh]], writes=[db])
                    tS = tmpS.next()
                    hs3 = hs.t[:, g, :].rearrange("p (h q) -> p h q", q=64)
                    T.op("pool" if is_own else "dve", lambda e, tS=tS, hs3=hs3, bc=bc: e.tensor_tensor(out=v3(tS), in0=hs3, in1=bc(64), op=ALU.mult), reads=[hs, sm], writes=[tS])
                    T.op("dve", lambda e, tS=tS, pss=pss, g=g: e.tensor_tensor(out=hs.t[:, g, :], in0=pss.t[:, :], in1=tS[:, :], op=ALU.add), reads=[pss, tS], writes=[hs])
                    if not is_own:
                        continue
                    zb = zs.next(stageC)
                    T.op("act", lambda e, zb=zb, psz=psz: e.activation(out=zb[:, :], in_=psz.t[:, :], func=AF.Silu), reads=[psz], writes=[zb])
                    Mb = Mm.next(stageC)
                    c3 = cbb.t[:, :].unsqueeze(1).to_broadcast([P, 8, P])
                    T.op("dve", lambda e, Mb=Mb, db=db, c3=c3: e.tensor_tensor(out=Mb[:, :, :], in0=db[:, :, :], in1=c3, op=ALU.mult), reads=[db, cbb], writes=[Mb])
                    nxt = (g, i + 1) if i + 1 < NT else (g + 1, 0)
                    if nxt[0] < 4:
                        Rnext = make_R(*nxt)

                    def stC(Mb=Mb, xdb=xdb, hb_=hb_, g=g, i=i, tsl=tsl, xg3=xg3, bc=bc, zb=zb):
                        psy = psA.next()
                        for h in range(8):
                            T.op("pe", lambda e, h=h: e.matmul(psy.t[:, h * 64:(h + 1) * 64], lhsT=Mb.t[:, h, :], rhs=xdb.t[:, h * 64:(h + 1) * 64], start=True, stop=True),
                                 reads=[Mb, xdb], writes=[psy], inc=(h == 7))
                        pso = psA.next()
                        T.op("pe", lambda e: e.matmul(pso.t[:, :], lhsT=CTt.t[:, g, tsl], rhs=hb_[:, :], start=True, stop=True), reads=[CTt, hb_], writes=[pso])
                        t1b = t1.next()
                        t2b = t2.next()
                        st2 = st2r.next()
                        dsk3 = CST.t[:, C_DSK + g * 8:C_DSK + g * 8 + 8].unsqueeze(2).to_broadcast([P, 8, 64])
                        T.op("dve", lambda e: e.tensor_tensor(out=v3(t1b), in0=pso.t[:, :].rearrange("p (h q) -> p h q", q=64), in1=bc(32), op=ALU.mult), reads=[pso, sm], writes=[t1b])
                        T.op("pool", lambda e: e.tensor_tensor(out=v3(t2b), in0=xg3, in1=dsk3, op=ALU.mult), reads=[x_tok, CST], writes=[t2b])
                        T.op("dve", lambda e: e.tensor_tensor(out=t1b[:, :], in0=psy.t[:, :], in1=t1b[:, :], op=ALU.add), reads=[psy, t1b], writes=[t1b])
                        T.op("dve", lambda e: e.tensor_tensor(out=t1b[:, :], in0=t1b[:, :], in1=t2b[:, :], op=ALU.add), reads=[t1b, t2b], writes=[t1b])
                        T.op("dve", lambda e: e.tensor_tensor(out=t1b[:, :], in0=t1b[:, :], in1=zb[:, :], op=ALU.mult), reads=[t1b, zb], writes=[t1b])
                        T.op("act", lambda e: e.activation(out=t2b[:, :], in_=t1b[:, :], func=AF.Square, accum_out=st2[:, 0:1]), reads=[t1b], writes=[t2b, st2])
                        rstd_from_ssq(st2[:, 0:1], st2[:, 1:2], 512, st2)
                        ynb = yn.next(stageD)
                        T.op("dve", lambda e: e.tensor_scalar(out=ynb[:, :], in0=t1b[:, :], scalar1=st2[:, 1:2], scalar2=None, op0=ALU.mult), reads=[t1b, st2], writes=[ynb])

                        def stD(ynb=ynb):
                            pst = psA.next()
                            for j in range(4):
                                T.op("pe", lambda e, j=j: e.transpose(out=psb(pst)[:, j, :], in_=ynb.t[:, j * P:(j + 1) * P], identity=ident_b[:]), reads=[ynb, ident_b], writes=[pst], inc=(j == 3))
                            for j in range(4):
                                c = g * 4 + j
                                T.op("act", lambda e, j=j, c=c: e.activation(out=y_mT.t[:, c, tsl], in_=psb(pst)[:, j, :], func=AF.Copy, scale=CST.t[:, C_GN + c:C_GN + c + 1]), reads=[pst, CST], writes=[y_mT])
                        defer(stageD, stD, [ynb])
                    defer(stageC, stC, [Mb, xdb, hb_, zb])
                    run_pipe(stageD, 1)
                    run_pipe(stageC, 1)
            run_pipe(stageC, 0)
            run_pipe(stageD, 0)
            if is_own:
                dump("x_tok%d" % bi, x_tok.t[:, :, :].rearrange("p a b -> p (a b)"), [P, NT * D], F32, [x_tok])
                dump("dt%d" % bi, dt_all.t[:, :, :].rearrange("p a b -> p (a b)"), [P, NT * 32], F32, [dt_all])
                dump("ymT%d" % bi, y_mT.t[:, :, :].rearrange("p a b -> p (a b)"), [P, 16 * TB], BF16, [y_mT])
            return y_mT

        def kv_proj(S, key0, is_own, qT, qiT, ws):
            ktmp = Rot([sbg("ktmp%d" % i, [P, TB], BF16, S) for i in range(2)])
            vtmp = Rot([sbg("vtmp%d" % i, [P, 4, 129], BF16, S) for i in range(2)])
            for v_ in vtmp.items:
                T.op("pool", lambda e, v_=v_: e.memset(v_[:, :, :], 1.0), writes=[v_])
            for grp in range(4):
                wt = wt_rot.next()
                load_w(Wb, OFF_K + grp * 512, 512, wt)

                def cons(ci, ps, grp=grp):
                    hh = grp * 4 + ci
                    kb_ = ktmp.next()
                    evac(kb_[:, :], ps.t[:, :], [ps], [kb_])
                    T.dma("pool", KT[hh, :, key0:key0 + TB], kb_[:, :], reads=[kb_], slot=kb_.name)
                proj_F(wt, 512, cons)
            for grp in range(4):
                wt = wt_rot.next()
                load_w(Wb, OFF_V + grp * 512, 512, wt)

                def cons(i, ps, grp=grp):
                    vb_ = vtmp.next()
                    evac(vb_.t[:, :, 0:P], ps.t[:, :].rearrange("p (h d) -> p h d", d=P), [ps], [vb_])
                    kbg = key0 // P + i
                    T.dma("pool", VS[grp * 4:(grp + 1) * 4, kbg // 16, :, kbg % 16, :].rearrange("h p d -> p h d"), vb_[:, :, :], reads=[vb_], slot=vb_.name)
                proj_T(wt, 512, cons)
            wt = wt_rot.next()
            load_w(Wb, OFF_KI, 64, wt, dcol=0)
            load_w(Wb, OFF_KI, 64, wt, dcol=64)

            def cons(ci, ps):
                kb_ = ktmp.next()
                evac(kb_[:, :], ps.t[:, :], [ps], [kb_])
                T.dma("pool", KI[0, 0:64, key0:key0 + TB], kb_[0:64, :], reads=[kb_], slot=kb_.name)
                T.dma("pool", KI[1, 64:128, key0:key0 + TB], kb_[64:128, :], reads=[kb_], slot=kb_.name)
            proj_F(wt, 128, cons)
            if not is_own:
                return
            for grp in range(4):
                wt = wt_rot.next()
                load_w(Wb, OFF_Q + grp * 512, 512, wt)

                def cons(ci, ps, grp=grp):
                    evac(qT.t[:, grp * 4 + ci, :], ps.t[:, :], [ps], [qT], scale=ATTN_SCALE)
                proj_F(wt, 512, cons)
            for grp in range(2):
                wt = wt_rot.next()
                load_w(Wb, OFF_QI + grp * 512, 512, wt)

                def cons(ci, ps, grp=grp):
                    evac(qiT.t[:, grp * 4 + ci, :], ps.t[:, :], [ps], [qiT])
                proj_F(wt, 512, cons)
            wt = wt_rot.next()
            load_w(Wb, OFF_WI, 16, wt)

            def cons(i, ps):
                T.op("dve", lambda e: e.tensor_scalar(out=ws.t[:, i, :], in0=ps.t[:, 0:16], scalar1=IDX_SCALE, scalar2=None, op0=ALU.mult), reads=[ps], writes=[ws])
            proj_T(wt, 16, cons)

        def dsa_idx(S, bi, key0, hb, qiT, ws, maskT):
            nkb = (key0 + 256 * (hb + 1)) // P
            W = nkb * P
            nb5 = (W + 511) // 512
            LA = 4
            LA2 = 6
            SI = ExitStack()
            Dgr = Rot([sbg("Dg%d" % i, [P, 16, P], BF16, SI) for i in range(2)])
            Rl = Rot([sbg("Rl%d" % i, [P, 512], BF16, SI) for i in range(6)])
            kib = Rot([sbg("kib%d" % i, [P, 2, 512], BF16, SI) for i in range(3)])

            def indexer(j2):
                i = 2 * hb + j2
                tsl = slice(i * P, (i + 1) * P)
                Dg = Dgr.next()
                pend = []
                for h in range(16):
                    T.op("pool", lambda e, h=h, i=i, Dg=Dg: e.tensor_scalar(out=Dg.t[:, h, :], in0=ident_f, scalar1=ws.t[:, i, h:h + 1], scalar2=1.0, op0=ALU.mult, op1=ALU.mult), reads=[CST, ws], writes=[Dg])
                for b5 in range(nb5):
                    c0 = b5 * 512
                    wd = min(512, W - c0)
                    kb_ = kib.next()
                    T.dma("sp", kb_[:, 0, 0:wd], KI[0, :, c0:c0 + wd], writes=[kb_], slot=kb_.name)
                    T.dma("sp", kb_[:, 1, 0:wd], KI[1, :, c0:c0 + wd], writes=[kb_], slot=kb_.name)
                    acc = psO.next(pend)
                    extra = []
                    if c0 < NPB * TB:
                        extra.append("pv")
                    if b5 == nb5 - 1:
                        extra.append("bp")
                    nacc = 16 + len(extra)
                    for h in range(16):
                        j, sub = h // 2, h % 2
                        ps = psG.next()
                        T.op("pe", lambda e, ps=ps, j=j, sub=sub, kb_=kb_, wd=wd, tsl=tsl: e.matmul(ps.t[:, 0:wd], lhsT=qiT.t[:, j, tsl], rhs=kb_.t[:, sub, 0:wd], start=True, stop=True), reads=[qiT, kb_], writes=[ps])
                        rb = Rl.next(pend)
                        if h % 2 == 0:
                            T.op("act", lambda e, rb=rb, ps=ps, wd=wd: e.activation(out=rb[:, 0:wd], in_=ps.t[:, 0:wd], func=AF.Relu), reads=[ps], writes=[rb])
                        else:
                            T.op("dve", lambda e, rb=rb, ps=ps, wd=wd: e.tensor_scalar(out=rb[:, 0:wd], in0=ps.t[:, 0:wd], scalar1=0.0, scalar2=None, op0=ALU.max), reads=[ps], writes=[rb])

                        def accmm(acc=acc, h=h, rb=rb, wd=wd, nacc=nacc, Dg=Dg):
                            T.op("pe", lambda e: e.matmul(acc.t[:, 0:wd], lhsT=Dg.t[:, h, :], rhs=rb.t[:, 0:wd], start=(h == 0), stop=(h == nacc - 1)), reads=[Dg, rb], writes=[acc])
                        defer(pend, accmm, [rb, acc])
                        run_pipe(pend, LA)

                    def fin(acc=acc, wd=wd, nacc=nacc, extra=tuple(extra), c0=c0):
                        k = 16
                        for ex in extra:
                            if ex == "pv":
                                T.op("pe", lambda e, k=k: e.matmul(acc.t[:, 0:wd], lhsT=pvm[:], rhs=ones_b.t[:, 0:wd], start=False, stop=(k == nacc - 1)), reads=[pvm, ones_b], writes=[acc])
                            else:
                                T.op("pe", lambda e, k=k: e.matmul(acc.t[:, wd - 256:wd], lhsT=ident_b[:], rhs=bpat_b.t[:, j2, :], start=False, stop=(k == nacc - 1)), reads=[ident_b, bpat_b], writes=[acc])
                            k += 1
                        T.op("act", lambda e: e.copy(out=score.t[:, c0:c0 + wd], in_=acc.t[:, 0:wd]), reads=[acc], writes=[score])
                    defer(pend, fin, [acc])
                run_pipe(pend, 0)

            def bisect(j2):
                i = 2 * hb + j2
                hi_t, mid_t, cnt_t, t_t, g_t, lo_t, sa_t = [sbg("bq%d_%d" % (j2, c), [P, 2], F32, SI) for c in range(7)]
                WD = min(4096, max(P, ((W * 45 // 100) // P) * P))
                nA = W - WD
                assert nA >= P and nA <= 8192
                hi, mid, cntc, tt, gg, lo, sa = [b_.t[:, 0:1] for b_ in (hi_t, mid_t, cnt_t, t_t, g_t, lo_t, sa_t)]
                T.op("dve", lambda e: e.reduce_max(out=hi, in_=score.t[:, 0:W], axis=AX.X), reads=[score], writes=[hi_t])
                T.op("dve", lambda e: e.tensor_scalar(out=mid, in0=hi, scalar1=-CLAMP * 0.5, scalar2=None, op0=ALU.add), reads=[hi_t], writes=[mid_t])
                for it in range(NITER):
                    hk = CLAMP * (0.5 ** (it + 1))
                    T.op("dve", lambda e: e.tensor_scalar(out=XM.t[:, 0:WD], in0=score.t[:, 0:WD], scalar1=mid, scalar2=None, op0=ALU.is_ge, op1=ALU.add, accum_out=cntc), reads=[score, mid_t], writes=[mjD, cnt_t])
                    T.op("act", lambda e: e.activation(out=XM8[:, 0:nA], in_=score.t[:, WD:W], func=AF.Sign, bias=mid, scale=-1.0, accum_out=sa), reads=[score, mid_t], writes=[mjA, sa_t])
                    T.op("dve", lambda e: e.scalar_tensor_tensor(out=tt, in0=cntc, scalar=2.0, in1=sa, op0=ALU.mult, op1=ALU.subtract), reads=[cnt_t, sa_t], writes=[t_t])
                    T.op("dve", lambda e: e.tensor_scalar(out=gg, in0=tt, scalar1=float(2 * TOPK - 1 - nA), scalar2=-0.5, op0=ALU.is_ge, op1=ALU.add), reads=[t_t], writes=[g_t])
                    T.op("dve", lambda e, hk=hk: e.scalar_tensor_tensor(out=mid, in0=gg, scalar=hk, in1=mid, op0=ALU.mult, op1=ALU.add), reads=[g_t, mid_t], writes=[mid_t])
                hN = CLAMP * (0.5 ** (NITER + 1))
                T.op("dve", lambda e: e.tensor_scalar(out=lo, in0=mid, scalar1=-hN, scalar2=None, op0=ALU.add), reads=[mid_t], writes=[lo_t])
                T.op("dve", lambda e: e.tensor_scalar(out=XM.t[:, 0:W], in0=score.t[:, 0:W], scalar1=lo, scalar2=None, op0=ALU.is_ge), reads=[score, lo_t], writes=[mjD, mjA])
                dump("thr%d_%d" % (bi, i), lo, [P, 1], F32, [lo_t])
                dump("score%d_%d" % (bi, i), score.t[:, 0:W], [P, W], F32, [score])

            def mask_transposes(j2):
                for kb0 in range(0, nkb, 8):
                    n = min(8, nkb - kb0)
                    ps = psG.next()
                    for t in range(n):
                        T.op("pe", lambda e, ps=ps, t=t, kb0=kb0: e.transpose(out=psb(ps)[:, t, :], in_=XM.t[:, (kb0 + t) * P:(kb0 + t + 1) * P], identity=ident_b[:]), reads=[mjD, mjA, ident_b], writes=[ps], inc=(t == n - 1))
                    T.op("act", lambda e, ps=ps, n=n, kb0=kb0, j2=j2: e.copy(out=maskT.t[:, kb0:kb0 + n, hb * 256 + j2 * P:hb * 256 + (j2 + 1) * P], in_=psb(ps)[:, 0:n, :]), reads=[ps], writes=[maskT])

            indexer(0)
            bisect(0)
            indexer(1)
            mask_transposes(0)
            bisect(1)
            mask_transposes(1)
            T.barrier()
            SI.close()


        def attention_block(S, bi, key0, qT, y_aT, maskT):
            nkbs = [(key0 + 256 * (hb + 1)) // P for hb in range(2)]
            nkb = nkbs[1]
            LA2 = 6
            wga = Rot([sbg("wga%d" % i, [P, 16, P], BF16, S) for i in range(1)])
            gaT = Rot([sbg("gaT%d" % i, [P, TB], F32, S) for i in range(2)])
            gex = Rot([sbg("gex%d" % i, [P, TB], F32, S) for i in range(1)])
            ktb = Rot([sbg("ktb%d" % i, [P, 2048], BF16, S) for i in range(2)])
            vtb = Rot([sbg("vtb%d" % i, [P, 16, 129], BF16, S) for i in range(2)])
            eb = Rot([sbg("eb%d" % i, [P, 512], BF16, S) for i in range(4)])
            pmb = Rot([sbg("pmb%d" % i, [P, 512], BF16, S) for i in range(8)])
            onb = Rot([sbg("onb%d" % i, [P, P], BF16, S) for i in range(4)])
            rdr = Rot([sbg("rd%d" % i, [P, 4], F32, S) for i in range(2)])
            psQ = Rot(PS[0:3])
            psX = Rot(PS[3:4])
            pend = []
            for hh in range(16):
                ga = gaT.next(pend)
                O = [psO.next(pend) for _ in range(4)]
                wg = wga.next()
                T.dma("sp", wg[:, :, :], Wb[:, OFF_GA + hh * P:OFF_GA + (hh + 1) * P].rearrange("(kc p) c -> p kc c", p=P), writes=[wg], slot=wg.name)
                psg = psX.next()
                mm_group(psg.t[:, :], [(wg.t[:, kc, :], hT.t[:, kc, :]) for kc in range(16)], [wg, hT], psg)
                gx = gex.next()
                T.op("act", lambda e, gx=gx, psg=psg: e.activation(out=gx[:, :], in_=psg.t[:, :], func=AF.Exp, scale=-1.0), reads=[psg], writes=[gx])
                T.op("dve", lambda e, gx=gx: e.tensor_scalar(out=gx[:, :], in0=gx[:, :], scalar1=1.0, scalar2=None, op0=ALU.add), reads=[gx], writes=[gx])
                T.op("dve", lambda e, gx=gx: e.reciprocal(out=gx[:, :], in_=gx[:, :]), reads=[gx], writes=[gx])
                T.op("dve", lambda e, ga=ga, gx=gx, psg=psg: e.tensor_tensor(out=ga[:, :], in0=psg.t[:, :], in1=gx[:, :], op=ALU.mult), reads=[psg, gx], writes=[ga])
                for c2 in range((nkb + 15) // 16):
                    kb_lo = c2 * 16
                    n = min(16, nkb - kb_lo)
                    kt = ktb.next(pend)
                    vt = vtb.next(pend)
                    T.dma("sp", kt[:, 0:n * P], KT[hh, :, kb_lo * P:(kb_lo + n) * P], writes=[kt], slot=kt.name)
                    T.dma("sp", vt[:, 0:n, :], VS[hh, c2, :, 0:n, :], writes=[vt], slot=vt.name)
                    for t in range(0, n, 2):
                        kb = kb_lo + t
                        for hb in range(2):
                            if kb >= nkbs[hb]:
                                continue
                            qsl = slice(hb * 256, (hb + 1) * 256)
                            ps = psQ.next()
                            for u in range(2):
                                T.op("pe", lambda e, ps=ps, kt=kt, t=t, u=u, hh=hh, qsl=qsl: e.matmul(ps.t[:, u * 256:(u + 1) * 256], lhsT=kt.t[:, (t + u) * P:(t + u + 1) * P], rhs=qT.t[:, hh, qsl], start=True, stop=True), reads=[kt, qT], writes=[ps], inc=(u == 1))
                            eb_ = eb.next()
                            T.op("act", lambda e, eb_=eb_, ps=ps: e.activation(out=eb_[:, :], in_=ps.t[:, :], func=AF.Exp), reads=[ps], writes=[eb_])
                            pm = pmb.next(pend)
                            T.op("dve", lambda e, pm=pm, eb_=eb_, kb=kb, qsl=qsl: e.tensor_tensor(out=pm.t[:, :].rearrange("p (a b) -> p a b", b=256), in0=eb_.t[:, :].rearrange("p (a b) -> p a b", b=256), in1=maskT.t[:, kb:kb + 2, qsl], op=ALU.mult), reads=[eb_, maskT], writes=[pm])

                            def pv(pm=pm, vt=vt, t=t, kb=kb, O=O, hb=hb):
                                for u in range(2):
                                    for j2 in range(2):
                                        T.op("pe", lambda e, u=u, j2=j2: e.matmul(O[2 * hb + j2].t[:, 0:129], lhsT=pm.t[:, u * 256 + j2 * P:u * 256 + (j2 + 1) * P], rhs=vt.t[:, t + u, :], start=(kb + u == 0), stop=(kb + u == nkbs[hb] - 1)),
                                             reads=[pm, vt], writes=[O[2 * hb + j2]], inc=(u == 1 and j2 == 1))
                            defer(pend, pv, [pm, vt] + O)
                            run_pipe(pend, LA2)

                def finalize(O=O, ga=ga, hh=hh):
                    pst = psX.next()
                    rd = rdr.next()
                    for jt in range(4):
                        T.op("dve", lambda e, jt=jt: e.reciprocal(out=rd.t[:, jt:jt + 1], in_=O[jt].t[:, 128:129]), reads=[O[jt]], writes=[rd])
                        ob = onb.next()
                        T.op("dve", lambda e, jt=jt, ob=ob: e.tensor_scalar(out=ob[:, :], in0=O[jt].t[:, 0:P], scalar1=rd.t[:, jt:jt + 1], scalar2=None, op0=ALU.mult), reads=[O[jt], rd], writes=[ob])
                        T.op("pe", lambda e, jt=jt, ob=ob: e.transpose(out=psb(pst)[:, jt, :], in_=ob[:, :], identity=ident_b[:]), reads=[ob, ident_b], writes=[pst])
                    T.op("dve", lambda e: e.tensor_tensor(out=y_aT.t[:, hh, :], in0=psb(pst)[:, 0:4, :].rearrange("p a b -> p (a b)"), in1=ga[:, :], op=ALU.mult), reads=[pst, ga], writes=[y_aT])
                defer(pend, finalize, [ga] + O)
            run_pipe(pend, 0)

        def out_block2(S, bi, orow0, y_aT):
            y_mT = sbg("y_mT2", [P, 16, TB], BF16, S)
            T.dma("sp", y_mT.t[:, :, :].rearrange("p a b -> p (a b)"), YM, writes=[y_mT], slot="ymld")
            mT = sbg("mT", [P, 16, TB], BF16, S)
            r = sbg("r", [P, NT, D], F32, S)
            nfb = sbg("nfb", [P, D], F32, S)
            stashb = sbg("stash", [P, 4, TB], F32, S)
            gm = Rot([sbg("gm%d" % i, [P, TB], F32, S) for i in range(2)])
            m2 = Rot([sbg("m2_%d" % i, [P, TB], F32, S) for i in range(2)])
            st3 = sbg("st3", [P, 8], F32, S)
            T.dma("sp", nfb[:, :], normf.partition_broadcast(P), writes=[nfb], slot="nfb")
            for i in range(NT):
                T.dma("sp", r.t[:, i, :], xo[orow0 + i * P:orow0 + (i + 1) * P, :], writes=[r], slot="rld")
            for cg in range(4):
                plan = ((Wbr, cg * 512, 0), (Wb, OFF_G + cg * 512, 0), (Wbr, cg * 512, D), (Wb, OFF_G + D + cg * 512, 0))
                for half in range(2):
                    wA = wt_rot.next()
                    load_w(plan[2 * half][0], plan[2 * half][1], 512, wA, rows0=plan[2 * half][2])
                    ysrc = y_mT if half == 0 else y_aT
                    psas = []
                    for ci in range(4):
                        psa = psA.next()
                        mm_group(psa.t[:, :], [(wA.t[:, kc, ci * P:(ci + 1) * P], ysrc.t[:, kc, :]) for kc in range(16)], [wA, ysrc], psa)
                        psas.append(psa)
                    wG = wt_rot.next()
                    load_w(plan[2 * half + 1][0], plan[2 * half + 1][1], 512, wG, rows0=plan[2 * half + 1][2])
                    for ci in range(4):
                        c = cg * 4 + ci
                        psa = psas[ci]
                        psg = psA.next()
                        mm_group(psg.t[:, :], [(wG.t[:, kc, ci * P:(ci + 1) * P], hT.t[:, kc, :]) for kc in range(16)], [wG, hT], psg)
                        gb = gm.next()
                        gcol = C_GB + half * 16 + c
                        T.op("act", lambda e, gb=gb, psg=psg, gcol=gcol: e.activation(out=gb[:, :], in_=psg.t[:, :], func=AF.Sigmoid, bias=CST.t[:, gcol:gcol + 1]), reads=[psg, CST], writes=[gb])
                        if half == 0:
                            T.op("dve", lambda e, psa=psa, gb=gb, ci=ci: e.tensor_tensor(out=stashb.t[:, ci, :], in0=psa.t[:, :], in1=gb[:, :], op=ALU.mult), reads=[psa, gb], writes=[stashb])
                        else:
                            mb2 = m2.next()
                            T.op("dve", lambda e, mb2=mb2, psa=psa, gb=gb: e.tensor_tensor(out=mb2[:, :], in0=psa.t[:, :], in1=gb[:, :], op=ALU.mult), reads=[psa, gb], writes=[mb2])
                            T.op("pool", lambda e, mb2=mb2, c=c, ci=ci: e.tensor_tensor(out=mT.t[:, c, :], in0=mb2[:, :], in1=stashb.t[:, ci, :], op=ALU.add), reads=[mb2, stashb], writes=[mT])
            dump("mT%d" % bi, mT.t[:, :, :].rearrange("p a b -> p (a b)"), [P, 16 * TB], BF16, [mT])
            for n in range(4):
                wo = wt_rot.next()
                load_w(Wo, n * 512, 512, wo)
                for i in range(NT):
                    ps = psA.next()
                    mm_group(ps.t[:, :], [(mT.t[:, kc, i * P:(i + 1) * P], wo.t[:, kc, :]) for kc in range(16)], [mT, wo], ps)
                    T.op("dve", lambda e, ps=ps, i=i, n=n: e.tensor_tensor(out=r.t[:, i, n * 512:(n + 1) * 512], in0=ps.t[:, :], in1=r.t[:, i, n * 512:(n + 1) * 512], op=ALU.add), reads=[ps, r], writes=[r])
            for i in range(NT):
                T.op("act", lambda e, i=i: e.activation(out=xs[:], in_=r.t[:, i, :], func=AF.Square, accum_out=st3[:, 0:1]), reads=[r], writes=[xs, st3])
                rstd_from_ssq(st3[:, 0:1], st3[:, 1:2], D, st3)
                T.op("dve", lambda e, i=i: e.scalar_tensor_tensor(out=r.t[:, i, :], in0=r.t[:, i, :], scalar=st3[:, 1:2], in1=nfb[:, :], op0=ALU.mult, op1=ALU.mult), reads=[r, st3, nfb], writes=[r])
                T.dma("pool", out[orow0 + i * P:orow0 + (i + 1) * P, :], r.t[:, i, :], reads=[r], slot="ost")

        for bi in range(NPB + NOB):
            is_own = bi >= NPB
            key0 = bi * TB
            xsrc, row0 = (xo, (bi - NPB) * TB) if is_own else (xp, bi * TB)
            build_hT(xsrc, row0)
            if bi == NPB:
                dump("hT", hT.t[:, :, :].rearrange("p a b -> p (a b)"), [P, 16 * TB], BF16, [hT])
            with ExitStack() as S1:
                y_mT = ssd_block(S1, is_own, bi)
                if is_own:
                    T.dma("pool", YM, y_mT.t[:, :, :].rearrange("p a b -> p (a b)"), reads=[y_mT], slot="ymst")
                if bi == NPB - 1:
                    T.op("dve", lambda e: e.tensor_scalar(out=hs.t[:, :, :], in0=hs.t[:, :, :], scalar1=CST.t[:, C_PV:C_PV + 1], scalar2=None, op0=ALU.mult), reads=[hs, CST], writes=[hs])
                T.barrier()
            if not is_own:
                with ExitStack() as S2:
                    kv_proj(S2, key0, False, None, None, None)
                    T.barrier()
                continue
            with ExitStack() as SO:
                y_aT = sbg("y_aT", [P, 16, TB], BF16, SO)
                with ExitStack() as S2:
                    qT = sbg("qT", [P, 16, TB], BF16, S2)
                    qiT = sbg("qiT", [P, 8, TB], BF16, S2)
                    ws = sbg("ws", [P, NT, 16], F32, S2)
                    with ExitStack() as S2a:
                        kv_proj(S2a, key0, True, qT, qiT, ws)
                        T.barrier()
                    with ExitStack() as S2b:
                        maskT = sbg("maskT", [P, 64, TB], mybir.dt.uint8, S2b)
                        for hb in range(2):
                            dsa_idx(S2b, bi, key0, hb, qiT, ws, maskT)
                        attention_block(S2b, bi, key0, qT, y_aT, maskT)
                        T.barrier()
                dump("yaT%d" % bi, y_aT.t[:, :, :].rearrange("p a b -> p (a b)"), [P, 16 * TB], BF16, [y_aT])
                with ExitStack() as S3:
                    out_block2(S3, bi, (bi - NPB) * TB, y_aT)
                    T.barrier()
        T.barrier()
        T.emit()
    return nc, dbg_out


def make_cst(inputs, half):
    c = np.zeros((P, NCST), np.float32)
    c[:, C_ID:C_ID + 128] = np.eye(128, dtype=np.float32)
    u = np.arange(128)
    c[:, C_L:C_L + 128] = (u[:, None] <= u[None, :]).astype(np.float32)
    c[:, C_U:C_U + 128] = (u[:, None] > u[None, :]).astype(np.float32)
    cw = np.asarray(inputs["conv_w"][0], np.float32)
    c[:, C_CW:C_CW + 96] = cw.T.reshape(24, 128, 4).transpose(1, 0, 2).reshape(128, 96)
    c[:, C_CB:C_CB + 24] = np.asarray(inputs["conv_b"][0], np.float32).reshape(24, 128).T
    c[:, C_GB:C_GB + 32] = np.asarray(inputs["gate_bias"][0], np.float32).reshape(32, 128).T
    c[:, C_G:C_G + 16] = np.asarray(inputs["norm_in"][0], np.float32).reshape(16, 128).T
    c[:, C_GN:C_GN + 16] = np.asarray(inputs["gn_m"][0], np.float32).reshape(16, 128).T
    c[:, C_DTB:C_DTB + 32] = np.asarray(inputs["dt_bias"][0], np.float32)[None, :]
    c[:, C_ALOG:C_ALOG + 32] = np.asarray(inputs["a_log"][0], np.float32)[None, :]
    c[:, C_DSK:C_DSK + 32] = np.asarray(inputs["d_skip"][0], np.float32)[None, :]
    c[:, C_PV] = 1.0 if half == 1 else 0.0
    c[:, C_CVEC:C_CVEC + NITER] = (0.5 ** np.arange(1, NITER + 1, dtype=np.float64)).astype(np.float32)[None, :]
    q = np.arange(128)[:, None]
    k = np.arange(128)[None, :]
    diag = np.where((q < 64) & (k >= 64), NEG, 0.0).astype(np.float32)
    full = np.full((128, 128), NEG, np.float32)
    zero = np.zeros((128, 128), np.float32)
    c[:, C_BP:C_BP + 256] = np.concatenate([diag, full], 1)
    c[:, C_BP + 256:C_BP + 512] = np.concatenate([zero, diag], 1)
    c[0, C_PVROW:C_PVROW + 128] = 0.0 if half == 1 else NEG
    return c


_NC_CACHE = {}


def run(inputs, L, dbg=()):
    B = inputs["x"].shape[0]
    assert B * 2 == 8
    H = L // 2
    NPB = NOB = H // TB
    key = (NPB, NOB, tuple(dbg))
    if key not in _NC_CACHE:
        _NC_CACHE[key] = build(NPB, NOB, dbg)
    nc, dbg_out = _NC_CACHE[key]
    x = np.asarray(inputs["x"], np.float32)
    w_in = np.ascontiguousarray(np.asarray(inputs["w_in"][0], np.float32))
    w_br = np.ascontiguousarray(np.asarray(inputs["w_branch"][0], np.float32))
    w_out = np.ascontiguousarray(np.asarray(inputs["w_out"][0], np.float32))
    normf = np.asarray(inputs["norm_final"], np.float32).reshape(1, D)
    csts = [make_cst(inputs, 0), make_cst(inputs, 1)]
    zeros = np.zeros((H, D), np.float32)
    in_maps = []
    for c in range(8):
        b, half = c // 2, c % 2
        in_maps.append({
            "xp": np.ascontiguousarray(x[b, 0:H]) if half == 1 else zeros,
            "xo": np.ascontiguousarray(x[b, half * H:(half + 1) * H]),
            "w_in": w_in, "w_br": w_br, "w_out": w_out,
            "cst": csts[half], "normf": normf,
        })
    res = run_bass_kernel_spmd(nc, in_maps, core_ids=list(range(8)))
    outp = np.empty((B, L, D), np.float32)
    for c in range(8):
        b, half = c // 2, c % 2
        outp[b, half * H:(half + 1) * H] = res.results[c]["out"]
    return outp, res


def kernel(**inputs):
    L = inputs["x"].shape[1]
    outp, _ = run(inputs, L)
    return outp
```

```python
import numpy as np
import ml_dtypes
from contextlib import ExitStack
import concourse.bass as bass
import concourse.mybir as mybir
from concourse.bass_utils import run_bass_kernel_spmd

F32 = mybir.dt.float32
BF16 = mybir.dt.bfloat16
AF = mybir.ActivationFunctionType
ALU = mybir.AluOpType
AX = mybir.AxisListType

P = 128
D = 2048
TB = 512
NT = 4
EPS = 1e-6
OFF_Z, OFF_XBC, OFF_DT, OFF_Q, OFF_K, OFF_V, OFF_GA, OFF_QI, OFF_KI, OFF_WI, OFF_G = (
    0, 2048, 5120, 5152, 7200, 9248, 11296, 13344, 14368, 14432, 14448)
D_IN = 18544
ATTN_SCALE = 128 ** -0.5
IDX_SCALE = (64 ** -0.5) * (16 ** -0.5)
TOPK = 256
NITER = 21
CLAMP = 8.0
NEG = -30000.0

C_ID, C_L, C_U, C_CW, C_CB, C_GB, C_G, C_GN, C_DTB, C_ALOG, C_DSK, C_PV, C_CVEC, C_BP, C_PVROW, NCST = (
    0, 128, 256, 384, 480, 504, 536, 552, 568, 600, 632, 664, 672, 704, 1216, 1344)


class Buf:
    def __init__(self, t, name=""):
        self.t = t
        self.name = name
        self.w = None
        self.r = []
        self.alias = []
        self.pending = 0

    def __getitem__(self, k):
        return self.t[k]


class Trk:
    LIM = 30000

    def __init__(self, nc, es):
        self.nc = nc
        self.es = es
        self.engs = {"pe": nc.tensor, "act": nc.scalar, "dve": nc.vector, "pool": nc.gpsimd, "sp": nc.sync}
        self.ctr = {}
        self.prog = {k: [] for k in self.engs}
        self.known = {k: {} for k in self.engs}
        self.nsem = 0
        for k in self.engs:
            self._newctr(k, 1)

    def _newsem(self, name):
        self.nsem += 1
        return self.es.enter_context(self.nc.semaphore("%s_%d" % (name, self.nsem)))

    def _newctr(self, name, step):
        self.ctr[name] = {"step": step, "ep": 0, "cnt": 0, "sems": {0: self._newsem(name)}}

    def _next_ms(self, name):
        c = self.ctr[name]
        if (c["cnt"] + 1) * c["step"] > self.LIM:
            return (name, c["ep"] + 1, 1)
        return (name, c["ep"], c["cnt"] + 1)

    def _inc(self, name):
        c = self.ctr[name]
        ms = self._next_ms(name)
        if ms[1] != c["ep"]:
            c["ep"] = ms[1]
            c["sems"][ms[1]] = self._newsem(name)
        c["cnt"] = ms[2]
        return ms, c["sems"][c["ep"]], c["step"]

    def _cur_ms(self, name):
        c = self.ctr[name]
        return (name, c["ep"], c["cnt"])

    def _need(self, e, deps):
        need = {}
        for d in deps:
            if d is None:
                continue
            name, ep, idx = d
            if idx == 0:
                continue
            k = (name, ep)
            if idx > need.get(k, 0):
                need[k] = idx
        for (name, ep), idx in need.items():
            c = self.ctr[name]
            if name == e and (ep, idx) > (c["ep"], c["cnt"]):
                continue
            if self.known[e].get((name, ep), 0) >= idx:
                continue
            later = [kk for kk in self.known[e] if kk[0] == name and kk[1] > ep]
            if later:
                continue
            self.prog[e].append((lambda en, v_=idx * c["step"], nm=(name, ep): en.wait_ge(self._sem_of(nm), v_)))
            self.known[e][(name, ep)] = idx

    def _sem_of(self, nm):
        return self.ctr[nm[0]]["sems"][nm[1]]

    def _deps(self, reads, writes):
        deps = []
        for b in reads:
            deps.append(b.w)
            for a in b.alias:
                deps.append(a.w)
        for b in writes:
            deps.append(b.w)
            deps.extend(b.r)
            for a in b.alias:
                deps.append(a.w)
                deps.extend(a.r)
        return deps

    def _mark(self, me, reads, writes):
        for b in reads:
            b.r.append(me)
        for b in writes:
            b.w = me
            b.r = []

    def op(self, e, fn, reads=(), writes=(), inc=True):
        self._need(e, self._deps(reads, writes))
        if inc:
            me, sem, step = self._inc(e)
            self.prog[e].append((lambda en, f_=fn, s_=sem: f_(en).then_inc(s_, 1)))
        else:
            me = self._next_ms(e)
            self.prog[e].append(fn)
        self._mark(me, reads, writes)

    def dma(self, q, out, in_, reads=(), writes=(), slot=None, **kw):
        self._need(q, self._deps(reads, writes))
        name = "d:" + slot
        if name not in self.ctr:
            self._newctr(name, 16)
        me, sem, step = self._inc(name)
        self.prog[q].append((lambda en, o_=out, i_=in_, k_=kw, s_=sem: en.dma_start(out=o_, in_=i_, **k_).then_inc(s_, 16)))
        self._mark(me, reads, writes)

    def barrier(self):
        allms = [self._cur_ms(n) for n in self.ctr]
        for e in self.engs:
            self._need(e, [m for m in allms if m[0] != e])

    def emit(self):
        with self.nc.Block() as block:
            @block.tensor
            def _(en):
                for f in self.prog["pe"]:
                    f(en)

            @block.scalar
            def _(en):
                for f in self.prog["act"]:
                    f(en)

            @block.vector
            def _(en):
                for f in self.prog["dve"]:
                    f(en)

            @block.gpsimd
            def _(en):
                for f in self.prog["pool"]:
                    f(en)

            @block.sync
            def _(en):
                for f in self.prog["sp"]:
                    f(en)


class Rot:
    def __init__(self, items):
        self.items = items
        self.i = 0

    def next(self, pend=None):
        b = self.items[self.i % len(self.items)]
        while b.pending > 0 and pend:
            pend.pop(0)()
        assert b.pending == 0, "rotation reuses %s while deferred users pending" % b.name
        self.i += 1
        return b


def defer(pend, fn, bufs):
    for b in bufs:
        b.pending += 1

    def w():
        fn()
        for b in bufs:
            b.pending -= 1
    pend.append(w)


def build(NPB, NOB, dbg=()):
    NK = (NPB + NOB) * TB
    nc = bass.Bass("TRN2", target_bir_lowering=False)
    dt_in = lambda n, s, d=F32: nc.dram_tensor(n, s, d, kind="ExternalInput").ap()
    xp = dt_in("xp", [max(NPB, 1) * TB, D])
    xo = dt_in("xo", [NOB * TB, D])
    w_in = dt_in("w_in", [D, D_IN])
    w_br = dt_in("w_br", [2 * D, D])
    w_out = dt_in("w_out", [D, D])
    cst = dt_in("cst", [P, NCST])
    normf = dt_in("normf", [1, D])
    out = nc.dram_tensor("out", [NOB * TB, D], F32, kind="ExternalOutput").ap()
    dti = lambda n, s, d=BF16: nc.dram_tensor(n, s, d, kind="Internal").ap()
    Wb = dti("Wb", [D, D_IN])
    Wbr = dti("Wbr", [2 * D, D])
    Wo = dti("Wo", [D, D])
    KT = dti("KT", [16, P, NK])
    NCH = (NK // P + 15) // 16
    VS = dti("VS", [16, NCH, P, 16, 129])
    KI = dti("KI", [2, P, NK])
    YM = dti("YM", [P, 16 * TB])
    dbg_out = {}

    with ExitStack() as es:
        T = Trk(nc, es)

        uniq = [0]

        def sbg(name, shape, dt, stack=es):
            uniq[0] += 1
            return Buf(stack.enter_context(nc.sbuf_tensor("%s_u%d" % (name, uniq[0]), shape, dt)), name)

        CST = sbg("CST", [P, NCST], F32)
        ident_b = sbg("ident_b", [P, P], BF16)
        bpat_b = sbg("bpat_b", [P, 2, 256], BF16)
        pvm = sbg("pvm", [P, P], BF16)
        ones_b = sbg("ones_b", [P, 512], BF16)
        ones_f = sbg("ones_f", [P, P], F32)
        aneg = sbg("aneg", [P, 32], F32)
        eps_t = sbg("eps_t", [P, 2], F32)
        hT = sbg("hT", [P, 16, TB], BF16)
        WT = sbg("WT", [P, 2 * 16 * 512], BF16)
        XM = sbg("XM", [P, 8192], BF16)
        xs = sbg("xs", [P, D], BF16)
        hs = sbg("hs", [P, 4, 512], F32)
        halo = sbg("halo", [P, 24, 3], F32)
        st = sbg("st", [P, 16], F32)
        PS = [Buf(es.enter_context(nc.psum_tensor("ps%d" % i, [P, 512], F32)), "ps%d" % i) for i in range(8)]
        psA = Rot(PS)
        psG = Rot(PS[0:4])
        psO = Rot(PS[4:8])

        Wt = [Buf(WT.t[:, i * 8192:(i + 1) * 8192].rearrange("p (k c) -> p k c", c=512), "Wt%d" % i) for i in range(2)]
        score = Buf(WT.t[:].bitcast(F32), "score")
        score.alias = [Wt[0], Wt[1]]
        for w in Wt:
            w.alias = [score]
        xtb = [Buf(XM.t[:, i * 4096:(i + 1) * 4096].bitcast(F32), "xt%d" % i) for i in range(2)]
        mjD = Buf(XM.t[:, 0:4096], "mjD")
        mjA = Buf(XM.t[:, 4096:8192], "mjA")
        mjD.alias = [xtb[0]]
        mjA.alias = [xtb[1]]
        xtb[0].alias = [mjD]
        xtb[1].alias = [mjA]
        XM8 = XM.t[:, 4096:8192].bitcast(mybir.dt.int8)
        wt_rot = Rot(Wt)
        xt_rot = Rot(xtb)

        ident_f = CST.t[:, C_ID:C_ID + 128]
        Lmask = CST.t[:, C_L:C_L + 128]
        Ustr = CST.t[:, C_U:C_U + 128]

        def cs(c0, n):
            return CST.t[:, c0:c0 + n]

        def psb(b):
            return b.t[:].bitcast(BF16).rearrange("p (a b) -> p a b", b=128)

        def dump(name, ap, shape, dtype, rd):
            if name not in dbg:
                return
            o = nc.dram_tensor("dbg_" + name, list(shape), dtype, kind="ExternalOutput").ap()
            dbg_out[name] = o
            T.dma("sp", o, ap, reads=rd, slot="dbg")

        T.dma("sp", CST[:], cst, writes=[CST], slot="cst")
        T.op("dve", lambda e: e.tensor_copy(out=ident_b[:], in_=ident_f), reads=[CST], writes=[ident_b])
        T.op("dve", lambda e: e.tensor_copy(out=bpat_b[:], in_=cs(C_BP, 512).rearrange("p (a b) -> p a b", b=256)), reads=[CST], writes=[bpat_b])
        T.op("dve", lambda e: e.tensor_copy(out=pvm[:], in_=cs(C_PVROW, 128)), reads=[CST], writes=[pvm])
        T.op("dve", lambda e: e.memset(ones_b[:], 1.0), writes=[ones_b])
        T.op("pool", lambda e: e.memset(ones_f[:], 1.0), writes=[ones_f])
        T.op("pool", lambda e: e.memset(eps_t[:], EPS), writes=[eps_t])
        T.op("pool", lambda e: e.memset(hs[:], 0.0), writes=[hs])
        T.op("pool", lambda e: e.memset(halo[:], 0.0), writes=[halo])
        T.op("act", lambda e: e.activation(out=aneg[:], in_=cs(C_ALOG, 32), func=AF.Exp), reads=[CST], writes=[aneg])
        T.op("dve", lambda e: e.tensor_scalar(out=aneg[:], in0=aneg[:], scalar1=-1.0, scalar2=None, op0=ALU.mult), reads=[aneg], writes=[aneg])

        with ExitStack() as s0:
            fb = Rot([sbg("pcf%d" % i, [P, 2048], F32, s0) for i in range(3)])
            bb = Rot([sbg("pcb%d" % i, [P, 2048], BF16, s0) for i in range(3)])
            cnt = 0

            def cast_piece(src_ap, dst_ap, n, scal):
                nonlocal cnt
                f = fb.next()
                b = bb.next()
                T.dma("sp", f[:, 0:n], src_ap, writes=[f], slot=f.name)
                eng = "dve" if cnt % 2 == 0 else "pool"
                if scal is None:
                    T.op(eng, lambda e: e.tensor_copy(out=b[:, 0:n], in_=f[:, 0:n]), reads=[f], writes=[b])
                else:
                    T.op(eng, lambda e: e.tensor_scalar(out=b[:, 0:n], in0=f[:, 0:n], scalar1=scal, scalar2=1.0, op0=ALU.mult, op1=ALU.mult), reads=[f, CST], writes=[b])
                T.dma("act", dst_ap, b[:, 0:n], reads=[b], slot=b.name)
                cnt += 1

            for kc in range(16):
                for c0 in range(0, D_IN, 2048):
                    n = min(2048, D_IN - c0)
                    cast_piece(w_in[kc * P:(kc + 1) * P, c0:c0 + n], Wb[kc * P:(kc + 1) * P, c0:c0 + n], n, CST.t[:, C_G + kc:C_G + kc + 1])
            for kc in range(32):
                cast_piece(w_br[kc * P:(kc + 1) * P, :], Wbr[kc * P:(kc + 1) * P, :], 2048, None)
            for kc in range(16):
                cast_piece(w_out[kc * P:(kc + 1) * P, :], Wo[kc * P:(kc + 1) * P, :], 2048, None)
            zt = sbg("zt", [P, 2048], BF16, s0)
            T.op("pool", lambda e: e.memset(zt[:], 0.0), writes=[zt])
            for c0 in range(0, NK, 2048):
                n = min(2048, NK - c0)
                T.dma("act", KI[0, 64:128, c0:c0 + n], zt[64:128, 0:n], reads=[zt], slot="zfill")
                T.dma("act", KI[1, 0:64, c0:c0 + n], zt[0:64, 0:n], reads=[zt], slot="zfill")
            T.barrier()

        def load_w(src, c0, n, wt, dcol=0, rows0=0):
            T.dma("sp", wt.t[:, :, dcol:dcol + n], src[rows0:rows0 + D, c0:c0 + n].rearrange("(kc p) c -> p kc c", p=P), writes=[wt], slot=wt.name)

        def mm_group(ps_ap, pairs, reads, ps_buf, extra_first_reads=()):
            n = len(pairs)
            for k, (l, r) in enumerate(pairs):
                T.op("pe", lambda e, l=l, r=r, k=k: e.matmul(ps_ap, lhsT=l, rhs=r, start=(k == 0), stop=(k == n - 1)),
                     reads=reads, writes=[ps_buf], inc=(k == n - 1))

        def proj_F(wt, ncols, consumer, rhs_cols=slice(0, TB), nfree=TB):
            for ci in range(ncols // P):
                ps = psA.next()
                mm_group(ps.t[:, 0:nfree], [(wt.t[:, kc, ci * P:(ci + 1) * P], hT.t[:, kc, rhs_cols]) for kc in range(16)], [wt, hT], ps)
                consumer(ci, ps)

        def proj_T(wt, ncols, consumer, tiles=range(NT)):
            for i in tiles:
                ps = psA.next()
                mm_group(ps.t[:, 0:ncols], [(hT.t[:, kc, i * P:(i + 1) * P], wt.t[:, kc, 0:ncols]) for kc in range(16)], [wt, hT], ps)
                consumer(i, ps)

        def run_pipe(pend, nkeep):
            while len(pend) > nkeep:
                pend.pop(0)()

        evac_flip = [0]

        def evac(out_ap, in_ap, reads, writes, scale=None):
            evac_flip[0] ^= 1
            if evac_flip[0]:
                if scale is None:
                    T.op("act", lambda e: e.copy(out=out_ap, in_=in_ap), reads=reads, writes=writes)
                else:
                    T.op("act", lambda e: e.activation(out=out_ap, in_=in_ap, func=AF.Copy, scale=scale), reads=reads, writes=writes)
            else:
                if scale is None:
                    T.op("dve", lambda e: e.tensor_copy(out=out_ap, in_=in_ap), reads=reads, writes=writes)
                else:
                    T.op("dve", lambda e: e.tensor_scalar(out=out_ap, in0=in_ap, scalar1=scale, scalar2=None, op0=ALU.mult), reads=reads, writes=writes)

        def rstd_from_ssq(ssq_ap, out_ap, n, buf):
            T.op("act", lambda e: e.activation(out=out_ap, in_=ssq_ap, func=AF.Ln, scale=1.0 / n, bias=eps_t[:, 0:1]), reads=[buf, eps_t], writes=[buf])
            T.op("act", lambda e: e.activation(out=out_ap, in_=out_ap, func=AF.Exp, scale=-0.5), reads=[buf], writes=[buf])

        def build_hT(xsrc, row0):
            for i in range(NT):
                xt = xt_rot.next()
                T.dma("sp", xt[:, :], xsrc[row0 + i * P:row0 + (i + 1) * P, :], writes=[xt], slot=xt.name)
                T.op("act", lambda e, xt=xt: e.activation(out=xs[:], in_=xt[:, :], func=AF.Square, accum_out=st[:, 0:1]), reads=[xt], writes=[xs, st])
                rstd_from_ssq(st[:, 0:1], st[:, 1:2], D, st)
                T.op("dve", lambda e, xt=xt: e.tensor_scalar(out=xs[:], in0=xt[:, :], scalar1=st[:, 1:2], scalar2=None, op0=ALU.mult), reads=[xt, st], writes=[xs])
                for hb in range(2):
                    ps = psA.next()
                    for k in range(8):
                        kc = hb * 8 + k
                        T.op("pe", lambda e, ps=ps, k=k, kc=kc: e.transpose(out=psb(ps)[:, k, :], in_=xs[:, kc * P:(kc + 1) * P], identity=ident_b[:]),
                             reads=[xs, ident_b], writes=[ps], inc=(k == 7))
                    evac(hT.t[:, hb * 8:(hb + 1) * 8, i * P:(i + 1) * P], psb(ps), [ps], [hT])

        def ssd_block(S, is_own, bi):
            x_tok = sbg("x_tok", [P, NT, D], F32, S)
            BT = sbg("BT", [P, 4, TB], BF16, S)
            CTt = sbg("CTt", [P, 4, TB], BF16, S)
            Btok = sbg("Btok", [P, NT, 4, P], BF16, S)
            dt_all = sbg("dt_all", [P, NT, 32], F32, S)
            sm = sbg("sm", [P, NT, 160], F32, S)
            y_mT = sbg("y_mT", [P, 16, TB], BF16, S) if is_own else None
            SX = ExitStack()
            ut = Rot([sbg("ut%d" % i, [P, 515], F32, SX) for i in range(3)])
            acc_r = Rot([sbg("cacc%d" % i, [P, TB], F32, SX) for i in range(2)])
            xsT = Rot([sbg("xsT%d" % i, [P, TB], F32, SX) for i in range(4)])

            cpend = []
            for grp in range(6):
                wt = wt_rot.next()
                load_w(Wb, OFF_XBC + grp * 512, 512, wt)

                def cons(ci, ps, grp=grp):
                    cg = grp * 4 + ci
                    u = ut.next()
                    T.op("act", lambda e: e.copy(out=u[:, 3:515], in_=ps.t[:, 0:512]), reads=[ps], writes=[u])
                    T.op("pool", lambda e: e.tensor_copy(out=u[:, 0:3], in_=halo[:, cg, :]), reads=[halo], writes=[u])
                    T.op("pool", lambda e: e.tensor_copy(out=halo[:, cg, :], in_=u[:, 512:515]), reads=[u], writes=[halo])
                    acc = acc_r.next()
                    cw = lambda k: CST.t[:, C_CW + cg * 4 + k:C_CW + cg * 4 + k + 1]
                    T.op("dve", lambda e: e.tensor_scalar(out=acc[:, :], in0=u[:, 3:515], scalar1=cw(3), scalar2=None, op0=ALU.mult), reads=[u, CST], writes=[acc])
                    for k in (2, 1, 0):
                        T.op("dve", lambda e, k=k: e.scalar_tensor_tensor(out=acc[:, :], in0=u[:, k:k + 512], scalar=cw(k), in1=acc[:, :], op0=ALU.mult, op1=ALU.add), reads=[u, CST, acc], writes=[acc])
                    cb = CST.t[:, C_CB + cg:C_CB + cg + 1]
                    if cg < 16:
                        xT = xsT.next(cpend)
                        T.op("act", lambda e: e.activation(out=xT[:, :], in_=acc[:, :], func=AF.Silu, bias=cb), reads=[acc, CST], writes=[xT])

                        def later(xT=xT, cg=cg):
                            ps2 = psA.next()
                            for i in range(NT):
                                T.op("pe", lambda e, i=i: e.transpose(out=ps2.t[:, i * P:(i + 1) * P], in_=xT[:, i * P:(i + 1) * P], identity=ident_f),
                                     reads=[xT, CST], writes=[ps2], inc=(i == NT - 1))
                            evac(x_tok.t[:, :, cg * P:(cg + 1) * P], ps2.t[:, :].rearrange("p (a b) -> p a b", b=P), [ps2], [x_tok])
                        defer(cpend, later, [xT])
                    elif cg < 20:
                        g = cg - 16
                        T.op("act", lambda e: e.activation(out=BT.t[:, g, :], in_=acc[:, :], func=AF.Silu, bias=cb), reads=[acc, CST], writes=[BT])

                        def later(g=g):
                            ps2 = psA.next()
                            for i in range(NT):
                                T.op("pe", lambda e, i=i: e.transpose(out=psb(ps2)[:, i, :], in_=BT.t[:, g, i * P:(i + 1) * P], identity=ident_b[:]),
                                     reads=[BT, ident_b], writes=[ps2], inc=(i == NT - 1))
                            evac(Btok.t[:, :, g, :], psb(ps2)[:, 0:NT, :], [ps2], [Btok])
                        defer(cpend, later, [])
                    else:
                        g = cg - 20
                        T.op("act", lambda e: e.activation(out=CTt.t[:, g, :], in_=acc[:, :], func=AF.Silu, bias=cb), reads=[acc, CST], writes=[CTt])
                    run_pipe(cpend, 2)

                proj_F(wt, 512, cons)
            run_pipe(cpend, 0)
            T.barrier()
            SX.close()

            wt = wt_rot.next()
            load_w(Wb, OFF_DT, 32, wt)

            def cons_dt(i, ps):
                T.op("dve", lambda e: e.tensor_tensor(out=dt_all.t[:, i, :], in0=ps.t[:, 0:32], in1=cs(C_DTB, 32), op=ALU.add), reads=[ps, CST], writes=[dt_all])
                T.op("act", lambda e: e.activation(out=dt_all.t[:, i, :], in_=dt_all.t[:, i, :], func=AF.Exp), reads=[dt_all], writes=[dt_all])
                T.op("act", lambda e: e.activation(out=dt_all.t[:, i, :], in_=dt_all.t[:, i, :], func=AF.Ln, bias=1.0), reads=[dt_all], writes=[dt_all])

            proj_T(wt, 32, cons_dt)

            for i in range(NT):
                a_i = sm.t[:, i, 0:32]
                T.op("dve", lambda e, i=i, a_i=a_i: e.tensor_tensor(out=a_i, in0=dt_all.t[:, i, :], in1=aneg[:], op=ALU.mult), reads=[dt_all, aneg], writes=[sm])
                ps = psA.next()
                T.op("pe", lambda e, a_i=a_i, ps=ps: e.matmul(ps.t[:, 0:32], lhsT=Lmask, rhs=a_i, start=True, stop=True), reads=[CST, sm], writes=[ps])
                T.op("pe", lambda e, a_i=a_i, ps=ps: e.matmul(ps.t[:, 32:64], lhsT=ones_f[:], rhs=a_i, start=True, stop=True), reads=[ones_f, sm], writes=[ps])
                T.op("act", lambda e, i=i, ps=ps: e.activation(out=sm.t[:, i, 32:96], in_=ps.t[:, 0:64], func=AF.Exp), reads=[ps], writes=[sm])
                T.op("dve", lambda e, i=i, ps=ps: e.tensor_copy(out=sm.t[:, i, 128:160], in_=ps.t[:, 0:32]), reads=[ps], writes=[sm])
                T.op("dve", lambda e, i=i, ps=ps: e.tensor_tensor(out=sm.t[:, i, 96:128], in0=ps.t[:, 32:64], in1=sm.t[:, i, 128:160], op=ALU.subtract), reads=[ps, sm], writes=[sm])
                T.op("act", lambda e, i=i: e.activation(out=sm.t[:, i, 96:128], in_=sm.t[:, i, 96:128], func=AF.Exp), reads=[sm], writes=[sm])
                T.op("dve", lambda e, i=i: e.tensor_tensor(out=sm.t[:, i, 96:128], in0=sm.t[:, i, 96:128], in1=dt_all.t[:, i, :], op=ALU.mult), reads=[sm, dt_all], writes=[sm])

            xw = Rot([sbg("xw%d" % i, [P, TB], BF16, S) for i in range(2)])
            tmpS = Rot([sbg("tmpS%d" % i, [P, TB], F32, S) for i in range(2)])
            if is_own:
                xdt = Rot([sbg("xdt%d" % i, [P, TB], BF16, S) for i in range(3)])
                hsb = Rot([sbg("hsb%d" % i, [P, TB], BF16, S) for i in range(3)])
                cbm = Rot([sbg("cbm%d" % i, [P, P], F32, S) for i in range(3)])
                Rr = Rot([sbg("Rr%d" % i, [P, 8, P], F32, S) for i in range(2)])
                dec = Rot([sbg("dec%d" % i, [P, 8, P], F32, S) for i in range(2)])
                Mm = Rot([sbg("Mm%d" % i, [P, 8, P], BF16, S) for i in range(3)])
                t1 = Rot([sbg("t1_%d" % i, [P, TB], F32, S) for i in range(3)])
                t2 = Rot([sbg("t2_%d" % i, [P, TB], F32, S) for i in range(2)])
                zs = Rot([sbg("zs%d" % i, [P, TB], F32, S) for i in range(3)])
                yn = Rot([sbg("yn%d" % i, [P, TB], BF16, S) for i in range(3)])
                st2r = Rot([sbg("st2_%d" % i, [P, 8], F32, S) for i in range(2)])
            v3 = lambda b: b.t[:, :].rearrange("p (h q) -> p h q", q=64)
            stageC = []
            stageD = []

            def make_R(g, i):
                Rb = Rr.next()
                a3 = sm.t[:, i, g * 8:(g + 1) * 8].unsqueeze(2).to_broadcast([P, 8, P])
                L3 = Lmask.unsqueeze(1).to_broadcast([P, 8, P])
                T.op("dve", lambda e: e.tensor_tensor(out=Rb[:, :, :], in0=a3, in1=L3, op=ALU.mult), reads=[sm, CST], writes=[Rb])
                return Rb
            Rnext = make_R(0, 0) if is_own else None
            for g in range(4):
                if is_own:
                    wz = wt_rot.next()
                    load_w(Wb, OFF_Z + g * 512, 512, wz)
                for i in range(NT):
                    tsl = slice(i * P, (i + 1) * P)
                    xg3 = x_tok.t[:, i, g * 512:(g + 1) * 512].rearrange("p (h q) -> p h q", q=64)
                    bc = lambda c0, i=i, g=g: sm.t[:, i, c0 + g * 8:c0 + g * 8 + 8].unsqueeze(2).to_broadcast([P, 8, 64])
                    xwb = xw.next()
                    T.op("dve" if not is_own else "pool", lambda e, xwb=xwb, xg3=xg3, bc=bc: e.tensor_tensor(out=v3(xwb), in0=xg3, in1=bc(96), op=ALU.mult), reads=[x_tok, sm], writes=[xwb])
                    if is_own:
                        xdb = xdt.next(stageC)
                        dtb3 = dt_all.t[:, i, g * 8:(g + 1) * 8].unsqueeze(2).to_broadcast([P, 8, 64])
                        T.op("pool", lambda e, xdb=xdb, xg3=xg3, dtb3=dtb3: e.tensor_tensor(out=v3(xdb), in0=xg3, in1=dtb3, op=ALU.mult), reads=[x_tok, dt_all], writes=[xdb])
                        hb_ = hsb.next(stageC)
                        T.op("act", lambda e, hb_=hb_, g=g: e.copy(out=hb_[:, :], in_=hs.t[:, g, :]), reads=[hs], writes=[hb_])
                        Rb = Rnext
                        psc = psA.next()
                        T.op("pe", lambda e, psc=psc, g=g, tsl=tsl: e.matmul(psc.t[:, 0:P], lhsT=BT.t[:, g, tsl], rhs=CTt.t[:, g, tsl], start=True, stop=True), reads=[BT, CTt], writes=[psc])
                        psd = [psA.next(), psA.next()]
                        for hh in range(2):
                            T.op("pe", lambda e, pd=psd[hh], Rb=Rb, hh=hh: e.matmul(pd.t[:, :], lhsT=Ustr, rhs=Rb.t[:, hh * 4:(hh + 1) * 4, :].rearrange("p a b -> p (a b)"), start=True, stop=True), reads=[CST, Rb], writes=[psd[hh]])
                    pss = psA.next()
                    T.op("pe", lambda e, pss=pss, xwb=xwb, i=i, g=g: e.matmul(pss.t[:, :], lhsT=Btok.t[:, i, g, :], rhs=xwb[:, :], start=True, stop=True), reads=[Btok, xwb], writes=[pss])
                    if is_own:
                        psz = psA.next()
                        mm_group(psz.t[:, :], [(hT.t[:, kc, tsl], wz.t[:, kc, :]) for kc in range(16)], [hT, wz], psz)
                        cbb = cbm.next()
                        T.op("dve", lambda e, psc=psc, cbb=cbb: e.tensor_tensor(out=cbb[:, :], in0=psc.t[:, 0:P], in1=Lmask, op=ALU.mult), reads=[psc, CST], writes=[cbb])
                        db = dec.next()
                        for hh in range(2):
                            T.op("act", lambda e, pd=psd[hh], db=db, hh=hh: e.activation(out=db.t[:, hh * 4:(hh + 1) * 4, :].rearrange("p a b -> p (a b)"), in_=pd.t[:, :], func=AF.Exp), reads=[psd[hh]], writes=[db])
                    tS = tmpS.next()
                    hs3 = hs.t[:, g, :].rearrange("p (h q) -> p h q", q=64)
                    T.op("pool" if is_own else "dve", lambda e, tS=tS, hs3=hs3, bc=bc: e.tensor_tensor(out=v3(tS), in0=hs3, in1=bc(64), op=ALU.mult), reads=[hs, sm], writes=[tS])
                    T.op("dve", lambda e, tS=tS, pss=pss, g=g: e.tensor_tensor(out=hs.t[:, g, :], in0=pss.t[:, :], in1=tS[:, :], op=ALU.add), reads=[pss, tS], writes=[hs])
                    if not is_own:
                        continue
                    zb = zs.next(stageC)
                    T.op("act", lambda e, zb=zb, psz=psz: e.activation(out=zb[:, :], in_=psz.t[:, :], func=AF.Silu), reads=[psz], writes=[zb])
                    Mb = Mm.next(stageC)
                    c3 = cbb.t[:, :].unsqueeze(1).to_broadcast([P, 8, P])
                    T.op("dve", lambda e, Mb=Mb, db=db, c3=c3: e.tensor_tensor(out=Mb[:, :, :], in0=db[:, :, :], in1=c3, op=ALU.mult), reads=[db, cbb], writes=[Mb])
                    nxt = (g, i + 1) if i + 1 < NT else (g + 1, 0)
                    if nxt[0] < 4:
                        Rnext = make_R(*nxt)

                    def stC(Mb=Mb, xdb=xdb, hb_=hb_, g=g, i=i, tsl=tsl, xg3=xg3, bc=bc, zb=zb):
                        psy = psA.next()
                        for h in range(8):
                            T.op("pe", lambda e, h=h: e.matmul(psy.t[:, h * 64:(h + 1) * 64], lhsT=Mb.t[:, h, :], rhs=xdb.t[:, h * 64:(h + 1) * 64], start=True, stop=True),
                                 reads=[Mb, xdb], writes=[psy], inc=(h == 7))
                        pso = psA.next()
                        T.op("pe", lambda e: e.matmul(pso.t[:, :], lhsT=CTt.t[:, g, tsl], rhs=hb_[:, :], start=True, stop=True), reads=[CTt, hb_], writes=[pso])
                        t1b = t1.next()
                        t2b = t2.next()
                        st2 = st2r.next()
                        dsk3 = CST.t[:, C_DSK + g * 8:C_DSK + g * 8 + 8].unsqueeze(2).to_broadcast([P, 8, 64])
                        T.op("dve", lambda e: e.tensor_tensor(out=v3(t1b), in0=pso.t[:, :].rearrange("p (h q) -> p h q", q=64), in1=bc(32), op=ALU.mult), reads=[pso, sm], writes=[t1b])
                        T.op("pool", lambda e: e.tensor_tensor(out=v3(t2b), in0=xg3, in1=dsk3, op=ALU.mult), reads=[x_tok, CST], writes=[t2b])
                        T.op("dve", lambda e: e.tensor_tensor(out=t1b[:, :], in0=psy.t[:, :], in1=t1b[:, :], op=ALU.add), reads=[psy, t1b], writes=[t1b])
                        T.op("dve", lambda e: e.tensor_tensor(out=t1b[:, :], in0=t1b[:, :], in1=t2b[:, :], op=ALU.add), reads=[t1b, t2b], writes=[t1b])
                        T.op("dve", lambda e: e.tensor_tensor(out=t1b[:, :], in0=t1b[:, :], in1=zb[:, :], op=ALU.mult), reads=[t1b, zb], writes=[t1b])
                        T.op("act", lambda e: e.activation(out=t2b[:, :], in_=t1b[:, :], func=AF.Square, accum_out=st2[:, 0:1]), reads=[t1b], writes=[t2b, st2])
                        rstd_from_ssq(st2[:, 0:1], st2[:, 1:2], 512, st2)
                        ynb = yn.next(stageD)
                        T.op("dve", lambda e: e.tensor_scalar(out=ynb[:, :], in0=t1b[:, :], scalar1=st2[:, 1:2], scalar2=None, op0=ALU.mult), reads=[t1b, st2], writes=[ynb])

                        def stD(ynb=ynb):
                            pst = psA.next()
                            for j in range(4):
                                T.op("pe", lambda e, j=j: e.transpose(out=psb(pst)[:, j, :], in_=ynb.t[:, j * P:(j + 1) * P], identity=ident_b[:]), reads=[ynb, ident_b], writes=[pst], inc=(j == 3))
                            for j in range(4):
                                c = g * 4 + j
                                T.op("act", lambda e, j=j, c=c: e.activation(out=y_mT.t[:, c, tsl], in_=psb(pst)[:, j, :], func=AF.Copy, scale=CST.t[:, C_GN + c:C_GN + c + 1]), reads=[pst, CST], writes=[y_mT])
                        defer(stageD, stD, [ynb])
                    defer(stageC, stC, [Mb, xdb, hb_, zb])
                    run_pipe(stageD, 1)
                    run_pipe(stageC, 1)
            run_pipe(stageC, 0)
            run_pipe(stageD, 0)
            if is_own:
                dump("x_tok%d" % bi, x_tok.t[:, :, :].rearrange("p a b -> p (a b)"), [P, NT * D], F32, [x_tok])
                dump("dt%d" % bi, dt_all.t[:, :, :].rearrange("p a b -> p (a b)"), [P, NT * 32], F32, [dt_all])
                dump("ymT%d" % bi, y_mT.t[:, :, :].rearrange("p a b -> p (a b)"), [P, 16 * TB], BF16, [y_mT])
            return y_mT

        def kv_proj(S, key0, is_own, qT, qiT, ws):
            ktmp = Rot([sbg("ktmp%d" % i, [P, TB], BF16, S) for i in range(2)])
            vtmp = Rot([sbg("vtmp%d" % i, [P, 4, 129], BF16, S) for i in range(2)])
            for v_ in vtmp.items:
                T.op("pool", lambda e, v_=v_: e.memset(v_[:, :, :], 1.0), writes=[v_])
            for grp in range(4):
                wt = wt_rot.next()
                load_w(Wb, OFF_K + grp * 512, 512, wt)

                def cons(ci, ps, grp=grp):
                    hh = grp * 4 + ci
                    kb_ = ktmp.next()
                    evac(kb_[:, :], ps.t[:, :], [ps], [kb_])
                    T.dma("pool", KT[hh, :, key0:key0 + TB], kb_[:, :], reads=[kb_], slot=kb_.name)
                proj_F(wt, 512, cons)
            for grp in range(4):
                wt = wt_rot.next()
                load_w(Wb, OFF_V + grp * 512, 512, wt)

                def cons(i, ps, grp=grp):
                    vb_ = vtmp.next()
                    evac(vb_.t[:, :, 0:P], ps.t[:, :].rearrange("p (h d) -> p h d", d=P), [ps], [vb_])
                    kbg = key0 // P + i
                    T.dma("pool", VS[grp * 4:(grp + 1) * 4, kbg // 16, :, kbg % 16, :].rearrange("h p d -> p h d"), vb_[:, :, :], reads=[vb_], slot=vb_.name)
                proj_T(wt, 512, cons)
            wt = wt_rot.next()
            load_w(Wb, OFF_KI, 64, wt, dcol=0)
            load_w(Wb, OFF_KI, 64, wt, dcol=64)

            def cons(ci, ps):
                kb_ = ktmp.next()
                evac(kb_[:, :], ps.t[:, :], [ps], [kb_])
                T.dma("pool", KI[0, 0:64, key0:key0 + TB], kb_[0:64, :], reads=[kb_], slot=kb_.name)
                T.dma("pool", KI[1, 64:128, key0:key0 + TB], kb_[64:128, :], reads=[kb_], slot=kb_.name)
            proj_F(wt, 128, cons)
            if not is_own:
                return
            for grp in range(4):
                wt = wt_rot.next()
                load_w(Wb, OFF_Q + grp * 512, 512, wt)

                def cons(ci, ps, grp=grp):
                    evac(qT.t[:, grp * 4 + ci, :], ps.t[:, :], [ps], [qT], scale=ATTN_SCALE)
                proj_F(wt, 512, cons)
            for grp in range(2):
                wt = wt_rot.next()
                load_w(Wb, OFF_QI + grp * 512, 512, wt)

                def cons(ci, ps, grp=grp):
                    evac(qiT.t[:, grp * 4 + ci, :], ps.t[:, :], [ps], [qiT])
                proj_F(wt, 512, cons)
            wt = wt_rot.next()
            load_w(Wb, OFF_WI, 16, wt)

            def cons(i, ps):
                T.op("dve", lambda e: e.tensor_scalar(out=ws.t[:, i, :], in0=ps.t[:, 0:16], scalar1=IDX_SCALE, scalar2=None, op0=ALU.mult), reads=[ps], writes=[ws])
            proj_T(wt, 16, cons)

        def dsa_idx(S, bi, key0, hb, qiT, ws, maskT):
            nkb = (key0 + 256 * (hb + 1)) // P
            W = nkb * P
            nb5 = (W + 511) // 512
            LA = 4
            LA2 = 6
            SI = ExitStack()
            Dgr = Rot([sbg("Dg%d" % i, [P, 16, P], BF16, SI) for i in range(2)])
            Rl = Rot([sbg("Rl%d" % i, [P, 512], BF16, SI) for i in range(6)])
            kib = Rot([sbg("kib%d" % i, [P, 2, 512], BF16, SI) for i in range(3)])

            def indexer(j2):
                i = 2 * hb + j2
                tsl = slice(i * P, (i + 1) * P)
                Dg = Dgr.next()
                pend = []
                for h in range(16):
                    T.op("pool", lambda e, h=h, i=i, Dg=Dg: e.tensor_scalar(out=Dg.t[:, h, :], in0=ident_f, scalar1=ws.t[:, i, h:h + 1], scalar2=1.0, op0=ALU.mult, op1=ALU.mult), reads=[CST, ws], writes=[Dg])
                for b5 in range(nb5):
                    c0 = b5 * 512
                    wd = min(512, W - c0)
                    kb_ = kib.next()
                    T.dma("sp", kb_[:, 0, 0:wd], KI[0, :, c0:c0 + wd], writes=[kb_], slot=kb_.name)
                    T.dma("sp", kb_[:, 1, 0:wd], KI[1, :, c0:c0 + wd], writes=[kb_], slot=kb_.name)
                    acc = psO.next(pend)
                    extra = []
                    if c0 < NPB * TB:
                        extra.append("pv")
                    if b5 == nb5 - 1:
                        extra.append("bp")
                    nacc = 16 + len(extra)
                    for h in range(16):
                        j, sub = h // 2, h % 2
                        ps = psG.next()
                        T.op("pe", lambda e, ps=ps, j=j, sub=sub, kb_=kb_, wd=wd, tsl=tsl: e.matmul(ps.t[:, 0:wd], lhsT=qiT.t[:, j, tsl], rhs=kb_.t[:, sub, 0:wd], start=True, stop=True), reads=[qiT, kb_], writes=[ps])
                        rb = Rl.next(pend)
                        if h % 2 == 0:
                            T.op("act", lambda e, rb=rb, ps=ps, wd=wd: e.activation(out=rb[:, 0:wd], in_=ps.t[:, 0:wd], func=AF.Relu), reads=[ps], writes=[rb])
                        else:
                            T.op("dve", lambda e, rb=rb, ps=ps, wd=wd: e.tensor_scalar(out=rb[:, 0:wd], in0=ps.t[:, 0:wd], scalar1=0.0, scalar2=None, op0=ALU.max), reads=[ps], writes=[rb])

                        def accmm(acc=acc, h=h, rb=rb, wd=wd, nacc=nacc, Dg=Dg):
                            T.op("pe", lambda e: e.matmul(acc.t[:, 0:wd], lhsT=Dg.t[:, h, :], rhs=rb.t[:, 0:wd], start=(h == 0), stop=(h == nacc - 1)), reads=[Dg, rb], writes=[acc])
                        defer(pend, accmm, [rb, acc])
                        run_pipe(pend, LA)

                    def fin(acc=acc, wd=wd, nacc=nacc, extra=tuple(extra), c0=c0):
                        k = 16
                        for ex in extra:
                            if ex == "pv":
                                T.op("pe", lambda e, k=k: e.matmul(acc.t[:, 0:wd], lhsT=pvm[:], rhs=ones_b.t[:, 0:wd], start=False, stop=(k == nacc - 1)), reads=[pvm, ones_b], writes=[acc])
                            else:
                                T.op("pe", lambda e, k=k: e.matmul(acc.t[:, wd - 256:wd], lhsT=ident_b[:], rhs=bpat_b.t[:, j2, :], start=False, stop=(k == nacc - 1)), reads=[ident_b, bpat_b], writes=[acc])
                            k += 1
                        T.op("act", lambda e: e.copy(out=score.t[:, c0:c0 + wd], in_=acc.t[:, 0:wd]), reads=[acc], writes=[score])
                    defer(pend, fin, [acc])
                run_pipe(pend, 0)

            def bisect(j2):
                i = 2 * hb + j2
                hi_t, mid_t, cnt_t, t_t, g_t, lo_t, sa_t = [sbg("bq%d_%d" % (j2, c), [P, 2], F32, SI) for c in range(7)]
                WD = min(4096, max(P, ((W * 45 // 100) // P) * P))
                nA = W - WD
                assert nA >= P and nA <= 8192
                hi, mid, cntc, tt, gg, lo, sa = [b_.t[:, 0:1] for b_ in (hi_t, mid_t, cnt_t, t_t, g_t, lo_t, sa_t)]
                T.op("dve", lambda e: e.reduce_max(out=hi, in_=score.t[:, 0:W], axis=AX.X), reads=[score], writes=[hi_t])
                T.op("dve", lambda e: e.tensor_scalar(out=mid, in0=hi, scalar1=-CLAMP * 0.5, scalar2=None, op0=ALU.add), reads=[hi_t], writes=[mid_t])
                for it in range(NITER):
                    hk = CLAMP * (0.5 ** (it + 1))
                    T.op("dve", lambda e: e.tensor_scalar(out=XM.t[:, 0:WD], in0=score.t[:, 0:WD], scalar1=mid, scalar2=None, op0=ALU.is_ge, op1=ALU.add, accum_out=cntc), reads=[score, mid_t], writes=[mjD, cnt_t])
                    T.op("act", lambda e: e.activation(out=XM8[:, 0:nA], in_=score.t[:, WD:W], func=AF.Sign, bias=mid, scale=-1.0, accum_out=sa), reads=[score, mid_t], writes=[mjA, sa_t])
                    T.op("dve", lambda e: e.scalar_tensor_tensor(out=tt, in0=cntc, scalar=2.0, in1=sa, op0=ALU.mult, op1=ALU.subtract), reads=[cnt_t, sa_t], writes=[t_t])
                    T.op("dve", lambda e: e.tensor_scalar(out=gg, in0=tt, scalar1=float(2 * TOPK - 1 - nA), scalar2=-0.5, op0=ALU.is_ge, op1=ALU.add), reads=[t_t], writes=[g_t])
                    T.op("dve", lambda e, hk=hk: e.scalar_tensor_tensor(out=mid, in0=gg, scalar=hk, in1=mid, op0=ALU.mult, op1=ALU.add), reads=[g_t, mid_t], writes=[mid_t])
                hN = CLAMP * (0.5 ** (NITER + 1))
                T.op("dve", lambda e: e.tensor_scalar(out=lo, in0=mid, scalar1=-hN, scalar2=None, op0=ALU.add), reads=[mid_t], writes=[lo_t])
                T.op("dve", lambda e: e.tensor_scalar(out=XM.t[:, 0:W], in0=score.t[:, 0:W], scalar1=lo, scalar2=None, op0=ALU.is_ge), reads=[score, lo_t], writes=[mjD, mjA])
                dump("thr%d_%d" % (bi, i), lo, [P, 1], F32, [lo_t])
                dump("score%d_%d" % (bi, i), score.t[:, 0:W], [P, W], F32, [score])

            def mask_transposes(j2):
                for kb0 in range(0, nkb, 8):
                    n = min(8, nkb - kb0)
                    ps = psG.next()
                    for t in range(n):
                        T.op("pe", lambda e, ps=ps, t=t, kb0=kb0: e.transpose(out=psb(ps)[:, t, :], in_=XM.t[:, (kb0 + t) * P:(kb0 + t + 1) * P], identity=ident_b[:]), reads=[mjD, mjA, ident_b], writes=[ps], inc=(t == n - 1))
                    T.op("act", lambda e, ps=ps, n=n, kb0=kb0, j2=j2: e.copy(out=maskT.t[:, kb0:kb0 + n, hb * 256 + j2 * P:hb * 256 + (j2 + 1) * P], in_=psb(ps)[:, 0:n, :]), reads=[ps], writes=[maskT])

            indexer(0)
            bisect(0)
            indexer(1)
            mask_transposes(0)
            bisect(1)
            mask_transposes(1)
            T.barrier()
            SI.close()


        def attention_block(S, bi, key0, qT, y_aT, maskT):
            nkbs = [(key0 + 256 * (hb + 1)) // P for hb in range(2)]
            nkb = nkbs[1]
            LA2 = 6
            wga = Rot([sbg("wga%d" % i, [P, 16, P], BF16, S) for i in range(1)])
            gaT = Rot([sbg("gaT%d" % i, [P, TB], F32, S) for i in range(2)])
            gex = Rot([sbg("gex%d" % i, [P, TB], F32, S) for i in range(1)])
            ktb = Rot([sbg("ktb%d" % i, [P, 2048], BF16, S) for i in range(2)])
            vtb = Rot([sbg("vtb%d" % i, [P, 16, 129], BF16, S) for i in range(2)])
            eb = Rot([sbg("eb%d" % i, [P, 512], BF16, S) for i in range(4)])
            pmb = Rot([sbg("pmb%d" % i, [P, 512], BF16, S) for i in range(8)])
            onb = Rot([sbg("onb%d" % i, [P, P], BF16, S) for i in range(4)])
            rdr = Rot([sbg("rd%d" % i, [P, 4], F32, S) for i in range(2)])
            psQ = Rot(PS[0:3])
            psX = Rot(PS[3:4])
            pend = []
            nstep = [0]
            for hh in range(16):
                ga = gaT.next(pend)
                O = [psO.next(pend) for _ in range(4)]
                wg = wga.next()
                T.dma("sp", wg[:, :, :], Wb[:, OFF_GA + hh * P:OFF_GA + (hh + 1) * P].rearrange("(kc p) c -> p kc c", p=P), writes=[wg], slot=wg.name)
                psg = psX.next()
                mm_group(psg.t[:, :], [(wg.t[:, kc, :], hT.t[:, kc, :]) for kc in range(16)], [wg, hT], psg)
                gx = gex.next()
                T.op("act", lambda e, gx=gx, psg=psg: e.activation(out=gx[:, :], in_=psg.t[:, :], func=AF.Exp, scale=-1.0), reads=[psg], writes=[gx])
                T.op("dve", lambda e, gx=gx: e.tensor_scalar(out=gx[:, :], in0=gx[:, :], scalar1=1.0, scalar2=None, op0=ALU.add), reads=[gx], writes=[gx])
                T.op("dve", lambda e, gx=gx: e.reciprocal(out=gx[:, :], in_=gx[:, :]), reads=[gx], writes=[gx])
                T.op("dve", lambda e, ga=ga, gx=gx, psg=psg: e.tensor_tensor(out=ga[:, :], in0=psg.t[:, :], in1=gx[:, :], op=ALU.mult), reads=[psg, gx], writes=[ga])
                for c2 in range((nkb + 15) // 16):
                    kb_lo = c2 * 16
                    n = min(16, nkb - kb_lo)
                    kt = ktb.next(pend)
                    vt = vtb.next(pend)
                    T.dma("sp", kt[:, 0:n * P], KT[hh, :, kb_lo * P:(kb_lo + n) * P], writes=[kt], slot=kt.name)
                    T.dma("sp", vt[:, 0:n, :], VS[hh, c2, :, 0:n, :], writes=[vt], slot=vt.name)
                    for t in range(0, n, 2):
                        kb = kb_lo + t
                        for hb in range(2):
                            if kb >= nkbs[hb]:
                                continue
                            qsl = slice(hb * 256, (hb + 1) * 256)
                            ps = psQ.next()
                            for u in range(2):
                                T.op("pe", lambda e, ps=ps, kt=kt, t=t, u=u, hh=hh, qsl=qsl: e.matmul(ps.t[:, u * 256:(u + 1) * 256], lhsT=kt.t[:, (t + u) * P:(t + u + 1) * P], rhs=qT.t[:, hh, qsl], start=True, stop=True), reads=[kt, qT], writes=[ps], inc=(u == 1))
                            eb_ = eb.next()
                            T.op("act", lambda e, eb_=eb_, ps=ps: e.activation(out=eb_[:, :], in_=ps.t[:, :], func=AF.Exp), reads=[ps], writes=[eb_])
                            pm = pmb.next(pend)
                            nstep[0] += 1
                            T.op("pool" if nstep[0] % 3 == 0 else "dve", lambda e, pm=pm, eb_=eb_, kb=kb, qsl=qsl: e.tensor_tensor(out=pm.t[:, :].rearrange("p (a b) -> p a b", b=256), in0=eb_.t[:, :].rearrange("p (a b) -> p a b", b=256), in1=maskT.t[:, kb:kb + 2, qsl], op=ALU.mult), reads=[eb_, maskT], writes=[pm])

                            def pv(pm=pm, vt=vt, t=t, kb=kb, O=O, hb=hb):
                                for u in range(2):
                                    for j2 in range(2):
                                        T.op("pe", lambda e, u=u, j2=j2: e.matmul(O[2 * hb + j2].t[:, 0:129], lhsT=pm.t[:, u * 256 + j2 * P:u * 256 + (j2 + 1) * P], rhs=vt.t[:, t + u, :], start=(kb + u == 0), stop=(kb + u == nkbs[hb] - 1)),
                                             reads=[pm, vt], writes=[O[2 * hb + j2]], inc=(u == 1 and j2 == 1))
                            defer(pend, pv, [pm, vt] + O)
                            run_pipe(pend, LA2)

                def finalize(O=O, ga=ga, hh=hh):
                    pst = psX.next()
                    rd = rdr.next()
                    for jt in range(4):
                        T.op("dve", lambda e, jt=jt: e.reciprocal(out=rd.t[:, jt:jt + 1], in_=O[jt].t[:, 128:129]), reads=[O[jt]], writes=[rd])
                        ob = onb.next()
                        T.op("dve", lambda e, jt=jt, ob=ob: e.tensor_scalar(out=ob[:, :], in0=O[jt].t[:, 0:P], scalar1=rd.t[:, jt:jt + 1], scalar2=None, op0=ALU.mult), reads=[O[jt], rd], writes=[ob])
                        T.op("pe", lambda e, jt=jt, ob=ob: e.transpose(out=psb(pst)[:, jt, :], in_=ob[:, :], identity=ident_b[:]), reads=[ob, ident_b], writes=[pst])
                    T.op("dve", lambda e: e.tensor_tensor(out=y_aT.t[:, hh, :], in0=psb(pst)[:, 0:4, :].rearrange("p a b -> p (a b)"), in1=ga[:, :], op=ALU.mult), reads=[pst, ga], writes=[y_aT])
                defer(pend, finalize, [ga] + O)
            run_pipe(pend, 0)

        def out_block2(S, bi, orow0, y_aT):
            y_mT = sbg("y_mT2", [P, 16, TB], BF16, S)
            T.dma("sp", y_mT.t[:, :, :].rearrange("p a b -> p (a b)"), YM, writes=[y_mT], slot="ymld")
            mT = sbg("mT", [P, 16, TB], BF16, S)
            r = sbg("r", [P, NT, D], F32, S)
            nfb = sbg("nfb", [P, D], F32, S)
            stashb = sbg("stash", [P, 4, TB], F32, S)
            gm = Rot([sbg("gm%d" % i, [P, TB], F32, S) for i in range(2)])
            m2 = Rot([sbg("m2_%d" % i, [P, TB], F32, S) for i in range(2)])
            st3 = sbg("st3", [P, 8], F32, S)
            T.dma("sp", nfb[:, :], normf.partition_broadcast(P), writes=[nfb], slot="nfb")
            for i in range(NT):
                T.dma("sp", r.t[:, i, :], xo[orow0 + i * P:orow0 + (i + 1) * P, :], writes=[r], slot="rld")
            for cg in range(4):
                plan = ((Wbr, cg * 512, 0), (Wb, OFF_G + cg * 512, 0), (Wbr, cg * 512, D), (Wb, OFF_G + D + cg * 512, 0))
                for half in range(2):
                    wA = wt_rot.next()
                    load_w(plan[2 * half][0], plan[2 * half][1], 512, wA, rows0=plan[2 * half][2])
                    ysrc = y_mT if half == 0 else y_aT
                    psas = []
                    for ci in range(4):
                        psa = psA.next()
                        mm_group(psa.t[:, :], [(wA.t[:, kc, ci * P:(ci + 1) * P], ysrc.t[:, kc, :]) for kc in range(16)], [wA, ysrc], psa)
                        psas.append(psa)
                    wG = wt_rot.next()
                    load_w(plan[2 * half + 1][0], plan[2 * half + 1][1], 512, wG, rows0=plan[2 * half + 1][2])
                    for ci in range(4):
                        c = cg * 4 + ci
                        psa = psas[ci]
                        psg = psA.next()
                        mm_group(psg.t[:, :], [(wG.t[:, kc, ci * P:(ci + 1) * P], hT.t[:, kc, :]) for kc in range(16)], [wG, hT], psg)
                        gb = gm.next()
                        gcol = C_GB + half * 16 + c
                        T.op("act", lambda e, gb=gb, psg=psg, gcol=gcol: e.activation(out=gb[:, :], in_=psg.t[:, :], func=AF.Sigmoid, bias=CST.t[:, gcol:gcol + 1]), reads=[psg, CST], writes=[gb])
                        if half == 0:
                            T.op("dve", lambda e, psa=psa, gb=gb, ci=ci: e.tensor_tensor(out=stashb.t[:, ci, :], in0=psa.t[:, :], in1=gb[:, :], op=ALU.mult), reads=[psa, gb], writes=[stashb])
                        else:
                            mb2 = m2.next()
                            T.op("dve", lambda e, mb2=mb2, psa=psa, gb=gb: e.tensor_tensor(out=mb2[:, :], in0=psa.t[:, :], in1=gb[:, :], op=ALU.mult), reads=[psa, gb], writes=[mb2])
                            T.op("pool", lambda e, mb2=mb2, c=c, ci=ci: e.tensor_tensor(out=mT.t[:, c, :], in0=mb2[:, :], in1=stashb.t[:, ci, :], op=ALU.add), reads=[mb2, stashb], writes=[mT])
            dump("mT%d" % bi, mT.t[:, :, :].rearrange("p a b -> p (a b)"), [P, 16 * TB], BF16, [mT])
            for n in range(4):
                wo = wt_rot.next()
                load_w(Wo, n * 512, 512, wo)
                for i in range(NT):
                    ps = psA.next()
                    mm_group(ps.t[:, :], [(mT.t[:, kc, i * P:(i + 1) * P], wo.t[:, kc, :]) for kc in range(16)], [mT, wo], ps)
                    T.op("dve", lambda e, ps=ps, i=i, n=n: e.tensor_tensor(out=r.t[:, i, n * 512:(n + 1) * 512], in0=ps.t[:, :], in1=r.t[:, i, n * 512:(n + 1) * 512], op=ALU.add), reads=[ps, r], writes=[r])
            for i in range(NT):
                T.op("act", lambda e, i=i: e.activation(out=xs[:], in_=r.t[:, i, :], func=AF.Square, accum_out=st3[:, 0:1]), reads=[r], writes=[xs, st3])
                rstd_from_ssq(st3[:, 0:1], st3[:, 1:2], D, st3)
                T.op("dve", lambda e, i=i: e.scalar_tensor_tensor(out=r.t[:, i, :], in0=r.t[:, i, :], scalar=st3[:, 1:2], in1=nfb[:, :], op0=ALU.mult, op1=ALU.mult), reads=[r, st3, nfb], writes=[r])
                T.dma("pool", out[orow0 + i * P:orow0 + (i + 1) * P, :], r.t[:, i, :], reads=[r], slot="ost")

        for bi in range(NPB + NOB):
            is_own = bi >= NPB
            key0 = bi * TB
            xsrc, row0 = (xo, (bi - NPB) * TB) if is_own else (xp, bi * TB)
            build_hT(xsrc, row0)
            if bi == NPB:
                dump("hT", hT.t[:, :, :].rearrange("p a b -> p (a b)"), [P, 16 * TB], BF16, [hT])
            with ExitStack() as S1:
                y_mT = ssd_block(S1, is_own, bi)
                if is_own:
                    T.dma("pool", YM, y_mT.t[:, :, :].rearrange("p a b -> p (a b)"), reads=[y_mT], slot="ymst")
                if bi == NPB - 1:
                    T.op("dve", lambda e: e.tensor_scalar(out=hs.t[:, :, :], in0=hs.t[:, :, :], scalar1=CST.t[:, C_PV:C_PV + 1], scalar2=None, op0=ALU.mult), reads=[hs, CST], writes=[hs])
                T.barrier()
            if not is_own:
                with ExitStack() as S2:
                    kv_proj(S2, key0, False, None, None, None)
                    T.barrier()
                continue
            with ExitStack() as SO:
                y_aT = sbg("y_aT", [P, 16, TB], BF16, SO)
                with ExitStack() as S2:
                    qT = sbg("qT", [P, 16, TB], BF16, S2)
                    qiT = sbg("qiT", [P, 8, TB], BF16, S2)
                    ws = sbg("ws", [P, NT, 16], F32, S2)
                    with ExitStack() as S2a:
                        kv_proj(S2a, key0, True, qT, qiT, ws)
                        T.barrier()
                    with ExitStack() as S2b:
                        maskT = sbg("maskT", [P, 64, TB], mybir.dt.uint8, S2b)
                        for hb in range(2):
                            dsa_idx(S2b, bi, key0, hb, qiT, ws, maskT)
                        attention_block(S2b, bi, key0, qT, y_aT, maskT)
                        T.barrier()
                dump("yaT%d" % bi, y_aT.t[:, :, :].rearrange("p a b -> p (a b)"), [P, 16 * TB], BF16, [y_aT])
                with ExitStack() as S3:
                    out_block2(S3, bi, (bi - NPB) * TB, y_aT)
                    T.barrier()
        T.barrier()
        T.emit()
    return nc, dbg_out


def make_cst(inputs, half):
    c = np.zeros((P, NCST), np.float32)
    c[:, C_ID:C_ID + 128] = np.eye(128, dtype=np.float32)
    u = np.arange(128)
    c[:, C_L:C_L + 128] = (u[:, None] <= u[None, :]).astype(np.float32)
    c[:, C_U:C_U + 128] = (u[:, None] > u[None, :]).astype(np.float32)
    cw = np.asarray(inputs["conv_w"][0], np.float32)
    c[:, C_CW:C_CW + 96] = cw.T.reshape(24, 128, 4).transpose(1, 0, 2).reshape(128, 96)
    c[:, C_CB:C_CB + 24] = np.asarray(inputs["conv_b"][0], np.float32).reshape(24, 128).T
    c[:, C_GB:C_GB + 32] = np.asarray(inputs["gate_bias"][0], np.float32).reshape(32, 128).T
    c[:, C_G:C_G + 16] = np.asarray(inputs["norm_in"][0], np.float32).reshape(16, 128).T
    c[:, C_GN:C_GN + 16] = np.asarray(inputs["gn_m"][0], np.float32).reshape(16, 128).T
    c[:, C_DTB:C_DTB + 32] = np.asarray(inputs["dt_bias"][0], np.float32)[None, :]
    c[:, C_ALOG:C_ALOG + 32] = np.asarray(inputs["a_log"][0], np.float32)[None, :]
    c[:, C_DSK:C_DSK + 32] = np.asarray(inputs["d_skip"][0], np.float32)[None, :]
    c[:, C_PV] = 1.0 if half == 1 else 0.0
    c[:, C_CVEC:C_CVEC + NITER] = (0.5 ** np.arange(1, NITER + 1, dtype=np.float64)).astype(np.float32)[None, :]
    q = np.arange(128)[:, None]
    k = np.arange(128)[None, :]
    diag = np.where((q < 64) & (k >= 64), NEG, 0.0).astype(np.float32)
    full = np.full((128, 128), NEG, np.float32)
    zero = np.zeros((128, 128), np.float32)
    c[:, C_BP:C_BP + 256] = np.concatenate([diag, full], 1)
    c[:, C_BP + 256:C_BP + 512] = np.concatenate([zero, diag], 1)
    c[0, C_PVROW:C_PVROW + 128] = 0.0 if half == 1 else NEG
    return c


_NC_CACHE = {}


def run(inputs, L, dbg=()):
    B = inputs["x"].shape[0]
    assert B * 2 == 8
    H = L // 2
    NPB = NOB = H // TB
    key = (NPB, NOB, tuple(dbg))
    if key not in _NC_CACHE:
        _NC_CACHE[key] = build(NPB, NOB, dbg)
    nc, dbg_out = _NC_CACHE[key]
    x = np.asarray(inputs["x"], np.float32)
    w_in = np.ascontiguousarray(np.asarray(inputs["w_in"][0], np.float32))
    w_br = np.ascontiguousarray(np.asarray(inputs["w_branch"][0], np.float32))
    w_out = np.ascontiguousarray(np.asarray(inputs["w_out"][0], np.float32))
    normf = np.asarray(inputs["norm_final"], np.float32).reshape(1, D)
    csts = [make_cst(inputs, 0), make_cst(inputs, 1)]
    zeros = np.zeros((H, D), np.float32)
    in_maps = []
    for c in range(8):
        b, half = c // 2, c % 2
        in_maps.append({
            "xp": np.ascontiguousarray(x[b, 0:H]) if half == 1 else zeros,
            "xo": np.ascontiguousarray(x[b, half * H:(half + 1) * H]),
            "w_in": w_in, "w_br": w_br, "w_out": w_out,
            "cst": csts[half], "normf": normf,
        })
    res = run_bass_kernel_spmd(nc, in_maps, core_ids=list(range(8)))
    outp = np.empty((B, L, D), np.float32)
    for c in range(8):
        b, half = c // 2, c % 2
        outp[b, half * H:(half + 1) * H] = res.results[c]["out"]
    return outp, res


def kernel(**inputs):
    L = inputs["x"].shape[1]
    outp, _ = run(inputs, L)
    return outp
```

```python
import numpy as np
import ml_dtypes
from contextlib import ExitStack
import concourse.bass as bass
import concourse.mybir as mybir
from concourse.bass_utils import run_bass_kernel_spmd

F32 = mybir.dt.float32
BF16 = mybir.dt.bfloat16
AF = mybir.ActivationFunctionType
ALU = mybir.AluOpType
AX = mybir.AxisListType

P = 128
D = 2048
TB = 512
NT = 4
EPS = 1e-6
OFF_Z, OFF_XBC, OFF_DT, OFF_Q, OFF_K, OFF_V, OFF_GA, OFF_QI, OFF_KI, OFF_WI, OFF_G = (
    0, 2048, 5120, 5152, 7200, 9248, 11296, 13344, 14368, 14432, 14448)
D_IN = 18544
ATTN_SCALE = 128 ** -0.5
IDX_SCALE = (64 ** -0.5) * (16 ** -0.5)
TOPK = 256
NITER = 21
CLAMP = 8.0
NEG = -30000.0

C_ID, C_L, C_U, C_CW, C_CB, C_GB, C_G, C_GN, C_DTB, C_ALOG, C_DSK, C_PV, C_CVEC, C_BP, C_PVROW, NCST = (
    0, 128, 256, 384, 480, 504, 536, 552, 568, 600, 632, 664, 672, 704, 1216, 1344)


class Buf:
    def __init__(self, t, name=""):
        self.t = t
        self.name = name
        self.w = None
        self.r = []
        self.alias = []
        self.pending = 0

    def __getitem__(self, k):
        return self.t[k]


class Trk:
    LIM = 30000

    def __init__(self, nc, es):
        self.nc = nc
        self.es = es
        self.engs = {"pe": nc.tensor, "act": nc.scalar, "dve": nc.vector, "pool": nc.gpsimd, "sp": nc.sync}
        self.ctr = {}
        self.prog = {k: [] for k in self.engs}
        self.known = {k: {} for k in self.engs}
        self.nsem = 0
        for k in self.engs:
            self._newctr(k, 1)

    def _newsem(self, name):
        self.nsem += 1
        return self.es.enter_context(self.nc.semaphore("%s_%d" % (name, self.nsem)))

    def _newctr(self, name, step):
        self.ctr[name] = {"step": step, "ep": 0, "cnt": 0, "sems": {0: self._newsem(name)}}

    def _next_ms(self, name):
        c = self.ctr[name]
        if (c["cnt"] + 1) * c["step"] > self.LIM:
            return (name, c["ep"] + 1, 1)
        return (name, c["ep"], c["cnt"] + 1)

    def _inc(self, name):
        c = self.ctr[name]
        ms = self._next_ms(name)
        if ms[1] != c["ep"]:
            c["ep"] = ms[1]
            c["sems"][ms[1]] = self._newsem(name)
        c["cnt"] = ms[2]
        return ms, c["sems"][c["ep"]], c["step"]

    def _cur_ms(self, name):
        c = self.ctr[name]
        return (name, c["ep"], c["cnt"])

    def _need(self, e, deps):
        need = {}
        for d in deps:
            if d is None:
                continue
            name, ep, idx = d
            if idx == 0:
                continue
            k = (name, ep)
            if idx > need.get(k, 0):
                need[k] = idx
        for (name, ep), idx in need.items():
            c = self.ctr[name]
            if name == e and (ep, idx) > (c["ep"], c["cnt"]):
                continue
            if self.known[e].get((name, ep), 0) >= idx:
                continue
            later = [kk for kk in self.known[e] if kk[0] == name and kk[1] > ep]
            if later:
                continue
            self.prog[e].append((lambda en, v_=idx * c["step"], nm=(name, ep): en.wait_ge(self._sem_of(nm), v_)))
            self.known[e][(name, ep)] = idx

    def _sem_of(self, nm):
        return self.ctr[nm[0]]["sems"][nm[1]]

    def _deps(self, reads, writes):
        deps = []
        for b in reads:
            deps.append(b.w)
            for a in b.alias:
                deps.append(a.w)
        for b in writes:
            deps.append(b.w)
            deps.extend(b.r)
            for a in b.alias:
                deps.append(a.w)
                deps.extend(a.r)
        return deps

    def _mark(self, me, reads, writes):
        for b in reads:
            b.r.append(me)
        for b in writes:
            b.w = me
            b.r = []

    def op(self, e, fn, reads=(), writes=(), inc=True):
        self._need(e, self._deps(reads, writes))
        if inc:
            me, sem, step = self._inc(e)
            self.prog[e].append((lambda en, f_=fn, s_=sem: f_(en).then_inc(s_, 1)))
        else:
            me = self._next_ms(e)
            self.prog[e].append(fn)
        self._mark(me, reads, writes)

    def dma(self, q, out, in_, reads=(), writes=(), slot=None, **kw):
        self._need(q, self._deps(reads, writes))
        name = "d:" + slot
        if name not in self.ctr:
            self._newctr(name, 16)
        me, sem, step = self._inc(name)
        self.prog[q].append((lambda en, o_=out, i_=in_, k_=kw, s_=sem: en.dma_start(out=o_, in_=i_, **k_).then_inc(s_, 16)))
        self._mark(me, reads, writes)

    def barrier(self):
        allms = [self._cur_ms(n) for n in self.ctr]
        for e in self.engs:
            self._need(e, [m for m in allms if m[0] != e])

    def emit(self):
        with self.nc.Block() as block:
            @block.tensor
            def _(en):
                for f in self.prog["pe"]:
                    f(en)

            @block.scalar
            def _(en):
                for f in self.prog["act"]:
                    f(en)

            @block.vector
            def _(en):
                for f in self.prog["dve"]:
                    f(en)

            @block.gpsimd
            def _(en):
                for f in self.prog["pool"]:
                    f(en)

            @block.sync
            def _(en):
                for f in self.prog["sp"]:
                    f(en)


class Rot:
    def __init__(self, items):
        self.items = items
        self.i = 0

    def next(self, pend=None):
        b = self.items[self.i % len(self.items)]
        while b.pending > 0 and pend:
            pend.pop(0)()
        assert b.pending == 0, "rotation reuses %s while deferred users pending" % b.name
        self.i += 1
        return b


def defer(pend, fn, bufs):
    for b in bufs:
        b.pending += 1

    def w():
        fn()
        for b in bufs:
            b.pending -= 1
    pend.append(w)


def build(NPB, NOB, dbg=()):
    NK = (NPB + NOB) * TB
    nc = bass.Bass("TRN2", target_bir_lowering=False)
    dt_in = lambda n, s, d=F32: nc.dram_tensor(n, s, d, kind="ExternalInput").ap()
    xp = dt_in("xp", [max(NPB, 1) * TB, D])
    xo = dt_in("xo", [NOB * TB, D])
    w_in = dt_in("w_in", [D, D_IN])
    w_br = dt_in("w_br", [2 * D, D])
    w_out = dt_in("w_out", [D, D])
    cst = dt_in("cst", [P, NCST])
    normf = dt_in("normf", [1, D])
    out = nc.dram_tensor("out", [NOB * TB, D], F32, kind="ExternalOutput").ap()
    dti = lambda n, s, d=BF16: nc.dram_tensor(n, s, d, kind="Internal").ap()
    Wb = dti("Wb", [D, D_IN])
    Wbr = dti("Wbr", [2 * D, D])
    Wo = dti("Wo", [D, D])
    KT = dti("KT", [16, P, NK])
    NCH = (NK // P + 15) // 16
    VS = dti("VS", [16, NCH, P, 16, 129])
    KI = dti("KI", [2, P, NK])
    YM = dti("YM", [P, 16 * TB])
    dbg_out = {}

    with ExitStack() as es:
        T = Trk(nc, es)

        uniq = [0]

        def sbg(name, shape, dt, stack=es):
            uniq[0] += 1
            return Buf(stack.enter_context(nc.sbuf_tensor("%s_u%d" % (name, uniq[0]), shape, dt)), name)

        CST = sbg("CST", [P, NCST], F32)
        ident_b = sbg("ident_b", [P, P], BF16)
        bpat_b = sbg("bpat_b", [P, 2, 256], BF16)
        pvm = sbg("pvm", [P, P], BF16)
        ones_b = sbg("ones_b", [P, 512], BF16)
        ones_f = sbg("ones_f", [P, P], F32)
        aneg = sbg("aneg", [P, 32], F32)
        eps_t = sbg("eps_t", [P, 2], F32)
        hT = sbg("hT", [P, 16, TB], BF16)
        WT = sbg("WT", [P, 2 * 16 * 512], BF16)
        XM = sbg("XM", [P, 8192], BF16)
        xs = sbg("xs", [P, D], BF16)
        hs = sbg("hs", [P, 4, 512], F32)
        halo = sbg("halo", [P, 24, 3], F32)
        st = sbg("st", [P, 16], F32)
        PS = [Buf(es.enter_context(nc.psum_tensor("ps%d" % i, [P, 512], F32)), "ps%d" % i) for i in range(8)]
        psA = Rot(PS)
        psG = Rot(PS[0:4])
        psO = Rot(PS[4:8])

        Wt = [Buf(WT.t[:, i * 8192:(i + 1) * 8192].rearrange("p (k c) -> p k c", c=512), "Wt%d" % i) for i in range(2)]
        score = Buf(WT.t[:].bitcast(F32), "score")
        score.alias = [Wt[0], Wt[1]]
        for w in Wt:
            w.alias = [score]
        xtb = [Buf(XM.t[:, i * 4096:(i + 1) * 4096].bitcast(F32), "xt%d" % i) for i in range(2)]
        mjD = Buf(XM.t[:, 0:4096], "mjD")
        mjA = Buf(XM.t[:, 4096:8192], "mjA")
        mjD.alias = [xtb[0]]
        mjA.alias = [xtb[1]]
        xtb[0].alias = [mjD]
        xtb[1].alias = [mjA]
        XM8 = XM.t[:, 4096:8192].bitcast(mybir.dt.int8)
        wt_rot = Rot(Wt)
        xt_rot = Rot(xtb)

        ident_f = CST.t[:, C_ID:C_ID + 128]
        Lmask = CST.t[:, C_L:C_L + 128]
        Ustr = CST.t[:, C_U:C_U + 128]

        def cs(c0, n):
            return CST.t[:, c0:c0 + n]

        def psb(b):
            return b.t[:].bitcast(BF16).rearrange("p (a b) -> p a b", b=128)

        def dump(name, ap, shape, dtype, rd):
            if name not in dbg:
                return
            o = nc.dram_tensor("dbg_" + name, list(shape), dtype, kind="ExternalOutput").ap()
            dbg_out[name] = o
            T.dma("sp", o, ap, reads=rd, slot="dbg")

        T.dma("sp", CST[:], cst, writes=[CST], slot="cst")
        T.op("dve", lambda e: e.tensor_copy(out=ident_b[:], in_=ident_f), reads=[CST], writes=[ident_b])
        T.op("dve", lambda e: e.tensor_copy(out=bpat_b[:], in_=cs(C_BP, 512).rearrange("p (a b) -> p a b", b=256)), reads=[CST], writes=[bpat_b])
        T.op("dve", lambda e: e.tensor_copy(out=pvm[:], in_=cs(C_PVROW, 128)), reads=[CST], writes=[pvm])
        T.op("dve", lambda e: e.memset(ones_b[:], 1.0), writes=[ones_b])
        T.op("pool", lambda e: e.memset(ones_f[:], 1.0), writes=[ones_f])
        T.op("pool", lambda e: e.memset(eps_t[:], EPS), writes=[eps_t])
        T.op("pool", lambda e: e.memset(hs[:], 0.0), writes=[hs])
        T.op("pool", lambda e: e.memset(halo[:], 0.0), writes=[halo])
        T.op("act", lambda e: e.activation(out=aneg[:], in_=cs(C_ALOG, 32), func=AF.Exp), reads=[CST], writes=[aneg])
        T.op("dve", lambda e: e.tensor_scalar(out=aneg[:], in0=aneg[:], scalar1=-1.0, scalar2=None, op0=ALU.mult), reads=[aneg], writes=[aneg])

        with ExitStack() as s0:
            fb = Rot([sbg("pcf%d" % i, [P, 2048], F32, s0) for i in range(3)])
            bb = Rot([sbg("pcb%d" % i, [P, 2048], BF16, s0) for i in range(3)])
            cnt = 0

            def cast_piece(src_ap, dst_ap, n, scal):
                nonlocal cnt
                f = fb.next()
                b = bb.next()
                T.dma("sp", f[:, 0:n], src_ap, writes=[f], slot=f.name)
                eng = "dve" if cnt % 2 == 0 else "pool"
                if scal is None:
                    T.op(eng, lambda e: e.tensor_copy(out=b[:, 0:n], in_=f[:, 0:n]), reads=[f], writes=[b])
                else:
                    T.op(eng, lambda e: e.tensor_scalar(out=b[:, 0:n], in0=f[:, 0:n], scalar1=scal, scalar2=1.0, op0=ALU.mult, op1=ALU.mult), reads=[f, CST], writes=[b])
                T.dma("act", dst_ap, b[:, 0:n], reads=[b], slot=b.name)
                cnt += 1

            for kc in range(16):
                for c0 in range(0, D_IN, 2048):
                    n = min(2048, D_IN - c0)
                    cast_piece(w_in[kc * P:(kc + 1) * P, c0:c0 + n], Wb[kc * P:(kc + 1) * P, c0:c0 + n], n, CST.t[:, C_G + kc:C_G + kc + 1])
            for kc in range(32):
                cast_piece(w_br[kc * P:(kc + 1) * P, :], Wbr[kc * P:(kc + 1) * P, :], 2048, None)
            for kc in range(16):
                cast_piece(w_out[kc * P:(kc + 1) * P, :], Wo[kc * P:(kc + 1) * P, :], 2048, None)
            zt = sbg("zt", [P, 2048], BF16, s0)
            T.op("pool", lambda e: e.memset(zt[:], 0.0), writes=[zt])
            for c0 in range(0, NK, 2048):
                n = min(2048, NK - c0)
                T.dma("act", KI[0, 64:128, c0:c0 + n], zt[64:128, 0:n], reads=[zt], slot="zfill")
                T.dma("act", KI[1, 0:64, c0:c0 + n], zt[0:64, 0:n], reads=[zt], slot="zfill")
            T.barrier()

        def load_w(src, c0, n, wt, dcol=0, rows0=0):
            T.dma("sp", wt.t[:, :, dcol:dcol + n], src[rows0:rows0 + D, c0:c0 + n].rearrange("(kc p) c -> p kc c", p=P), writes=[wt], slot=wt.name)

        def mm_group(ps_ap, pairs, reads, ps_buf, extra_first_reads=()):
            n = len(pairs)
            for k, (l, r) in enumerate(pairs):
                T.op("pe", lambda e, l=l, r=r, k=k: e.matmul(ps_ap, lhsT=l, rhs=r, start=(k == 0), stop=(k == n - 1)),
                     reads=reads, writes=[ps_buf], inc=(k == n - 1))

        def proj_F(wt, ncols, consumer, rhs_cols=slice(0, TB), nfree=TB):
            for ci in range(ncols // P):
                ps = psA.next()
                mm_group(ps.t[:, 0:nfree], [(wt.t[:, kc, ci * P:(ci + 1) * P], hT.t[:, kc, rhs_cols]) for kc in range(16)], [wt, hT], ps)
                consumer(ci, ps)

        def proj_T(wt, ncols, consumer, tiles=range(NT)):
            for i in tiles:
                ps = psA.next()
                mm_group(ps.t[:, 0:ncols], [(hT.t[:, kc, i * P:(i + 1) * P], wt.t[:, kc, 0:ncols]) for kc in range(16)], [wt, hT], ps)
                consumer(i, ps)

        def run_pipe(pend, nkeep):
            while len(pend) > nkeep:
                pend.pop(0)()

        evac_flip = [0]

        def evac(out_ap, in_ap, reads, writes, scale=None):
            evac_flip[0] ^= 1
            if evac_flip[0]:
                if scale is None:
                    T.op("act", lambda e: e.copy(out=out_ap, in_=in_ap), reads=reads, writes=writes)
                else:
                    T.op("act", lambda e: e.activation(out=out_ap, in_=in_ap, func=AF.Copy, scale=scale), reads=reads, writes=writes)
            else:
                if scale is None:
                    T.op("dve", lambda e: e.tensor_copy(out=out_ap, in_=in_ap), reads=reads, writes=writes)
                else:
                    T.op("dve", lambda e: e.tensor_scalar(out=out_ap, in0=in_ap, scalar1=scale, scalar2=None, op0=ALU.mult), reads=reads, writes=writes)

        def rstd_from_ssq(ssq_ap, out_ap, n, buf):
            T.op("act", lambda e: e.activation(out=out_ap, in_=ssq_ap, func=AF.Ln, scale=1.0 / n, bias=eps_t[:, 0:1]), reads=[buf, eps_t], writes=[buf])
            T.op("act", lambda e: e.activation(out=out_ap, in_=out_ap, func=AF.Exp, scale=-0.5), reads=[buf], writes=[buf])

        def build_hT(xsrc, row0):
            for i in range(NT):
                xt = xt_rot.next()
                T.dma("sp", xt[:, :], xsrc[row0 + i * P:row0 + (i + 1) * P, :], writes=[xt], slot=xt.name)
                T.op("act", lambda e, xt=xt: e.activation(out=xs[:], in_=xt[:, :], func=AF.Square, accum_out=st[:, 0:1]), reads=[xt], writes=[xs, st])
                rstd_from_ssq(st[:, 0:1], st[:, 1:2], D, st)
                T.op("dve", lambda e, xt=xt: e.tensor_scalar(out=xs[:], in0=xt[:, :], scalar1=st[:, 1:2], scalar2=None, op0=ALU.mult), reads=[xt, st], writes=[xs])
                for hb in range(2):
                    ps = psA.next()
                    for k in range(8):
                        kc = hb * 8 + k
                        T.op("pe", lambda e, ps=ps, k=k, kc=kc: e.transpose(out=psb(ps)[:, k, :], in_=xs[:, kc * P:(kc + 1) * P], identity=ident_b[:]),
                             reads=[xs, ident_b], writes=[ps], inc=(k == 7))
                    evac(hT.t[:, hb * 8:(hb + 1) * 8, i * P:(i + 1) * P], psb(ps), [ps], [hT])

        def ssd_block(S, is_own, bi):
            x_tok = sbg("x_tok", [P, NT, D], F32, S)
            BT = sbg("BT", [P, 4, TB], BF16, S)
            CTt = sbg("CTt", [P, 4, TB], BF16, S)
            Btok = sbg("Btok", [P, NT, 4, P], BF16, S)
            dt_all = sbg("dt_all", [P, NT, 32], F32, S)
            sm = sbg("sm", [P, NT, 160], F32, S)
            y_mT = sbg("y_mT", [P, 16, TB], BF16, S) if is_own else None
            SX = ExitStack()
            ut = Rot([sbg("ut%d" % i, [P, 515], F32, SX) for i in range(3)])
            acc_r = Rot([sbg("cacc%d" % i, [P, TB], F32, SX) for i in range(2)])
            xsT = Rot([sbg("xsT%d" % i, [P, TB], F32, SX) for i in range(4)])

            cpend = []
            for grp in range(6):
                wt = wt_rot.next()
                load_w(Wb, OFF_XBC + grp * 512, 512, wt)

                def cons(ci, ps, grp=grp):
                    cg = grp * 4 + ci
                    u = ut.next()
                    T.op("act", lambda e: e.copy(out=u[:, 3:515], in_=ps.t[:, 0:512]), reads=[ps], writes=[u])
                    T.op("pool", lambda e: e.tensor_copy(out=u[:, 0:3], in_=halo[:, cg, :]), reads=[halo], writes=[u])
                    T.op("pool", lambda e: e.tensor_copy(out=halo[:, cg, :], in_=u[:, 512:515]), reads=[u], writes=[halo])
                    acc = acc_r.next()
                    cw = lambda k: CST.t[:, C_CW + cg * 4 + k:C_CW + cg * 4 + k + 1]
                    T.op("dve", lambda e: e.tensor_scalar(out=acc[:, :], in0=u[:, 3:515], scalar1=cw(3), scalar2=None, op0=ALU.mult), reads=[u, CST], writes=[acc])
                    for k in (2, 1, 0):
                        T.op("dve", lambda e, k=k: e.scalar_tensor_tensor(out=acc[:, :], in0=u[:, k:k + 512], scalar=cw(k), in1=acc[:, :], op0=ALU.mult, op1=ALU.add), reads=[u, CST, acc], writes=[acc])
                    cb = CST.t[:, C_CB + cg:C_CB + cg + 1]
                    if cg < 16:
                        xT = xsT.next(cpend)
                        T.op("act", lambda e: e.activation(out=xT[:, :], in_=acc[:, :], func=AF.Silu, bias=cb), reads=[acc, CST], writes=[xT])

                        def later(xT=xT, cg=cg):
                            ps2 = psA.next()
                            for i in range(NT):
                                T.op("pe", lambda e, i=i: e.transpose(out=ps2.t[:, i * P:(i + 1) * P], in_=xT[:, i * P:(i + 1) * P], identity=ident_f),
                                     reads=[xT, CST], writes=[ps2], inc=(i == NT - 1))
                            evac(x_tok.t[:, :, cg * P:(cg + 1) * P], ps2.t[:, :].rearrange("p (a b) -> p a b", b=P), [ps2], [x_tok])
                        defer(cpend, later, [xT])
                    elif cg < 20:
                        g = cg - 16
                        T.op("act", lambda e: e.activation(out=BT.t[:, g, :], in_=acc[:, :], func=AF.Silu, bias=cb), reads=[acc, CST], writes=[BT])

                        def later(g=g):
                            ps2 = psA.next()
                            for i in range(NT):
                                T.op("pe", lambda e, i=i: e.transpose(out=psb(ps2)[:, i, :], in_=BT.t[:, g, i * P:(i + 1) * P], identity=ident_b[:]),
                                     reads=[BT, ident_b], writes=[ps2], inc=(i == NT - 1))
                            evac(Btok.t[:, :, g, :], psb(ps2)[:, 0:NT, :], [ps2], [Btok])
                        defer(cpend, later, [])
                    else:
                        g = cg - 20
                        T.op("act", lambda e: e.activation(out=CTt.t[:, g, :], in_=acc[:, :], func=AF.Silu, bias=cb), reads=[acc, CST], writes=[CTt])
                    run_pipe(cpend, 2)

                proj_F(wt, 512, cons)
            run_pipe(cpend, 0)
            T.barrier()
            SX.close()

            wt = wt_rot.next()
            load_w(Wb, OFF_DT, 32, wt)

            def cons_dt(i, ps):
                T.op("dve", lambda e: e.tensor_tensor(out=dt_all.t[:, i, :], in0=ps.t[:, 0:32], in1=cs(C_DTB, 32), op=ALU.add), reads=[ps, CST], writes=[dt_all])
                T.op("act", lambda e: e.activation(out=dt_all.t[:, i, :], in_=dt_all.t[:, i, :], func=AF.Exp), reads=[dt_all], writes=[dt_all])
                T.op("act", lambda e: e.activation(out=dt_all.t[:, i, :], in_=dt_all.t[:, i, :], func=AF.Ln, bias=1.0), reads=[dt_all], writes=[dt_all])

            proj_T(wt, 32, cons_dt)

            for i in range(NT):
                a_i = sm.t[:, i, 0:32]
                T.op("dve", lambda e, i=i, a_i=a_i: e.tensor_tensor(out=a_i, in0=dt_all.t[:, i, :], in1=aneg[:], op=ALU.mult), reads=[dt_all, aneg], writes=[sm])
                ps = psA.next()
                T.op("pe", lambda e, a_i=a_i, ps=ps: e.matmul(ps.t[:, 0:32], lhsT=Lmask, rhs=a_i, start=True, stop=True), reads=[CST, sm], writes=[ps])
                T.op("pe", lambda e, a_i=a_i, ps=ps: e.matmul(ps.t[:, 32:64], lhsT=ones_f[:], rhs=a_i, start=True, stop=True), reads=[ones_f, sm], writes=[ps])
                T.op("act", lambda e, i=i, ps=ps: e.activation(out=sm.t[:, i, 32:96], in_=ps.t[:, 0:64], func=AF.Exp), reads=[ps], writes=[sm])
                T.op("dve", lambda e, i=i, ps=ps: e.tensor_copy(out=sm.t[:, i, 128:160], in_=ps.t[:, 0:32]), reads=[ps], writes=[sm])
                T.op("dve", lambda e, i=i, ps=ps: e.tensor_tensor(out=sm.t[:, i, 96:128], in0=ps.t[:, 32:64], in1=sm.t[:, i, 128:160], op=ALU.subtract), reads=[ps, sm], writes=[sm])
                T.op("act", lambda e, i=i: e.activation(out=sm.t[:, i, 96:128], in_=sm.t[:, i, 96:128], func=AF.Exp), reads=[sm], writes=[sm])
                T.op("dve", lambda e, i=i: e.tensor_tensor(out=sm.t[:, i, 96:128], in0=sm.t[:, i, 96:128], in1=dt_all.t[:, i, :], op=ALU.mult), reads=[sm, dt_all], writes=[sm])

            xw = Rot([sbg("xw%d" % i, [P, TB], BF16, S) for i in range(2)])
            tmpS = Rot([sbg("tmpS%d" % i, [P, TB], F32, S) for i in range(2)])
            if is_own:
                xdt = Rot([sbg("xdt%d" % i, [P, TB], BF16, S) for i in range(3)])
                hsb = Rot([sbg("hsb%d" % i, [P, TB], BF16, S) for i in range(3)])
                cbm = Rot([sbg("cbm%d" % i, [P, P], F32, S) for i in range(3)])
                Rr = Rot([sbg("Rr%d" % i, [P, 8, P], F32, S) for i in range(2)])
                dec = Rot([sbg("dec%d" % i, [P, 8, P], F32, S) for i in range(2)])
                Mm = Rot([sbg("Mm%d" % i, [P, 8, P], BF16, S) for i in range(3)])
                t1 = Rot([sbg("t1_%d" % i, [P, TB], F32, S) for i in range(3)])
                t2 = Rot([sbg("t2_%d" % i, [P, TB], F32, S) for i in range(2)])
                zs = Rot([sbg("zs%d" % i, [P, TB], F32, S) for i in range(3)])
                yn = Rot([sbg("yn%d" % i, [P, TB], BF16, S) for i in range(3)])
                st2r = Rot([sbg("st2_%d" % i, [P, 8], F32, S) for i in range(2)])
            v3 = lambda b: b.t[:, :].rearrange("p (h q) -> p h q", q=64)
            stageC = []
            stageD = []

            def make_R(g, i):
                Rb = Rr.next()
                a3 = sm.t[:, i, g * 8:(g + 1) * 8].unsqueeze(2).to_broadcast([P, 8, P])
                L3 = Lmask.unsqueeze(1).to_broadcast([P, 8, P])
                T.op("dve", lambda e: e.tensor_tensor(out=Rb[:, :, :], in0=a3, in1=L3, op=ALU.mult), reads=[sm, CST], writes=[Rb])
                return Rb
            Rnext = make_R(0, 0) if is_own else None
            for g in range(4):
                if is_own:
                    wz = wt_rot.next()
                    load_w(Wb, OFF_Z + g * 512, 512, wz)
                for i in range(NT):
                    tsl = slice(i * P, (i + 1) * P)
                    xg3 = x_tok.t[:, i, g * 512:(g + 1) * 512].rearrange("p (h q) -> p h q", q=64)
                    bc = lambda c0, i=i, g=g: sm.t[:, i, c0 + g * 8:c0 + g * 8 + 8].unsqueeze(2).to_broadcast([P, 8, 64])
                    xwb = xw.next()
                    T.op("dve" if not is_own else "pool", lambda e, xwb=xwb, xg3=xg3, bc=bc: e.tensor_tensor(out=v3(xwb), in0=xg3, in1=bc(96), op=ALU.mult), reads=[x_tok, sm], writes=[xwb])
                    if is_own:
                        xdb = xdt.next(stageC)
                        dtb3 = dt_all.t[:, i, g * 8:(g + 1) * 8].unsqueeze(2).to_broadcast([P, 8, 64])
                        T.op("pool", lambda e, xdb=xdb, xg3=xg3, dtb3=dtb3: e.tensor_tensor(out=v3(xdb), in0=xg3, in1=dtb3, op=ALU.mult), reads=[x_tok, dt_all], writes=[xdb])
                        hb_ = hsb.next(stageC)
                        T.op("act", lambda e, hb_=hb_, g=g: e.copy(out=hb_[:, :], in_=hs.t[:, g, :]), reads=[hs], writes=[hb_])
                        Rb = Rnext
                        psc = psA.next()
                        T.op("pe", lambda e, psc=psc, g=g, tsl=tsl: e.matmul(psc.t[:, 0:P], lhsT=BT.t[:, g, tsl], rhs=CTt.t[:, g, tsl], start=True, stop=True), reads=[BT, CTt], writes=[psc])
                        psd = [psA.next(), psA.next()]
                        for hh in range(2):
                            T.op("pe", lambda e, pd=psd[hh], Rb=Rb, hh=hh: e.matmul(pd.t[:, :], lhsT=Ustr, rhs=Rb.t[:, hh * 4:(hh + 1) * 4, :].rearrange("p a b -> p (a b)"), start=True, stop=True), reads=[CST, Rb], writes=[psd[hh]])
                    pss = psA.next()
                    T.op("pe", lambda e, pss=pss, xwb=xwb, i=i, g=g: e.matmul(pss.t[:, :], lhsT=Btok.t[:, i, g, :], rhs=xwb[:, :], start=True, stop=True), reads=[Btok, xwb], writes=[pss])
                    if is_own:
                        psz = psA.next()
                        mm_group(psz.t[:, :], [(hT.t[:, kc, tsl], wz.t[:, kc, :]) for kc in range(16)], [hT, wz], psz)
                        cbb = cbm.next()
                        T.op("dve", lambda e, psc=psc, cbb=cbb: e.tensor_tensor(out=cbb[:, :], in0=psc.t[:, 0:P], in1=Lmask, op=ALU.mult), reads=[psc, CST], writes=[cbb])
                        db = dec.next()
                        for hh in range(2):
                            T.op("act", lambda e, pd=psd[hh], db=db, hh=hh: e.activation(out=db.t[:, hh * 4:(hh + 1) * 4, :].rearrange("p a b -> p (a b)"), in_=pd.t[:, :], func=AF.Exp), reads=[psd[hh]], writes=[db])
                    tS = tmpS.next()
                    hs3 = hs.t[:, g, :].rearrange("p (h q) -> p h q", q=64)
                    T.op("pool" if is_own else "dve", lambda e, tS=tS, hs3=hs3, bc=bc: e.tensor_tensor(out=v3(tS), in0=hs3, in1=bc(64), op=ALU.mult), reads=[hs, sm], writes=[tS])
                    T.op("dve", lambda e, tS=tS, pss=pss, g=g: e.tensor_tensor(out=hs.t[:, g, :], in0=pss.t[:, :], in1=tS[:, :], op=ALU.add), reads=[pss, tS], writes=[hs])
                    if not is_own:
                        continue
                    zb = zs.next(stageC)
                    T.op("act", lambda e, zb=zb, psz=psz: e.activation(out=zb[:, :], in_=psz.t[:, :], func=AF.Silu), reads=[psz], writes=[zb])
                    Mb = Mm.next(stageC)
                    c3 = cbb.t[:, :].unsqueeze(1).to_broadcast([P, 8, P])
                    T.op("dve", lambda e, Mb=Mb, db=db, c3=c3: e.tensor_tensor(out=Mb[:, :, :], in0=db[:, :, :], in1=c3, op=ALU.mult), reads=[db, cbb], writes=[Mb])
                    nxt = (g, i + 1) if i + 1 < NT else (g + 1, 0)
                    if nxt[0] < 4:
                        Rnext = make_R(*nxt)

                    def stC(Mb=Mb, xdb=xdb, hb_=hb_, g=g, i=i, tsl=tsl, xg3=xg3, bc=bc, zb=zb):
                        psy = psA.next()
                        for h in range(8):
                            T.op("pe", lambda e, h=h: e.matmul(psy.t[:, h * 64:(h + 1) * 64], lhsT=Mb.t[:, h, :], rhs=xdb.t[:, h * 64:(h + 1) * 64], start=True, stop=True),
                                 reads=[Mb, xdb], writes=[psy], inc=(h == 7))
                        pso = psA.next()
                        T.op("pe", lambda e: e.matmul(pso.t[:, :], lhsT=CTt.t[:, g, tsl], rhs=hb_[:, :], start=True, stop=True), reads=[CTt, hb_], writes=[pso])
                        t1b = t1.next()
                        t2b = t2.next()
                        st2 = st2r.next()
                        dsk3 = CST.t[:, C_DSK + g * 8:C_DSK + g * 8 + 8].unsqueeze(2).to_broadcast([P, 8, 64])
                        T.op("dve", lambda e: e.tensor_tensor(out=v3(t1b), in0=pso.t[:, :].rearrange("p (h q) -> p h q", q=64), in1=bc(32), op=ALU.mult), reads=[pso, sm], writes=[t1b])
                        T.op("pool", lambda e: e.tensor_tensor(out=v3(t2b), in0=xg3, in1=dsk3, op=ALU.mult), reads=[x_tok, CST], writes=[t2b])
                        T.op("dve", lambda e: e.tensor_tensor(out=t1b[:, :], in0=psy.t[:, :], in1=t1b[:, :], op=ALU.add), reads=[psy, t1b], writes=[t1b])
                        T.op("dve", lambda e: e.tensor_tensor(out=t1b[:, :], in0=t1b[:, :], in1=t2b[:, :], op=ALU.add), reads=[t1b, t2b], writes=[t1b])
                        T.op("dve", lambda e: e.tensor_tensor(out=t1b[:, :], in0=t1b[:, :], in1=zb[:, :], op=ALU.mult), reads=[t1b, zb], writes=[t1b])
                        T.op("act", lambda e: e.activation(out=t2b[:, :], in_=t1b[:, :], func=AF.Square, accum_out=st2[:, 0:1]), reads=[t1b], writes=[t2b, st2])
                        rstd_from_ssq(st2[:, 0:1], st2[:, 1:2], 512, st2)
                        ynb = yn.next(stageD)
                        T.op("dve", lambda e: e.tensor_scalar(out=ynb[:, :], in0=t1b[:, :], scalar1=st2[:, 1:2], scalar2=None, op0=ALU.mult), reads=[t1b, st2], writes=[ynb])

                        def stD(ynb=ynb):
                            pst = psA.next()
                            for j in range(4):
                                T.op("pe", lambda e, j=j: e.transpose(out=psb(pst)[:, j, :], in_=ynb.t[:, j * P:(j + 1) * P], identity=ident_b[:]), reads=[ynb, ident_b], writes=[pst], inc=(j == 3))
                            for j in range(4):
                                c = g * 4 + j
                                T.op("act", lambda e, j=j, c=c: e.activation(out=y_mT.t[:, c, tsl], in_=psb(pst)[:, j, :], func=AF.Copy, scale=CST.t[:, C_GN + c:C_GN + c + 1]), reads=[pst, CST], writes=[y_mT])
                        defer(stageD, stD, [ynb])
                    defer(stageC, stC, [Mb, xdb, hb_, zb])
                    run_pipe(stageD, 1)
                    run_pipe(stageC, 1)
            run_pipe(stageC, 0)
            run_pipe(stageD, 0)
            if is_own:
                dump("x_tok%d" % bi, x_tok.t[:, :, :].rearrange("p a b -> p (a b)"), [P, NT * D], F32, [x_tok])
                dump("dt%d" % bi, dt_all.t[:, :, :].rearrange("p a b -> p (a b)"), [P, NT * 32], F32, [dt_all])
                dump("ymT%d" % bi, y_mT.t[:, :, :].rearrange("p a b -> p (a b)"), [P, 16 * TB], BF16, [y_mT])
            return y_mT

        def kv_proj(S, key0, is_own, qT, qiT, ws):
            ktmp = Rot([sbg("ktmp%d" % i, [P, TB], BF16, S) for i in range(2)])
            vtmp = Rot([sbg("vtmp%d" % i, [P, 4, 129], BF16, S) for i in range(2)])
            for v_ in vtmp.items:
                T.op("pool", lambda e, v_=v_: e.memset(v_[:, :, :], 1.0), writes=[v_])
            for grp in range(4):
                wt = wt_rot.next()
                load_w(Wb, OFF_K + grp * 512, 512, wt)

                def cons(ci, ps, grp=grp):
                    hh = grp * 4 + ci
                    kb_ = ktmp.next()
                    evac(kb_[:, :], ps.t[:, :], [ps], [kb_])
                    T.dma("pool", KT[hh, :, key0:key0 + TB], kb_[:, :], reads=[kb_], slot=kb_.name)
                proj_F(wt, 512, cons)
            for grp in range(4):
                wt = wt_rot.next()
                load_w(Wb, OFF_V + grp * 512, 512, wt)

                def cons(i, ps, grp=grp):
                    vb_ = vtmp.next()
                    evac(vb_.t[:, :, 0:P], ps.t[:, :].rearrange("p (h d) -> p h d", d=P), [ps], [vb_])
                    kbg = key0 // P + i
                    T.dma("pool", VS[grp * 4:(grp + 1) * 4, kbg // 16, :, kbg % 16, :].rearrange("h p d -> p h d"), vb_[:, :, :], reads=[vb_], slot=vb_.name)
                proj_T(wt, 512, cons)
            wt = wt_rot.next()
            load_w(Wb, OFF_KI, 64, wt, dcol=0)
            load_w(Wb, OFF_KI, 64, wt, dcol=64)

            def cons(ci, ps):
                kb_ = ktmp.next()
                evac(kb_[:, :], ps.t[:, :], [ps], [kb_])
                T.dma("pool", KI[0, 0:64, key0:key0 + TB], kb_[0:64, :], reads=[kb_], slot=kb_.name)
                T.dma("pool", KI[1, 64:128, key0:key0 + TB], kb_[64:128, :], reads=[kb_], slot=kb_.name)
            proj_F(wt, 128, cons)
            if not is_own:
                return
            for grp in range(4):
                wt = wt_rot.next()
                load_w(Wb, OFF_Q + grp * 512, 512, wt)

                def cons(ci, ps, grp=grp):
                    evac(qT.t[:, grp * 4 + ci, :], ps.t[:, :], [ps], [qT], scale=ATTN_SCALE)
                proj_F(wt, 512, cons)
            for grp in range(2):
                wt = wt_rot.next()
                load_w(Wb, OFF_QI + grp * 512, 512, wt)

                def cons(ci, ps, grp=grp):
                    evac(qiT.t[:, grp * 4 + ci, :], ps.t[:, :], [ps], [qiT])
                proj_F(wt, 512, cons)
            wt = wt_rot.next()
            load_w(Wb, OFF_WI, 16, wt)

            def cons(i, ps):
                T.op("dve", lambda e: e.tensor_scalar(out=ws.t[:, i, :], in0=ps.t[:, 0:16], scalar1=IDX_SCALE, scalar2=None, op0=ALU.mult), reads=[ps], writes=[ws])
            proj_T(wt, 16, cons)

        def dsa_idx(S, bi, key0, hb, qiT, ws, maskT):
            nkb = (key0 + 256 * (hb + 1)) // P
            W = nkb * P
            nb5 = (W + 511) // 512
            LA = 4
            LA2 = 6
            SI = ExitStack()
            Dgr = Rot([sbg("Dg%d" % i, [P, 16, P], BF16, SI) for i in range(2)])
            Rl = Rot([sbg("Rl%d" % i, [P, 512], BF16, SI) for i in range(6)])
            kib = Rot([sbg("kib%d" % i, [P, 2, 512], BF16, SI) for i in range(3)])

            def indexer(j2):
                i = 2 * hb + j2
                tsl = slice(i * P, (i + 1) * P)
                Dg = Dgr.next()
                pend = []
                for h in range(16):
                    T.op("pool", lambda e, h=h, i=i, Dg=Dg: e.tensor_scalar(out=Dg.t[:, h, :], in0=ident_f, scalar1=ws.t[:, i, h:h + 1], scalar2=1.0, op0=ALU.mult, op1=ALU.mult), reads=[CST, ws], writes=[Dg])
                for b5 in range(nb5):
                    c0 = b5 * 512
                    wd = min(512, W - c0)
                    kb_ = kib.next()
                    T.dma("sp", kb_[:, 0, 0:wd], KI[0, :, c0:c0 + wd], writes=[kb_], slot=kb_.name)
                    T.dma("sp", kb_[:, 1, 0:wd], KI[1, :, c0:c0 + wd], writes=[kb_], slot=kb_.name)
                    acc = psO.next(pend)
                    extra = []
                    if c0 < NPB * TB:
                        extra.append("pv")
                    if b5 == nb5 - 1:
                        extra.append("bp")
                    nacc = 16 + len(extra)
                    for h in range(16):
                        j, sub = h // 2, h % 2
                        ps = psG.next()
                        T.op("pe", lambda e, ps=ps, j=j, sub=sub, kb_=kb_, wd=wd, tsl=tsl: e.matmul(ps.t[:, 0:wd], lhsT=qiT.t[:, j, tsl], rhs=kb_.t[:, sub, 0:wd], start=True, stop=True), reads=[qiT, kb_], writes=[ps])
                        rb = Rl.next(pend)
                        if h % 2 == 0:
                            T.op("act", lambda e, rb=rb, ps=ps, wd=wd: e.activation(out=rb[:, 0:wd], in_=ps.t[:, 0:wd], func=AF.Relu), reads=[ps], writes=[rb])
                        else:
                            T.op("dve", lambda e, rb=rb, ps=ps, wd=wd: e.tensor_scalar(out=rb[:, 0:wd], in0=ps.t[:, 0:wd], scalar1=0.0, scalar2=None, op0=ALU.max), reads=[ps], writes=[rb])

                        def accmm(acc=acc, h=h, rb=rb, wd=wd, nacc=nacc, Dg=Dg):
                            T.op("pe", lambda e: e.matmul(acc.t[:, 0:wd], lhsT=Dg.t[:, h, :], rhs=rb.t[:, 0:wd], start=(h == 0), stop=(h == nacc - 1)), reads=[Dg, rb], writes=[acc])
                        defer(pend, accmm, [rb, acc])
                        run_pipe(pend, LA)

                    def fin(acc=acc, wd=wd, nacc=nacc, extra=tuple(extra), c0=c0):
                        k = 16
                        for ex in extra:
                            if ex == "pv":
                                T.op("pe", lambda e, k=k: e.matmul(acc.t[:, 0:wd], lhsT=pvm[:], rhs=ones_b.t[:, 0:wd], start=False, stop=(k == nacc - 1)), reads=[pvm, ones_b], writes=[acc])
                            else:
                                T.op("pe", lambda e, k=k: e.matmul(acc.t[:, wd - 256:wd], lhsT=ident_b[:], rhs=bpat_b.t[:, j2, :], start=False, stop=(k == nacc - 1)), reads=[ident_b, bpat_b], writes=[acc])
                            k += 1
                        T.op("act", lambda e: e.copy(out=score.t[:, c0:c0 + wd], in_=acc.t[:, 0:wd]), reads=[acc], writes=[score])
                    defer(pend, fin, [acc])
                run_pipe(pend, 0)

            def bisect(j2):
                i = 2 * hb + j2
                hi_t, mid_t, cnt_t, t_t, g_t, lo_t, sa_t = [sbg("bq%d_%d" % (j2, c), [P, 2], F32, SI) for c in range(7)]
                WD = min(4096, max(P, ((W * 45 // 100) // P) * P))
                nA = W - WD
                assert nA >= P and nA <= 8192
                hi, mid, cntc, tt, gg, lo, sa = [b_.t[:, 0:1] for b_ in (hi_t, mid_t, cnt_t, t_t, g_t, lo_t, sa_t)]
                T.op("dve", lambda e: e.reduce_max(out=hi, in_=score.t[:, 0:W], axis=AX.X), reads=[score], writes=[hi_t])
                T.op("dve", lambda e: e.tensor_scalar(out=mid, in0=hi, scalar1=-CLAMP * 0.5, scalar2=None, op0=ALU.add), reads=[hi_t], writes=[mid_t])
                for it in range(NITER):
                    hk = CLAMP * (0.5 ** (it + 1))
                    T.op("dve", lambda e: e.tensor_scalar(out=XM.t[:, 0:WD], in0=score.t[:, 0:WD], scalar1=mid, scalar2=None, op0=ALU.is_ge, op1=ALU.add, accum_out=cntc), reads=[score, mid_t], writes=[mjD, cnt_t])
                    T.op("act", lambda e: e.activation(out=XM8[:, 0:nA], in_=score.t[:, WD:W], func=AF.Sign, bias=mid, scale=-1.0, accum_out=sa), reads=[score, mid_t], writes=[mjA, sa_t])
                    T.op("dve", lambda e: e.scalar_tensor_tensor(out=tt, in0=cntc, scalar=2.0, in1=sa, op0=ALU.mult, op1=ALU.subtract), reads=[cnt_t, sa_t], writes=[t_t])
                    T.op("dve", lambda e: e.tensor_scalar(out=gg, in0=tt, scalar1=float(2 * TOPK - 1 - nA), scalar2=-0.5, op0=ALU.is_ge, op1=ALU.add), reads=[t_t], writes=[g_t])
                    T.op("dve", lambda e, hk=hk: e.scalar_tensor_tensor(out=mid, in0=gg, scalar=hk, in1=mid, op0=ALU.mult, op1=ALU.add), reads=[g_t, mid_t], writes=[mid_t])
                hN = CLAMP * (0.5 ** (NITER + 1))
                T.op("dve", lambda e: e.tensor_scalar(out=lo, in0=mid, scalar1=-hN, scalar2=None, op0=ALU.add), reads=[mid_t], writes=[lo_t])
                T.op("dve", lambda e: e.tensor_scalar(out=XM.t[:, 0:W], in0=score.t[:, 0:W], scalar1=lo, scalar2=None, op0=ALU.is_ge), reads=[score, lo_t], writes=[mjD, mjA])
                dump("thr%d_%d" % (bi, i), lo, [P, 1], F32, [lo_t])
                dump("score%d_%d" % (bi, i), score.t[:, 0:W], [P, W], F32, [score])

            def mask_transposes(j2):
                for kb0 in range(0, nkb, 8):
                    n = min(8, nkb - kb0)
                    ps = psG.next()
                    for t in range(n):
                        T.op("pe", lambda e, ps=ps, t=t, kb0=kb0: e.transpose(out=psb(ps)[:, t, :], in_=XM.t[:, (kb0 + t) * P:(kb0 + t + 1) * P], identity=ident_b[:]), reads=[mjD, mjA, ident_b], writes=[ps], inc=(t == n - 1))
                    T.op("act", lambda e, ps=ps, n=n, kb0=kb0, j2=j2: e.copy(out=maskT.t[:, kb0:kb0 + n, hb * 256 + j2 * P:hb * 256 + (j2 + 1) * P], in_=psb(ps)[:, 0:n, :]), reads=[ps], writes=[maskT])

            indexer(0)
            bisect(0)
            indexer(1)
            mask_transposes(0)
            bisect(1)
            mask_transposes(1)
            T.barrier()
            SI.close()


        def attention_block(S, bi, key0, qT, y_aT, maskT):
            nkbs = [(key0 + 256 * (hb + 1)) // P for hb in range(2)]
            nkb = nkbs[1]
            LA2 = 6
            wga = Rot([sbg("wga%d" % i, [P, 16, P], BF16, S) for i in range(1)])
            gaT = Rot([sbg("gaT%d" % i, [P, TB], F32, S) for i in range(2)])
            gex = Rot([sbg("gex%d" % i, [P, TB], F32, S) for i in range(1)])
            ktb = Rot([sbg("ktb%d" % i, [P, 2048], BF16, S) for i in range(2)])
            vtb = Rot([sbg("vtb%d" % i, [P, 16, 129], BF16, S) for i in range(2)])
            eb = Rot([sbg("eb%d" % i, [P, 512], BF16, S) for i in range(4)])
            pmb = Rot([sbg("pmb%d" % i, [P, 512], BF16, S) for i in range(8)])
            onb = Rot([sbg("onb%d" % i, [P, P], BF16, S) for i in range(4)])
            rdr = Rot([sbg("rd%d" % i, [P, 4], F32, S) for i in range(2)])
            psQ = Rot(PS[0:3])
            psX = Rot(PS[3:4])
            pend = []
            for hh in range(16):
                ga = gaT.next(pend)
                O = list(PS[4:8])
                wg = wga.next()
                T.dma("sp", wg[:, :, :], Wb[:, OFF_GA + hh * P:OFF_GA + (hh + 1) * P].rearrange("(kc p) c -> p kc c", p=P), writes=[wg], slot=wg.name)
                psg = psX.next()
                mm_group(psg.t[:, :], [(wg.t[:, kc, :], hT.t[:, kc, :]) for kc in range(16)], [wg, hT], psg)
                gx = gex.next()
                T.op("act", lambda e, gx=gx, psg=psg: e.activation(out=gx[:, :], in_=psg.t[:, :], func=AF.Exp, scale=-1.0), reads=[psg], writes=[gx])
                T.op("dve", lambda e, gx=gx: e.tensor_scalar(out=gx[:, :], in0=gx[:, :], scalar1=1.0, scalar2=None, op0=ALU.add), reads=[gx], writes=[gx])
                T.op("dve", lambda e, gx=gx: e.reciprocal(out=gx[:, :], in_=gx[:, :]), reads=[gx], writes=[gx])
                T.op("dve", lambda e, ga=ga, gx=gx, psg=psg: e.tensor_tensor(out=ga[:, :], in0=psg.t[:, :], in1=gx[:, :], op=ALU.mult), reads=[psg, gx], writes=[ga])
                for c2 in range((nkb + 15) // 16):
                    kb_lo = c2 * 16
                    n = min(16, nkb - kb_lo)
                    kt = ktb.next(pend)
                    vt = vtb.next(pend)
                    T.dma("sp", kt[:, 0:n * P], KT[hh, :, kb_lo * P:(kb_lo + n) * P], writes=[kt], slot=kt.name)
                    T.dma("sp", vt[:, 0:n, :], VS[hh, c2, :, 0:n, :], writes=[vt], slot=vt.name)
                    for t in range(0, n, 2):
                        kb = kb_lo + t
                        for hb in range(2):
                            if kb >= nkbs[hb]:
                                continue
                            qsl = slice(hb * 256, (hb + 1) * 256)
                            ps = psQ.next()
                            for u in range(2):
                                T.op("pe", lambda e, ps=ps, kt=kt, t=t, u=u, hh=hh, qsl=qsl: e.matmul(ps.t[:, u * 256:(u + 1) * 256], lhsT=kt.t[:, (t + u) * P:(t + u + 1) * P], rhs=qT.t[:, hh, qsl], start=True, stop=True), reads=[kt, qT], writes=[ps], inc=(u == 1))
                            eb_ = eb.next()
                            T.op("act", lambda e, eb_=eb_, ps=ps: e.activation(out=eb_[:, :], in_=ps.t[:, :], func=AF.Exp), reads=[ps], writes=[eb_])
                            pm = pmb.next(pend)
                            T.op("dve", lambda e, pm=pm, eb_=eb_, kb=kb, qsl=qsl: e.tensor_tensor(out=pm.t[:, :].rearrange("p (a b) -> p a b", b=256), in0=eb_.t[:, :].rearrange("p (a b) -> p a b", b=256), in1=maskT.t[:, kb:kb + 2, qsl], op=ALU.mult), reads=[eb_, maskT], writes=[pm])

                            def pv(pm=pm, vt=vt, t=t, kb=kb, O=O, hb=hb):
                                for u in range(2):
                                    for j2 in range(2):
                                        T.op("pe", lambda e, u=u, j2=j2: e.matmul(O[2 * hb + j2].t[:, 0:129], lhsT=pm.t[:, u * 256 + j2 * P:u * 256 + (j2 + 1) * P], rhs=vt.t[:, t + u, :], start=(kb + u == 0), stop=(kb + u == nkbs[hb] - 1)),
                                             reads=[pm, vt], writes=[O[2 * hb + j2]], inc=(u == 1 and j2 == 1))
                            defer(pend, pv, [pm, vt] + O)
                            run_pipe(pend, LA2)

                def finalize(O=O, ga=ga, hh=hh):
                    pst = psX.next()
                    rd = rdr.next()
                    for jt in range(4):
                        T.op("dve", lambda e, jt=jt: e.reciprocal(out=rd.t[:, jt:jt + 1], in_=O[jt].t[:, 128:129]), reads=[O[jt]], writes=[rd])
                        ob = onb.next()
                        T.op("dve", lambda e, jt=jt, ob=ob: e.tensor_scalar(out=ob[:, :], in0=O[jt].t[:, 0:P], scalar1=rd.t[:, jt:jt + 1], scalar2=None, op0=ALU.mult), reads=[O[jt], rd], writes=[ob])
                        T.op("pe", lambda e, jt=jt, ob=ob: e.transpose(out=psb(pst)[:, jt, :], in_=ob[:, :], identity=ident_b[:]), reads=[ob, ident_b], writes=[pst])
                    T.op("dve", lambda e: e.tensor_tensor(out=y_aT.t[:, hh, :], in0=psb(pst)[:, 0:4, :].rearrange("p a b -> p (a b)"), in1=ga[:, :], op=ALU.mult), reads=[pst, ga], writes=[y_aT])
                defer(pend, finalize, [ga] + O)
            run_pipe(pend, 0)

        def out_block2(S, bi, orow0, y_aT):
            y_mT = sbg("y_mT2", [P, 16, TB], BF16, S)
            T.dma("sp", y_mT.t[:, :, :].rearrange("p a b -> p (a b)"), YM, writes=[y_mT], slot="ymld")
            mT = sbg("mT", [P, 16, TB], BF16, S)
            r = sbg("r", [P, NT, D], F32, S)
            nfb = sbg("nfb", [P, D], F32, S)
            stashb = sbg("stash", [P, 4, TB], F32, S)
            gm = Rot([sbg("gm%d" % i, [P, TB], F32, S) for i in range(2)])
            m2 = Rot([sbg("m2_%d" % i, [P, TB], F32, S) for i in range(2)])
            st3 = sbg("st3", [P, 8], F32, S)
            T.dma("sp", nfb[:, :], normf.partition_broadcast(P), writes=[nfb], slot="nfb")
            for i in range(NT):
                T.dma("sp", r.t[:, i, :], xo[orow0 + i * P:orow0 + (i + 1) * P, :], writes=[r], slot="rld")
            for cg in range(4):
                plan = ((Wbr, cg * 512, 0), (Wb, OFF_G + cg * 512, 0), (Wbr, cg * 512, D), (Wb, OFF_G + D + cg * 512, 0))
                for half in range(2):
                    wA = wt_rot.next()
                    load_w(plan[2 * half][0], plan[2 * half][1], 512, wA, rows0=plan[2 * half][2])
                    ysrc = y_mT if half == 0 else y_aT
                    psas = []
                    for ci in range(4):
                        psa = psA.next()
                        mm_group(psa.t[:, :], [(wA.t[:, kc, ci * P:(ci + 1) * P], ysrc.t[:, kc, :]) for kc in range(16)], [wA, ysrc], psa)
                        psas.append(psa)
                    wG = wt_rot.next()
                    load_w(plan[2 * half + 1][0], plan[2 * half + 1][1], 512, wG, rows0=plan[2 * half + 1][2])
                    for ci in range(4):
                        c = cg * 4 + ci
                        psa = psas[ci]
                        psg = psA.next()
                        mm_group(psg.t[:, :], [(wG.t[:, kc, ci * P:(ci + 1) * P], hT.t[:, kc, :]) for kc in range(16)], [wG, hT], psg)
                        gb = gm.next()
                        gcol = C_GB + half * 16 + c
                        T.op("act", lambda e, gb=gb, psg=psg, gcol=gcol: e.activation(out=gb[:, :], in_=psg.t[:, :], func=AF.Sigmoid, bias=CST.t[:, gcol:gcol + 1]), reads=[psg, CST], writes=[gb])
                        if half == 0:
                            T.op("dve", lambda e, psa=psa, gb=gb, ci=ci: e.tensor_tensor(out=stashb.t[:, ci, :], in0=psa.t[:, :], in1=gb[:, :], op=ALU.mult), reads=[psa, gb], writes=[stashb])
                        else:
                            mb2 = m2.next()
                            T.op("dve", lambda e, mb2=mb2, psa=psa, gb=gb: e.tensor_tensor(out=mb2[:, :], in0=psa.t[:, :], in1=gb[:, :], op=ALU.mult), reads=[psa, gb], writes=[mb2])
                            T.op("pool", lambda e, mb2=mb2, c=c, ci=ci: e.tensor_tensor(out=mT.t[:, c, :], in0=mb2[:, :], in1=stashb.t[:, ci, :], op=ALU.add), reads=[mb2, stashb], writes=[mT])
            dump("mT%d" % bi, mT.t[:, :, :].rearrange("p a b -> p (a b)"), [P, 16 * TB], BF16, [mT])
            for n in range(4):
                wo = wt_rot.next()
                load_w(Wo, n * 512, 512, wo)
                for i in range(NT):
                    ps = psA.next()
                    mm_group(ps.t[:, :], [(mT.t[:, kc, i * P:(i + 1) * P], wo.t[:, kc, :]) for kc in range(16)], [mT, wo], ps)
                    T.op("dve", lambda e, ps=ps, i=i, n=n: e.tensor_tensor(out=r.t[:, i, n * 512:(n + 1) * 512], in0=ps.t[:, :], in1=r.t[:, i, n * 512:(n + 1) * 512], op=ALU.add), reads=[ps, r], writes=[r])
            for i in range(NT):
                T.op("act", lambda e, i=i: e.activation(out=xs[:], in_=r.t[:, i, :], func=AF.Square, accum_out=st3[:, 0:1]), reads=[r], writes=[xs, st3])
                rstd_from_ssq(st3[:, 0:1], st3[:, 1:2], D, st3)
                T.op("dve", lambda e, i=i: e.scalar_tensor_tensor(out=r.t[:, i, :], in0=r.t[:, i, :], scalar=st3[:, 1:2], in1=nfb[:, :], op0=ALU.mult, op1=ALU.mult), reads=[r, st3, nfb], writes=[r])
                T.dma("pool", out[orow0 + i * P:orow0 + (i + 1) * P, :], r.t[:, i, :], reads=[r], slot="ost")

        for bi in range(NPB + NOB):
            is_own = bi >= NPB
            key0 = bi * TB
            xsrc, row0 = (xo, (bi - NPB) * TB) if is_own else (xp, bi * TB)
            build_hT(xsrc, row0)
            if bi == NPB:
                dump("hT", hT.t[:, :, :].rearrange("p a b -> p (a b)"), [P, 16 * TB], BF16, [hT])
            with ExitStack() as S1:
                y_mT = ssd_block(S1, is_own, bi)
                if is_own:
                    T.dma("pool", YM, y_mT.t[:, :, :].rearrange("p a b -> p (a b)"), reads=[y_mT], slot="ymst")
                if bi == NPB - 1:
                    T.op("dve", lambda e: e.tensor_scalar(out=hs.t[:, :, :], in0=hs.t[:, :, :], scalar1=CST.t[:, C_PV:C_PV + 1], scalar2=None, op0=ALU.mult), reads=[hs, CST], writes=[hs])
                T.barrier()
            if not is_own:
                with ExitStack() as S2:
                    kv_proj(S2, key0, False, None, None, None)
                    T.barrier()
                continue
            with ExitStack() as SO:
                y_aT = sbg("y_aT", [P, 16, TB], BF16, SO)
                with ExitStack() as S2:
                    qT = sbg("qT", [P, 16, TB], BF16, S2)
                    qiT = sbg("qiT", [P, 8, TB], BF16, S2)
                    ws = sbg("ws", [P, NT, 16], F32, S2)
                    with ExitStack() as S2a:
                        kv_proj(S2a, key0, True, qT, qiT, ws)
                        T.barrier()
                    with ExitStack() as S2b:
                        maskT = sbg("maskT", [P, 64, TB], mybir.dt.uint8, S2b)
                        for hb in range(2):
                            dsa_idx(S2b, bi, key0, hb, qiT, ws, maskT)
                        attention_block(S2b, bi, key0, qT, y_aT, maskT)
                        T.barrier()
                dump("yaT%d" % bi, y_aT.t[:, :, :].rearrange("p a b -> p (a b)"), [P, 16 * TB], BF16, [y_aT])
                with ExitStack() as S3:
                    out_block2(S3, bi, (bi - NPB) * TB, y_aT)
                    T.barrier()
        T.barrier()
        T.emit()
    return nc, dbg_out


def make_cst(inputs, half):
    c = np.zeros((P, NCST), np.float32)
    c[:, C_ID:C_ID + 128] = np.eye(128, dtype=np.float32)
    u = np.arange(128)
    c[:, C_L:C_L + 128] = (u[:, None] <= u[None, :]).astype(np.float32)
    c[:, C_U:C_U + 128] = (u[:, None] > u[None, :]).astype(np.float32)
    cw = np.asarray(inputs["conv_w"][0], np.float32)
    c[:, C_CW:C_CW + 96] = cw.T.reshape(24, 128, 4).transpose(1, 0, 2).reshape(128, 96)
    c[:, C_CB:C_CB + 24] = np.asarray(inputs["conv_b"][0], np.float32).reshape(24, 128).T
    c[:, C_GB:C_GB + 32] = np.asarray(inputs["gate_bias"][0], np.float32).reshape(32, 128).T
    c[:, C_G:C_G + 16] = np.asarray(inputs["norm_in"][0], np.float32).reshape(16, 128).T
    c[:, C_GN:C_GN + 16] = np.asarray(inputs["gn_m"][0], np.float32).reshape(16, 128).T
    c[:, C_DTB:C_DTB + 32] = np.asarray(inputs["dt_bias"][0], np.float32)[None, :]
    c[:, C_ALOG:C_ALOG + 32] = np.asarray(inputs["a_log"][0], np.float32)[None, :]
    c[:, C_DSK:C_DSK + 32] = np.asarray(inputs["d_skip"][0], np.float32)[None, :]
    c[:, C_PV] = 1.0 if half == 1 else 0.0
    c[:, C_CVEC:C_CVEC + NITER] = (0.5 ** np.arange(1, NITER + 1, dtype=np.float64)).astype(np.float32)[None, :]
    q = np.arange(128)[:, None]
    k = np.arange(128)[None, :]
    diag = np.where((q < 64) & (k >= 64), NEG, 0.0).astype(np.float32)
    full = np.full((128, 128), NEG, np.float32)
    zero = np.zeros((128, 128), np.float32)
    c[:, C_BP:C_BP + 256] = np.concatenate([diag, full], 1)
    c[:, C_BP + 256:C_BP + 512] = np.concatenate([zero, diag], 1)
    c[0, C_PVROW:C_PVROW + 128] = 0.0 if half == 1 else NEG
    return c


_NC_CACHE = {}


def run(inputs, L, dbg=()):
    B = inputs["x"].shape[0]
    assert B * 2 == 8
    H = L // 2
    NPB = NOB = H // TB
    key = (NPB, NOB, tuple(dbg))
    if key not in _NC_CACHE:
        _NC_CACHE[key] = build(NPB, NOB, dbg)
    nc, dbg_out = _NC_CACHE[key]
    x = np.asarray(inputs["x"], np.float32)
    w_in = np.ascontiguousarray(np.asarray(inputs["w_in"][0], np.float32))
    w_br = np.ascontiguousarray(np.asarray(inputs["w_branch"][0], np.float32))
    w_out = np.ascontiguousarray(np.asarray(inputs["w_out"][0], np.float32))
    normf = np.asarray(inputs["norm_final"], np.float32).reshape(1, D)
    csts = [make_cst(inputs, 0), make_cst(inputs, 1)]
    zeros = np.zeros((H, D), np.float32)
    in_maps = []
    for c in range(8):
        b, half = c // 2, c % 2
        in_maps.append({
            "xp": np.ascontiguousarray(x[b, 0:H]) if half == 1 else zeros,
            "xo": np.ascontiguousarray(x[b, half * H:(half + 1) * H]),
            "w_in": w_in, "w_br": w_br, "w_out": w_out,
            "cst": csts[half], "normf": normf,
        })
    res = run_bass_kernel_spmd(nc, in_maps, core_ids=list(range(8)))
    outp = np.empty((B, L, D), np.float32)
    for c in range(8):
        b, half = c // 2, c % 2
        outp[b, half * H:(half + 1) * H] = res.results[c]["out"]
    return outp, res


def kernel(**inputs):
    L = inputs["x"].shape[1]
    outp, _ = run(inputs, L)
    return outp
```

```python
import numpy as np
import ml_dtypes
from contextlib import ExitStack
import concourse.bass as bass
import concourse.mybir as mybir
from concourse.bass_utils import run_bass_kernel_spmd

F32 = mybir.dt.float32
BF16 = mybir.dt.bfloat16
AF = mybir.ActivationFunctionType
ALU = mybir.AluOpType
AX = mybir.AxisListType

P = 128
D = 2048
TB = 512
NT = 4
EPS = 1e-6
OFF_Z, OFF_XBC, OFF_DT, OFF_Q, OFF_K, OFF_V, OFF_GA, OFF_QI, OFF_KI, OFF_WI, OFF_G = (
    0, 2048, 5120, 5152, 7200, 9248, 11296, 13344, 14368, 14432, 14448)
D_IN = 18544
ATTN_SCALE = 128 ** -0.5
IDX_SCALE = (64 ** -0.5) * (16 ** -0.5)
TOPK = 256
NITER = 21
CLAMP = 8.0
NEG = -30000.0

C_ID, C_L, C_U, C_CW, C_CB, C_GB, C_G, C_GN, C_DTB, C_ALOG, C_DSK, C_PV, C_CVEC, C_BP, C_PVROW, NCST = (
    0, 128, 256, 384, 480, 504, 536, 552, 568, 600, 632, 664, 672, 704, 1216, 1344)


class Buf:
    def __init__(self, t, name=""):
        self.t = t
        self.name = name
        self.w = None
        self.r = []
        self.alias = []
        self.pending = 0

    def __getitem__(self, k):
        return self.t[k]


class Trk:
    LIM = 30000

    def __init__(self, nc, es):
        self.nc = nc
        self.es = es
        self.engs = {"pe": nc.tensor, "act": nc.scalar, "dve": nc.vector, "pool": nc.gpsimd, "sp": nc.sync}
        self.ctr = {}
        self.prog = {k: [] for k in self.engs}
        self.known = {k: {} for k in self.engs}
        self.nsem = 0
        for k in self.engs:
            self._newctr(k, 1)

    def _newsem(self, name):
        self.nsem += 1
        return self.es.enter_context(self.nc.semaphore("%s_%d" % (name, self.nsem)))

    def _newctr(self, name, step):
        self.ctr[name] = {"step": step, "ep": 0, "cnt": 0, "sems": {0: self._newsem(name)}}

    def _next_ms(self, name):
        c = self.ctr[name]
        if (c["cnt"] + 1) * c["step"] > self.LIM:
            return (name, c["ep"] + 1, 1)
        return (name, c["ep"], c["cnt"] + 1)

    def _inc(self, name):
        c = self.ctr[name]
        ms = self._next_ms(name)
        if ms[1] != c["ep"]:
            c["ep"] = ms[1]
            c["sems"][ms[1]] = self._newsem(name)
        c["cnt"] = ms[2]
        return ms, c["sems"][c["ep"]], c["step"]

    def _cur_ms(self, name):
        c = self.ctr[name]
        return (name, c["ep"], c["cnt"])

    def _need(self, e, deps):
        need = {}
        for d in deps:
            if d is None:
                continue
            name, ep, idx = d
            if idx == 0:
                continue
            k = (name, ep)
            if idx > need.get(k, 0):
                need[k] = idx
        for (name, ep), idx in need.items():
            c = self.ctr[name]
            if name == e and (ep, idx) > (c["ep"], c["cnt"]):
                continue
            if self.known[e].get((name, ep), 0) >= idx:
                continue
            later = [kk for kk in self.known[e] if kk[0] == name and kk[1] > ep]
            if later:
                continue
            self.prog[e].append((lambda en, v_=idx * c["step"], nm=(name, ep): en.wait_ge(self._sem_of(nm), v_)))
            self.known[e][(name, ep)] = idx

    def _sem_of(self, nm):
        return self.ctr[nm[0]]["sems"][nm[1]]

    def _deps(self, reads, writes):
        deps = []
        for b in reads:
            deps.append(b.w)
            for a in b.alias:
                deps.append(a.w)
        for b in writes:
            deps.append(b.w)
            deps.extend(b.r)
            for a in b.alias:
                deps.append(a.w)
                deps.extend(a.r)
        return deps

    def _mark(self, me, reads, writes):
        for b in reads:
            b.r.append(me)
        for b in writes:
            b.w = me
            b.r = []

    def op(self, e, fn, reads=(), writes=(), inc=True):
        self._need(e, self._deps(reads, writes))
        if inc:
            me, sem, step = self._inc(e)
            self.prog[e].append((lambda en, f_=fn, s_=sem: f_(en).then_inc(s_, 1)))
        else:
            me = self._next_ms(e)
            self.prog[e].append(fn)
        self._mark(me, reads, writes)

    def dma(self, q, out, in_, reads=(), writes=(), slot=None, **kw):
        self._need(q, self._deps(reads, writes))
        name = "d:" + slot
        if name not in self.ctr:
            self._newctr(name, 16)
        me, sem, step = self._inc(name)
        self.prog[q].append((lambda en, o_=out, i_=in_, k_=kw, s_=sem: en.dma_start(out=o_, in_=i_, **k_).then_inc(s_, 16)))
        self._mark(me, reads, writes)

    def barrier(self):
        allms = [self._cur_ms(n) for n in self.ctr]
        for e in self.engs:
            self._need(e, [m for m in allms if m[0] != e])

    def emit(self):
        with self.nc.Block() as block:
            @block.tensor
            def _(en):
                for f in self.prog["pe"]:
                    f(en)

            @block.scalar
            def _(en):
                for f in self.prog["act"]:
                    f(en)

            @block.vector
            def _(en):
                for f in self.prog["dve"]:
                    f(en)

            @block.gpsimd
            def _(en):
                for f in self.prog["pool"]:
                    f(en)

            @block.sync
            def _(en):
                for f in self.prog["sp"]:
                    f(en)


class Rot:
    def __init__(self, items):
        self.items = items
        self.i = 0

    def next(self, pend=None):
        b = self.items[self.i % len(self.items)]
        while b.pending > 0 and pend:
            pend.pop(0)()
        assert b.pending == 0, "rotation reuses %s while deferred users pending" % b.name
        self.i += 1
        return b


def defer(pend, fn, bufs):
    for b in bufs:
        b.pending += 1

    def w():
        fn()
        for b in bufs:
            b.pending -= 1
    pend.append(w)


def build(NPB, NOB, dbg=()):
    NK = (NPB + NOB) * TB
    nc = bass.Bass("TRN2", target_bir_lowering=False)
    dt_in = lambda n, s, d=F32: nc.dram_tensor(n, s, d, kind="ExternalInput").ap()
    xp = dt_in("xp", [max(NPB, 1) * TB, D])
    xo = dt_in("xo", [NOB * TB, D])
    w_in = dt_in("w_in", [D, D_IN])
    w_br = dt_in("w_br", [2 * D, D])
    w_out = dt_in("w_out", [D, D])
    cst = dt_in("cst", [P, NCST])
    normf = dt_in("normf", [1, D])
    out = nc.dram_tensor("out", [NOB * TB, D], F32, kind="ExternalOutput").ap()
    dti = lambda n, s, d=BF16: nc.dram_tensor(n, s, d, kind="Internal").ap()
    Wb = dti("Wb", [D, D_IN])
    Wbr = dti("Wbr", [2 * D, D])
    Wo = dti("Wo", [D, D])
    KT = dti("KT", [16, P, NK])
    NCH = (NK // P + 15) // 16
    VS = dti("VS", [16, NCH, P, 16, 129])
    KI = dti("KI", [2, P, NK])
    YM = dti("YM", [P, 16 * TB])
    dbg_out = {}

    with ExitStack() as es:
        T = Trk(nc, es)

        uniq = [0]

        def sbg(name, shape, dt, stack=es):
            uniq[0] += 1
            return Buf(stack.enter_context(nc.sbuf_tensor("%s_u%d" % (name, uniq[0]), shape, dt)), name)

        CST = sbg("CST", [P, NCST], F32)
        ident_b = sbg("ident_b", [P, P], BF16)
        bpat_b = sbg("bpat_b", [P, 2, 256], BF16)
        pvm = sbg("pvm", [P, P], BF16)
        ones_b = sbg("ones_b", [P, 512], BF16)
        ones_f = sbg("ones_f", [P, P], F32)
        aneg = sbg("aneg", [P, 32], F32)
        eps_t = sbg("eps_t", [P, 2], F32)
        hT = sbg("hT", [P, 16, TB], BF16)
        WT = sbg("WT", [P, 2 * 16 * 512], BF16)
        XM = sbg("XM", [P, 8192], BF16)
        xs = sbg("xs", [P, D], BF16)
        hs = sbg("hs", [P, 4, 512], F32)
        halo = sbg("halo", [P, 24, 3], F32)
        st = sbg("st", [P, 16], F32)
        PS = [Buf(es.enter_context(nc.psum_tensor("ps%d" % i, [P, 512], F32)), "ps%d" % i) for i in range(8)]
        psA = Rot(PS)
        psG = Rot(PS[0:4])
        psO = Rot(PS[4:8])

        Wt = [Buf(WT.t[:, i * 8192:(i + 1) * 8192].rearrange("p (k c) -> p k c", c=512), "Wt%d" % i) for i in range(2)]
        score = Buf(WT.t[:].bitcast(F32), "score")
        score.alias = [Wt[0], Wt[1]]
        for w in Wt:
            w.alias = [score]
        xtb = [Buf(XM.t[:, i * 4096:(i + 1) * 4096].bitcast(F32), "xt%d" % i) for i in range(2)]
        mjD = Buf(XM.t[:, 0:4096], "mjD")
        mjA = Buf(XM.t[:, 4096:8192], "mjA")
        mjD.alias = [xtb[0]]
        mjA.alias = [xtb[1]]
        xtb[0].alias = [mjD]
        xtb[1].alias = [mjA]
        XM8 = XM.t[:, 4096:8192].bitcast(mybir.dt.int8)
        wt_rot = Rot(Wt)
        xt_rot = Rot(xtb)

        ident_f = CST.t[:, C_ID:C_ID + 128]
        Lmask = CST.t[:, C_L:C_L + 128]
        Ustr = CST.t[:, C_U:C_U + 128]

        def cs(c0, n):
            return CST.t[:, c0:c0 + n]

        def psb(b):
            return b.t[:].bitcast(BF16).rearrange("p (a b) -> p a b", b=128)

        def dump(name, ap, shape, dtype, rd):
            if name not in dbg:
                return
            o = nc.dram_tensor("dbg_" + name, list(shape), dtype, kind="ExternalOutput").ap()
            dbg_out[name] = o
            T.dma("sp", o, ap, reads=rd, slot="dbg")

        T.dma("sp", CST[:], cst, writes=[CST], slot="cst")
        T.op("dve", lambda e: e.tensor_copy(out=ident_b[:], in_=ident_f), reads=[CST], writes=[ident_b])
        T.op("dve", lambda e: e.tensor_copy(out=bpat_b[:], in_=cs(C_BP, 512).rearrange("p (a b) -> p a b", b=256)), reads=[CST], writes=[bpat_b])
        T.op("dve", lambda e: e.tensor_copy(out=pvm[:], in_=cs(C_PVROW, 128)), reads=[CST], writes=[pvm])
        T.op("dve", lambda e: e.memset(ones_b[:], 1.0), writes=[ones_b])
        T.op("pool", lambda e: e.memset(ones_f[:], 1.0), writes=[ones_f])
        T.op("pool", lambda e: e.memset(eps_t[:], EPS), writes=[eps_t])
        T.op("pool", lambda e: e.memset(hs[:], 0.0), writes=[hs])
        T.op("pool", lambda e: e.memset(halo[:], 0.0), writes=[halo])
        T.op("act", lambda e: e.activation(out=aneg[:], in_=cs(C_ALOG, 32), func=AF.Exp), reads=[CST], writes=[aneg])
        T.op("dve", lambda e: e.tensor_scalar(out=aneg[:], in0=aneg[:], scalar1=-1.0, scalar2=None, op0=ALU.mult), reads=[aneg], writes=[aneg])

        with ExitStack() as s0:
            fb = Rot([sbg("pcf%d" % i, [P, 2048], F32, s0) for i in range(3)])
            bb = Rot([sbg("pcb%d" % i, [P, 2048], BF16, s0) for i in range(3)])
            cnt = 0

            def cast_piece(src_ap, dst_ap, n, scal):
                nonlocal cnt
                f = fb.next()
                b = bb.next()
                T.dma("sp", f[:, 0:n], src_ap, writes=[f], slot=f.name)
                eng = "dve" if cnt % 2 == 0 else "pool"
                if scal is None:
                    T.op(eng, lambda e: e.tensor_copy(out=b[:, 0:n], in_=f[:, 0:n]), reads=[f], writes=[b])
                else:
                    T.op(eng, lambda e: e.tensor_scalar(out=b[:, 0:n], in0=f[:, 0:n], scalar1=scal, scalar2=1.0, op0=ALU.mult, op1=ALU.mult), reads=[f, CST], writes=[b])
                T.dma("act", dst_ap, b[:, 0:n], reads=[b], slot=b.name)
                cnt += 1

            for kc in range(16):
                for c0 in range(0, D_IN, 2048):
                    n = min(2048, D_IN - c0)
                    cast_piece(w_in[kc * P:(kc + 1) * P, c0:c0 + n], Wb[kc * P:(kc + 1) * P, c0:c0 + n], n, CST.t[:, C_G + kc:C_G + kc + 1])
            for kc in range(32):
                cast_piece(w_br[kc * P:(kc + 1) * P, :], Wbr[kc * P:(kc + 1) * P, :], 2048, None)
            for kc in range(16):
                cast_piece(w_out[kc * P:(kc + 1) * P, :], Wo[kc * P:(kc + 1) * P, :], 2048, None)
            zt = sbg("zt", [P, 2048], BF16, s0)
            T.op("pool", lambda e: e.memset(zt[:], 0.0), writes=[zt])
            for c0 in range(0, NK, 2048):
                n = min(2048, NK - c0)
                T.dma("act", KI[0, 64:128, c0:c0 + n], zt[64:128, 0:n], reads=[zt], slot="zfill")
                T.dma("act", KI[1, 0:64, c0:c0 + n], zt[0:64, 0:n], reads=[zt], slot="zfill")
            T.barrier()

        def load_w(src, c0, n, wt, dcol=0, rows0=0):
            T.dma("sp", wt.t[:, :, dcol:dcol + n], src[rows0:rows0 + D, c0:c0 + n].rearrange("(kc p) c -> p kc c", p=P), writes=[wt], slot=wt.name)

        def mm_group(ps_ap, pairs, reads, ps_buf, extra_first_reads=()):
            n = len(pairs)
            for k, (l, r) in enumerate(pairs):
                T.op("pe", lambda e, l=l, r=r, k=k: e.matmul(ps_ap, lhsT=l, rhs=r, start=(k == 0), stop=(k == n - 1)),
                     reads=reads, writes=[ps_buf], inc=(k == n - 1))

        def proj_F(wt, ncols, consumer, rhs_cols=slice(0, TB), nfree=TB):
            for ci in range(ncols // P):
                ps = psA.next()
                mm_group(ps.t[:, 0:nfree], [(wt.t[:, kc, ci * P:(ci + 1) * P], hT.t[:, kc, rhs_cols]) for kc in range(16)], [wt, hT], ps)
                consumer(ci, ps)

        def proj_T(wt, ncols, consumer, tiles=range(NT)):
            for i in tiles:
                ps = psA.next()
                mm_group(ps.t[:, 0:ncols], [(hT.t[:, kc, i * P:(i + 1) * P], wt.t[:, kc, 0:ncols]) for kc in range(16)], [wt, hT], ps)
                consumer(i, ps)

        def run_pipe(pend, nkeep):
            while len(pend) > nkeep:
                pend.pop(0)()

        evac_flip = [0]

        def evac(out_ap, in_ap, reads, writes, scale=None):
            evac_flip[0] ^= 1
            if evac_flip[0]:
                if scale is None:
                    T.op("act", lambda e: e.copy(out=out_ap, in_=in_ap), reads=reads, writes=writes)
                else:
                    T.op("act", lambda e: e.activation(out=out_ap, in_=in_ap, func=AF.Copy, scale=scale), reads=reads, writes=writes)
            else:
                if scale is None:
                    T.op("dve", lambda e: e.tensor_copy(out=out_ap, in_=in_ap), reads=reads, writes=writes)
                else:
                    T.op("dve", lambda e: e.tensor_scalar(out=out_ap, in0=in_ap, scalar1=scale, scalar2=None, op0=ALU.mult), reads=reads, writes=writes)

        def rstd_from_ssq(ssq_ap, out_ap, n, buf):
            T.op("act", lambda e: e.activation(out=out_ap, in_=ssq_ap, func=AF.Ln, scale=1.0 / n, bias=eps_t[:, 0:1]), reads=[buf, eps_t], writes=[buf])
            T.op("act", lambda e: e.activation(out=out_ap, in_=out_ap, func=AF.Exp, scale=-0.5), reads=[buf], writes=[buf])

        def build_hT(xsrc, row0):
            for i in range(NT):
                xt = xt_rot.next()
                T.dma("sp", xt[:, :], xsrc[row0 + i * P:row0 + (i + 1) * P, :], writes=[xt], slot=xt.name)
                T.op("act", lambda e, xt=xt: e.activation(out=xs[:], in_=xt[:, :], func=AF.Square, accum_out=st[:, 0:1]), reads=[xt], writes=[xs, st])
                rstd_from_ssq(st[:, 0:1], st[:, 1:2], D, st)
                T.op("dve", lambda e, xt=xt: e.tensor_scalar(out=xs[:], in0=xt[:, :], scalar1=st[:, 1:2], scalar2=None, op0=ALU.mult), reads=[xt, st], writes=[xs])
                for hb in range(2):
                    ps = psA.next()
                    for k in range(8):
                        kc = hb * 8 + k
                        T.op("pe", lambda e, ps=ps, k=k, kc=kc: e.transpose(out=psb(ps)[:, k, :], in_=xs[:, kc * P:(kc + 1) * P], identity=ident_b[:]),
                             reads=[xs, ident_b], writes=[ps], inc=(k == 7))
                    evac(hT.t[:, hb * 8:(hb + 1) * 8, i * P:(i + 1) * P], psb(ps), [ps], [hT])

        def ssd_block(S, is_own, bi):
            x_tok = sbg("x_tok", [P, NT, D], F32, S)
            BT = sbg("BT", [P, 4, TB], BF16, S)
            CTt = sbg("CTt", [P, 4, TB], BF16, S)
            Btok = sbg("Btok", [P, NT, 4, P], BF16, S)
            dt_all = sbg("dt_all", [P, NT, 32], F32, S)
            sm = sbg("sm", [P, NT, 160], F32, S)
            y_mT = sbg("y_mT", [P, 16, TB], BF16, S) if is_own else None
            SX = ExitStack()
            ut = Rot([sbg("ut%d" % i, [P, 515], F32, SX) for i in range(3)])
            acc_r = Rot([sbg("cacc%d" % i, [P, TB], F32, SX) for i in range(2)])
            xsT = Rot([sbg("xsT%d" % i, [P, TB], F32, SX) for i in range(4)])

            cpend = []
            for grp in range(6):
                wt = wt_rot.next()
                load_w(Wb, OFF_XBC + grp * 512, 512, wt)

                def cons(ci, ps, grp=grp):
                    cg = grp * 4 + ci
                    u = ut.next()
                    T.op("act", lambda e: e.copy(out=u[:, 3:515], in_=ps.t[:, 0:512]), reads=[ps], writes=[u])
                    T.op("pool", lambda e: e.tensor_copy(out=u[:, 0:3], in_=halo[:, cg, :]), reads=[halo], writes=[u])
                    T.op("pool", lambda e: e.tensor_copy(out=halo[:, cg, :], in_=u[:, 512:515]), reads=[u], writes=[halo])
                    acc = acc_r.next()
                    cw = lambda k: CST.t[:, C_CW + cg * 4 + k:C_CW + cg * 4 + k + 1]
                    T.op("dve", lambda e: e.tensor_scalar(out=acc[:, :], in0=u[:, 3:515], scalar1=cw(3), scalar2=None, op0=ALU.mult), reads=[u, CST], writes=[acc])
                    for k in (2, 1, 0):
                        T.op("dve", lambda e, k=k: e.scalar_tensor_tensor(out=acc[:, :], in0=u[:, k:k + 512], scalar=cw(k), in1=acc[:, :], op0=ALU.mult, op1=ALU.add), reads=[u, CST, acc], writes=[acc])
                    cb = CST.t[:, C_CB + cg:C_CB + cg + 1]
                    if cg < 16:
                        xT = xsT.next(cpend)
                        T.op("act", lambda e: e.activation(out=xT[:, :], in_=acc[:, :], func=AF.Silu, bias=cb), reads=[acc, CST], writes=[xT])

                        def later(xT=xT, cg=cg):
                            ps2 = psA.next()
                            for i in range(NT):
                                T.op("pe", lambda e, i=i: e.transpose(out=ps2.t[:, i * P:(i + 1) * P], in_=xT[:, i * P:(i + 1) * P], identity=ident_f),
                                     reads=[xT, CST], writes=[ps2], inc=(i == NT - 1))
                            evac(x_tok.t[:, :, cg * P:(cg + 1) * P], ps2.t[:, :].rearrange("p (a b) -> p a b", b=P), [ps2], [x_tok])
                        defer(cpend, later, [xT])
                    elif cg < 20:
                        g = cg - 16
                        T.op("act", lambda e: e.activation(out=BT.t[:, g, :], in_=acc[:, :], func=AF.Silu, bias=cb), reads=[acc, CST], writes=[BT])

                        def later(g=g):
                            ps2 = psA.next()
                            for i in range(NT):
                                T.op("pe", lambda e, i=i: e.transpose(out=psb(ps2)[:, i, :], in_=BT.t[:, g, i * P:(i + 1) * P], identity=ident_b[:]),
                                     reads=[BT, ident_b], writes=[ps2], inc=(i == NT - 1))
                            evac(Btok.t[:, :, g, :], psb(ps2)[:, 0:NT, :], [ps2], [Btok])
                        defer(cpend, later, [])
                    else:
                        g = cg - 20
                        T.op("act", lambda e: e.activation(out=CTt.t[:, g, :], in_=acc[:, :], func=AF.Silu, bias=cb), reads=[acc, CST], writes=[CTt])
                    run_pipe(cpend, 2)

                proj_F(wt, 512, cons)
            run_pipe(cpend, 0)
            T.barrier()
            SX.close()

            wt = wt_rot.next()
            load_w(Wb, OFF_DT, 32, wt)

            def cons_dt(i, ps):
                T.op("dve", lambda e: e.tensor_tensor(out=dt_all.t[:, i, :], in0=ps.t[:, 0:32], in1=cs(C_DTB, 32), op=ALU.add), reads=[ps, CST], writes=[dt_all])
                T.op("act", lambda e: e.activation(out=dt_all.t[:, i, :], in_=dt_all.t[:, i, :], func=AF.Exp), reads=[dt_all], writes=[dt_all])
                T.op("act", lambda e: e.activation(out=dt_all.t[:, i, :], in_=dt_all.t[:, i, :], func=AF.Ln, bias=1.0), reads=[dt_all], writes=[dt_all])

            proj_T(wt, 32, cons_dt)

            for i in range(NT):
                a_i = sm.t[:, i, 0:32]
                T.op("dve", lambda e, i=i, a_i=a_i: e.tensor_tensor(out=a_i, in0=dt_all.t[:, i, :], in1=aneg[:], op=ALU.mult), reads=[dt_all, aneg], writes=[sm])
                ps = psA.next()
                T.op("pe", lambda e, a_i=a_i, ps=ps: e.matmul(ps.t[:, 0:32], lhsT=Lmask, rhs=a_i, start=True, stop=True), reads=[CST, sm], writes=[ps])
                T.op("pe", lambda e, a_i=a_i, ps=ps: e.matmul(ps.t[:, 32:64], lhsT=ones_f[:], rhs=a_i, start=True, stop=True), reads=[ones_f, sm], writes=[ps])
                T.op("act", lambda e, i=i, ps=ps: e.activation(out=sm.t[:, i, 32:96], in_=ps.t[:, 0:64], func=AF.Exp), reads=[ps], writes=[sm])
                T.op("dve", lambda e, i=i, ps=ps: e.tensor_copy(out=sm.t[:, i, 128:160], in_=ps.t[:, 0:32]), reads=[ps], writes=[sm])
                T.op("dve", lambda e, i=i, ps=ps: e.tensor_tensor(out=sm.t[:, i, 96:128], in0=ps.t[:, 32:64], in1=sm.t[:, i, 128:160], op=ALU.subtract), reads=[ps, sm], writes=[sm])
                T.op("act", lambda e, i=i: e.activation(out=sm.t[:, i, 96:128], in_=sm.t[:, i, 96:128], func=AF.Exp), reads=[sm], writes=[sm])
                T.op("dve", lambda e, i=i: e.tensor_tensor(out=sm.t[:, i, 96:128], in0=sm.t[:, i, 96:128], in1=dt_all.t[:, i, :], op=ALU.mult), reads=[sm, dt_all], writes=[sm])

            xw = Rot([sbg("xw%d" % i, [P, TB], BF16, S) for i in range(2)])
            tmpS = Rot([sbg("tmpS%d" % i, [P, TB], F32, S) for i in range(2)])
            if is_own:
                xdt = Rot([sbg("xdt%d" % i, [P, TB], BF16, S) for i in range(3)])
                hsb = Rot([sbg("hsb%d" % i, [P, TB], BF16, S) for i in range(3)])
                cbm = Rot([sbg("cbm%d" % i, [P, P], F32, S) for i in range(3)])
                Rr = Rot([sbg("Rr%d" % i, [P, 8, P], F32, S) for i in range(2)])
                dec = Rot([sbg("dec%d" % i, [P, 8, P], F32, S) for i in range(2)])
                Mm = Rot([sbg("Mm%d" % i, [P, 8, P], BF16, S) for i in range(3)])
                t1 = Rot([sbg("t1_%d" % i, [P, TB], F32, S) for i in range(3)])
                t2 = Rot([sbg("t2_%d" % i, [P, TB], F32, S) for i in range(2)])
                zs = Rot([sbg("zs%d" % i, [P, TB], F32, S) for i in range(3)])
                yn = Rot([sbg("yn%d" % i, [P, TB], BF16, S) for i in range(3)])
                st2r = Rot([sbg("st2_%d" % i, [P, 8], F32, S) for i in range(2)])
            v3 = lambda b: b.t[:, :].rearrange("p (h q) -> p h q", q=64)
            stageC = []
            stageD = []

            def make_R(g, i):
                Rb = Rr.next()
                a3 = sm.t[:, i, g * 8:(g + 1) * 8].unsqueeze(2).to_broadcast([P, 8, P])
                L3 = Lmask.unsqueeze(1).to_broadcast([P, 8, P])
                T.op("dve", lambda e: e.tensor_tensor(out=Rb[:, :, :], in0=a3, in1=L3, op=ALU.mult), reads=[sm, CST], writes=[Rb])
                return Rb
            Rnext = make_R(0, 0) if is_own else None
            for g in range(4):
                if is_own:
                    wz = wt_rot.next()
                    load_w(Wb, OFF_Z + g * 512, 512, wz)
                for i in range(NT):
                    tsl = slice(i * P, (i + 1) * P)
                    xg3 = x_tok.t[:, i, g * 512:(g + 1) * 512].rearrange("p (h q) -> p h q", q=64)
                    bc = lambda c0, i=i, g=g: sm.t[:, i, c0 + g * 8:c0 + g * 8 + 8].unsqueeze(2).to_broadcast([P, 8, 64])
                    xwb = xw.next()
                    T.op("dve" if not is_own else "pool", lambda e, xwb=xwb, xg3=xg3, bc=bc: e.tensor_tensor(out=v3(xwb), in0=xg3, in1=bc(96), op=ALU.mult), reads=[x_tok, sm], writes=[xwb])
                    if is_own:
                        xdb = xdt.next(stageC)
                        dtb3 = dt_all.t[:, i, g * 8:(g + 1) * 8].unsqueeze(2).to_broadcast([P, 8, 64])
                        T.op("pool", lambda e, xdb=xdb, xg3=xg3, dtb3=dtb3: e.tensor_tensor(out=v3(xdb), in0=xg3, in1=dtb3, op=ALU.mult), reads=[x_tok, dt_all], writes=[xdb])
                        hb_ = hsb.next(stageC)
                        T.op("act", lambda e, hb_=hb_, g=g: e.copy(out=hb_[:, :], in_=hs.t[:, g, :]), reads=[hs], writes=[hb_])
                        Rb = Rnext
                        psc = psA.next()
                        T.op("pe", lambda e, psc=psc, g=g, tsl=tsl: e.matmul(psc.t[:, 0:P], lhsT=BT.t[:, g, tsl], rhs=CTt.t[:, g, tsl], start=True, stop=True), reads=[BT, CTt], writes=[psc])
                        psd = [psA.next(), psA.next()]
                        for hh in range(2):
                            T.op("pe", lambda e, pd=psd[hh], Rb=Rb, hh=hh: e.matmul(pd.t[:, :], lhsT=Ustr, rhs=Rb.t[:, hh * 4:(hh + 1) * 4, :].rearrange("p a b -> p (a b)"), start=True, stop=True), reads=[CST, Rb], writes=[psd[hh]])
                    pss = psA.next()
                    T.op("pe", lambda e, pss=pss, xwb=xwb, i=i, g=g: e.matmul(pss.t[:, :], lhsT=Btok.t[:, i, g, :], rhs=xwb[:, :], start=True, stop=True), reads=[Btok, xwb], writes=[pss])
                    if is_own:
                        psz = psA.next()
                        mm_group(psz.t[:, :], [(hT.t[:, kc, tsl], wz.t[:, kc, :]) for kc in range(16)], [hT, wz], psz)
                        cbb = cbm.next()
                        T.op("dve", lambda e, psc=psc, cbb=cbb: e.tensor_tensor(out=cbb[:, :], in0=psc.t[:, 0:P], in1=Lmask, op=ALU.mult), reads=[psc, CST], writes=[cbb])
                        db = dec.next()
                        for hh in range(2):
                            T.op("act", lambda e, pd=psd[hh], db=db, hh=hh: e.activation(out=db.t[:, hh * 4:(hh + 1) * 4, :].rearrange("p a b -> p (a b)"), in_=pd.t[:, :], func=AF.Exp), reads=[psd[hh]], writes=[db])
                    tS = tmpS.next()
                    hs3 = hs.t[:, g, :].rearrange("p (h q) -> p h q", q=64)
                    T.op("pool" if is_own else "dve", lambda e, tS=tS, hs3=hs3, bc=bc: e.tensor_tensor(out=v3(tS), in0=hs3, in1=bc(64), op=ALU.mult), reads=[hs, sm], writes=[tS])
                    T.op("dve", lambda e, tS=tS, pss=pss, g=g: e.tensor_tensor(out=hs.t[:, g, :], in0=pss.t[:, :], in1=tS[:, :], op=ALU.add), reads=[pss, tS], writes=[hs])
                    if not is_own:
                        continue
                    zb = zs.next(stageC)
                    T.op("act", lambda e, zb=zb, psz=psz: e.activation(out=zb[:, :], in_=psz.t[:, :], func=AF.Silu), reads=[psz], writes=[zb])
                    Mb = Mm.next(stageC)
                    c3 = cbb.t[:, :].unsqueeze(1).to_broadcast([P, 8, P])
                    T.op("dve", lambda e, Mb=Mb, db=db, c3=c3: e.tensor_tensor(out=Mb[:, :, :], in0=db[:, :, :], in1=c3, op=ALU.mult), reads=[db, cbb], writes=[Mb])
                    nxt = (g, i + 1) if i + 1 < NT else (g + 1, 0)
                    if nxt[0] < 4:
                        Rnext = make_R(*nxt)

                    def stC(Mb=Mb, xdb=xdb, hb_=hb_, g=g, i=i, tsl=tsl, xg3=xg3, bc=bc, zb=zb):
                        psy = psA.next()
                        for h in range(8):
                            T.op("pe", lambda e, h=h: e.matmul(psy.t[:, h * 64:(h + 1) * 64], lhsT=Mb.t[:, h, :], rhs=xdb.t[:, h * 64:(h + 1) * 64], start=True, stop=True),
                                 reads=[Mb, xdb], writes=[psy], inc=(h == 7))
                        pso = psA.next()
                        T.op("pe", lambda e: e.matmul(pso.t[:, :], lhsT=CTt.t[:, g, tsl], rhs=hb_[:, :], start=True, stop=True), reads=[CTt, hb_], writes=[pso])
                        t1b = t1.next()
                        t2b = t2.next()
                        st2 = st2r.next()
                        dsk3 = CST.t[:, C_DSK + g * 8:C_DSK + g * 8 + 8].unsqueeze(2).to_broadcast([P, 8, 64])
                        T.op("dve", lambda e: e.tensor_tensor(out=v3(t1b), in0=pso.t[:, :].rearrange("p (h q) -> p h q", q=64), in1=bc(32), op=ALU.mult), reads=[pso, sm], writes=[t1b])
                        T.op("pool", lambda e: e.tensor_tensor(out=v3(t2b), in0=xg3, in1=dsk3, op=ALU.mult), reads=[x_tok, CST], writes=[t2b])
                        T.op("dve", lambda e: e.tensor_tensor(out=t1b[:, :], in0=psy.t[:, :], in1=t1b[:, :], op=ALU.add), reads=[psy, t1b], writes=[t1b])
                        T.op("dve", lambda e: e.tensor_tensor(out=t1b[:, :], in0=t1b[:, :], in1=t2b[:, :], op=ALU.add), reads=[t1b, t2b], writes=[t1b])
                        T.op("dve", lambda e: e.tensor_tensor(out=t1b[:, :], in0=t1b[:, :], in1=zb[:, :], op=ALU.mult), reads=[t1b, zb], writes=[t1b])
                        T.op("act", lambda e: e.activation(out=t2b[:, :], in_=t1b[:, :], func=AF.Square, accum_out=st2[:, 0:1]), reads=[t1b], writes=[t2b, st2])
                        rstd_from_ssq(st2[:, 0:1], st2[:, 1:2], 512, st2)
                        ynb = yn.next(stageD)
                        T.op("dve", lambda e: e.tensor_scalar(out=ynb[:, :], in0=t1b[:, :], scalar1=st2[:, 1:2], scalar2=None, op0=ALU.mult), reads=[t1b, st2], writes=[ynb])

                        def stD(ynb=ynb):
                            pst = psA.next()
                            for j in range(4):
                                T.op("pe", lambda e, j=j: e.transpose(out=psb(pst)[:, j, :], in_=ynb.t[:, j * P:(j + 1) * P], identity=ident_b[:]), reads=[ynb, ident_b], writes=[pst], inc=(j == 3))
                            for j in range(4):
                                c = g * 4 + j
                                T.op("act", lambda e, j=j, c=c: e.activation(out=y_mT.t[:, c, tsl], in_=psb(pst)[:, j, :], func=AF.Copy, scale=CST.t[:, C_GN + c:C_GN + c + 1]), reads=[pst, CST], writes=[y_mT])
                        defer(stageD, stD, [ynb])
                    defer(stageC, stC, [Mb, xdb, hb_, zb])
                    run_pipe(stageD, 1)
                    run_pipe(stageC, 1)
            run_pipe(stageC, 0)
            run_pipe(stageD, 0)
            if is_own:
                dump("x_tok%d" % bi, x_tok.t[:, :, :].rearrange("p a b -> p (a b)"), [P, NT * D], F32, [x_tok])
                dump("dt%d" % bi, dt_all.t[:, :, :].rearrange("p a b -> p (a b)"), [P, NT * 32], F32, [dt_all])
                dump("ymT%d" % bi, y_mT.t[:, :, :].rearrange("p a b -> p (a b)"), [P, 16 * TB], BF16, [y_mT])
            return y_mT

        def kv_proj(S, key0, is_own, qT, qiT, ws):
            ktmp = Rot([sbg("ktmp%d" % i, [P, TB], BF16, S) for i in range(2)])
            vtmp = Rot([sbg("vtmp%d" % i, [P, 4, 129], BF16, S) for i in range(2)])
            for v_ in vtmp.items:
                T.op("pool", lambda e, v_=v_: e.memset(v_[:, :, :], 1.0), writes=[v_])
            for grp in range(4):
                wt = wt_rot.next()
                load_w(Wb, OFF_K + grp * 512, 512, wt)

                def cons(ci, ps, grp=grp):
                    hh = grp * 4 + ci
                    kb_ = ktmp.next()
                    evac(kb_[:, :], ps.t[:, :], [ps], [kb_])
                    T.dma("pool", KT[hh, :, key0:key0 + TB], kb_[:, :], reads=[kb_], slot=kb_.name)
                proj_F(wt, 512, cons)
            for grp in range(4):
                wt = wt_rot.next()
                load_w(Wb, OFF_V + grp * 512, 512, wt)

                def cons(i, ps, grp=grp):
                    vb_ = vtmp.next()
                    evac(vb_.t[:, :, 0:P], ps.t[:, :].rearrange("p (h d) -> p h d", d=P), [ps], [vb_])
                    kbg = key0 // P + i
                    T.dma("pool", VS[grp * 4:(grp + 1) * 4, kbg // 16, :, kbg % 16, :].rearrange("h p d -> p h d"), vb_[:, :, :], reads=[vb_], slot=vb_.name)
                proj_T(wt, 512, cons)
            wt = wt_rot.next()
            load_w(Wb, OFF_KI, 64, wt, dcol=0)
            load_w(Wb, OFF_KI, 64, wt, dcol=64)

            def cons(ci, ps):
                kb_ = ktmp.next()
                evac(kb_[:, :], ps.t[:, :], [ps], [kb_])
                T.dma("pool", KI[0, 0:64, key0:key0 + TB], kb_[0:64, :], reads=[kb_], slot=kb_.name)
                T.dma("pool", KI[1, 64:128, key0:key0 + TB], kb_[64:128, :], reads=[kb_], slot=kb_.name)
            proj_F(wt, 128, cons)
            if not is_own:
                return
            for grp in range(4):
                wt = wt_rot.next()
                load_w(Wb, OFF_Q + grp * 512, 512, wt)

                def cons(ci, ps, grp=grp):
                    evac(qT.t[:, grp * 4 + ci, :], ps.t[:, :], [ps], [qT], scale=ATTN_SCALE)
                proj_F(wt, 512, cons)
            for grp in range(2):
                wt = wt_rot.next()
                load_w(Wb, OFF_QI + grp * 512, 512, wt)

                def cons(ci, ps, grp=grp):
                    evac(qiT.t[:, grp * 4 + ci, :], ps.t[:, :], [ps], [qiT])
                proj_F(wt, 512, cons)
            wt = wt_rot.next()
            load_w(Wb, OFF_WI, 16, wt)

            def cons(i, ps):
                T.op("dve", lambda e: e.tensor_scalar(out=ws.t[:, i, :], in0=ps.t[:, 0:16], scalar1=IDX_SCALE, scalar2=None, op0=ALU.mult), reads=[ps], writes=[ws])
            proj_T(wt, 16, cons)

        def dsa_idx(S, bi, key0, hb, qiT, ws, maskT):
            nkb = (key0 + 256 * (hb + 1)) // P
            W = nkb * P
            nb5 = (W + 511) // 512
            LA = 4
            LA2 = 6
            SI = ExitStack()
            Dgr = Rot([sbg("Dg%d" % i, [P, 16, P], BF16, SI) for i in range(2)])
            Rl = Rot([sbg("Rl%d" % i, [P, 512], BF16, SI) for i in range(6)])
            kib = Rot([sbg("kib%d" % i, [P, 2, 512], BF16, SI) for i in range(3)])

            def indexer(j2):
                i = 2 * hb + j2
                tsl = slice(i * P, (i + 1) * P)
                Dg = Dgr.next()
                pend = []
                for h in range(16):
                    T.op("pool", lambda e, h=h, i=i, Dg=Dg: e.tensor_scalar(out=Dg.t[:, h, :], in0=ident_f, scalar1=ws.t[:, i, h:h + 1], scalar2=1.0, op0=ALU.mult, op1=ALU.mult), reads=[CST, ws], writes=[Dg])
                for b5 in range(nb5):
                    c0 = b5 * 512
                    wd = min(512, W - c0)
                    kb_ = kib.next()
                    T.dma("sp", kb_[:, 0, 0:wd], KI[0, :, c0:c0 + wd], writes=[kb_], slot=kb_.name)
                    T.dma("sp", kb_[:, 1, 0:wd], KI[1, :, c0:c0 + wd], writes=[kb_], slot=kb_.name)
                    acc = psO.next(pend)
                    extra = []
                    if c0 < NPB * TB:
                        extra.append("pv")
                    if b5 == nb5 - 1:
                        extra.append("bp")
                    nacc = 16 + len(extra)
                    for h in range(16):
                        j, sub = h // 2, h % 2
                        ps = psG.next()
                        T.op("pe", lambda e, ps=ps, j=j, sub=sub, kb_=kb_, wd=wd, tsl=tsl: e.matmul(ps.t[:, 0:wd], lhsT=qiT.t[:, j, tsl], rhs=kb_.t[:, sub, 0:wd], start=True, stop=True), reads=[qiT, kb_], writes=[ps])
                        rb = Rl.next(pend)
                        if h % 2 == 0:
                            T.op("act", lambda e, rb=rb, ps=ps, wd=wd: e.activation(out=rb[:, 0:wd], in_=ps.t[:, 0:wd], func=AF.Relu), reads=[ps], writes=[rb])
                        else:
                            T.op("dve", lambda e, rb=rb, ps=ps, wd=wd: e.tensor_scalar(out=rb[:, 0:wd], in0=ps.t[:, 0:wd], scalar1=0.0, scalar2=None, op0=ALU.max), reads=[ps], writes=[rb])

                        def accmm(acc=acc, h=h, rb=rb, wd=wd, nacc=nacc, Dg=Dg):
                            T.op("pe", lambda e: e.matmul(acc.t[:, 0:wd], lhsT=Dg.t[:, h, :], rhs=rb.t[:, 0:wd], start=(h == 0), stop=(h == nacc - 1)), reads=[Dg, rb], writes=[acc])
                        defer(pend, accmm, [rb, acc])
                        run_pipe(pend, LA)

                    def fin(acc=acc, wd=wd, nacc=nacc, extra=tuple(extra), c0=c0):
                        k = 16
                        for ex in extra:
                            if ex == "pv":
                                T.op("pe", lambda e, k=k: e.matmul(acc.t[:, 0:wd], lhsT=pvm[:], rhs=ones_b.t[:, 0:wd], start=False, stop=(k == nacc - 1)), reads=[pvm, ones_b], writes=[acc])
                            else:
                                T.op("pe", lambda e, k=k: e.matmul(acc.t[:, wd - 256:wd], lhsT=ident_b[:], rhs=bpat_b.t[:, j2, :], start=False, stop=(k == nacc - 1)), reads=[ident_b, bpat_b], writes=[acc])
                            k += 1
                        T.op("act", lambda e: e.copy(out=score.t[:, c0:c0 + wd], in_=acc.t[:, 0:wd]), reads=[acc], writes=[score])
                    defer(pend, fin, [acc])
                run_pipe(pend, 0)

            def bisect(j2):
                i = 2 * hb + j2
                hi_t, mid_t, cnt_t, t_t, g_t, lo_t, sa_t = [sbg("bq%d_%d" % (j2, c), [P, 2], F32, SI) for c in range(7)]
                WD = min(4096, max(P, ((W * 45 // 100) // P) * P))
                nA = W - WD
                assert nA >= P and nA <= 8192
                hi, mid, cntc, tt, gg, lo, sa = [b_.t[:, 0:1] for b_ in (hi_t, mid_t, cnt_t, t_t, g_t, lo_t, sa_t)]
                T.op("dve", lambda e: e.reduce_max(out=hi, in_=score.t[:, 0:W], axis=AX.X), reads=[score], writes=[hi_t])
                T.op("dve", lambda e: e.tensor_scalar(out=mid, in0=hi, scalar1=-CLAMP * 0.5, scalar2=None, op0=ALU.add), reads=[hi_t], writes=[mid_t])
                for it in range(NITER):
                    hk = CLAMP * (0.5 ** (it + 1))
                    T.op("dve", lambda e: e.tensor_scalar(out=XM.t[:, 0:WD], in0=score.t[:, 0:WD], scalar1=mid, scalar2=None, op0=ALU.is_ge, op1=ALU.add, accum_out=cntc), reads=[score, mid_t], writes=[mjD, cnt_t])
                    T.op("act", lambda e: e.activation(out=XM8[:, 0:nA], in_=score.t[:, WD:W], func=AF.Sign, bias=mid, scale=-1.0, accum_out=sa), reads=[score, mid_t], writes=[mjA, sa_t])
                    T.op("dve", lambda e: e.scalar_tensor_tensor(out=tt, in0=cntc, scalar=2.0, in1=sa, op0=ALU.mult, op1=ALU.subtract), reads=[cnt_t, sa_t], writes=[t_t])
                    T.op("dve", lambda e: e.tensor_scalar(out=gg, in0=tt, scalar1=float(2 * TOPK - 1 - nA), scalar2=-0.5, op0=ALU.is_ge, op1=ALU.add), reads=[t_t], writes=[g_t])
                    T.op("dve", lambda e, hk=hk: e.scalar_tensor_tensor(out=mid, in0=gg, scalar=hk, in1=mid, op0=ALU.mult, op1=ALU.add), reads=[g_t, mid_t], writes=[mid_t])
                hN = CLAMP * (0.5 ** (NITER + 1))
                T.op("dve", lambda e: e.tensor_scalar(out=lo, in0=mid, scalar1=-hN, scalar2=None, op0=ALU.add), reads=[mid_t], writes=[lo_t])
                T.op("dve", lambda e: e.tensor_scalar(out=XM.t[:, 0:W], in0=score.t[:, 0:W], scalar1=lo, scalar2=None, op0=ALU.is_ge), reads=[score, lo_t], writes=[mjD, mjA])
                dump("thr%d_%d" % (bi, i), lo, [P, 1], F32, [lo_t])
                dump("score%d_%d" % (bi, i), score.t[:, 0:W], [P, W], F32, [score])

            def mask_transposes(j2):
                for kb0 in range(0, nkb, 8):
                    n = min(8, nkb - kb0)
                    ps = psG.next()
                    for t in range(n):
                        T.op("pe", lambda e, ps=ps, t=t, kb0=kb0: e.transpose(out=psb(ps)[:, t, :], in_=XM.t[:, (kb0 + t) * P:(kb0 + t + 1) * P], identity=ident_b[:]), reads=[mjD, mjA, ident_b], writes=[ps], inc=(t == n - 1))
                    T.op("act", lambda e, ps=ps, n=n, kb0=kb0, j2=j2: e.copy(out=maskT.t[:, kb0:kb0 + n, hb * 256 + j2 * P:hb * 256 + (j2 + 1) * P], in_=psb(ps)[:, 0:n, :]), reads=[ps], writes=[maskT])

            indexer(0)
            bisect(0)
            indexer(1)
            mask_transposes(0)
            bisect(1)
            mask_transposes(1)
            T.barrier()
            SI.close()


        def attention_block(S, bi, key0, qT, y_aT, maskT):
            nkbs = [(key0 + 256 * (hb + 1)) // P for hb in range(2)]
            nkb = nkbs[1]
            LA2 = 8
            wga = Rot([sbg("wga%d" % i, [P, 16, P], BF16, S) for i in range(1)])
            gaT = Rot([sbg("gaT%d" % i, [P, TB], F32, S) for i in range(2)])
            gex = Rot([sbg("gex%d" % i, [P, TB], F32, S) for i in range(1)])
            ktb = Rot([sbg("ktb%d" % i, [P, 2048], BF16, S) for i in range(2)])
            vtb = Rot([sbg("vtb%d" % i, [P, 16, 129], BF16, S) for i in range(2)])
            eb = Rot([sbg("eb%d" % i, [P, 512], BF16, S) for i in range(4)])
            pmb = Rot([sbg("pmb%d" % i, [P, 512], BF16, S) for i in range(10)])
            onb = Rot([sbg("onb%d" % i, [P, P], BF16, S) for i in range(4)])
            rdr = Rot([sbg("rd%d" % i, [P, 4], F32, S) for i in range(2)])
            psQ = Rot(PS[0:3])
            psX = Rot(PS[3:4])
            pend = []
            for hh in range(16):
                ga = gaT.next(pend)
                O = [psO.next(pend) for _ in range(4)]
                wg = wga.next()
                T.dma("sp", wg[:, :, :], Wb[:, OFF_GA + hh * P:OFF_GA + (hh + 1) * P].rearrange("(kc p) c -> p kc c", p=P), writes=[wg], slot=wg.name)
                psg = psX.next()
                mm_group(psg.t[:, :], [(wg.t[:, kc, :], hT.t[:, kc, :]) for kc in range(16)], [wg, hT], psg)
                gx = gex.next()
                T.op("act", lambda e, gx=gx, psg=psg: e.activation(out=gx[:, :], in_=psg.t[:, :], func=AF.Exp, scale=-1.0), reads=[psg], writes=[gx])
                T.op("dve", lambda e, gx=gx: e.tensor_scalar(out=gx[:, :], in0=gx[:, :], scalar1=1.0, scalar2=None, op0=ALU.add), reads=[gx], writes=[gx])
                T.op("dve", lambda e, gx=gx: e.reciprocal(out=gx[:, :], in_=gx[:, :]), reads=[gx], writes=[gx])
                T.op("dve", lambda e, ga=ga, gx=gx, psg=psg: e.tensor_tensor(out=ga[:, :], in0=psg.t[:, :], in1=gx[:, :], op=ALU.mult), reads=[psg, gx], writes=[ga])
                for c2 in range((nkb + 15) // 16):
                    kb_lo = c2 * 16
                    n = min(16, nkb - kb_lo)
                    kt = ktb.next(pend)
                    vt = vtb.next(pend)
                    T.dma("sp", kt[:, 0:n * P], KT[hh, :, kb_lo * P:(kb_lo + n) * P], writes=[kt], slot=kt.name)
                    T.dma("sp", vt[:, 0:n, :], VS[hh, c2, :, 0:n, :], writes=[vt], slot=vt.name)
                    for t in range(0, n, 2):
                        kb = kb_lo + t
                        for hb in range(2):
                            if kb >= nkbs[hb]:
                                continue
                            qsl = slice(hb * 256, (hb + 1) * 256)
                            ps = psQ.next()
                            for u in range(2):
                                T.op("pe", lambda e, ps=ps, kt=kt, t=t, u=u, hh=hh, qsl=qsl: e.matmul(ps.t[:, u * 256:(u + 1) * 256], lhsT=kt.t[:, (t + u) * P:(t + u + 1) * P], rhs=qT.t[:, hh, qsl], start=True, stop=True), reads=[kt, qT], writes=[ps], inc=(u == 1))
                            eb_ = eb.next()
                            T.op("act", lambda e, eb_=eb_, ps=ps: e.activation(out=eb_[:, :], in_=ps.t[:, :], func=AF.Exp), reads=[ps], writes=[eb_])
                            pm = pmb.next(pend)
                            T.op("dve", lambda e, pm=pm, eb_=eb_, kb=kb, qsl=qsl: e.tensor_tensor(out=pm.t[:, :].rearrange("p (a b) -> p a b", b=256), in0=eb_.t[:, :].rearrange("p (a b) -> p a b", b=256), in1=maskT.t[:, kb:kb + 2, qsl], op=ALU.mult), reads=[eb_, maskT], writes=[pm])

                            def pv(pm=pm, vt=vt, t=t, kb=kb, O=O, hb=hb):
                                for u in range(2):
                                    for j2 in range(2):
                                        T.op("pe", lambda e, u=u, j2=j2: e.matmul(O[2 * hb + j2].t[:, 0:129], lhsT=pm.t[:, u * 256 + j2 * P:u * 256 + (j2 + 1) * P], rhs=vt.t[:, t + u, :], start=(kb + u == 0), stop=(kb + u == nkbs[hb] - 1)),
                                             reads=[pm, vt], writes=[O[2 * hb + j2]], inc=(u == 1 and j2 == 1))
                            defer(pend, pv, [pm, vt] + O)
                            run_pipe(pend, LA2)

                def finalize(O=O, ga=ga, hh=hh):
                    pst = psX.next()
                    rd = rdr.next()
                    for jt in range(4):
                        T.op("dve", lambda e, jt=jt: e.reciprocal(out=rd.t[:, jt:jt + 1], in_=O[jt].t[:, 128:129]), reads=[O[jt]], writes=[rd])
                        ob = onb.next()
                        T.op("dve", lambda e, jt=jt, ob=ob: e.tensor_scalar(out=ob[:, :], in0=O[jt].t[:, 0:P], scalar1=rd.t[:, jt:jt + 1], scalar2=None, op0=ALU.mult), reads=[O[jt], rd], writes=[ob])
                        T.op("pe", lambda e, jt=jt, ob=ob: e.transpose(out=psb(pst)[:, jt, :], in_=ob[:, :], identity=ident_b[:]), reads=[ob, ident_b], writes=[pst])
                    T.op("dve", lambda e: e.tensor_tensor(out=y_aT.t[:, hh, :], in0=psb(pst)[:, 0:4, :].rearrange("p a b -> p (a b)"), in1=ga[:, :], op=ALU.mult), reads=[pst, ga], writes=[y_aT])
                defer(pend, finalize, [ga] + O)
            run_pipe(pend, 0)

        def out_block2(S, bi, orow0, y_aT):
            y_mT = sbg("y_mT2", [P, 16, TB], BF16, S)
            T.dma("sp", y_mT.t[:, :, :].rearrange("p a b -> p (a b)"), YM, writes=[y_mT], slot="ymld")
            mT = sbg("mT", [P, 16, TB], BF16, S)
            r = sbg("r", [P, NT, D], F32, S)
            nfb = sbg("nfb", [P, D], F32, S)
            stashb = sbg("stash", [P, 4, TB], F32, S)
            gm = Rot([sbg("gm%d" % i, [P, TB], F32, S) for i in range(2)])
            m2 = Rot([sbg("m2_%d" % i, [P, TB], F32, S) for i in range(2)])
            st3 = sbg("st3", [P, 8], F32, S)
            T.dma("sp", nfb[:, :], normf.partition_broadcast(P), writes=[nfb], slot="nfb")
            for i in range(NT):
                T.dma("sp", r.t[:, i, :], xo[orow0 + i * P:orow0 + (i + 1) * P, :], writes=[r], slot="rld")
            for cg in range(4):
                plan = ((Wbr, cg * 512, 0), (Wb, OFF_G + cg * 512, 0), (Wbr, cg * 512, D), (Wb, OFF_G + D + cg * 512, 0))
                for half in range(2):
                    wA = wt_rot.next()
                    load_w(plan[2 * half][0], plan[2 * half][1], 512, wA, rows0=plan[2 * half][2])
                    ysrc = y_mT if half == 0 else y_aT
                    psas = []
                    for ci in range(4):
                        psa = psA.next()
                        mm_group(psa.t[:, :], [(wA.t[:, kc, ci * P:(ci + 1) * P], ysrc.t[:, kc, :]) for kc in range(16)], [wA, ysrc], psa)
                        psas.append(psa)
                    wG = wt_rot.next()
                    load_w(plan[2 * half + 1][0], plan[2 * half + 1][1], 512, wG, rows0=plan[2 * half + 1][2])
                    for ci in range(4):
                        c = cg * 4 + ci
                        psa = psas[ci]
                        psg = psA.next()
                        mm_group(psg.t[:, :], [(wG.t[:, kc, ci * P:(ci + 1) * P], hT.t[:, kc, :]) for kc in range(16)], [wG, hT], psg)
                        gb = gm.next()
                        gcol = C_GB + half * 16 + c
                        T.op("act", lambda e, gb=gb, psg=psg, gcol=gcol: e.activation(out=gb[:, :], in_=psg.t[:, :], func=AF.Sigmoid, bias=CST.t[:, gcol:gcol + 1]), reads=[psg, CST], writes=[gb])
                        if half == 0:
                            T.op("dve", lambda e, psa=psa, gb=gb, ci=ci: e.tensor_tensor(out=stashb.t[:, ci, :], in0=psa.t[:, :], in1=gb[:, :], op=ALU.mult), reads=[psa, gb], writes=[stashb])
                        else:
                            mb2 = m2.next()
                            T.op("dve", lambda e, mb2=mb2, psa=psa, gb=gb: e.tensor_tensor(out=mb2[:, :], in0=psa.t[:, :], in1=gb[:, :], op=ALU.mult), reads=[psa, gb], writes=[mb2])
                            T.op("pool", lambda e, mb2=mb2, c=c, ci=ci: e.tensor_tensor(out=mT.t[:, c, :], in0=mb2[:, :], in1=stashb.t[:, ci, :], op=ALU.add), reads=[mb2, stashb], writes=[mT])
            dump("mT%d" % bi, mT.t[:, :, :].rearrange("p a b -> p (a b)"), [P, 16 * TB], BF16, [mT])
            for n in range(4):
                wo = wt_rot.next()
                load_w(Wo, n * 512, 512, wo)
                for i in range(NT):
                    ps = psA.next()
                    mm_group(ps.t[:, :], [(mT.t[:, kc, i * P:(i + 1) * P], wo.t[:, kc, :]) for kc in range(16)], [mT, wo], ps)
                    T.op("dve", lambda e, ps=ps, i=i, n=n: e.tensor_tensor(out=r.t[:, i, n * 512:(n + 1) * 512], in0=ps.t[:, :], in1=r.t[:, i, n * 512:(n + 1) * 512], op=ALU.add), reads=[ps, r], writes=[r])
            for i in range(NT):
                T.op("act", lambda e, i=i: e.activation(out=xs[:], in_=r.t[:, i, :], func=AF.Square, accum_out=st3[:, 0:1]), reads=[r], writes=[xs, st3])
                rstd_from_ssq(st3[:, 0:1], st3[:, 1:2], D, st3)
                T.op("dve", lambda e, i=i: e.scalar_tensor_tensor(out=r.t[:, i, :], in0=r.t[:, i, :], scalar=st3[:, 1:2], in1=nfb[:, :], op0=ALU.mult, op1=ALU.mult), reads=[r, st3, nfb], writes=[r])
                T.dma("pool", out[orow0 + i * P:orow0 + (i + 1) * P, :], r.t[:, i, :], reads=[r], slot="ost")

        for bi in range(NPB + NOB):
            is_own = bi >= NPB
            key0 = bi * TB
            xsrc, row0 = (xo, (bi - NPB) * TB) if is_own else (xp, bi * TB)
            build_hT(xsrc, row0)
            if bi == NPB:
                dump("hT", hT.t[:, :, :].rearrange("p a b -> p (a b)"), [P, 16 * TB], BF16, [hT])
            with ExitStack() as S1:
                y_mT = ssd_block(S1, is_own, bi)
                if is_own:
                    T.dma("pool", YM, y_mT.t[:, :, :].rearrange("p a b -> p (a b)"), reads=[y_mT], slot="ymst")
                if bi == NPB - 1:
                    T.op("dve", lambda e: e.tensor_scalar(out=hs.t[:, :, :], in0=hs.t[:, :, :], scalar1=CST.t[:, C_PV:C_PV + 1], scalar2=None, op0=ALU.mult), reads=[hs, CST], writes=[hs])
                T.barrier()
            if not is_own:
                with ExitStack() as S2:
                    kv_proj(S2, key0, False, None, None, None)
                    T.barrier()
                continue
            with ExitStack() as SO:
                y_aT = sbg("y_aT", [P, 16, TB], BF16, SO)
                with ExitStack() as S2:
                    qT = sbg("qT", [P, 16, TB], BF16, S2)
                    qiT = sbg("qiT", [P, 8, TB], BF16, S2)
                    ws = sbg("ws", [P, NT, 16], F32, S2)
                    with ExitStack() as S2a:
                        kv_proj(S2a, key0, True, qT, qiT, ws)
                        T.barrier()
                    with ExitStack() as S2b:
                        maskT = sbg("maskT", [P, 64, TB], mybir.dt.uint8, S2b)
                        for hb in range(2):
                            dsa_idx(S2b, bi, key0, hb, qiT, ws, maskT)
                        attention_block(S2b, bi, key0, qT, y_aT, maskT)
                        T.barrier()
                dump("yaT%d" % bi, y_aT.t[:, :, :].rearrange("p a b -> p (a b)"), [P, 16 * TB], BF16, [y_aT])
                with ExitStack() as S3:
                    out_block2(S3, bi, (bi - NPB) * TB, y_aT)
                    T.barrier()
        T.barrier()
        T.emit()
    return nc, dbg_out


def make_cst(inputs, half):
    c = np.zeros((P, NCST), np.float32)
    c[:, C_ID:C_ID + 128] = np.eye(128, dtype=np.float32)
    u = np.arange(128)
    c[:, C_L:C_L + 128] = (u[:, None] <= u[None, :]).astype(np.float32)
    c[:, C_U:C_U + 128] = (u[:, None] > u[None, :]).astype(np.float32)
    cw = np.asarray(inputs["conv_w"][0], np.float32)
    c[:, C_CW:C_CW + 96] = cw.T.reshape(24, 128, 4).transpose(1, 0, 2).reshape(128, 96)
    c[:, C_CB:C_CB + 24] = np.asarray(inputs["conv_b"][0], np.float32).reshape(24, 128).T
    c[:, C_GB:C_GB + 32] = np.asarray(inputs["gate_bias"][0], np.float32).reshape(32, 128).T
    c[:, C_G:C_G + 16] = np.asarray(inputs["norm_in"][0], np.float32).reshape(16, 128).T
    c[:, C_GN:C_GN + 16] = np.asarray(inputs["gn_m"][0], np.float32).reshape(16, 128).T
    c[:, C_DTB:C_DTB + 32] = np.asarray(inputs["dt_bias"][0], np.float32)[None, :]
    c[:, C_ALOG:C_ALOG + 32] = np.asarray(inputs["a_log"][0], np.float32)[None, :]
    c[:, C_DSK:C_DSK + 32] = np.asarray(inputs["d_skip"][0], np.float32)[None, :]
    c[:, C_PV] = 1.0 if half == 1 else 0.0
    c[:, C_CVEC:C_CVEC + NITER] = (0.5 ** np.arange(1, NITER + 1, dtype=np.float64)).astype(np.float32)[None, :]
    q = np.arange(128)[:, None]
    k = np.arange(128)[None, :]
    diag = np.where((q < 64) & (k >= 64), NEG, 0.0).astype(np.float32)
    full = np.full((128, 128), NEG, np.float32)
    zero = np.zeros((128, 128), np.float32)
    c[:, C_BP:C_BP + 256] = np.concatenate([diag, full], 1)
    c[:, C_BP + 256:C_BP + 512] = np.concatenate([zero, diag], 1)
    c[0, C_PVROW:C_PVROW + 128] = 0.0 if half == 1 else NEG
    return c


_NC_CACHE = {}


def run(inputs, L, dbg=()):
    B = inputs["x"].shape[0]
    assert B * 2 == 8
    H = L // 2
    NPB = NOB = H // TB
    key = (NPB, NOB, tuple(dbg))
    if key not in _NC_CACHE:
        _NC_CACHE[key] = build(NPB, NOB, dbg)
    nc, dbg_out = _NC_CACHE[key]
    x = np.asarray(inputs["x"], np.float32)
    w_in = np.ascontiguousarray(np.asarray(inputs["w_in"][0], np.float32))
    w_br = np.ascontiguousarray(np.asarray(inputs["w_branch"][0], np.float32))
    w_out = np.ascontiguousarray(np.asarray(inputs["w_out"][0], np.float32))
    normf = np.asarray(inputs["norm_final"], np.float32).reshape(1, D)
    csts = [make_cst(inputs, 0), make_cst(inputs, 1)]
    zeros = np.zeros((H, D), np.float32)
    in_maps = []
    for c in range(8):
        b, half = c // 2, c % 2
        in_maps.append({
            "xp": np.ascontiguousarray(x[b, 0:H]) if half == 1 else zeros,
            "xo": np.ascontiguousarray(x[b, half * H:(half + 1) * H]),
            "w_in": w_in, "w_br": w_br, "w_out": w_out,
            "cst": csts[half], "normf": normf,
        })
    res = run_bass_kernel_spmd(nc, in_maps, core_ids=list(range(8)))
    outp = np.empty((B, L, D), np.float32)
    for c in range(8):
        b, half = c // 2, c % 2
        outp[b, half * H:(half + 1) * H] = res.results[c]["out"]
    return outp, res


def kernel(**inputs):
    L = inputs["x"].shape[1]
    outp, _ = run(inputs, L)
    return outp
```
